# Optimizing a Trainium2 kernel written in Bass

```python
import jax, jax.numpy as jnp
from jax import lax
import numpy as np

D_MODEL = 2048
BATCH = 2
SEQ = 4096
DEPTH = 1

D_MIX = D_MODEL
RET_HEADS = 8
RET_DK = 128
RET_DV = 128
RET_QK_WIDTH = RET_HEADS * RET_DK
RET_WIDTH = RET_HEADS * RET_DV
RET_CHUNK = 128
SWA_HEADS = 16
SWA_KV_HEADS = 2
SWA_HEAD_DIM = 64
SWA_WIDTH = SWA_HEADS * SWA_HEAD_DIM
SWA_KV_WIDTH = SWA_KV_HEADS * SWA_HEAD_DIM
WINDOW = 128
MEM_LEN = 256
XA_HEADS = 4
XA_HEAD_DIM = D_MODEL // XA_HEADS
D_FF = 5632
ROPE_THETA = 10000.0
EPS = 1e-6

IN_SIZES = [RET_QK_WIDTH, RET_QK_WIDTH, RET_WIDTH, RET_WIDTH, SWA_WIDTH, SWA_KV_WIDTH, SWA_KV_WIDTH]
IN_COLS = sum(IN_SIZES)
IN_SPLITS = [int(v) for v in np.cumsum(IN_SIZES)[:-1]]

kernel_name = "hymba_retention_swa_macaron_block"


def rms_norm(x, g):
    xf = x.astype(jnp.float32)
    y = xf * lax.rsqrt(jnp.mean(xf * xf, axis=-1, keepdims=True) + EPS)
    return (y * g.astype(jnp.float32)).astype(x.dtype)


def rope(x, pos):
    d = x.shape[-1]
    inv_freq = ROPE_THETA ** (-jnp.arange(0, d, 2, dtype=jnp.float32) / d)
    ang = pos[:, None] * inv_freq[None, :]
    cos = jnp.cos(ang)[None, :, None, :]
    sin = jnp.sin(ang)[None, :, None, :]
    xf = x.astype(jnp.float32)
    x1, x2 = xf[..., : d // 2], xf[..., d // 2:]
    out = jnp.concatenate([x1 * cos - x2 * sin, x2 * cos + x1 * sin], axis=-1)
    return out.astype(x.dtype)


def swiglu(x, w_gate, w_up, w_down):
    return (jax.nn.silu(x @ w_gate) * (x @ w_up)) @ w_down


def retention(q, k, v):
    B, S, H, dk = q.shape
    dv = v.shape[-1]
    C = RET_CHUNK
    N = S // C
    f32 = jnp.float32
    log_gamma = jnp.log1p(-jnp.exp2(-5.0 - jnp.arange(H, dtype=f32)))
    qc = q.astype(f32).reshape(B, N, C, H, dk)
    kc = (k.astype(f32) * (dk ** -0.5)).reshape(B, N, C, H, dk)
    vc = v.astype(f32).reshape(B, N, C, H, dv)
    idx = jnp.arange(C, dtype=f32)
    diff = idx[:, None] - idx[None, :]
    dmat = jnp.where(diff[None] >= 0,
                     jnp.exp(jnp.maximum(diff, 0.0)[None] * log_gamma[:, None, None]),
                     0.0)
    s = jnp.einsum('bnchd,bnjhd->bnhcj', qc, kc) * dmat[None, None]
    intra = jnp.einsum('bnhcj,bnjhe->bnche', s, vc)
    zeta = jnp.exp((C - 1.0 - idx)[None, :] * log_gamma[:, None])
    kv = jnp.einsum('bnjhd,hj,bnjhe->nbhde', kc, zeta, vc)
    chunk_decay = jnp.exp(C * log_gamma)[None, :, None, None]

    def step(state, kv_n):
        return state * chunk_decay + kv_n, state

    _, prev_states = lax.scan(step, jnp.zeros((B, H, dk, dv), f32), kv)
    xi = jnp.exp((idx + 1.0)[None, :] * log_gamma[:, None])
    cross = jnp.einsum('bnchd,hc,nbhde->bnche', qc, xi, prev_states)
    return (intra + cross).reshape(B, S, H, dv)


def sliding_window_attention(q, k, v, sinks):
    B, S, Hq, d = q.shape
    Hkv = k.shape[2]
    G = Hq // Hkv
    W = WINDOW
    N = S // W
    qb = q.reshape(B, N, W, Hkv, G, d)
    kb = k.reshape(B, N, W, Hkv, d)
    vb = v.reshape(B, N, W, Hkv, d)
    kk = jnp.concatenate([jnp.concatenate([jnp.zeros_like(kb[:, :1]), kb[:, :-1]], axis=1), kb], axis=2)
    vv = jnp.concatenate([jnp.concatenate([jnp.zeros_like(vb[:, :1]), vb[:, :-1]], axis=1), vb], axis=2)
    s = jnp.einsum('bnqhgd,bnkhd->bnhgqk', qb, kk).astype(jnp.float32) * (d ** -0.5)
    qi = jnp.arange(W)[:, None] + W
    ki = jnp.arange(2 * W)[None, :]
    rel = qi - ki
    band = (rel >= 0) & (rel < W)
    valid = band[None] & ((jnp.arange(N)[:, None, None] > 0) | (ki[None] >= W))
    s = jnp.where(valid[None, :, None, None], s, jnp.finfo(jnp.float32).min)
    sink = sinks.astype(jnp.float32).reshape(Hkv, G)[None, None, :, :, None, None]
    m = jnp.maximum(jnp.max(s, axis=-1, keepdims=True), sink)
    p = jnp.exp(s - m)
    probs = p / (jnp.sum(p, axis=-1, keepdims=True) + jnp.exp(sink - m))
    o = jnp.einsum('bnhgqk,bnkhd->bnqhgd', probs.astype(v.dtype), vv)
    return o.reshape(B, S, Hq * d)


def memory_cross_attention(hn, memn, wq, wkv, wo):
    B, S, D = hn.shape
    M = memn.shape[1]
    q = (hn @ wq).reshape(B, S, XA_HEADS, XA_HEAD_DIM)
    k, v = jnp.split(memn @ wkv, 2, axis=-1)
    k = k.reshape(B, M, XA_HEADS, XA_HEAD_DIM)
    v = v.reshape(B, M, XA_HEADS, XA_HEAD_DIM)
    s = jnp.einsum('bshd,bmhd->bhsm', q, k).astype(jnp.float32) * (XA_HEAD_DIM ** -0.5)
    p = jax.nn.softmax(s, axis=-1)
    o = jnp.einsum('bhsm,bmhd->bshd', p.astype(v.dtype), v).reshape(B, S, D)
    return o @ wo


def setup_inputs(seed: int = 0) -> dict:
    key = jax.random.key(seed)
    ks = jax.random.split(key, 24)
    f32 = jnp.float32
    L, D = DEPTH, D_MODEL

    def w(k, shape, fan_in):
        return jax.random.normal(k, shape, f32) * (fan_in ** -0.5)

    def gain(k, shape):
        return 1.0 + 0.02 * jax.random.normal(k, shape, f32)

    return {
        "x": jax.random.normal(ks[0], (BATCH, SEQ, D), f32),
        "mem": jax.random.normal(ks[1], (BATCH, MEM_LEN, D), f32),
        "ffn1_norm": gain(ks[2], (L, D)),
        "ffn1_w_gate": w(ks[3], (L, D, D_FF), D),
        "ffn1_w_up": w(ks[4], (L, D, D_FF), D),
        "ffn1_w_down": w(ks[5], (L, D_FF, D), D_FF),
        "mix_norm": gain(ks[6], (L, D)),
        "w_in": w(ks[7], (L, D, IN_COLS), D),
        "ret_gn_gain": gain(ks[8], (L, RET_WIDTH)),
        "swa_sinks": 0.5 * jax.random.normal(ks[9], (L, SWA_HEADS), f32),
        "w_out": w(ks[10], (L, D_MIX, D), D_MIX),
        "xa_norm": gain(ks[11], (L, D)),
        "mem_norm": gain(ks[12], (L, D)),
        "xa_wq": w(ks[13], (L, D, D), D),
        "xa_wkv": w(ks[14], (L, D, 2 * D), D),
        "xa_wo": w(ks[15], (L, D, D), D),
        "ffn2_norm": gain(ks[16], (L, D)),
        "ffn2_w_gate": w(ks[17], (L, D, D_FF), D),
        "ffn2_w_up": w(ks[18], (L, D, D_FF), D),
        "ffn2_w_down": w(ks[19], (L, D_FF, D), D_FF),
        "final_norm": gain(ks[20], (D,)),
    }


def reference(x, mem, ffn1_norm, ffn1_w_gate, ffn1_w_up, ffn1_w_down, mix_norm, w_in, ret_gn_gain,
              swa_sinks, w_out, xa_norm, mem_norm, xa_wq, xa_wkv, xa_wo, ffn2_norm, ffn2_w_gate,
              ffn2_w_up, ffn2_w_down, final_norm):
    B, S, _ = x.shape
    pos = jnp.arange(S, dtype=jnp.float32)
    h = x
    for l in range(DEPTH):
        h = h + 0.5 * swiglu(rms_norm(h, ffn1_norm[l]), ffn1_w_gate[l], ffn1_w_up[l], ffn1_w_down[l])

        n = rms_norm(h, mix_norm[l])
        rq, rk, rv, rg, sq, sk, sv = jnp.split(n @ w_in[l], IN_SPLITS, axis=-1)

        rq = rope(rq.reshape(B, S, RET_HEADS, RET_DK), pos)
        rk = rope(rk.reshape(B, S, RET_HEADS, RET_DK), pos)
        ret = retention(rq, rk, rv.reshape(B, S, RET_HEADS, RET_DV))
        mu = jnp.mean(ret, axis=-1, keepdims=True)
        var = jnp.mean(jnp.square(ret - mu), axis=-1, keepdims=True)
        ret = (ret - mu) * lax.rsqrt(var + EPS) * ret_gn_gain[l].astype(jnp.float32).reshape(RET_HEADS, RET_DV)
        ret = (jax.nn.silu(rg.astype(jnp.float32)) * ret.reshape(B, S, RET_WIDTH)).astype(h.dtype)

        sq = rope(sq.reshape(B, S, SWA_HEADS, SWA_HEAD_DIM), pos)
        sk = rope(sk.reshape(B, S, SWA_KV_HEADS, SWA_HEAD_DIM), pos)
        swa = sliding_window_attention(sq, sk, sv.reshape(B, S, SWA_KV_HEADS, SWA_HEAD_DIM), swa_sinks[l])

        h = h + jnp.concatenate([ret, swa.astype(h.dtype)], axis=-1) @ w_out[l]

        h = h + memory_cross_attention(rms_norm(h, xa_norm[l]), rms_norm(mem, mem_norm[l]),
                                       xa_wq[l], xa_wkv[l], xa_wo[l])

        h = h + 0.5 * swiglu(rms_norm(h, ffn2_norm[l]), ffn2_w_gate[l], ffn2_w_up[l], ffn2_w_down[l])
    return rms_norm(h, final_norm)
```

```python
import numpy as np
from contextlib import ExitStack
import concourse.bass as bass
import concourse.mybir as mybir
from concourse.bass_utils import run_bass_kernel_spmd

F32 = mybir.dt.float32
BF16 = mybir.dt.bfloat16
ALU = mybir.AluOpType
AF = mybir.ActivationFunctionType
AX = mybir.AxisListType

D = 2048
KC = 16
T = 1024
TH = 512
DFF = 5632
FC = 44
EPS = 1e-6
NCORES = 8

STAGES = ("ffn1", "mixer", "xattn", "ffn2", "final")
GAMMA = [1.0 - 2.0 ** (-5 - h) for h in range(8)]
MC_DMAT, MC_XI, MC_ZETA, MC_GNG, MC_SINK, MC_MASK = 0, 1024, 2048, 2056, 2064, 2080
MIXC_W = 2336
NEG = -30000.0
DEBUG_A = False


class Cell:
    __slots__ = ("name", "space", "off", "size", "last_w", "readers", "ov")

    def __init__(self, name, space, off=0, size=0):
        self.name, self.space, self.off, self.size = name, space, off, size
        self.last_w = []
        self.readers = {}
        self.ov = []


class Op:
    __slots__ = ("eng", "fn", "deps", "pos", "signal", "tick", "is_dma", "sem", "val", "waits", "gidx", "group", "own")

    def __init__(self, eng, fn, is_dma):
        self.eng, self.fn, self.is_dma = eng, fn, is_dma
        self.deps = []
        self.signal = False
        self.tick = None
        self.sem = None
        self.val = None
        self.waits = []


class Sched:
    ENGS = ["sync", "gpsimd", "scalar", "vector", "tensor"]
    NDQ = 8

    def __init__(self):
        self.ops = []
        self.sb_cells = []
        self.count = {e: 0 for e in self.ENGS}

    def sb_cell(self, name, off, size):
        c = Cell(name, "sb", off, size)
        for o in self.sb_cells:
            if o.off < off + size and off < o.off + o.size:
                o.ov.append(c)
                c.ov.append(o)
        self.sb_cells.append(c)
        return c

    def cell(self, name):
        return Cell(name, "x")

    def add(self, eng, meth, kw, reads=(), writes=(), dma=False, after=(), group=None, own_sem=False):
        op = Op(eng, (meth, kw), dma)
        op.group = group
        op.own = own_sem
        op.pos = self.count[eng]
        self.count[eng] += 1
        op.gidx = len(self.ops)
        deps = {}

        def dep(o):
            if o is not None and o is not op and not (group is not None and o.group == group):
                deps[id(o)] = o

        for o in after:
            dep(o)
        for c in reads:
            for y in [c] + c.ov:
                for w in y.last_w:
                    dep(w)
        for c in writes:
            for y in [c] + c.ov:
                for w in y.last_w:
                    dep(w)
                for r in y.readers.values():
                    dep(r)
        for c in reads:
            for y in [c] + c.ov:
                key = ("d", op.gidx) if dma else eng
                y.readers[key] = op
        for c in writes:
            for y in [c] + c.ov:
                if group is not None and y.last_w and y.last_w[0].group == group:
                    y.last_w.append(op)
                else:
                    y.last_w = [op]
                y.readers = {}
        best = {}
        for o in deps.values():
            if o.is_dma or o.own:
                best[("d", id(o))] = o
            else:
                if eng == "tensor" and o.eng == "tensor":
                    continue
                k = o.eng
                if k not in best or best[k].pos < o.pos:
                    best[k] = o
        op.deps = list(best.values())
        for o in op.deps:
            o.signal = True
        self.ops.append(op)
        return op

    def emit(self, nc, es):
        sems = {e: es.enter_context(nc.semaphore("c_" + e)) for e in self.ENGS}
        dq = {e: [es.enter_context(nc.semaphore(f"dq_{e}_{i}")) for i in range(self.NDQ)]
              for e in ("sync", "gpsimd", "scalar")}
        streams = {e: [] for e in self.ENGS}
        ticks = {e: 0 for e in self.ENGS}
        ndma = {e: 0 for e in self.ENGS}
        for op in self.ops:
            streams[op.eng].append(op)
            if op.is_dma:
                i = ndma[op.eng]
                ndma[op.eng] += 1
                op.sem = dq[op.eng][i % self.NDQ]
                op.val = 16 * (i // self.NDQ + 1)
                op.signal = True
            elif op.own:
                op.sem = es.enter_context(nc.semaphore(f"own_{op.gidx}"))
                op.val = 1
                op.signal = True
            elif op.signal:
                ticks[op.eng] += 1
                op.sem = sems[op.eng]
                op.val = ticks[op.eng]
        for e in self.ENGS:
            seen = {}
            for op in streams[e]:
                w = []
                if op.is_dma and op.val > 16:
                    w.append((op.sem, op.val - 16))
                for d in op.deps:
                    w.append((d.sem, d.val))
                for (s, v) in w:
                    if seen.get(id(s), 0) >= v:
                        continue
                    seen[id(s)] = v
                    op.waits.append((s, v))
        block = es.enter_context(nc.Block())

        def run(e, stream):
            for op in stream:
                for (s, v) in op.waits:
                    e.wait_ge(s, v)
                ins = getattr(e, op.fn[0])(**op.fn[1])
                if op.signal:
                    ins.then_inc(op.sem, 16 if op.is_dma else 1)

        @block.sync
        def _(e):
            run(e, streams["sync"])

        @block.gpsimd
        def _(e):
            run(e, streams["gpsimd"])

        @block.scalar
        def _(e):
            run(e, streams["scalar"])

        @block.vector
        def _(e):
            run(e, streams["vector"])

        @block.tensor
        def _(e):
            run(e, streams["tensor"])


class SbT:
    def __init__(self, S, arena, name, off, ncell, clen, dt):
        self.esz = 4 if dt == F32 else 2
        self.ncell, self.clen, self.dt = ncell, clen, dt
        self.off = off
        self.nbytes = ncell * clen * self.esz
        assert off % 4 == 0 and self.nbytes % 4 == 0
        w0, w1 = off // 4, (off + self.nbytes) // 4
        v = arena[:, w0:w1]
        if dt != F32:
            v = v.bitcast(dt)
        self.flat = v
        self.v = v.rearrange("p (a b) -> p a b", b=clen)
        self.cells = [S.sb_cell(f"{name}{i}", off + i * clen * self.esz, clen * self.esz) for i in range(ncell)]

    def ap(self, i):
        return self.v[:, i, :]

    def c(self, i):
        return self.cells[i]


class Arena:
    def __init__(self, S, arena, total):
        self.S, self.arena, self.total = S, arena, total
        self.top = 0

    def alloc(self, name, ncell, clen, dt, at=None):
        esz = 4 if dt == F32 else 2
        nb = ncell * clen * esz
        nb4 = (nb + 31) // 32 * 32
        if at is None:
            at = self.top
            self.top += nb4
        assert at + nb <= self.total, (name, at, nb, self.total)
        return SbT(self.S, self.arena, name, at, ncell, clen, dt)


ARENA_BYTES = 207 * 1024


class Pool:
    def __init__(self, items):
        self.free = list(items)

    def get(self):
        return self.free.pop(0) if self.free else None

    def put(self, x):
        self.free.append(x)


def run_tasks(tasks, width):
    done, started, active = set(), set(), []
    while len(done) < len(tasks):
        for i in range(len(tasks)):
            if len(active) >= width:
                break
            if i in started:
                continue
            if all(d in done for d in tasks[i][1]):
                started.add(i)
                active.append((i, tasks[i][0]()))
        assert active, "task deadlock"
        for (i, g) in list(active):
            try:
                next(g)
            except StopIteration:
                active.remove((i, g))
                done.add(i)


def build_program(stages=("ffn1", "mixer", "xattn", "ffn2", "final")):
    stages = set(stages)
    nc = bass.Bass("TRN2", target_bir_lowering=False)
    S = Sched()
    es = ExitStack()

    def din(name, shape, dt=F32):
        return nc.dram_tensor(name, shape, dt, kind="ExternalInput").ap()

    xT = din("xT", [D, T])
    gains = din("gains", [128, 6 * KC])
    if "ffn1" in stages:
        wg1 = din("ffn1_w_gate", [D, DFF]); wu1 = din("ffn1_w_up", [D, DFF]); wd1 = din("ffn1_w_down", [DFF, D])
    if "ffn2" in stages:
        wg2 = din("ffn2_w_gate", [D, DFF]); wu2 = din("ffn2_w_up", [D, DFF]); wd2 = din("ffn2_w_down", [DFF, D])
    if "mixer" in stages:
        w_in = din("w_in_p", [D, 42 * 128])
        w_out = din("w_out_p", [D, D])
        ropeR_d = din("ropeR", [128, 2 * T])
        ropeS_d = din("ropeS", [128, 2 * T])
        mixc_d = din("mixc", [128, MIXC_W])
        coef_d = din("coef", [128, 40])
        mask0_d = din("mask0", [128, 256])
        ident_d = din("ident", [128, 128])
        hspill = nc.dram_tensor("hspill", [D, T], F32).ap()
        send_t = nc.dram_tensor("xsend", [10 * 128, 128], F32)
        recv_t = nc.dram_tensor("xrecv", [4 * 10 * 128, 128], F32)
    if "xattn" in stages:
        memT = din("memT", [D, 256])
        wq_d = din("xa_wq", [D, D]); wkv_d = din("xa_wkv", [D, 2 * D]); wo_d = din("xa_wo", [D, D])
        ident_d2 = ident_d if "mixer" in stages else din("ident", [128, 128])
    outT = nc.dram_tensor("outT", [D, T], F32, kind="ExternalOutput").ap()

    arena_t = es.enter_context(nc.sbuf_tensor("arena", [128, ARENA_BYTES // 4], F32))
    A = Arena(S, arena_t, ARENA_BYTES)
    psum_all = es.enter_context(nc.psum_tensor("psall", [128, 8 * 512], F32))
    psum = [psum_all[:, i * 512:(i + 1) * 512] for i in range(8)]
    pcell = [S.cell(f"ps{i}") for i in range(8)]

    hT = A.alloc("hT", KC * 2, TH, F32)
    nT = A.alloc("nT", KC * 2, TH, BF16)
    wslot = [A.alloc(f"ws{i}", 2, 4096, BF16) for i in range(2)]
    gn = A.alloc("gn", 1, 6 * KC, F32)
    ones_b = A.alloc("ones_b", 1, 128, BF16)
    ones_f = A.alloc("ones_f", 1, 128, F32)
    identb = A.alloc("identb", 1, 128, BF16)
    epsc = A.alloc("epsc", 1, 8, F32)
    phase_base = A.top

    psi = [0]

    def next_ps():
        i = psi[0] % 6
        psi[0] += 1
        return i

    def next_ps_pair():
        if psi[0] % 2:
            psi[0] += 1
        i = psi[0] % 6
        psi[0] += 2
        return i

    hpsi = [0]

    def hold_ps():
        i = 6 + hpsi[0] % 2
        hpsi[0] += 1
        return i

    wsi = [0]

    def next_wcell():
        i = wsi[0] % 4
        wsi[0] += 1
        return wslot[i // 2], i % 2

    def next_ws():
        if wsi[0] % 2:
            wsi[0] += 1
        i = (wsi[0] // 2) % 2
        wsi[0] += 2
        return wslot[i]

    def H(kc, half):
        return kc * 2 + half

    gid = [0]

    def wdma(dst3, src2, nk, cells):
        gid[0] += 1
        for k0 in range(0, nk, 4):
            k1 = min(nk, k0 + 4)
            S.add("gpsimd", "dma_start", dict(out=dst3[:, k0:k1, :],
                                               in_=src2[k0 * 128:k1 * 128, :].rearrange("(k p) c -> p k c", p=128)),
                  writes=cells, dma=True, group=gid[0])

    def V(eng, meth, reads, writes, **kw):
        return S.add(eng, meth, kw, reads=reads, writes=writes)

    def MM(out, lhsT, rhs, start, stop, reads, writes):
        return S.add("tensor", "matmul", dict(out=out, lhsT=lhsT, rhs=rhs, start=start, stop=stop), reads=reads, writes=writes)

    def LD(out, in_, writes, reads=(), eng="sync"):
        return S.add(eng, "dma_start", dict(out=out, in_=in_), reads=reads, writes=writes, dma=True)

    V("vector", "memset", [], [ones_b.c(0)], ap=ones_b.ap(0), constant=1.0)
    V("vector", "memset", [], [ones_f.c(0)], ap=ones_f.ap(0), constant=1.0)
    V("vector", "memset", [], [epsc.c(0)], ap=epsc.ap(0), constant=EPS)
    LD(gn.ap(0), gains, [gn.c(0)])
    for kc in range(KC):
        for half in range(2):
            LD(hT.ap(H(kc, half)), xT[kc * 128:(kc + 1) * 128, half * TH:(half + 1) * TH], [hT.c(H(kc, half))])

    NRM = Arena(S, arena_t, ARENA_BYTES)
    NRM.top = ARENA_BYTES - 16 * 1024
    n_ot = NRM.alloc("n_ot", 4, TH, F32)
    n_sq = NRM.alloc("n_sq", 4, TH, BF16)
    n_rstd = NRM.alloc("n_rstd", 2, TH, F32)
    final_dmas = []

    def norm_half_gen(gi, half, to_out=False):
        sq, rstd, ot = n_sq, n_rstd, n_ot
        pb = next_ps()
        for kc in range(KC):
            s = (kc % 2) + 2 * half
            hc = H(kc, half)
            if kc % 2 == 0:
                V("scalar", "activation", [hT.c(hc)], [sq.c(s)], out=sq.ap(s), in_=hT.ap(hc), func=AF.Square)
            else:
                V("vector", "tensor_tensor", [hT.c(hc)], [sq.c(s)], out=sq.ap(s), in0=hT.ap(hc), in1=hT.ap(hc), op=ALU.mult)
            MM(psum[pb][:, :], ones_b.ap(0), sq.ap(s), kc == 0, kc == KC - 1, [sq.c(s), ones_b.c(0)], [pcell[pb]])
            if kc % 4 == 3:
                yield
        V("vector", "tensor_scalar", [pcell[pb]], [rstd.c(half)], out=rstd.ap(half), in0=psum[pb][:, :],
          scalar1=1.0 / D, scalar2=EPS, op0=ALU.mult, op1=ALU.add)
        yield
        V("scalar", "activation", [rstd.c(half)], [rstd.c(half)], out=rstd.ap(half), in_=rstd.ap(half), func=AF.Sqrt)
        yield
        V("vector", "reciprocal", [rstd.c(half)], [rstd.c(half)], out=rstd.ap(half), in_=rstd.ap(half))
        yield
        for kc in range(KC):
            hc = H(kc, half)
            gsc = gn.ap(0)[:, gi * KC + kc:gi * KC + kc + 1]
            eng = "vector"
            if not to_out:
                V(eng, "scalar_tensor_tensor", [hT.c(hc), gn.c(0), rstd.c(half)], [nT.c(hc)],
                  out=nT.ap(hc), in0=hT.ap(hc), scalar=gsc, in1=rstd.ap(half), op0=ALU.mult, op1=ALU.mult)
            else:
                o = (kc % 2) + 2 * half
                V(eng, "scalar_tensor_tensor", [hT.c(hc), gn.c(0), rstd.c(half)], [ot.c(o)],
                  out=ot.ap(o), in0=hT.ap(hc), scalar=gsc, in1=rstd.ap(half), op0=ALU.mult, op1=ALU.mult)
                final_dmas.append(LD(outT[kc * 128:(kc + 1) * 128, half * TH:(half + 1) * TH], ot.ap(o), [], reads=[ot.c(o)]))
            if kc % 4 == 3:
                yield

    def norm_half(gi, half, to_out=False):
        for _ in norm_half_gen(gi, half, to_out):
            pass

    def norm_both(gi, to_out=False):
        run_tasks([((lambda h=h: norm_half_gen(gi, h, to_out)), []) for h in range(2)], 2)

    def rmsnorm(gi, tmp_base=None, to_out=False):
        norm_both(gi, to_out)

    def ffn(wg, wu, wd, tmp_base, after_half=None):
        At = Arena(S, arena_t, ARENA_BYTES)
        At.top = tmp_base
        act = At.alloc("act", 22 * 2, TH, BF16)
        sg = At.alloc("sg", 2, TH, F32)
        sgi = 0
        for ffh in range(2):
            for p in range(11):
                f0 = (ffh * 22 + 2 * p) * 128
                wt = next_ws()
                wv = wt.flat.rearrange("p (g k c) -> p g k c", g=2, k=KC)
                wdma(wv[:, 0], wg[:, f0:f0 + 256], KC, [wt.c(0)])
                wdma(wv[:, 1], wu[:, f0:f0 + 256], KC, [wt.c(1)])
                for j in range(2):
                    fl = 2 * p + j
                    for half in range(2):
                        pg, pu = next_ps(), next_ps()
                        for kc in range(KC):
                            MM(psum[pg][:, :], wv[:, 0, kc, j * 128:(j + 1) * 128], nT.ap(H(kc, half)), kc == 0, kc == KC - 1,
                               [wt.c(0), nT.c(H(kc, half))], [pcell[pg]])
                        for kc in range(KC):
                            MM(psum[pu][:, :], wv[:, 1, kc, j * 128:(j + 1) * 128], nT.ap(H(kc, half)), kc == 0, kc == KC - 1,
                               [wt.c(1), nT.c(H(kc, half))], [pcell[pu]])
                        si = sgi % 2
                        sgi += 1
                        V("scalar", "activation", [pcell[pg]], [sg.c(si)], out=sg.ap(si), in_=psum[pg][:, :], func=AF.Silu)
                        V("vector", "tensor_tensor", [pcell[pu], sg.c(si)], [act.c(fl * 2 + half)],
                          out=act.ap(fl * 2 + half), in0=psum[pu][:, :], in1=sg.ap(si), op=ALU.mult)
            halfsets = [(0, 1)]
            for hs in halfsets:
              for dblk in range(8):
                wt = next_ws()
                wv = wt.flat[:, 0:22 * 256].rearrange("p (k c) -> p k c", k=22)
                r0 = ffh * 22 * 128
                wdma(wv, wd[r0:r0 + 22 * 128, dblk * 256:(dblk + 1) * 256], 22, [wt.c(0), wt.c(1)])
                for j in range(2):
                    dc = dblk * 2 + j
                    for half in hs:
                        pb = next_ps()
                        for fl in range(22):
                            MM(psum[pb][:, :], wv[:, fl, j * 128:(j + 1) * 128], act.ap(fl * 2 + half), fl == 0, fl == 21,
                               [wt.c(0), wt.c(1), act.c(fl * 2 + half)], [pcell[pb]])
                        V("vector", "scalar_tensor_tensor", [pcell[pb], hT.c(H(dc, half))], [hT.c(H(dc, half))],
                          out=hT.ap(H(dc, half)), in0=psum[pb][:, :], scalar=0.5, in1=hT.ap(H(dc, half)), op0=ALU.mult, op1=ALU.add)
            if ffh == 1 and after_half is not None:
                after_half(None)

    def proj_residual(w_d, src, reload=None, after_half=None):
        for dblk in range(8):
            wt, ci = next_wcell()
            wv = wt.v[:, ci, :].rearrange("p (k c) -> p k c", k=KC)
            wdma(wv, w_d[:, dblk * 256:(dblk + 1) * 256], KC, [wt.c(ci)])
            for j in range(2):
                dc = dblk * 2 + j
                for half in range(2):
                    pb = next_ps()
                    for ac in range(KC):
                        MM(psum[pb][:, :], wv[:, ac, j * 128:(j + 1) * 128], src.ap(H(ac, half)), ac == 0, ac == KC - 1,
                           [wt.c(ci), src.c(H(ac, half))], [pcell[pb]])
                    hc = H(dc, half)
                    if reload is not None:
                        LD(hT.ap(hc), reload[dc * 128:(dc + 1) * 128, half * TH:(half + 1) * TH], [hT.c(hc)])
                    V("vector", "tensor_tensor", [pcell[pb], hT.c(hc)], [hT.c(hc)],
                      out=hT.ap(hc), in0=psum[pb][:, :], in1=hT.ap(hc), op=ALU.add)
        if after_half is not None:
            after_half(None)

    def rope(pb, tab, half, dh, out_ap, out_cell, t1, t2):
        V("vector", "tensor_tensor", [pcell[pb], tab.c(half)], [t1.c(0)], out=t1.ap(0), in0=psum[pb][:, :], in1=tab.ap(half), op=ALU.mult)
        hd = dh // 2
        for base in range(0, 128, dh):
            for (dst, src) in ((base, base + hd), (base + hd, base)):
                V("vector", "tensor_tensor", [pcell[pb], tab.c(2 + half)], [t2.c(0)],
                  out=t2.ap(0)[dst:dst + hd, :], in0=psum[pb][src:src + hd, :], in1=tab.ap(2 + half)[dst:dst + hd, :], op=ALU.mult)
        if isinstance(out_ap, tuple):
            for (lo, oap) in ((0, out_ap[0]), (64, out_ap[1])):
                V("gpsimd", "tensor_tensor", [t1.c(0), t2.c(0)], out_cell, out=oap[lo:lo + 64, :], in0=t1.ap(0)[lo:lo + 64, :], in1=t2.ap(0)[lo:lo + 64, :], op=ALU.add)
        else:
            V("gpsimd", "tensor_tensor", [t1.c(0), t2.c(0)], out_cell if isinstance(out_cell, list) else [out_cell], out=out_ap, in0=t1.ap(0), in1=t2.ap(0), op=ALU.add)

    def mixer(pre_normed=False):
        if not pre_normed:
            rmsnorm(1)
        for kc in range(KC):
            for half in range(2):
                S.add("sync", "dma_start", dict(out=hspill[kc * 128:(kc + 1) * 128, half * TH:(half + 1) * TH], in_=hT.ap(H(kc, half))),
                      reads=[hT.c(H(kc, half))], writes=[hsp_cell[H(kc, half)]], dma=True)
        R = Arena(S, arena_t, ARENA_BYTES)
        R.top = hT.off
        kT = R.alloc("kT", 16, TH, BF16)
        vtok = R.alloc("vtok", 64, 128, BF16)
        Sloc = R.alloc("Sloc", 64, 128, BF16)
        ropeR = R.alloc("ropeR", 4, TH, F32)
        ropeS = R.alloc("ropeS", 4, TH, F32)
        assert R.top <= hT.off + hT.nbytes
        P = Arena(S, arena_t, ARENA_BYTES)
        P.top = phase_base
        aT = P.alloc("aT", 32, TH, BF16)
        mixc = P.alloc("mixc", 1, MIXC_W, F32)
        coef = P.alloc("coef", 1, 40, F32)
        mask0 = P.alloc("mask0", 1, 256, F32)
        skA = P.alloc("skA", 9, 128, BF16)
        skB = P.alloc("skB", 9, 128, BF16)
        svt = P.alloc("svt", 9, 128, BF16)
        t1 = P.alloc("t1", 1, TH, F32)
        t2 = P.alloc("t2", 1, TH, F32)
        common_top = P.top
        mc = mixc.ap(0)
        dmat = mc[:, MC_DMAT:MC_DMAT + 1024].rearrange("p (h c) -> p h c", h=8)
        xi = mc[:, MC_XI:MC_XI + 1024].rearrange("p (h c) -> p h c", h=8)
        zeta = mc[:, MC_ZETA:MC_ZETA + 8]
        gng = mc[:, MC_GNG:MC_GNG + 8]
        sinks = mc[:, MC_SINK:MC_SINK + 16]
        maskg = mc[:, MC_MASK:MC_MASK + 256]
        V("vector", "memset", [], [skA.c(i) for i in range(9)], ap=skA.flat, constant=0.0)
        V("vector", "memset", [], [skB.c(i) for i in range(9)], ap=skB.flat, constant=0.0)
        LD(mixc.ap(0), mixc_d, [mixc.c(0)])
        LD(coef.ap(0), coef_d, [coef.c(0)])
        LD(mask0.ap(0), mask0_d, [mask0.c(0)])
        for i in range(4):
            LD(ropeR.ap(i), ropeR_d[:, i * TH:(i + 1) * TH], [ropeR.c(i)])
            LD(ropeS.ap(i), ropeS_d[:, i * TH:(i + 1) * TH], [ropeS.c(i)])
        S.add("gpsimd", "dma_start", dict(out=identb.ap(0), in_=ident_d), writes=[identb.c(0)], dma=True)

        def wblock(b):
            wt, ci = next_wcell()
            wv = wt.v[:, ci, :].rearrange("p (k c) -> p k c", k=KC)
            wdma(wv, w_in[:, b * 256:(b + 1) * 256], KC, [wt.c(ci)])
            return wv, wt.c(ci)

        def proj_fm(wv, wc, j, half):
            pb = next_ps()
            for kc in range(KC):
                MM(psum[pb][:, :], wv[:, kc, j * 128:(j + 1) * 128], nT.ap(H(kc, half)), kc == 0, kc == KC - 1,
                   [wc, nT.c(H(kc, half))], [pcell[pb]])
            return pb

        P1 = Arena(S, arena_t, ARENA_BYTES)
        P1.top = common_top
        kz = P1.alloc("kz", 2, 1024, BF16)
        Rst = P1.alloc("Rst", 2, 128, F32)
        Lst = P1.alloc("Lst", 10, 128, F32)
        for hp in range(4):
            wk, wkc = wblock(hp)
            wvv, wvc = wblock(4 + hp)
            for j in range(2):
                h = 2 * hp + j
                for half in range(2):
                    pb = proj_fm(wk, wkc, j, half)
                    rope(pb, ropeR, half, 128, kT.ap(h * 2 + half), kT.c(h * 2 + half), t1, t2)
            for n in range(8):
                pb = next_ps()
                half, o = n // 4, (n % 4) * 128
                for kc in range(KC):
                    MM(psum[pb][:, 0:256], nT.ap(H(kc, half))[:, o:o + 128], wvv[:, kc, :], kc == 0, kc == KC - 1,
                       [wvc, nT.c(H(kc, half))], [pcell[pb]])
                for j in range(2):
                    h = 2 * hp + j
                    V("scalar", "activation", [pcell[pb]], [vtok.c(n * 8 + h)], out=vtok.ap(n * 8 + h), in_=psum[pb][:, j * 128:(j + 1) * 128], func=AF.Copy)
            for j in range(2):
                h = 2 * hp + j
                gC = float(GAMMA[h] ** 128)
                pb = next_ps()
                pbv = psum[pb][:, :].bitcast(BF16)
                for n in range(8):
                    half, o = n // 4, (n % 4) * 128
                    S.add("tensor", "transpose", dict(out=pbv[:, n * 128:(n + 1) * 128], in_=kT.ap(h * 2 + half)[:, o:o + 128], identity=identb.ap(0)),
                          reads=[kT.c(h * 2 + half), identb.c(0)], writes=[pcell[pb]])
                kzi = h % 2
                V("vector", "tensor_scalar", [pcell[pb], mixc.c(0)], [kz.c(kzi)], out=kz.ap(kzi), in0=pbv, scalar1=zeta[:, h:h + 1], scalar2=None, op0=ALU.mult)
                pbs = [next_ps(), next_ps()]
                for n in range(8):
                    pb2 = pbs[n // 4]
                    o = (n % 4) * 128
                    MM(psum[pb2][:, o:o + 128], kz.ap(kzi)[:, n * 128:(n + 1) * 128], vtok.ap(n * 8 + h), True, True,
                       [kz.c(kzi), vtok.c(n * 8 + h)], [pcell[pb2]])
                ri = 0
                for n in range(8):
                    pb2 = pbs[n // 4]
                    o = (n % 4) * 128
                    dst_ap, dst_c = (Rst.ap(1 - ri), Rst.c(1 - ri)) if n < 7 else (Lst.ap(h), Lst.c(h))
                    if n == 0:
                        V("vector", "tensor_copy", [pcell[pb2]], [dst_c], out=dst_ap, in_=psum[pb2][:, o:o + 128])
                    else:
                        V("vector", "scalar_tensor_tensor", [pcell[pb2], Rst.c(ri)], [dst_c], out=dst_ap, in0=Rst.ap(ri), scalar=gC,
                          in1=psum[pb2][:, o:o + 128], op0=ALU.mult, op1=ALU.add)
                    ri = 1 - ri
                    if n < 7:
                        V("scalar", "activation", [Rst.c(ri)], [Sloc.c(h * 8 + n + 1)], out=Sloc.ap(h * 8 + n + 1), in_=Rst.ap(ri), func=AF.Copy)
        wsk, wskc = wblock(8)
        for half in range(2):
            pb = proj_fm(wsk, wskc, 0, half)
            rope(pb, ropeS, half, 64, (skA.flat[:, 128 + half * TH:128 + (half + 1) * TH], skB.flat[:, 128 + half * TH:128 + (half + 1) * TH]),
                 [skA.c(1 + half * 4 + nn) for nn in range(4)] + [skB.c(1 + half * 4 + nn) for nn in range(4)], t1, t2)
        for n in range(8):
            pb = next_ps()
            half, o = n // 4, (n % 4) * 128
            for kc in range(KC):
                MM(psum[pb][:, 0:128], nT.ap(H(kc, half))[:, o:o + 128], wsk[:, kc, 128:256], kc == 0, kc == KC - 1,
                   [wskc, nT.c(H(kc, half))], [pcell[pb]])
            V("scalar", "activation", [pcell[pb]], [svt.c(1 + n)], out=svt.ap(1 + n), in_=psum[pb][:, 0:128], func=AF.Copy)
        V("vector", "tensor_copy", [skA.c(8)], [Lst.c(8)], out=Lst.ap(8)[0:64, :], in_=skA.ap(8)[0:64, :])
        V("vector", "tensor_copy", [skB.c(8)], [Lst.c(8)], out=Lst.ap(8)[64:128, :], in_=skB.ap(8)[64:128, :])
        V("vector", "tensor_copy", [svt.c(8)], [Lst.c(9)], out=Lst.ap(9), in_=svt.ap(8))
        snd = S.add("gpsimd", "dma_start", dict(out=send_t.ap().rearrange("(a p) e -> p a e", p=128), in_=Lst.v),
                    reads=[Lst.c(i) for i in range(10)], writes=[xs_cell], dma=True)
        S.add("gpsimd", "collective_compute", dict(kind="AllGather", op=ALU.bypass, replica_groups=[[0, 1, 2, 3], [4, 5, 6, 7]],
                                                    ins=[send_t.ap().opt()], outs=[recv_t.ap().opt()]),
              reads=[xs_cell], writes=[xr_cell], own_sem=True)
        recv4 = recv_t.ap().rearrange("(r a p) e -> a p r e", r=4, a=10)
        P2 = Arena(S, arena_t, ARENA_BYTES)
        P2.top = common_top
        Sin32 = P2.alloc("Sin32", 8, 128, F32)
        p2_top = P2.top
        rcv = P2.alloc("rcv", 2, 512, F32)
        acc = P2.alloc("acc", 2, 128, F32)
        cf = coef.ap(0)
        for piece in list(range(8)) + [8, 9]:
            ri = piece % 2
            LD(rcv.ap(ri).rearrange("p (r e) -> p r e", r=4), recv4[piece], [rcv.c(ri)], reads=[xr_cell])
            rv3 = rcv.ap(ri).rearrange("p (r e) -> p r e", r=4)
            for r in range(4):
                csc = cf[:, r * 8 + piece:r * 8 + piece + 1] if piece < 8 else cf[:, 32 + r:33 + r]
                if r == 0:
                    V("vector", "tensor_scalar", [rcv.c(ri), coef.c(0)], [acc.c(ri)], out=acc.ap(ri), in0=rv3[:, 0, :], scalar1=csc, scalar2=None, op0=ALU.mult)
                else:
                    V("vector", "scalar_tensor_tensor", [rcv.c(ri), coef.c(0), acc.c(ri)], [acc.c(ri)], out=acc.ap(ri), in0=rv3[:, r, :], scalar=csc,
                      in1=acc.ap(ri), op0=ALU.mult, op1=ALU.add)
            if piece < 8:
                V("vector", "tensor_copy", [acc.c(ri)], [Sin32.c(piece)], out=Sin32.ap(piece), in_=acc.ap(ri))
            elif piece == 8:
                V("vector", "tensor_copy", [acc.c(ri)], [skA.c(0)], out=skA.ap(0)[0:64, :], in_=acc.ap(ri)[0:64, :])
                V("vector", "tensor_copy", [acc.c(ri)], [skB.c(0)], out=skB.ap(0)[64:128, :], in_=acc.ap(ri)[64:128, :])
            else:
                V("vector", "tensor_copy", [acc.c(ri)], [svt.c(0)], out=svt.ap(0), in_=acc.ap(ri))

        W2 = Arena(S, arena_t, ARENA_BYTES)
        W2.top = p2_top
        NB = 3
        sqT = W2.alloc("sqT", 4, TH, BF16)
        Ss = W2.alloc("Ss", NB, 512, F32)
        Pb = W2.alloc("Pb", NB, 512, BF16)
        Pn = W2.alloc("Pn", NB, 512, BF16)
        PT = W2.alloc("PT", NB, 512, BF16)
        st = W2.alloc("st", NB, 16, F32)
        pp = Pool(range(6))
        hp = Pool([6, 7])
        bp = Pool(range(NB))
        wq_state = {}
        po_bank = {}

        def prep(c):
            def g():
                if c % 2 == 0:
                    wq_state["w"] = wblock(9 + c // 2)
                wsq, wsqc = wq_state["w"]
                for half in range(2):
                    pb = pp.get()
                    while pb is None:
                        yield
                        pb = pp.get()
                    for kc in range(KC):
                        MM(psum[pb][:, :], wsq[:, kc, (c % 2) * 128:(c % 2 + 1) * 128], nT.ap(H(kc, half)), kc == 0, kc == KC - 1,
                           [wsqc, nT.c(H(kc, half))], [pcell[pb]])
                    yield
                    rope(pb, ropeS, half, 64, sqT.ap((c % 2) * 2 + half), sqT.c((c % 2) * 2 + half), t1, t2)
                    pp.put(pb)
                    yield
            return g

        def block(c, half, nb):
            def g():
                n = half * 4 + nb
                sq_ = sqT.ap((c % 2) * 2 + half)
                sqc = sqT.c((c % 2) * 2 + half)
                it = bp.get()
                while it is None:
                    yield
                    it = bp.get()
                if (c, half) not in po_bank:
                    po = hp.get()
                    while po is None:
                        yield
                        po = hp.get()
                    po_bank[(c, half)] = po
                po = po_bank[(c, half)]
                ps_ = pp.get()
                while ps_ is None:
                    yield
                    ps_ = pp.get()
                ps3 = psum[ps_][:, :].rearrange("p (i k) -> p i k", i=2)
                for i, skx in enumerate((skA, skB)):
                    MM(ps3[:, i, :], sq_[:, nb * 128:(nb + 1) * 128], skx.flat[:, n * 128:n * 128 + 256], True, True,
                       [sqc, skx.c(n), skx.c(n + 1)], [pcell[ps_]])
                yield
                msk = (mask0.ap(0) if n == 0 else maskg)
                ss3 = Ss.ap(it).rearrange("p (i k) -> p i k", i=2)
                sv_ = st.ap(it)
                V("vector", "scalar_tensor_tensor", [pcell[ps_], mask0.c(0), mixc.c(0)], [Ss.c(it)], out=ss3, in0=ps3, scalar=0.125,
                  in1=msk.unsqueeze(1).to_broadcast([128, 2, 256]), op0=ALU.mult, op1=ALU.add)
                pp.put(ps_)
                V("vector", "tensor_reduce", [Ss.c(it)], [st.c(it)], out=sv_[:, 0:2], in_=ss3, axis=AX.X, op=ALU.max)
                V("vector", "tensor_tensor", [st.c(it), mixc.c(0)], [st.c(it)], out=sv_[:, 2:4], in0=sv_[:, 0:2], in1=sinks[:, 2 * c:2 * c + 2], op=ALU.max)
                V("vector", "tensor_scalar", [st.c(it)], [st.c(it)], out=sv_[:, 4:6], in0=sv_[:, 2:4], scalar1=-1.0, scalar2=None, op0=ALU.mult)
                V("vector", "tensor_tensor", [st.c(it), mixc.c(0)], [st.c(it)], out=sv_[:, 8:10], in0=sinks[:, 2 * c:2 * c + 2], in1=sv_[:, 2:4], op=ALU.subtract)
                yield
                pb3 = Pb.ap(it).rearrange("p (i k) -> p i k", i=2)
                for i in range(2):
                    V("scalar", "activation", [Ss.c(it), st.c(it)], [Pb.c(it), st.c(it)], out=pb3[:, i, :], in_=ss3[:, i, :], func=AF.Exp,
                      bias=sv_[:, 4 + i:5 + i], scale=1.0, accum_out=sv_[:, 6 + i:7 + i])
                V("scalar", "activation", [st.c(it)], [st.c(it)], out=sv_[:, 8:10], in_=sv_[:, 8:10], func=AF.Exp)
                yield
                V("vector", "tensor_tensor", [st.c(it)], [st.c(it)], out=sv_[:, 10:12], in0=sv_[:, 6:8], in1=sv_[:, 8:10], op=ALU.add)
                V("vector", "reciprocal", [st.c(it)], [st.c(it)], out=sv_[:, 12:14], in_=sv_[:, 10:12])
                pn3 = Pn.ap(it).rearrange("p (i k) -> p i k", i=2)
                for i in range(2):
                    V("gpsimd", "tensor_scalar", [Pb.c(it), st.c(it)], [Pn.c(it)], out=pn3[:, i, :], in0=pb3[:, i, :], scalar1=sv_[:, 12 + i:13 + i],
                      scalar2=None, op0=ALU.mult)
                yield
                pt_ = pp.get()
                while pt_ is None:
                    yield
                    pt_ = pp.get()
                ptv = psum[pt_][:, :].bitcast(BF16)[:, 0:512]
                for i in range(2):
                    for kt in range(2):
                        S.add("tensor", "transpose", dict(out=ptv[:, (i * 2 + kt) * 128:(i * 2 + kt + 1) * 128],
                                                           in_=pn3[:, i, kt * 128:(kt + 1) * 128], identity=identb.ap(0)),
                              reads=[Pn.c(it), identb.c(0)], writes=[pcell[pt_]])
                yield
                V("scalar", "activation", [pcell[pt_]], [PT.c(it)], out=PT.ap(it), in_=ptv, func=AF.Copy)
                pp.put(pt_)
                yield
                for i in range(2):
                    for kt in range(2):
                        MM(psum[po][64 * i:64 * i + 64, nb * 128:(nb + 1) * 128], svt.ap(n + kt)[:, 64 * i:64 * i + 64],
                           PT.ap(it)[:, (i * 2 + kt) * 128:(i * 2 + kt + 1) * 128], kt == 0, kt == 1,
                           [svt.c(n + kt), PT.c(it)], [pcell[po]])
                bp.put(it)
            return g

        def finish(c, half):
            def g():
                po = po_bank[(c, half)]
                V("scalar", "activation", [pcell[po]], [aT.c((8 + c) * 2 + half)], out=aT.ap((8 + c) * 2 + half), in_=psum[po][:, :], func=AF.Copy)
                hp.put(po)
                yield
            return g

        tasks = []
        prep_id, fin_ids = {}, {}
        for c in range(8):
            deps = [prep_id[c - 1]] if c >= 1 else []
            if c >= 2:
                deps += fin_ids[c - 2]
            prep_id[c] = len(tasks)
            tasks.append((prep(c), deps))
            fin_ids[c] = []
            for half in range(2):
                bl = []
                for nb in range(4):
                    bl.append(len(tasks))
                    tasks.append((block(c, half, nb), [prep_id[c]]))
                fin_ids[c].append(len(tasks))
                tasks.append((finish(c, half), bl))
        run_tasks(tasks, 4)

        W3 = Arena(S, arena_t, ARENA_BYTES)
        W3.top = p2_top
        Ra = Arena(S, arena_t, ARENA_BYTES)
        Ra.top = ropeS.off
        Rb = Arena(S, arena_t, ARENA_BYTES)
        Rb.top = skA.off
        hb = []
        for k_ in range(2):
            A1 = W3 if k_ == 0 else Ra
            A2 = W3 if k_ == 0 else Rb
            d_ = dict(qT=A1.alloc(f"qT{k_}", 2, TH, BF16), qxi=A1.alloc(f"qxi{k_}", 2, TH, BF16), q32=A1.alloc(f"q32{k_}", 1, TH, F32),
                      rstd=A1.alloc(f"rstdg{k_}", 1, TH, F32), sgt=A2.alloc(f"sgt{k_}", 2, TH, F32), sTm=A2.alloc(f"sTm{k_}", 2, TH, BF16),
                      SinS=W3.alloc(f"SinS{k_}", 8, 128, BF16))
            hb.append(d_)
        assert Ra.top <= ropeS.off + ropeS.nbytes and Rb.top <= svt.off + svt.nbytes
        r2 = t1
        mean = t2
        pp = Pool(range(6))
        hbp = Pool(range(2))

        def getbank():
            b_ = pp.get()
            while b_ is None:
                yield None
                b_ = pp.get()
            yield b_

        def head(h):
            def g():
                k_ = hbp.get()
                while k_ is None:
                    yield
                    k_ = hbp.get()
                B_ = hb[k_]
                qT, qxi, q32, rstd, sgt, sTm, SinS = B_["qT"], B_["qxi"], B_["q32"], B_["rstd"], B_["sgt"], B_["sTm"], B_["SinS"]
                r32 = q32
                wq_, wqc = wblock(13 + h)
                for n in range(8):
                    V("scalar", "mul", [Sin32.c(h)], [SinS.c(n)], out=SinS.ap(n), in_=Sin32.ap(h), mul=float(GAMMA[h] ** (128 * n)))
                for half in range(2):
                    pb = None
                    while pb is None:
                        pb = pp.get()
                        if pb is None:
                            yield
                    for kc in range(KC):
                        MM(psum[pb][:, :], wq_[:, kc, 0:128], nT.ap(H(kc, half)), kc == 0, kc == KC - 1, [wqc, nT.c(H(kc, half))], [pcell[pb]])
                    yield
                    rope(pb, ropeR, half, 128, q32.ap(0), q32.c(0), t1, t2)
                    pp.put(pb)
                    V("scalar", "activation", [q32.c(0)], [qT.c(half)], out=qT.ap(half), in_=q32.ap(0), func=AF.Copy)
                    V("gpsimd", "tensor_tensor", [q32.c(0), mixc.c(0)], [qxi.c(half)], out=qxi.ap(half).rearrange("p (n c) -> p n c", n=4),
                      in0=q32.ap(0).rearrange("p (n c) -> p n c", n=4), in1=xi[:, h, :].unsqueeze(1).to_broadcast([128, 4, 128]), op=ALU.mult)
                    yield
                    pg = None
                    while pg is None:
                        pg = pp.get()
                        if pg is None:
                            yield
                    for kc in range(KC):
                        MM(psum[pg][:, :], wq_[:, kc, 128:256], nT.ap(H(kc, half)), kc == 0, kc == KC - 1, [wqc, nT.c(H(kc, half))], [pcell[pg]])
                    yield
                    V("scalar", "activation", [pcell[pg]], [sgt.c(half)], out=sgt.ap(half), in_=psum[pg][:, :], func=AF.Silu)
                    pp.put(pg)
                    yield
                for half in range(2):
                    pss = None
                    while pss is None:
                        pss = pp.get()
                        if pss is None:
                            yield
                    for nb in range(4):
                        MM(psum[pss][:, nb * 128:(nb + 1) * 128], kT.ap(h * 2 + half)[:, nb * 128:(nb + 1) * 128], qT.ap(half)[:, nb * 128:(nb + 1) * 128],
                           True, True, [kT.c(h * 2 + half), qT.c(half)], [pcell[pss]])
                    yield
                    V("vector", "tensor_tensor", [pcell[pss], mixc.c(0)], [sTm.c(half)], out=sTm.ap(half).rearrange("p (n c) -> p n c", n=4),
                      in0=psum[pss][:, :].rearrange("p (n c) -> p n c", n=4), in1=dmat[:, h, :].unsqueeze(1).to_broadcast([128, 4, 128]), op=ALU.mult)
                    pp.put(pss)
                    yield
                    pr = None
                    while pr is None:
                        pr = pp.get()
                        if pr is None:
                            yield
                    for nb in range(4):
                        n = half * 4 + nb
                        oap = psum[pr][:, nb * 128:(nb + 1) * 128]
                        qx = qxi.ap(half)[:, nb * 128:(nb + 1) * 128]
                        MM(oap, vtok.ap(n * 8 + h), sTm.ap(half)[:, nb * 128:(nb + 1) * 128], True, False,
                           [vtok.c(n * 8 + h), sTm.c(half)], [pcell[pr]])
                        if n > 0:
                            MM(oap, Sloc.ap(h * 8 + n), qx, False, False, [Sloc.c(h * 8 + n), qxi.c(half)], [pcell[pr]])
                        MM(oap, SinS.ap(n), qx, False, True, [SinS.c(n), qxi.c(half)], [pcell[pr]])
                    yield
                    p1 = None
                    while p1 is None:
                        p1 = pp.get()
                        if p1 is None:
                            yield
                    p2 = None
                    while p2 is None:
                        p2 = pp.get()
                        if p2 is None:
                            yield
                    V("scalar", "activation", [pcell[pr]], [r32.c(0)], out=r32.ap(0), in_=psum[pr][:, :], func=AF.Copy)
                    V("scalar", "activation", [pcell[pr]], [r2.c(0)], out=r2.ap(0), in_=psum[pr][:, :], func=AF.Square)
                    pp.put(pr)
                    MM(psum[p1][:, :], ones_f.ap(0), r32.ap(0), True, True, [ones_f.c(0), r32.c(0)], [pcell[p1]])
                    MM(psum[p2][:, :], ones_f.ap(0), r2.ap(0), True, True, [ones_f.c(0), r2.c(0)], [pcell[p2]])
                    V("vector", "tensor_scalar", [pcell[p1]], [mean.c(0)], out=mean.ap(0), in0=psum[p1][:, :], scalar1=1.0 / 128, scalar2=None, op0=ALU.mult)
                    V("vector", "tensor_tensor", [mean.c(0)], [rstd.c(0)], out=rstd.ap(0), in0=mean.ap(0), in1=mean.ap(0), op=ALU.mult)
                    V("vector", "scalar_tensor_tensor", [pcell[p2], rstd.c(0)], [rstd.c(0)], out=rstd.ap(0), in0=psum[p2][:, :], scalar=1.0 / 128,
                      in1=rstd.ap(0), op0=ALU.mult, op1=ALU.subtract)
                    pp.put(p1)
                    pp.put(p2)
                    V("scalar", "activation", [rstd.c(0), epsc.c(0)], [rstd.c(0)], out=rstd.ap(0), in_=rstd.ap(0), func=AF.Sqrt, bias=epsc.ap(0)[:, 0:1], scale=1.0)
                    V("vector", "reciprocal", [rstd.c(0)], [rstd.c(0)], out=rstd.ap(0), in_=rstd.ap(0))
                    V("gpsimd", "tensor_tensor", [r32.c(0), mean.c(0)], [r32.c(0)], out=r32.ap(0), in0=r32.ap(0), in1=mean.ap(0), op=ALU.subtract)
                    V("gpsimd", "tensor_tensor", [r32.c(0), rstd.c(0)], [r32.c(0)], out=r32.ap(0), in0=r32.ap(0), in1=rstd.ap(0), op=ALU.mult)
                    V("vector", "scalar_tensor_tensor", [r32.c(0), mixc.c(0), sgt.c(half)], [aT.c(h * 2 + half)], out=aT.ap(h * 2 + half), in0=r32.ap(0),
                      scalar=gng[:, h:h + 1], in1=sgt.ap(half), op0=ALU.mult, op1=ALU.mult)
                    yield
                hbp.put(k_)
            return g

        run_tasks([(head(h), []) for h in range(8)], 2)

        if DEBUG_A:
            for kc in range(KC):
                for half in range(2):
                    V("vector", "tensor_copy", [aT.c(H(kc, half))], [hT.c(H(kc, half))], out=hT.ap(H(kc, half)), in_=aT.ap(H(kc, half)))
        else:
            proj_residual(w_out, aT, reload=hspill, after_half=(lambda half: norm_both(2)) if "xattn" in stages else None)

    def xattn(pre_normed=False):
        if not pre_normed:
            rmsnorm(2)
        X = Arena(S, arena_t, ARENA_BYTES)
        X.top = phase_base
        qo = X.alloc("qo", 32, TH, BF16)
        kmT = X.alloc("kmT", 16, 256, BF16)
        vm = X.alloc("vm", 2 * 8, 256, BF16)
        mnT = X.alloc("mnT", 16, 256, BF16)
        xtop = X.top
        m32 = X.alloc("m32", 16, 256, F32)
        msq = X.alloc("msq", 2, 256, BF16)
        mrs = X.alloc("mrs", 1, 256, F32)
        if "mixer" not in stages:
            S.add("gpsimd", "dma_start", dict(out=identb.ap(0), in_=ident_d2), writes=[identb.c(0)], dma=True)
        for kc in range(KC):
            LD(m32.ap(kc), memT[kc * 128:(kc + 1) * 128, :], [m32.c(kc)])
        pb = next_ps()
        for kc in range(KC):
            s_ = kc % 2
            V("vector", "tensor_tensor", [m32.c(kc)], [msq.c(s_)], out=msq.ap(s_), in0=m32.ap(kc), in1=m32.ap(kc), op=ALU.mult)
            MM(psum[pb][:, 0:256], ones_b.ap(0), msq.ap(s_), kc == 0, kc == KC - 1, [ones_b.c(0), msq.c(s_)], [pcell[pb]])
        V("vector", "tensor_scalar", [pcell[pb]], [mrs.c(0)], out=mrs.ap(0), in0=psum[pb][:, 0:256], scalar1=1.0 / D, scalar2=EPS, op0=ALU.mult, op1=ALU.add)
        V("scalar", "activation", [mrs.c(0)], [mrs.c(0)], out=mrs.ap(0), in_=mrs.ap(0), func=AF.Sqrt)
        V("vector", "reciprocal", [mrs.c(0)], [mrs.c(0)], out=mrs.ap(0), in_=mrs.ap(0))
        for kc in range(KC):
            V("vector", "scalar_tensor_tensor", [m32.c(kc), gn.c(0), mrs.c(0)], [mnT.c(kc)], out=mnT.ap(kc), in0=m32.ap(kc),
              scalar=gn.ap(0)[:, 3 * KC + kc:3 * KC + kc + 1], in1=mrs.ap(0), op0=ALU.mult, op1=ALU.mult)
        for blk in range(8):
            wt, ci = next_wcell()
            wv = wt.v[:, ci, :].rearrange("p (k c) -> p k c", k=KC)
            wdma(wv, wkv_d[:, blk * 256:(blk + 1) * 256], KC, [wt.c(ci)])
            for j in range(2):
                pb = next_ps()
                for kc in range(KC):
                    MM(psum[pb][:, 0:256], wv[:, kc, j * 128:(j + 1) * 128], mnT.ap(kc), kc == 0, kc == KC - 1, [wt.c(ci), mnT.c(kc)], [pcell[pb]])
                V("scalar", "activation", [pcell[pb]], [kmT.c(blk * 2 + j)], out=kmT.ap(blk * 2 + j), in_=psum[pb][:, 0:256], func=AF.Copy)
        for blk in range(8):
            wt, ci = next_wcell()
            wv = wt.v[:, ci, :].rearrange("p (k c) -> p k c", k=KC)
            wdma(wv, wkv_d[:, D + blk * 256:D + (blk + 1) * 256], KC, [wt.c(ci)])
            for mt in range(2):
                pb = next_ps()
                for kc in range(KC):
                    MM(psum[pb][:, 0:256], mnT.ap(kc)[:, mt * 128:(mt + 1) * 128], wv[:, kc, :], kc == 0, kc == KC - 1, [wt.c(ci), mnT.c(kc)], [pcell[pb]])
                V("scalar", "activation", [pcell[pb]], [vm.c(mt * 8 + blk)], out=vm.ap(mt * 8 + blk), in_=psum[pb][:, 0:256], func=AF.Copy)
        for blk in range(8):
            wt, ci = next_wcell()
            wv = wt.v[:, ci, :].rearrange("p (k c) -> p k c", k=KC)
            wdma(wv, wq_d[:, blk * 256:(blk + 1) * 256], KC, [wt.c(ci)])
            for j in range(2):
                for half in range(2):
                    pb = next_ps()
                    for kc in range(KC):
                        MM(psum[pb][:, :], wv[:, kc, j * 128:(j + 1) * 128], nT.ap(H(kc, half)), kc == 0, kc == KC - 1, [wt.c(ci), nT.c(H(kc, half))], [pcell[pb]])
                    qc = H(blk * 2 + j, half)
                    V("scalar", "activation", [pcell[pb]], [qo.c(qc)], out=qo.ap(qc), in_=psum[pb][:, :], func=AF.Copy)
        X2 = Arena(S, arena_t, ARENA_BYTES)
        X2.top = xtop
        NX = 3
        Px = X2.alloc("Px", NX, 256, BF16)
        Pnx = X2.alloc("Pnx", NX, 256, BF16)
        stx = X2.alloc("stx", NX, 8, F32)
        PTx = X2.alloc("PTx", 2, 1024, BF16)
        SC = float(512 ** -0.5)
        pp = Pool(range(6))
        xbp = Pool(range(NX))
        ptp = Pool(range(2))
        pt_slot = {}

        def xtile(hd, half, tq):
            def g():
                it = xbp.get()
                while it is None:
                    yield
                    it = xbp.get()
                if (hd, half) not in pt_slot:
                    sl = ptp.get()
                    while sl is None:
                        yield
                        sl = ptp.get()
                    pt_slot[(hd, half)] = sl
                sl = pt_slot[(hd, half)]
                ptx3 = PTx.ap(sl).rearrange("p (m q) -> p m q", m=2)
                ps_ = pp.get()
                while ps_ is None:
                    yield
                    ps_ = pp.get()
                for cc in range(4):
                    MM(psum[ps_][:, 0:256], qo.ap(H(hd * 4 + cc, half))[:, tq * 128:(tq + 1) * 128], kmT.ap(hd * 4 + cc), cc == 0, cc == 3,
                       [qo.c(H(hd * 4 + cc, half)), kmT.c(hd * 4 + cc)], [pcell[ps_]])
                yield
                sv_ = stx.ap(it)
                V("vector", "tensor_reduce", [pcell[ps_]], [stx.c(it)], out=sv_[:, 0:1], in_=psum[ps_][:, 0:256], axis=AX.X, op=ALU.max)
                V("vector", "tensor_scalar", [stx.c(it)], [stx.c(it)], out=sv_[:, 1:2], in0=sv_[:, 0:1], scalar1=-SC, scalar2=None, op0=ALU.mult)
                yield
                V("scalar", "activation", [pcell[ps_], stx.c(it)], [Px.c(it), stx.c(it)], out=Px.ap(it), in_=psum[ps_][:, 0:256], func=AF.Exp,
                  bias=sv_[:, 1:2], scale=SC, accum_out=sv_[:, 2:3])
                pp.put(ps_)
                yield
                V("vector", "reciprocal", [stx.c(it)], [stx.c(it)], out=sv_[:, 3:4], in_=sv_[:, 2:3])
                V("vector", "tensor_scalar", [Px.c(it), stx.c(it)], [Pnx.c(it)], out=Pnx.ap(it), in0=Px.ap(it), scalar1=sv_[:, 3:4], scalar2=None, op0=ALU.mult)
                yield
                pt_ = pp.get()
                while pt_ is None:
                    yield
                    pt_ = pp.get()
                ptv = psum[pt_][:, :].bitcast(BF16)[:, 0:256]
                for mt in range(2):
                    S.add("tensor", "transpose", dict(out=ptv[:, mt * 128:(mt + 1) * 128], in_=Pnx.ap(it)[:, mt * 128:(mt + 1) * 128], identity=identb.ap(0)),
                          reads=[Pnx.c(it), identb.c(0)], writes=[pcell[pt_]])
                yield
                V("scalar", "activation", [pcell[pt_]], [PTx.c(sl)], out=ptx3[:, :, tq * 128:(tq + 1) * 128],
                  in_=ptv.rearrange("p (m q) -> p m q", m=2), func=AF.Copy)
                pp.put(pt_)
                xbp.put(it)
            return g

        def xfin(hd, half):
            def g():
                sl = pt_slot[(hd, half)]
                ptx3 = PTx.ap(sl).rearrange("p (m q) -> p m q", m=2)
                for cc in range(4):
                    dchunk = hd * 4 + cc
                    po = pp.get()
                    while po is None:
                        yield
                        po = pp.get()
                    for mt in range(2):
                        MM(psum[po][:, :], vm.ap(mt * 8 + dchunk // 2)[:, (dchunk % 2) * 128:(dchunk % 2 + 1) * 128], ptx3[:, mt, :], mt == 0, mt == 1,
                           [vm.c(mt * 8 + dchunk // 2), PTx.c(sl)], [pcell[po]])
                    yield
                    V("scalar", "activation", [pcell[po]], [qo.c(H(dchunk, half))], out=qo.ap(H(dchunk, half)), in_=psum[po][:, :], func=AF.Copy)
                    pp.put(po)
                ptp.put(sl)
            return g

        xt = []
        for hd in range(4):
            for half in range(2):
                ids = []
                for tq in range(4):
                    ids.append(len(xt))
                    xt.append((xtile(hd, half, tq), []))
                xt.append((xfin(hd, half), ids))
        run_tasks(xt, 4)
        proj_residual(wo_d, qo, after_half=(lambda half: norm_both(4)) if "ffn2" in stages else None)

    hsp_cell = [S.cell(f"hsp{i}") for i in range(32)]
    xs_cell = S.cell("xsend")
    xr_cell = S.cell("xrecv")
    if "ffn1" in stages:
        rmsnorm(0)
        ffn(wg1, wu1, wd1, phase_base, after_half=(lambda half: norm_both(1)) if "mixer" in stages else None)
    if "mixer" in stages:
        mixer(pre_normed="ffn1" in stages)
    if "xattn" in stages:
        xattn(pre_normed="mixer" in stages)
    if "ffn2" in stages:
        if "xattn" not in stages:
            rmsnorm(4)
        ffn(wg2, wu2, wd2, phase_base, after_half=(lambda half: norm_both(5, True)) if "final" in stages else None)
    if "final" in stages:
        if "ffn2" not in stages:
            rmsnorm(5, to_out=True)
        final = final_dmas
    else:
        final = []
        for kc in range(KC):
            for half in range(2):
                final.append(LD(outT[kc * 128:(kc + 1) * 128, half * TH:(half + 1) * TH], hT.ap(H(kc, half)), [], reads=[hT.c(H(kc, half))]))
    S.add("sync", "nop", dict(), after=final)
    S.emit(nc, es)
    es.close()
    return nc


_CACHE = {}


def _gain_cols(v):
    return np.ascontiguousarray(np.asarray(v, np.float32).reshape(KC, 128).T)


def _rope_table(d, pos):
    f32 = np.float32
    inv = (f32(10000.0) ** (-np.arange(0, d, 2, dtype=f32) / f32(d))).astype(f32)
    ang = (pos.astype(f32)[None, :] * inv[:, None]).astype(f32)
    cos = np.cos(ang.astype(np.float64)).astype(f32)
    sin = np.sin(ang.astype(np.float64)).astype(f32)
    p = np.arange(128)
    pp = p % d
    fi = pp % (d // 2)
    sign = np.where(pp < d // 2, -1.0, 1.0).astype(f32)
    return np.ascontiguousarray(np.concatenate([cos[fi], sin[fi] * sign[:, None]], axis=1), f32)


def _mix_consts(ret_gn_gain, swa_sinks):
    g = np.array(GAMMA, np.float64)
    j = np.arange(128)[:, None]
    c = np.arange(128)[None, :]
    m = np.zeros((128, MIXC_W), np.float32)
    sc = 128.0 ** -0.5
    for h in range(8):
        dm = np.where(c >= j, sc * g[h] ** np.maximum(c - j, 0), 0.0)
        m[:, MC_DMAT + h * 128:MC_DMAT + (h + 1) * 128] = dm
        m[:, MC_XI + h * 128:MC_XI + (h + 1) * 128] = (g[h] ** (np.arange(128) + 1))[None, :]
        m[:, MC_ZETA + h] = sc * g[h] ** (127 - np.arange(128))
    m[:, MC_GNG:MC_GNG + 8] = np.asarray(ret_gn_gain, np.float32).reshape(8, 128).T
    sk = np.asarray(swa_sinks, np.float32).reshape(16)
    order = [cc + 8 * i for cc in range(8) for i in range(2)]
    m[:, MC_SINK:MC_SINK + 16] = sk[order][None, :]
    i = np.arange(128)[:, None]
    kk = np.arange(256)[None, :]
    m[:, MC_MASK:MC_MASK + 256] = np.where((kk > i) & (kk <= i + 128), 0.0, NEG)
    return m


def _w_in_perm():
    cols = []
    for h in range(8):
        cols += list(range(1024 + h * 128, 1024 + (h + 1) * 128))
    for h in range(8):
        cols += list(range(2048 + h * 128, 2048 + (h + 1) * 128))
    cols += list(range(5120, 5376))
    for c in range(8):
        cols += list(range(4096 + c * 64, 4096 + (c + 1) * 64)) + list(range(4096 + (c + 8) * 64, 4096 + (c + 9) * 64))
    for h in range(8):
        cols += list(range(h * 128, (h + 1) * 128)) + list(range(3072 + h * 128, 3072 + (h + 1) * 128))
    return np.array(cols)


def _w_out_perm():
    rows = list(range(1024))
    for c in range(8):
        rows += list(range(1024 + c * 64, 1024 + (c + 1) * 64)) + list(range(1024 + (c + 8) * 64, 1024 + (c + 9) * 64))
    return np.array(rows)


def kernel(x, mem, ffn1_norm, ffn1_w_gate, ffn1_w_up, ffn1_w_down, mix_norm, w_in, ret_gn_gain,
           swa_sinks, w_out, xa_norm, mem_norm, xa_wq, xa_wkv, xa_wo, ffn2_norm, ffn2_w_gate,
           ffn2_w_up, ffn2_w_down, final_norm):
    st = tuple(STAGES)
    x = np.asarray(x, np.float32)
    mem = np.asarray(mem, np.float32)
    key = ("nc", st)
    if key not in _CACHE:
        _CACHE[key] = build_program(st)
    nc = _CACHE[key]
    f = lambda a: np.ascontiguousarray(np.asarray(a, np.float32)[0])
    gains = np.concatenate([_gain_cols(np.asarray(g).reshape(-1)) for g in
                            (ffn1_norm, mix_norm, xa_norm, mem_norm, ffn2_norm, final_norm)], axis=1)
    shared = {"gains": np.ascontiguousarray(gains, np.float32)}
    if "ffn1" in st:
        shared.update(ffn1_w_gate=f(ffn1_w_gate), ffn1_w_up=f(ffn1_w_up), ffn1_w_down=f(ffn1_w_down))
    if "ffn2" in st:
        shared.update(ffn2_w_gate=f(ffn2_w_gate), ffn2_w_up=f(ffn2_w_up), ffn2_w_down=f(ffn2_w_down))
    if "mixer" in st:
        shared["w_in_p"] = np.ascontiguousarray(f(w_in)[:, _w_in_perm()])
        shared["w_out_p"] = np.ascontiguousarray(f(w_out)[_w_out_perm(), :])
        shared["mixc"] = _mix_consts(np.asarray(ret_gn_gain).reshape(-1), np.asarray(swa_sinks).reshape(-1))
    if "mixer" in st or "xattn" in st:
        shared["ident"] = np.eye(128, dtype=np.float32)
    if "xattn" in st:
        shared.update(xa_wq=f(xa_wq), xa_wkv=f(xa_wkv), xa_wo=f(xa_wo))
    in_maps = []
    g64 = np.array(GAMMA, np.float64)
    for c in range(NCORES):
        b, q = c // 4, c % 4
        m = dict(shared)
        m["xT"] = np.ascontiguousarray(x[b, q * T:(q + 1) * T, :].T)
        if "mixer" in st:
            pos = np.arange(q * T, (q + 1) * T)
            m["ropeR"] = _rope_table(128, pos)
            m["ropeS"] = _rope_table(64, pos)
            cf = np.zeros((128, 40), np.float32)
            for r in range(4):
                if r < q:
                    cf[:, r * 8:(r + 1) * 8] = (g64 ** (1024 * (q - 1 - r)))[None, :]
                if r == q - 1:
                    cf[:, 32 + r] = 1.0
            m["coef"] = cf
            mk = shared["mixc"][:, MC_MASK:MC_MASK + 256].copy()
            if q == 0:
                mk[:, 0:128] = NEG
            m["mask0"] = np.ascontiguousarray(mk)
        if "xattn" in st:
            m["memT"] = np.ascontiguousarray(mem[b].T)
        in_maps.append(m)
    res = run_bass_kernel_spmd(nc, in_maps, core_ids=list(range(NCORES)))
    out = np.empty((2, 4096, D), np.float32)
    for c in range(NCORES):
        b, q = c // 4, c % 4
        out[b, q * T:(q + 1) * T, :] = res.results[c]["outT"].T
    return out
```

```python
import numpy as np
from contextlib import ExitStack
import concourse.bass as bass
import concourse.mybir as mybir
from concourse.bass_utils import run_bass_kernel_spmd

F32 = mybir.dt.float32
BF16 = mybir.dt.bfloat16
ALU = mybir.AluOpType
AF = mybir.ActivationFunctionType
AX = mybir.AxisListType

D = 2048
KC = 16
T = 1024
TH = 512
DFF = 5632
FC = 44
EPS = 1e-6
NCORES = 8

STAGES = ("ffn1", "mixer", "xattn", "ffn2", "final")
GAMMA = [1.0 - 2.0 ** (-5 - h) for h in range(8)]
MC_DMAT, MC_XI, MC_ZETA, MC_GNG, MC_SINK, MC_MASK = 0, 1024, 2048, 2056, 2064, 2080
MIXC_W = 2336
NEG = -30000.0
DEBUG_A = False


class Cell:
    __slots__ = ("name", "space", "off", "size", "last_w", "readers", "ov")

    def __init__(self, name, space, off=0, size=0):
        self.name, self.space, self.off, self.size = name, space, off, size
        self.last_w = []
        self.readers = {}
        self.ov = []


class Op:
    __slots__ = ("eng", "fn", "deps", "pos", "signal", "tick", "is_dma", "sem", "val", "waits", "gidx", "group", "own")

    def __init__(self, eng, fn, is_dma):
        self.eng, self.fn, self.is_dma = eng, fn, is_dma
        self.deps = []
        self.signal = False
        self.tick = None
        self.sem = None
        self.val = None
        self.waits = []


class Sched:
    ENGS = ["sync", "gpsimd", "scalar", "vector", "tensor"]
    NDQ = 8

    def __init__(self):
        self.ops = []
        self.sb_cells = []
        self.count = {e: 0 for e in self.ENGS}

    def sb_cell(self, name, off, size):
        c = Cell(name, "sb", off, size)
        for o in self.sb_cells:
            if o.off < off + size and off < o.off + o.size:
                o.ov.append(c)
                c.ov.append(o)
        self.sb_cells.append(c)
        return c

    def cell(self, name):
        return Cell(name, "x")

    def add(self, eng, meth, kw, reads=(), writes=(), dma=False, after=(), group=None, own_sem=False):
        op = Op(eng, (meth, kw), dma)
        op.group = group
        op.own = own_sem
        op.pos = self.count[eng]
        self.count[eng] += 1
        op.gidx = len(self.ops)
        deps = {}

        def dep(o):
            if o is not None and o is not op and not (group is not None and o.group == group):
                deps[id(o)] = o

        for o in after:
            dep(o)
        for c in reads:
            for y in [c] + c.ov:
                for w in y.last_w:
                    dep(w)
        for c in writes:
            for y in [c] + c.ov:
                for w in y.last_w:
                    dep(w)
                for r in y.readers.values():
                    dep(r)
        for c in reads:
            for y in [c] + c.ov:
                key = ("d", op.gidx) if dma else eng
                y.readers[key] = op
        for c in writes:
            for y in [c] + c.ov:
                if group is not None and y.last_w and y.last_w[0].group == group:
                    y.last_w.append(op)
                else:
                    y.last_w = [op]
                y.readers = {}
        best = {}
        for o in deps.values():
            if o.is_dma or o.own:
                best[("d", id(o))] = o
            else:
                if eng == "tensor" and o.eng == "tensor":
                    continue
                k = o.eng
                if k not in best or best[k].pos < o.pos:
                    best[k] = o
        op.deps = list(best.values())
        for o in op.deps:
            o.signal = True
        self.ops.append(op)
        return op

    def emit(self, nc, es):
        sems = {e: es.enter_context(nc.semaphore("c_" + e)) for e in self.ENGS}
        dq = {e: [es.enter_context(nc.semaphore(f"dq_{e}_{i}")) for i in range(self.NDQ)]
              for e in ("sync", "gpsimd", "scalar")}
        streams = {e: [] for e in self.ENGS}
        ticks = {e: 0 for e in self.ENGS}
        ndma = {e: 0 for e in self.ENGS}
        for op in self.ops:
            streams[op.eng].append(op)
            if op.is_dma:
                i = ndma[op.eng]
                ndma[op.eng] += 1
                op.sem = dq[op.eng][i % self.NDQ]
                op.val = 16 * (i // self.NDQ + 1)
                op.signal = True
            elif op.own:
                op.sem = es.enter_context(nc.semaphore(f"own_{op.gidx}"))
                op.val = 1
                op.signal = True
            elif op.signal:
                ticks[op.eng] += 1
                op.sem = sems[op.eng]
                op.val = ticks[op.eng]
        for e in self.ENGS:
            seen = {}
            for op in streams[e]:
                w = []
                if op.is_dma and op.val > 16:
                    w.append((op.sem, op.val - 16))
                for d in op.deps:
                    w.append((d.sem, d.val))
                for (s, v) in w:
                    if seen.get(id(s), 0) >= v:
                        continue
                    seen[id(s)] = v
                    op.waits.append((s, v))
        block = es.enter_context(nc.Block())

        def run(e, stream):
            for op in stream:
                for (s, v) in op.waits:
                    e.wait_ge(s, v)
                ins = getattr(e, op.fn[0])(**op.fn[1])
                if op.signal:
                    ins.then_inc(op.sem, 16 if op.is_dma else 1)

        @block.sync
        def _(e):
            run(e, streams["sync"])

        @block.gpsimd
        def _(e):
            run(e, streams["gpsimd"])

        @block.scalar
        def _(e):
            run(e, streams["scalar"])

        @block.vector
        def _(e):
            run(e, streams["vector"])

        @block.tensor
        def _(e):
            run(e, streams["tensor"])


class SbT:
    def __init__(self, S, arena, name, off, ncell, clen, dt):
        self.esz = 4 if dt == F32 else 2
        self.ncell, self.clen, self.dt = ncell, clen, dt
        self.off = off
        self.nbytes = ncell * clen * self.esz
        assert off % 4 == 0 and self.nbytes % 4 == 0
        w0, w1 = off // 4, (off + self.nbytes) // 4
        v = arena[:, w0:w1]
        if dt != F32:
            v = v.bitcast(dt)
        self.flat = v
        self.v = v.rearrange("p (a b) -> p a b", b=clen)
        self.cells = [S.sb_cell(f"{name}{i}", off + i * clen * self.esz, clen * self.esz) for i in range(ncell)]

    def ap(self, i):
        return self.v[:, i, :]

    def c(self, i):
        return self.cells[i]


class Arena:
    def __init__(self, S, arena, total):
        self.S, self.arena, self.total = S, arena, total
        self.top = 0

    def alloc(self, name, ncell, clen, dt, at=None):
        esz = 4 if dt == F32 else 2
        nb = ncell * clen * esz
        nb4 = (nb + 31) // 32 * 32
        if at is None:
            at = self.top
            self.top += nb4
        assert at + nb <= self.total, (name, at, nb, self.total)
        return SbT(self.S, self.arena, name, at, ncell, clen, dt)


ARENA_BYTES = 207 * 1024


class Pool:
    def __init__(self, items):
        self.free = list(items)

    def get(self):
        return self.free.pop(0) if self.free else None

    def put(self, x):
        self.free.append(x)


def run_tasks(tasks, width):
    done, started, active = set(), set(), []
    while len(done) < len(tasks):
        for i in range(len(tasks)):
            if len(active) >= width:
                break
            if i in started:
                continue
            if all(d in done for d in tasks[i][1]):
                started.add(i)
                active.append((i, tasks[i][0]()))
        assert active, "task deadlock"
        for (i, g) in list(active):
            try:
                next(g)
            except StopIteration:
                active.remove((i, g))
                done.add(i)


def build_program(stages=("ffn1", "mixer", "xattn", "ffn2", "final")):
    stages = set(stages)
    nc = bass.Bass("TRN2", target_bir_lowering=False)
    S = Sched()
    es = ExitStack()

    def din(name, shape, dt=F32):
        return nc.dram_tensor(name, shape, dt, kind="ExternalInput").ap()

    xT = din("xT", [D, T])
    gains = din("gains", [128, 6 * KC])
    if "ffn1" in stages:
        wg1 = din("ffn1_w_gate", [D, DFF]); wu1 = din("ffn1_w_up", [D, DFF]); wd1 = din("ffn1_w_down", [DFF, D])
    if "ffn2" in stages:
        wg2 = din("ffn2_w_gate", [D, DFF]); wu2 = din("ffn2_w_up", [D, DFF]); wd2 = din("ffn2_w_down", [DFF, D])
    if "mixer" in stages:
        w_in = din("w_in_p", [D, 42 * 128])
        w_out = din("w_out_p", [D, D])
        ropeR_d = din("ropeR", [128, 2 * T])
        ropeS_d = din("ropeS", [128, 2 * T])
        mixc_d = din("mixc", [128, MIXC_W])
        coef_d = din("coef", [128, 40])
        mask0_d = din("mask0", [128, 256])
        ident_d = din("ident", [128, 128])
        hspill = nc.dram_tensor("hspill", [D, T], F32).ap()
        send_t = nc.dram_tensor("xsend", [10 * 128, 128], F32)
        recv_t = nc.dram_tensor("xrecv", [4 * 10 * 128, 128], F32)
    if "xattn" in stages:
        memT = din("memT", [D, 256])
        wq_d = din("xa_wq", [D, D]); wkv_d = din("xa_wkv", [D, 2 * D]); wo_d = din("xa_wo", [D, D])
        ident_d2 = ident_d if "mixer" in stages else din("ident", [128, 128])
    outT = nc.dram_tensor("outT", [D, T], F32, kind="ExternalOutput").ap()

    arena_t = es.enter_context(nc.sbuf_tensor("arena", [128, ARENA_BYTES // 4], F32))
    A = Arena(S, arena_t, ARENA_BYTES)
    psum_all = es.enter_context(nc.psum_tensor("psall", [128, 8 * 512], F32))
    psum = [psum_all[:, i * 512:(i + 1) * 512] for i in range(8)]
    pcell = [S.cell(f"ps{i}") for i in range(8)]

    hT = A.alloc("hT", KC * 2, TH, F32)
    nT = A.alloc("nT", KC * 2, TH, BF16)
    wslot = [A.alloc(f"ws{i}", 2, 4096, BF16) for i in range(2)]
    gn = A.alloc("gn", 1, 6 * KC, F32)
    ones_b = A.alloc("ones_b", 1, 128, BF16)
    ones_f = A.alloc("ones_f", 1, 128, F32)
    identb = A.alloc("identb", 1, 128, BF16)
    epsc = A.alloc("epsc", 1, 8, F32)
    phase_base = A.top

    psi = [0]

    def next_ps():
        i = psi[0] % 6
        psi[0] += 1
        return i

    def next_ps_pair():
        if psi[0] % 2:
            psi[0] += 1
        i = psi[0] % 6
        psi[0] += 2
        return i

    hpsi = [0]

    def hold_ps():
        i = 6 + hpsi[0] % 2
        hpsi[0] += 1
        return i

    wsi = [0]

    def next_wcell():
        i = wsi[0] % 4
        wsi[0] += 1
        return wslot[i // 2], i % 2

    def next_ws():
        if wsi[0] % 2:
            wsi[0] += 1
        i = (wsi[0] // 2) % 2
        wsi[0] += 2
        return wslot[i]

    def H(kc, half):
        return kc * 2 + half

    gid = [0]

    def wdma(dst3, src2, nk, cells):
        gid[0] += 1
        for k0 in range(0, nk, 4):
            k1 = min(nk, k0 + 4)
            S.add("gpsimd", "dma_start", dict(out=dst3[:, k0:k1, :],
                                               in_=src2[k0 * 128:k1 * 128, :].rearrange("(k p) c -> p k c", p=128)),
                  writes=cells, dma=True, group=gid[0])

    def V(eng, meth, reads, writes, **kw):
        return S.add(eng, meth, kw, reads=reads, writes=writes)

    def MM(out, lhsT, rhs, start, stop, reads, writes):
        return S.add("tensor", "matmul", dict(out=out, lhsT=lhsT, rhs=rhs, start=start, stop=stop), reads=reads, writes=writes)

    def LD(out, in_, writes, reads=(), eng="sync"):
        return S.add(eng, "dma_start", dict(out=out, in_=in_), reads=reads, writes=writes, dma=True)

    V("vector", "memset", [], [ones_b.c(0)], ap=ones_b.ap(0), constant=1.0)
    V("vector", "memset", [], [ones_f.c(0)], ap=ones_f.ap(0), constant=1.0)
    V("vector", "memset", [], [epsc.c(0)], ap=epsc.ap(0), constant=EPS)
    LD(gn.ap(0), gains, [gn.c(0)])
    for kc in range(KC):
        for half in range(2):
            LD(hT.ap(H(kc, half)), xT[kc * 128:(kc + 1) * 128, half * TH:(half + 1) * TH], [hT.c(H(kc, half))])

    NRM = Arena(S, arena_t, ARENA_BYTES)
    NRM.top = ARENA_BYTES - 16 * 1024
    n_ot = NRM.alloc("n_ot", 4, TH, F32)
    n_sq = NRM.alloc("n_sq", 4, TH, BF16)
    n_rstd = NRM.alloc("n_rstd", 2, TH, F32)
    final_dmas = []

    def norm_half_gen(gi, half, to_out=False):
        sq, rstd, ot = n_sq, n_rstd, n_ot
        pb = next_ps()
        for kc in range(KC):
            s = (kc % 2) + 2 * half
            hc = H(kc, half)
            if kc % 2 == 0:
                V("scalar", "activation", [hT.c(hc)], [sq.c(s)], out=sq.ap(s), in_=hT.ap(hc), func=AF.Square)
            else:
                V("vector", "tensor_tensor", [hT.c(hc)], [sq.c(s)], out=sq.ap(s), in0=hT.ap(hc), in1=hT.ap(hc), op=ALU.mult)
            MM(psum[pb][:, :], ones_b.ap(0), sq.ap(s), kc == 0, kc == KC - 1, [sq.c(s), ones_b.c(0)], [pcell[pb]])
            if kc % 4 == 3:
                yield
        V("vector", "tensor_scalar", [pcell[pb]], [rstd.c(half)], out=rstd.ap(half), in0=psum[pb][:, :],
          scalar1=1.0 / D, scalar2=EPS, op0=ALU.mult, op1=ALU.add)
        yield
        V("scalar", "activation", [rstd.c(half)], [rstd.c(half)], out=rstd.ap(half), in_=rstd.ap(half), func=AF.Sqrt)
        yield
        V("vector", "reciprocal", [rstd.c(half)], [rstd.c(half)], out=rstd.ap(half), in_=rstd.ap(half))
        yield
        for kc in range(KC):
            hc = H(kc, half)
            gsc = gn.ap(0)[:, gi * KC + kc:gi * KC + kc + 1]
            eng = "vector"
            if not to_out:
                V(eng, "scalar_tensor_tensor", [hT.c(hc), gn.c(0), rstd.c(half)], [nT.c(hc)],
                  out=nT.ap(hc), in0=hT.ap(hc), scalar=gsc, in1=rstd.ap(half), op0=ALU.mult, op1=ALU.mult)
            else:
                o = (kc % 2) + 2 * half
                V(eng, "scalar_tensor_tensor", [hT.c(hc), gn.c(0), rstd.c(half)], [ot.c(o)],
                  out=ot.ap(o), in0=hT.ap(hc), scalar=gsc, in1=rstd.ap(half), op0=ALU.mult, op1=ALU.mult)
                final_dmas.append(LD(outT[kc * 128:(kc + 1) * 128, half * TH:(half + 1) * TH], ot.ap(o), [], reads=[ot.c(o)]))
            if kc % 4 == 3:
                yield

    def norm_half(gi, half, to_out=False):
        for _ in norm_half_gen(gi, half, to_out):
            pass

    def norm_both(gi, to_out=False):
        run_tasks([((lambda h=h: norm_half_gen(gi, h, to_out)), []) for h in range(2)], 2)

    def rmsnorm(gi, tmp_base=None, to_out=False):
        norm_both(gi, to_out)

    def ffn(wg, wu, wd, tmp_base, after_half=None):
        At = Arena(S, arena_t, ARENA_BYTES)
        At.top = tmp_base
        act = At.alloc("act", 22 * 2, TH, BF16)
        sg = At.alloc("sg", 2, TH, F32)
        sgi = 0
        for ffh in range(2):
            for p in range(11):
                f0 = (ffh * 22 + 2 * p) * 128
                wt = next_ws()
                wv = wt.flat.rearrange("p (g k c) -> p g k c", g=2, k=KC)
                wdma(wv[:, 0], wg[:, f0:f0 + 256], KC, [wt.c(0)])
                wdma(wv[:, 1], wu[:, f0:f0 + 256], KC, [wt.c(1)])
                for j in range(2):
                    fl = 2 * p + j
                    for half in range(2):
                        pg, pu = next_ps(), next_ps()
                        for kc in range(KC):
                            MM(psum[pg][:, :], wv[:, 0, kc, j * 128:(j + 1) * 128], nT.ap(H(kc, half)), kc == 0, kc == KC - 1,
                               [wt.c(0), nT.c(H(kc, half))], [pcell[pg]])
                        for kc in range(KC):
                            MM(psum[pu][:, :], wv[:, 1, kc, j * 128:(j + 1) * 128], nT.ap(H(kc, half)), kc == 0, kc == KC - 1,
                               [wt.c(1), nT.c(H(kc, half))], [pcell[pu]])
                        si = sgi % 2
                        sgi += 1
                        V("scalar", "activation", [pcell[pg]], [sg.c(si)], out=sg.ap(si), in_=psum[pg][:, :], func=AF.Silu)
                        V("vector", "tensor_tensor", [pcell[pu], sg.c(si)], [act.c(fl * 2 + half)],
                          out=act.ap(fl * 2 + half), in0=psum[pu][:, :], in1=sg.ap(si), op=ALU.mult)
            halfsets = [(0, 1)]
            for hs in halfsets:
              for dblk in range(8):
                wt = next_ws()
                wv = wt.flat[:, 0:22 * 256].rearrange("p (k c) -> p k c", k=22)
                r0 = ffh * 22 * 128
                wdma(wv, wd[r0:r0 + 22 * 128, dblk * 256:(dblk + 1) * 256], 22, [wt.c(0), wt.c(1)])
                for j in range(2):
                    dc = dblk * 2 + j
                    for half in hs:
                        pb = next_ps()
                        for fl in range(22):
                            MM(psum[pb][:, :], wv[:, fl, j * 128:(j + 1) * 128], act.ap(fl * 2 + half), fl == 0, fl == 21,
                               [wt.c(0), wt.c(1), act.c(fl * 2 + half)], [pcell[pb]])
                        V("vector", "scalar_tensor_tensor", [pcell[pb], hT.c(H(dc, half))], [hT.c(H(dc, half))],
                          out=hT.ap(H(dc, half)), in0=psum[pb][:, :], scalar=0.5, in1=hT.ap(H(dc, half)), op0=ALU.mult, op1=ALU.add)
            if ffh == 1 and after_half is not None:
                after_half(None)

    def proj_residual(w_d, src, reload=None, after_half=None):
        for dblk in range(8):
            wt, ci = next_wcell()
            wv = wt.v[:, ci, :].rearrange("p (k c) -> p k c", k=KC)
            wdma(wv, w_d[:, dblk * 256:(dblk + 1) * 256], KC, [wt.c(ci)])
            for j in range(2):
                dc = dblk * 2 + j
                for half in range(2):
                    pb = next_ps()
                    for ac in range(KC):
                        MM(psum[pb][:, :], wv[:, ac, j * 128:(j + 1) * 128], src.ap(H(ac, half)), ac == 0, ac == KC - 1,
                           [wt.c(ci), src.c(H(ac, half))], [pcell[pb]])
                    hc = H(dc, half)
                    if reload is not None:
                        LD(hT.ap(hc), reload[dc * 128:(dc + 1) * 128, half * TH:(half + 1) * TH], [hT.c(hc)])
                    V("vector", "tensor_tensor", [pcell[pb], hT.c(hc)], [hT.c(hc)],
                      out=hT.ap(hc), in0=psum[pb][:, :], in1=hT.ap(hc), op=ALU.add)
        if after_half is not None:
            after_half(None)

    def rope(pb, tab, half, dh, out_ap, out_cell, t1, t2):
        V("vector", "tensor_tensor", [pcell[pb], tab.c(half)], [t1.c(0)], out=t1.ap(0), in0=psum[pb][:, :], in1=tab.ap(half), op=ALU.mult)
        hd = dh // 2
        for base in range(0, 128, dh):
            for (dst, src) in ((base, base + hd), (base + hd, base)):
                V("vector", "tensor_tensor", [pcell[pb], tab.c(2 + half)], [t2.c(0)],
                  out=t2.ap(0)[dst:dst + hd, :], in0=psum[pb][src:src + hd, :], in1=tab.ap(2 + half)[dst:dst + hd, :], op=ALU.mult)
        if isinstance(out_ap, tuple):
            for (lo, oap) in ((0, out_ap[0]), (64, out_ap[1])):
                V("vector", "tensor_tensor", [t1.c(0), t2.c(0)], out_cell, out=oap[lo:lo + 64, :], in0=t1.ap(0)[lo:lo + 64, :], in1=t2.ap(0)[lo:lo + 64, :], op=ALU.add)
        else:
            V("vector", "tensor_tensor", [t1.c(0), t2.c(0)], out_cell if isinstance(out_cell, list) else [out_cell], out=out_ap, in0=t1.ap(0), in1=t2.ap(0), op=ALU.add)

    def mixer(pre_normed=False):
        if not pre_normed:
            rmsnorm(1)
        for kc in range(KC):
            for half in range(2):
                S.add("sync", "dma_start", dict(out=hspill[kc * 128:(kc + 1) * 128, half * TH:(half + 1) * TH], in_=hT.ap(H(kc, half))),
                      reads=[hT.c(H(kc, half))], writes=[hsp_cell[H(kc, half)]], dma=True)
        R = Arena(S, arena_t, ARENA_BYTES)
        R.top = hT.off
        kT = R.alloc("kT", 16, TH, BF16)
        vtok = R.alloc("vtok", 64, 128, BF16)
        Sloc = R.alloc("Sloc", 64, 128, BF16)
        ropeR = R.alloc("ropeR", 4, TH, F32)
        ropeS = R.alloc("ropeS", 4, TH, F32)
        assert R.top <= hT.off + hT.nbytes
        P = Arena(S, arena_t, ARENA_BYTES)
        P.top = phase_base
        aT = P.alloc("aT", 32, TH, BF16)
        mixc = P.alloc("mixc", 1, MIXC_W, F32)
        coef = P.alloc("coef", 1, 40, F32)
        mask0 = P.alloc("mask0", 1, 256, F32)
        skA = P.alloc("skA", 9, 128, BF16)
        skB = P.alloc("skB", 9, 128, BF16)
        svt = P.alloc("svt", 9, 128, BF16)
        t1 = P.alloc("t1", 1, TH, F32)
        t2 = P.alloc("t2", 1, TH, F32)
        common_top = P.top
        mc = mixc.ap(0)
        dmat = mc[:, MC_DMAT:MC_DMAT + 1024].rearrange("p (h c) -> p h c", h=8)
        xi = mc[:, MC_XI:MC_XI + 1024].rearrange("p (h c) -> p h c", h=8)
        zeta = mc[:, MC_ZETA:MC_ZETA + 8]
        gng = mc[:, MC_GNG:MC_GNG + 8]
        sinks = mc[:, MC_SINK:MC_SINK + 16]
        maskg = mc[:, MC_MASK:MC_MASK + 256]
        V("vector", "memset", [], [skA.c(i) for i in range(9)], ap=skA.flat, constant=0.0)
        V("vector", "memset", [], [skB.c(i) for i in range(9)], ap=skB.flat, constant=0.0)
        LD(mixc.ap(0), mixc_d, [mixc.c(0)])
        LD(coef.ap(0), coef_d, [coef.c(0)])
        LD(mask0.ap(0), mask0_d, [mask0.c(0)])
        for i in range(4):
            LD(ropeR.ap(i), ropeR_d[:, i * TH:(i + 1) * TH], [ropeR.c(i)])
            LD(ropeS.ap(i), ropeS_d[:, i * TH:(i + 1) * TH], [ropeS.c(i)])
        S.add("gpsimd", "dma_start", dict(out=identb.ap(0), in_=ident_d), writes=[identb.c(0)], dma=True)

        def wblock(b):
            wt, ci = next_wcell()
            wv = wt.v[:, ci, :].rearrange("p (k c) -> p k c", k=KC)
            wdma(wv, w_in[:, b * 256:(b + 1) * 256], KC, [wt.c(ci)])
            return wv, wt.c(ci)

        def proj_fm(wv, wc, j, half):
            pb = next_ps()
            for kc in range(KC):
                MM(psum[pb][:, :], wv[:, kc, j * 128:(j + 1) * 128], nT.ap(H(kc, half)), kc == 0, kc == KC - 1,
                   [wc, nT.c(H(kc, half))], [pcell[pb]])
            return pb

        P1 = Arena(S, arena_t, ARENA_BYTES)
        P1.top = common_top
        kz = P1.alloc("kz", 2, 1024, BF16)
        Rst = P1.alloc("Rst", 2, 128, F32)
        Lst = P1.alloc("Lst", 10, 128, F32)
        for hp in range(4):
            wk, wkc = wblock(hp)
            wvv, wvc = wblock(4 + hp)
            for j in range(2):
                h = 2 * hp + j
                for half in range(2):
                    pb = proj_fm(wk, wkc, j, half)
                    rope(pb, ropeR, half, 128, kT.ap(h * 2 + half), kT.c(h * 2 + half), t1, t2)
            for n in range(8):
                pb = next_ps()
                half, o = n // 4, (n % 4) * 128
                for kc in range(KC):
                    MM(psum[pb][:, 0:256], nT.ap(H(kc, half))[:, o:o + 128], wvv[:, kc, :], kc == 0, kc == KC - 1,
                       [wvc, nT.c(H(kc, half))], [pcell[pb]])
                for j in range(2):
                    h = 2 * hp + j
                    V("scalar", "activation", [pcell[pb]], [vtok.c(n * 8 + h)], out=vtok.ap(n * 8 + h), in_=psum[pb][:, j * 128:(j + 1) * 128], func=AF.Copy)
            for j in range(2):
                h = 2 * hp + j
                gC = float(GAMMA[h] ** 128)
                pb = next_ps()
                pbv = psum[pb][:, :].bitcast(BF16)
                for n in range(8):
                    half, o = n // 4, (n % 4) * 128
                    S.add("tensor", "transpose", dict(out=pbv[:, n * 128:(n + 1) * 128], in_=kT.ap(h * 2 + half)[:, o:o + 128], identity=identb.ap(0)),
                          reads=[kT.c(h * 2 + half), identb.c(0)], writes=[pcell[pb]])
                kzi = h % 2
                V("vector", "tensor_scalar", [pcell[pb], mixc.c(0)], [kz.c(kzi)], out=kz.ap(kzi), in0=pbv, scalar1=zeta[:, h:h + 1], scalar2=None, op0=ALU.mult)
                pbs = [next_ps(), next_ps()]
                for n in range(8):
                    pb2 = pbs[n // 4]
                    o = (n % 4) * 128
                    MM(psum[pb2][:, o:o + 128], kz.ap(kzi)[:, n * 128:(n + 1) * 128], vtok.ap(n * 8 + h), True, True,
                       [kz.c(kzi), vtok.c(n * 8 + h)], [pcell[pb2]])
                ri = 0
                for n in range(8):
                    pb2 = pbs[n // 4]
                    o = (n % 4) * 128
                    dst_ap, dst_c = (Rst.ap(1 - ri), Rst.c(1 - ri)) if n < 7 else (Lst.ap(h), Lst.c(h))
                    if n == 0:
                        V("vector", "tensor_copy", [pcell[pb2]], [dst_c], out=dst_ap, in_=psum[pb2][:, o:o + 128])
                    else:
                        V("vector", "scalar_tensor_tensor", [pcell[pb2], Rst.c(ri)], [dst_c], out=dst_ap, in0=Rst.ap(ri), scalar=gC,
                          in1=psum[pb2][:, o:o + 128], op0=ALU.mult, op1=ALU.add)
                    ri = 1 - ri
                    if n < 7:
                        V("scalar", "activation", [Rst.c(ri)], [Sloc.c(h * 8 + n + 1)], out=Sloc.ap(h * 8 + n + 1), in_=Rst.ap(ri), func=AF.Copy)
        wsk, wskc = wblock(8)
        for half in range(2):
            pb = proj_fm(wsk, wskc, 0, half)
            rope(pb, ropeS, half, 64, (skA.flat[:, 128 + half * TH:128 + (half + 1) * TH], skB.flat[:, 128 + half * TH:128 + (half + 1) * TH]),
                 [skA.c(1 + half * 4 + nn) for nn in range(4)] + [skB.c(1 + half * 4 + nn) for nn in range(4)], t1, t2)
        for n in range(8):
            pb = next_ps()
            half, o = n // 4, (n % 4) * 128
            for kc in range(KC):
                MM(psum[pb][:, 0:128], nT.ap(H(kc, half))[:, o:o + 128], wsk[:, kc, 128:256], kc == 0, kc == KC - 1,
                   [wskc, nT.c(H(kc, half))], [pcell[pb]])
            V("scalar", "activation", [pcell[pb]], [svt.c(1 + n)], out=svt.ap(1 + n), in_=psum[pb][:, 0:128], func=AF.Copy)
        V("vector", "tensor_copy", [skA.c(8)], [Lst.c(8)], out=Lst.ap(8)[0:64, :], in_=skA.ap(8)[0:64, :])
        V("vector", "tensor_copy", [skB.c(8)], [Lst.c(8)], out=Lst.ap(8)[64:128, :], in_=skB.ap(8)[64:128, :])
        V("vector", "tensor_copy", [svt.c(8)], [Lst.c(9)], out=Lst.ap(9), in_=svt.ap(8))
        snd = S.add("gpsimd", "dma_start", dict(out=send_t.ap().rearrange("(a p) e -> p a e", p=128), in_=Lst.v),
                    reads=[Lst.c(i) for i in range(10)], writes=[xs_cell], dma=True)
        S.add("gpsimd", "collective_compute", dict(kind="AllGather", op=ALU.bypass, replica_groups=[[0, 1, 2, 3], [4, 5, 6, 7]],
                                                    ins=[send_t.ap().opt()], outs=[recv_t.ap().opt()]),
              reads=[xs_cell], writes=[xr_cell], own_sem=True)
        recv4 = recv_t.ap().rearrange("(r a p) e -> a p r e", r=4, a=10)
        P2 = Arena(S, arena_t, ARENA_BYTES)
        P2.top = common_top
        Sin32 = P2.alloc("Sin32", 8, 128, F32)
        p2_top = P2.top
        rcv = P2.alloc("rcv", 2, 512, F32)
        acc = P2.alloc("acc", 2, 128, F32)
        cf = coef.ap(0)
        for piece in list(range(8)) + [8, 9]:
            ri = piece % 2
            LD(rcv.ap(ri).rearrange("p (r e) -> p r e", r=4), recv4[piece], [rcv.c(ri)], reads=[xr_cell])
            rv3 = rcv.ap(ri).rearrange("p (r e) -> p r e", r=4)
            for r in range(4):
                csc = cf[:, r * 8 + piece:r * 8 + piece + 1] if piece < 8 else cf[:, 32 + r:33 + r]
                if r == 0:
                    V("vector", "tensor_scalar", [rcv.c(ri), coef.c(0)], [acc.c(ri)], out=acc.ap(ri), in0=rv3[:, 0, :], scalar1=csc, scalar2=None, op0=ALU.mult)
                else:
                    V("vector", "scalar_tensor_tensor", [rcv.c(ri), coef.c(0), acc.c(ri)], [acc.c(ri)], out=acc.ap(ri), in0=rv3[:, r, :], scalar=csc,
                      in1=acc.ap(ri), op0=ALU.mult, op1=ALU.add)
            if piece < 8:
                V("vector", "tensor_copy", [acc.c(ri)], [Sin32.c(piece)], out=Sin32.ap(piece), in_=acc.ap(ri))
            elif piece == 8:
                V("vector", "tensor_copy", [acc.c(ri)], [skA.c(0)], out=skA.ap(0)[0:64, :], in_=acc.ap(ri)[0:64, :])
                V("vector", "tensor_copy", [acc.c(ri)], [skB.c(0)], out=skB.ap(0)[64:128, :], in_=acc.ap(ri)[64:128, :])
            else:
                V("vector", "tensor_copy", [acc.c(ri)], [svt.c(0)], out=svt.ap(0), in_=acc.ap(ri))

        W2 = Arena(S, arena_t, ARENA_BYTES)
        W2.top = p2_top
        NB = 3
        sqT = W2.alloc("sqT", 4, TH, BF16)
        Ss = W2.alloc("Ss", NB, 512, F32)
        Pb = W2.alloc("Pb", NB, 512, BF16)
        Pn = W2.alloc("Pn", NB, 512, BF16)
        PT = W2.alloc("PT", NB, 512, BF16)
        st = W2.alloc("st", NB, 16, F32)
        pp = Pool(range(6))
        hp = Pool([6, 7])
        bp = Pool(range(NB))
        wq_state = {}
        po_bank = {}

        def prep(c):
            def g():
                if c % 2 == 0:
                    wq_state["w"] = wblock(9 + c // 2)
                wsq, wsqc = wq_state["w"]
                for half in range(2):
                    pb = pp.get()
                    while pb is None:
                        yield
                        pb = pp.get()
                    for kc in range(KC):
                        MM(psum[pb][:, :], wsq[:, kc, (c % 2) * 128:(c % 2 + 1) * 128], nT.ap(H(kc, half)), kc == 0, kc == KC - 1,
                           [wsqc, nT.c(H(kc, half))], [pcell[pb]])
                    yield
                    rope(pb, ropeS, half, 64, sqT.ap((c % 2) * 2 + half), sqT.c((c % 2) * 2 + half), t1, t2)
                    pp.put(pb)
                    yield
            return g

        def block(c, half, nb):
            def g():
                n = half * 4 + nb
                sq_ = sqT.ap((c % 2) * 2 + half)
                sqc = sqT.c((c % 2) * 2 + half)
                it = bp.get()
                while it is None:
                    yield
                    it = bp.get()
                if (c, half) not in po_bank:
                    po = hp.get()
                    while po is None:
                        yield
                        po = hp.get()
                    po_bank[(c, half)] = po
                po = po_bank[(c, half)]
                ps_ = pp.get()
                while ps_ is None:
                    yield
                    ps_ = pp.get()
                ps3 = psum[ps_][:, :].rearrange("p (i k) -> p i k", i=2)
                for i, skx in enumerate((skA, skB)):
                    MM(ps3[:, i, :], sq_[:, nb * 128:(nb + 1) * 128], skx.flat[:, n * 128:n * 128 + 256], True, True,
                       [sqc, skx.c(n), skx.c(n + 1)], [pcell[ps_]])
                yield
                msk = (mask0.ap(0) if n == 0 else maskg)
                ss3 = Ss.ap(it).rearrange("p (i k) -> p i k", i=2)
                sv_ = st.ap(it)
                V("vector", "scalar_tensor_tensor", [pcell[ps_], mask0.c(0), mixc.c(0)], [Ss.c(it)], out=ss3, in0=ps3, scalar=0.125,
                  in1=msk.unsqueeze(1).to_broadcast([128, 2, 256]), op0=ALU.mult, op1=ALU.add)
                pp.put(ps_)
                V("vector", "tensor_reduce", [Ss.c(it)], [st.c(it)], out=sv_[:, 0:2], in_=ss3, axis=AX.X, op=ALU.max)
                V("vector", "tensor_tensor", [st.c(it), mixc.c(0)], [st.c(it)], out=sv_[:, 2:4], in0=sv_[:, 0:2], in1=sinks[:, 2 * c:2 * c + 2], op=ALU.max)
                V("vector", "tensor_scalar", [st.c(it)], [st.c(it)], out=sv_[:, 4:6], in0=sv_[:, 2:4], scalar1=-1.0, scalar2=None, op0=ALU.mult)
                V("vector", "tensor_tensor", [st.c(it), mixc.c(0)], [st.c(it)], out=sv_[:, 8:10], in0=sinks[:, 2 * c:2 * c + 2], in1=sv_[:, 2:4], op=ALU.subtract)
                yield
                pb3 = Pb.ap(it).rearrange("p (i k) -> p i k", i=2)
                for i in range(2):
                    V("scalar", "activation", [Ss.c(it), st.c(it)], [Pb.c(it), st.c(it)], out=pb3[:, i, :], in_=ss3[:, i, :], func=AF.Exp,
                      bias=sv_[:, 4 + i:5 + i], scale=1.0, accum_out=sv_[:, 6 + i:7 + i])
                V("scalar", "activation", [st.c(it)], [st.c(it)], out=sv_[:, 8:10], in_=sv_[:, 8:10], func=AF.Exp)
                yield
                V("vector", "tensor_tensor", [st.c(it)], [st.c(it)], out=sv_[:, 10:12], in0=sv_[:, 6:8], in1=sv_[:, 8:10], op=ALU.add)
                V("vector", "reciprocal", [st.c(it)], [st.c(it)], out=sv_[:, 12:14], in_=sv_[:, 10:12])
                pn3 = Pn.ap(it).rearrange("p (i k) -> p i k", i=2)
                for i in range(2):
                    V("vector", "tensor_scalar", [Pb.c(it), st.c(it)], [Pn.c(it)], out=pn3[:, i, :], in0=pb3[:, i, :], scalar1=sv_[:, 12 + i:13 + i],
                      scalar2=None, op0=ALU.mult)
                yield
                pt_ = pp.get()
                while pt_ is None:
                    yield
                    pt_ = pp.get()
                ptv = psum[pt_][:, :].bitcast(BF16)[:, 0:512]
                for i in range(2):
                    for kt in range(2):
                        S.add("tensor", "transpose", dict(out=ptv[:, (i * 2 + kt) * 128:(i * 2 + kt + 1) * 128],
                                                           in_=pn3[:, i, kt * 128:(kt + 1) * 128], identity=identb.ap(0)),
                              reads=[Pn.c(it), identb.c(0)], writes=[pcell[pt_]])
                yield
                V("scalar", "activation", [pcell[pt_]], [PT.c(it)], out=PT.ap(it), in_=ptv, func=AF.Copy)
                pp.put(pt_)
                yield
                for i in range(2):
                    for kt in range(2):
                        MM(psum[po][64 * i:64 * i + 64, nb * 128:(nb + 1) * 128], svt.ap(n + kt)[:, 64 * i:64 * i + 64],
                           PT.ap(it)[:, (i * 2 + kt) * 128:(i * 2 + kt + 1) * 128], kt == 0, kt == 1,
                           [svt.c(n + kt), PT.c(it)], [pcell[po]])
                bp.put(it)
            return g

        def finish(c, half):
            def g():
                po = po_bank[(c, half)]
                V("scalar", "activation", [pcell[po]], [aT.c((8 + c) * 2 + half)], out=aT.ap((8 + c) * 2 + half), in_=psum[po][:, :], func=AF.Copy)
                hp.put(po)
                yield
            return g

        tasks = []
        prep_id, fin_ids = {}, {}
        for c in range(8):
            deps = [prep_id[c - 1]] if c >= 1 else []
            if c >= 2:
                deps += fin_ids[c - 2]
            prep_id[c] = len(tasks)
            tasks.append((prep(c), deps))
            fin_ids[c] = []
            for half in range(2):
                bl = []
                for nb in range(4):
                    bl.append(len(tasks))
                    tasks.append((block(c, half, nb), [prep_id[c]]))
                fin_ids[c].append(len(tasks))
                tasks.append((finish(c, half), bl))
        run_tasks(tasks, 4)

        W3 = Arena(S, arena_t, ARENA_BYTES)
        W3.top = p2_top
        Ra = Arena(S, arena_t, ARENA_BYTES)
        Ra.top = ropeS.off
        Rb = Arena(S, arena_t, ARENA_BYTES)
        Rb.top = skA.off
        hb = []
        for k_ in range(2):
            A1 = W3 if k_ == 0 else Ra
            A2 = W3 if k_ == 0 else Rb
            d_ = dict(qT=A1.alloc(f"qT{k_}", 2, TH, BF16), qxi=A1.alloc(f"qxi{k_}", 2, TH, BF16), q32=A1.alloc(f"q32{k_}", 1, TH, F32),
                      rstd=A1.alloc(f"rstdg{k_}", 1, TH, F32), sgt=A2.alloc(f"sgt{k_}", 2, TH, F32), sTm=A2.alloc(f"sTm{k_}", 2, TH, BF16),
                      SinS=W3.alloc(f"SinS{k_}", 8, 128, BF16))
            hb.append(d_)
        assert Ra.top <= ropeS.off + ropeS.nbytes and Rb.top <= svt.off + svt.nbytes
        r2 = t1
        mean = t2
        pp = Pool(range(6))
        hbp = Pool(range(2))

        def getbank():
            b_ = pp.get()
            while b_ is None:
                yield None
                b_ = pp.get()
            yield b_

        def head(h):
            def g():
                k_ = hbp.get()
                while k_ is None:
                    yield
                    k_ = hbp.get()
                B_ = hb[k_]
                qT, qxi, q32, rstd, sgt, sTm, SinS = B_["qT"], B_["qxi"], B_["q32"], B_["rstd"], B_["sgt"], B_["sTm"], B_["SinS"]
                r32 = q32
                wq_, wqc = wblock(13 + h)
                for n in range(8):
                    V("scalar", "mul", [Sin32.c(h)], [SinS.c(n)], out=SinS.ap(n), in_=Sin32.ap(h), mul=float(GAMMA[h] ** (128 * n)))
                for half in range(2):
                    pb = None
                    while pb is None:
                        pb = pp.get()
                        if pb is None:
                            yield
                    for kc in range(KC):
                        MM(psum[pb][:, :], wq_[:, kc, 0:128], nT.ap(H(kc, half)), kc == 0, kc == KC - 1, [wqc, nT.c(H(kc, half))], [pcell[pb]])
                    yield
                    rope(pb, ropeR, half, 128, q32.ap(0), q32.c(0), t1, t2)
                    pp.put(pb)
                    V("scalar", "activation", [q32.c(0)], [qT.c(half)], out=qT.ap(half), in_=q32.ap(0), func=AF.Copy)
                    V("vector", "tensor_tensor", [q32.c(0), mixc.c(0)], [qxi.c(half)], out=qxi.ap(half).rearrange("p (n c) -> p n c", n=4),
                      in0=q32.ap(0).rearrange("p (n c) -> p n c", n=4), in1=xi[:, h, :].unsqueeze(1).to_broadcast([128, 4, 128]), op=ALU.mult)
                    yield
                    pg = None
                    while pg is None:
                        pg = pp.get()
                        if pg is None:
                            yield
                    for kc in range(KC):
                        MM(psum[pg][:, :], wq_[:, kc, 128:256], nT.ap(H(kc, half)), kc == 0, kc == KC - 1, [wqc, nT.c(H(kc, half))], [pcell[pg]])
                    yield
                    V("scalar", "activation", [pcell[pg]], [sgt.c(half)], out=sgt.ap(half), in_=psum[pg][:, :], func=AF.Silu)
                    pp.put(pg)
                    yield
                for half in range(2):
                    pss = None
                    while pss is None:
                        pss = pp.get()
                        if pss is None:
                            yield
                    for nb in range(4):
                        MM(psum[pss][:, nb * 128:(nb + 1) * 128], kT.ap(h * 2 + half)[:, nb * 128:(nb + 1) * 128], qT.ap(half)[:, nb * 128:(nb + 1) * 128],
                           True, True, [kT.c(h * 2 + half), qT.c(half)], [pcell[pss]])
                    yield
                    V("vector", "tensor_tensor", [pcell[pss], mixc.c(0)], [sTm.c(half)], out=sTm.ap(half).rearrange("p (n c) -> p n c", n=4),
                      in0=psum[pss][:, :].rearrange("p (n c) -> p n c", n=4), in1=dmat[:, h, :].unsqueeze(1).to_broadcast([128, 4, 128]), op=ALU.mult)
                    pp.put(pss)
                    yield
                    pr = None
                    while pr is None:
                        pr = pp.get()
                        if pr is None:
                            yield
                    for nb in range(4):
                        n = half * 4 + nb
                        oap = psum[pr][:, nb * 128:(nb + 1) * 128]
                        qx = qxi.ap(half)[:, nb * 128:(nb + 1) * 128]
                        MM(oap, vtok.ap(n * 8 + h), sTm.ap(half)[:, nb * 128:(nb + 1) * 128], True, False,
                           [vtok.c(n * 8 + h), sTm.c(half)], [pcell[pr]])
                        if n > 0:
                            MM(oap, Sloc.ap(h * 8 + n), qx, False, False, [Sloc.c(h * 8 + n), qxi.c(half)], [pcell[pr]])
                        MM(oap, SinS.ap(n), qx, False, True, [SinS.c(n), qxi.c(half)], [pcell[pr]])
                    yield
                    p1 = None
                    while p1 is None:
                        p1 = pp.get()
                        if p1 is None:
                            yield
                    p2 = None
                    while p2 is None:
                        p2 = pp.get()
                        if p2 is None:
                            yield
                    V("scalar", "activation", [pcell[pr]], [r32.c(0)], out=r32.ap(0), in_=psum[pr][:, :], func=AF.Copy)
                    V("scalar", "activation", [pcell[pr]], [r2.c(0)], out=r2.ap(0), in_=psum[pr][:, :], func=AF.Square)
                    pp.put(pr)
                    MM(psum[p1][:, :], ones_f.ap(0), r32.ap(0), True, True, [ones_f.c(0), r32.c(0)], [pcell[p1]])
                    MM(psum[p2][:, :], ones_f.ap(0), r2.ap(0), True, True, [ones_f.c(0), r2.c(0)], [pcell[p2]])
                    V("vector", "tensor_scalar", [pcell[p1]], [mean.c(0)], out=mean.ap(0), in0=psum[p1][:, :], scalar1=1.0 / 128, scalar2=None, op0=ALU.mult)
                    V("vector", "tensor_tensor", [mean.c(0)], [rstd.c(0)], out=rstd.ap(0), in0=mean.ap(0), in1=mean.ap(0), op=ALU.mult)
                    V("vector", "scalar_tensor_tensor", [pcell[p2], rstd.c(0)], [rstd.c(0)], out=rstd.ap(0), in0=psum[p2][:, :], scalar=1.0 / 128,
                      in1=rstd.ap(0), op0=ALU.mult, op1=ALU.subtract)
                    pp.put(p1)
                    pp.put(p2)
                    V("scalar", "activation", [rstd.c(0), epsc.c(0)], [rstd.c(0)], out=rstd.ap(0), in_=rstd.ap(0), func=AF.Sqrt, bias=epsc.ap(0)[:, 0:1], scale=1.0)
                    V("vector", "reciprocal", [rstd.c(0)], [rstd.c(0)], out=rstd.ap(0), in_=rstd.ap(0))
                    V("vector", "tensor_tensor", [r32.c(0), mean.c(0)], [r32.c(0)], out=r32.ap(0), in0=r32.ap(0), in1=mean.ap(0), op=ALU.subtract)
                    V("vector", "tensor_tensor", [r32.c(0), rstd.c(0)], [r32.c(0)], out=r32.ap(0), in0=r32.ap(0), in1=rstd.ap(0), op=ALU.mult)
                    V("vector", "scalar_tensor_tensor", [r32.c(0), mixc.c(0), sgt.c(half)], [aT.c(h * 2 + half)], out=aT.ap(h * 2 + half), in0=r32.ap(0),
                      scalar=gng[:, h:h + 1], in1=sgt.ap(half), op0=ALU.mult, op1=ALU.mult)
                    yield
                hbp.put(k_)
            return g

        run_tasks([(head(h), []) for h in range(8)], 2)

        if DEBUG_A:
            for kc in range(KC):
                for half in range(2):
                    V("vector", "tensor_copy", [aT.c(H(kc, half))], [hT.c(H(kc, half))], out=hT.ap(H(kc, half)), in_=aT.ap(H(kc, half)))
        else:
            proj_residual(w_out, aT, reload=hspill, after_half=(lambda half: norm_both(2)) if "xattn" in stages else None)

    def xattn(pre_normed=False):
        if not pre_normed:
            rmsnorm(2)
        X = Arena(S, arena_t, ARENA_BYTES)
        X.top = phase_base
        qo = X.alloc("qo", 32, TH, BF16)
        kmT = X.alloc("kmT", 16, 256, BF16)
        vm = X.alloc("vm", 2 * 8, 256, BF16)
        mnT = X.alloc("mnT", 16, 256, BF16)
        xtop = X.top
        m32 = X.alloc("m32", 16, 256, F32)
        msq = X.alloc("msq", 2, 256, BF16)
        mrs = X.alloc("mrs", 1, 256, F32)
        if "mixer" not in stages:
            S.add("gpsimd", "dma_start", dict(out=identb.ap(0), in_=ident_d2), writes=[identb.c(0)], dma=True)
        for kc in range(KC):
            LD(m32.ap(kc), memT[kc * 128:(kc + 1) * 128, :], [m32.c(kc)])
        pb = next_ps()
        for kc in range(KC):
            s_ = kc % 2
            V("vector", "tensor_tensor", [m32.c(kc)], [msq.c(s_)], out=msq.ap(s_), in0=m32.ap(kc), in1=m32.ap(kc), op=ALU.mult)
            MM(psum[pb][:, 0:256], ones_b.ap(0), msq.ap(s_), kc == 0, kc == KC - 1, [ones_b.c(0), msq.c(s_)], [pcell[pb]])
        V("vector", "tensor_scalar", [pcell[pb]], [mrs.c(0)], out=mrs.ap(0), in0=psum[pb][:, 0:256], scalar1=1.0 / D, scalar2=EPS, op0=ALU.mult, op1=ALU.add)
        V("scalar", "activation", [mrs.c(0)], [mrs.c(0)], out=mrs.ap(0), in_=mrs.ap(0), func=AF.Sqrt)
        V("vector", "reciprocal", [mrs.c(0)], [mrs.c(0)], out=mrs.ap(0), in_=mrs.ap(0))
        for kc in range(KC):
            V("vector", "scalar_tensor_tensor", [m32.c(kc), gn.c(0), mrs.c(0)], [mnT.c(kc)], out=mnT.ap(kc), in0=m32.ap(kc),
              scalar=gn.ap(0)[:, 3 * KC + kc:3 * KC + kc + 1], in1=mrs.ap(0), op0=ALU.mult, op1=ALU.mult)
        for blk in range(8):
            wt, ci = next_wcell()
            wv = wt.v[:, ci, :].rearrange("p (k c) -> p k c", k=KC)
            wdma(wv, wkv_d[:, blk * 256:(blk + 1) * 256], KC, [wt.c(ci)])
            for j in range(2):
                pb = next_ps()
                for kc in range(KC):
                    MM(psum[pb][:, 0:256], wv[:, kc, j * 128:(j + 1) * 128], mnT.ap(kc), kc == 0, kc == KC - 1, [wt.c(ci), mnT.c(kc)], [pcell[pb]])
                V("scalar", "activation", [pcell[pb]], [kmT.c(blk * 2 + j)], out=kmT.ap(blk * 2 + j), in_=psum[pb][:, 0:256], func=AF.Copy)
        for blk in range(8):
            wt, ci = next_wcell()
            wv = wt.v[:, ci, :].rearrange("p (k c) -> p k c", k=KC)
            wdma(wv, wkv_d[:, D + blk * 256:D + (blk + 1) * 256], KC, [wt.c(ci)])
            for mt in range(2):
                pb = next_ps()
                for kc in range(KC):
                    MM(psum[pb][:, 0:256], mnT.ap(kc)[:, mt * 128:(mt + 1) * 128], wv[:, kc, :], kc == 0, kc == KC - 1, [wt.c(ci), mnT.c(kc)], [pcell[pb]])
                V("scalar", "activation", [pcell[pb]], [vm.c(mt * 8 + blk)], out=vm.ap(mt * 8 + blk), in_=psum[pb][:, 0:256], func=AF.Copy)
        for blk in range(8):
            wt, ci = next_wcell()
            wv = wt.v[:, ci, :].rearrange("p (k c) -> p k c", k=KC)
            wdma(wv, wq_d[:, blk * 256:(blk + 1) * 256], KC, [wt.c(ci)])
            for j in range(2):
                for half in range(2):
                    pb = next_ps()
                    for kc in range(KC):
                        MM(psum[pb][:, :], wv[:, kc, j * 128:(j + 1) * 128], nT.ap(H(kc, half)), kc == 0, kc == KC - 1, [wt.c(ci), nT.c(H(kc, half))], [pcell[pb]])
                    qc = H(blk * 2 + j, half)
                    V("scalar", "activation", [pcell[pb]], [qo.c(qc)], out=qo.ap(qc), in_=psum[pb][:, :], func=AF.Copy)
        X2 = Arena(S, arena_t, ARENA_BYTES)
        X2.top = xtop
        NX = 3
        Px = X2.alloc("Px", NX, 256, BF16)
        Pnx = X2.alloc("Pnx", NX, 256, BF16)
        stx = X2.alloc("stx", NX, 8, F32)
        PTx = X2.alloc("PTx", 2, 1024, BF16)
        SC = float(512 ** -0.5)
        pp = Pool(range(6))
        xbp = Pool(range(NX))
        ptp = Pool(range(2))
        pt_slot = {}

        def xtile(hd, half, tq):
            def g():
                it = xbp.get()
                while it is None:
                    yield
                    it = xbp.get()
                if (hd, half) not in pt_slot:
                    sl = ptp.get()
                    while sl is None:
                        yield
                        sl = ptp.get()
                    pt_slot[(hd, half)] = sl
                sl = pt_slot[(hd, half)]
                ptx3 = PTx.ap(sl).rearrange("p (m q) -> p m q", m=2)
                ps_ = pp.get()
                while ps_ is None:
                    yield
                    ps_ = pp.get()
                for cc in range(4):
                    MM(psum[ps_][:, 0:256], qo.ap(H(hd * 4 + cc, half))[:, tq * 128:(tq + 1) * 128], kmT.ap(hd * 4 + cc), cc == 0, cc == 3,
                       [qo.c(H(hd * 4 + cc, half)), kmT.c(hd * 4 + cc)], [pcell[ps_]])
                yield
                sv_ = stx.ap(it)
                V("vector", "tensor_reduce", [pcell[ps_]], [stx.c(it)], out=sv_[:, 0:1], in_=psum[ps_][:, 0:256], axis=AX.X, op=ALU.max)
                V("vector", "tensor_scalar", [stx.c(it)], [stx.c(it)], out=sv_[:, 1:2], in0=sv_[:, 0:1], scalar1=-SC, scalar2=None, op0=ALU.mult)
                yield
                V("scalar", "activation", [pcell[ps_], stx.c(it)], [Px.c(it), stx.c(it)], out=Px.ap(it), in_=psum[ps_][:, 0:256], func=AF.Exp,
                  bias=sv_[:, 1:2], scale=SC, accum_out=sv_[:, 2:3])
                pp.put(ps_)
                yield
                V("vector", "reciprocal", [stx.c(it)], [stx.c(it)], out=sv_[:, 3:4], in_=sv_[:, 2:3])
                V("vector", "tensor_scalar", [Px.c(it), stx.c(it)], [Pnx.c(it)], out=Pnx.ap(it), in0=Px.ap(it), scalar1=sv_[:, 3:4], scalar2=None, op0=ALU.mult)
                yield
                pt_ = pp.get()
                while pt_ is None:
                    yield
                    pt_ = pp.get()
                ptv = psum[pt_][:, :].bitcast(BF16)[:, 0:256]
                for mt in range(2):
                    S.add("tensor", "transpose", dict(out=ptv[:, mt * 128:(mt + 1) * 128], in_=Pnx.ap(it)[:, mt * 128:(mt + 1) * 128], identity=identb.ap(0)),
                          reads=[Pnx.c(it), identb.c(0)], writes=[pcell[pt_]])
                yield
                V("scalar", "activation", [pcell[pt_]], [PTx.c(sl)], out=ptx3[:, :, tq * 128:(tq + 1) * 128],
                  in_=ptv.rearrange("p (m q) -> p m q", m=2), func=AF.Copy)
                pp.put(pt_)
                xbp.put(it)
            return g

        def xfin(hd, half):
            def g():
                sl = pt_slot[(hd, half)]
                ptx3 = PTx.ap(sl).rearrange("p (m q) -> p m q", m=2)
                for cc in range(4):
                    dchunk = hd * 4 + cc
                    po = pp.get()
                    while po is None:
                        yield
                        po = pp.get()
                    for mt in range(2):
                        MM(psum[po][:, :], vm.ap(mt * 8 + dchunk // 2)[:, (dchunk % 2) * 128:(dchunk % 2 + 1) * 128], ptx3[:, mt, :], mt == 0, mt == 1,
                           [vm.c(mt * 8 + dchunk // 2), PTx.c(sl)], [pcell[po]])
                    yield
                    V("scalar", "activation", [pcell[po]], [qo.c(H(dchunk, half))], out=qo.ap(H(dchunk, half)), in_=psum[po][:, :], func=AF.Copy)
                    pp.put(po)
                ptp.put(sl)
            return g

        xt = []
        for hd in range(4):
            for half in range(2):
                ids = []
                for tq in range(4):
                    ids.append(len(xt))
                    xt.append((xtile(hd, half, tq), []))
                xt.append((xfin(hd, half), ids))
        run_tasks(xt, 4)
        proj_residual(wo_d, qo, after_half=(lambda half: norm_both(4)) if "ffn2" in stages else None)

    hsp_cell = [S.cell(f"hsp{i}") for i in range(32)]
    xs_cell = S.cell("xsend")
    xr_cell = S.cell("xrecv")
    if "ffn1" in stages:
        rmsnorm(0)
        ffn(wg1, wu1, wd1, phase_base, after_half=(lambda half: norm_both(1)) if "mixer" in stages else None)
    if "mixer" in stages:
        mixer(pre_normed="ffn1" in stages)
    if "xattn" in stages:
        xattn(pre_normed="mixer" in stages)
    if "ffn2" in stages:
        if "xattn" not in stages:
            rmsnorm(4)
        ffn(wg2, wu2, wd2, phase_base, after_half=(lambda half: norm_both(5, True)) if "final" in stages else None)
    if "final" in stages:
        if "ffn2" not in stages:
            rmsnorm(5, to_out=True)
        final = final_dmas
    else:
        final = []
        for kc in range(KC):
            for half in range(2):
                final.append(LD(outT[kc * 128:(kc + 1) * 128, half * TH:(half + 1) * TH], hT.ap(H(kc, half)), [], reads=[hT.c(H(kc, half))]))
    S.add("sync", "nop", dict(), after=final)
    S.emit(nc, es)
    es.close()
    return nc


_CACHE = {}


def _gain_cols(v):
    return np.ascontiguousarray(np.asarray(v, np.float32).reshape(KC, 128).T)


def _rope_table(d, pos):
    f32 = np.float32
    inv = (f32(10000.0) ** (-np.arange(0, d, 2, dtype=f32) / f32(d))).astype(f32)
    ang = (pos.astype(f32)[None, :] * inv[:, None]).astype(f32)
    cos = np.cos(ang.astype(np.float64)).astype(f32)
    sin = np.sin(ang.astype(np.float64)).astype(f32)
    p = np.arange(128)
    pp = p % d
    fi = pp % (d // 2)
    sign = np.where(pp < d // 2, -1.0, 1.0).astype(f32)
    return np.ascontiguousarray(np.concatenate([cos[fi], sin[fi] * sign[:, None]], axis=1), f32)


def _mix_consts(ret_gn_gain, swa_sinks):
    g = np.array(GAMMA, np.float64)
    j = np.arange(128)[:, None]
    c = np.arange(128)[None, :]
    m = np.zeros((128, MIXC_W), np.float32)
    sc = 128.0 ** -0.5
    for h in range(8):
        dm = np.where(c >= j, sc * g[h] ** np.maximum(c - j, 0), 0.0)
        m[:, MC_DMAT + h * 128:MC_DMAT + (h + 1) * 128] = dm
        m[:, MC_XI + h * 128:MC_XI + (h + 1) * 128] = (g[h] ** (np.arange(128) + 1))[None, :]
        m[:, MC_ZETA + h] = sc * g[h] ** (127 - np.arange(128))
    m[:, MC_GNG:MC_GNG + 8] = np.asarray(ret_gn_gain, np.float32).reshape(8, 128).T
    sk = np.asarray(swa_sinks, np.float32).reshape(16)
    order = [cc + 8 * i for cc in range(8) for i in range(2)]
    m[:, MC_SINK:MC_SINK + 16] = sk[order][None, :]
    i = np.arange(128)[:, None]
    kk = np.arange(256)[None, :]
    m[:, MC_MASK:MC_MASK + 256] = np.where((kk > i) & (kk <= i + 128), 0.0, NEG)
    return m


def _w_in_perm():
    cols = []
    for h in range(8):
        cols += list(range(1024 + h * 128, 1024 + (h + 1) * 128))
    for h in range(8):
        cols += list(range(2048 + h * 128, 2048 + (h + 1) * 128))
    cols += list(range(5120, 5376))
    for c in range(8):
        cols += list(range(4096 + c * 64, 4096 + (c + 1) * 64)) + list(range(4096 + (c + 8) * 64, 4096 + (c + 9) * 64))
    for h in range(8):
        cols += list(range(h * 128, (h + 1) * 128)) + list(range(3072 + h * 128, 3072 + (h + 1) * 128))
    return np.array(cols)


def _w_out_perm():
    rows = list(range(1024))
    for c in range(8):
        rows += list(range(1024 + c * 64, 1024 + (c + 1) * 64)) + list(range(1024 + (c + 8) * 64, 1024 + (c + 9) * 64))
    return np.array(rows)


def kernel(x, mem, ffn1_norm, ffn1_w_gate, ffn1_w_up, ffn1_w_down, mix_norm, w_in, ret_gn_gain,
           swa_sinks, w_out, xa_norm, mem_norm, xa_wq, xa_wkv, xa_wo, ffn2_norm, ffn2_w_gate,
           ffn2_w_up, ffn2_w_down, final_norm):
    st = tuple(STAGES)
    x = np.asarray(x, np.float32)
    mem = np.asarray(mem, np.float32)
    key = ("nc", st)
    if key not in _CACHE:
        _CACHE[key] = build_program(st)
    nc = _CACHE[key]
    f = lambda a: np.ascontiguousarray(np.asarray(a, np.float32)[0])
    gains = np.concatenate([_gain_cols(np.asarray(g).reshape(-1)) for g in
                            (ffn1_norm, mix_norm, xa_norm, mem_norm, ffn2_norm, final_norm)], axis=1)
    shared = {"gains": np.ascontiguousarray(gains, np.float32)}
    if "ffn1" in st:
        shared.update(ffn1_w_gate=f(ffn1_w_gate), ffn1_w_up=f(ffn1_w_up), ffn1_w_down=f(ffn1_w_down))
    if "ffn2" in st:
        shared.update(ffn2_w_gate=f(ffn2_w_gate), ffn2_w_up=f(ffn2_w_up), ffn2_w_down=f(ffn2_w_down))
    if "mixer" in st:
        shared["w_in_p"] = np.ascontiguousarray(f(w_in)[:, _w_in_perm()])
        shared["w_out_p"] = np.ascontiguousarray(f(w_out)[_w_out_perm(), :])
        shared["mixc"] = _mix_consts(np.asarray(ret_gn_gain).reshape(-1), np.asarray(swa_sinks).reshape(-1))
    if "mixer" in st or "xattn" in st:
        shared["ident"] = np.eye(128, dtype=np.float32)
    if "xattn" in st:
        shared.update(xa_wq=f(xa_wq), xa_wkv=f(xa_wkv), xa_wo=f(xa_wo))
    in_maps = []
    g64 = np.array(GAMMA, np.float64)
    for c in range(NCORES):
        b, q = c // 4, c % 4
        m = dict(shared)
        m["xT"] = np.ascontiguousarray(x[b, q * T:(q + 1) * T, :].T)
        if "mixer" in st:
            pos = np.arange(q * T, (q + 1) * T)
            m["ropeR"] = _rope_table(128, pos)
            m["ropeS"] = _rope_table(64, pos)
            cf = np.zeros((128, 40), np.float32)
            for r in range(4):
                if r < q:
                    cf[:, r * 8:(r + 1) * 8] = (g64 ** (1024 * (q - 1 - r)))[None, :]
                if r == q - 1:
                    cf[:, 32 + r] = 1.0
            m["coef"] = cf
            mk = shared["mixc"][:, MC_MASK:MC_MASK + 256].copy()
            if q == 0:
                mk[:, 0:128] = NEG
            m["mask0"] = np.ascontiguousarray(mk)
        if "xattn" in st:
            m["memT"] = np.ascontiguousarray(mem[b].T)
        in_maps.append(m)
    res = run_bass_kernel_spmd(nc, in_maps, core_ids=list(range(NCORES)))
    out = np.empty((2, 4096, D), np.float32)
    for c in range(NCORES):
        b, q = c // 4, c % 4
        out[b, q * T:(q + 1) * T, :] = res.results[c]["outT"].T
    return out
```

```python
import numpy as np
from contextlib import ExitStack
import concourse.bass as bass
import concourse.mybir as mybir
from concourse.bass_utils import run_bass_kernel_spmd

F32 = mybir.dt.float32
BF16 = mybir.dt.bfloat16
ALU = mybir.AluOpType
AF = mybir.ActivationFunctionType
AX = mybir.AxisListType

D = 2048
KC = 16
T = 1024
TH = 512
DFF = 5632
FC = 44
EPS = 1e-6
NCORES = 8

STAGES = ("ffn1", "mixer", "xattn", "ffn2", "final")
GAMMA = [1.0 - 2.0 ** (-5 - h) for h in range(8)]
MC_DMAT, MC_XI, MC_ZETA, MC_GNG, MC_SINK, MC_MASK = 0, 1024, 2048, 2056, 2064, 2080
MIXC_W = 2336
NEG = -30000.0
DEBUG_A = False


class Cell:
    __slots__ = ("name", "space", "off", "size", "last_w", "readers", "ov")

    def __init__(self, name, space, off=0, size=0):
        self.name, self.space, self.off, self.size = name, space, off, size
        self.last_w = []
        self.readers = {}
        self.ov = []


class Op:
    __slots__ = ("eng", "fn", "deps", "pos", "signal", "tick", "is_dma", "sem", "val", "waits", "gidx", "group", "own")

    def __init__(self, eng, fn, is_dma):
        self.eng, self.fn, self.is_dma = eng, fn, is_dma
        self.deps = []
        self.signal = False
        self.tick = None
        self.sem = None
        self.val = None
        self.waits = []


class Sched:
    ENGS = ["sync", "gpsimd", "scalar", "vector", "tensor"]
    NDQ = 8

    def __init__(self):
        self.ops = []
        self.sb_cells = []
        self.count = {e: 0 for e in self.ENGS}

    def sb_cell(self, name, off, size):
        c = Cell(name, "sb", off, size)
        for o in self.sb_cells:
            if o.off < off + size and off < o.off + o.size:
                o.ov.append(c)
                c.ov.append(o)
        self.sb_cells.append(c)
        return c

    def cell(self, name):
        return Cell(name, "x")

    def add(self, eng, meth, kw, reads=(), writes=(), dma=False, after=(), group=None, own_sem=False):
        op = Op(eng, (meth, kw), dma)
        op.group = group
        op.own = own_sem
        op.pos = self.count[eng]
        self.count[eng] += 1
        op.gidx = len(self.ops)
        deps = {}

        def dep(o):
            if o is not None and o is not op and not (group is not None and o.group == group):
                deps[id(o)] = o

        for o in after:
            dep(o)
        for c in reads:
            for y in [c] + c.ov:
                for w in y.last_w:
                    dep(w)
        for c in writes:
            for y in [c] + c.ov:
                for w in y.last_w:
                    dep(w)
                for r in y.readers.values():
                    dep(r)
        for c in reads:
            for y in [c] + c.ov:
                key = ("d", op.gidx) if dma else eng
                y.readers[key] = op
        for c in writes:
            for y in [c] + c.ov:
                if group is not None and y.last_w and y.last_w[0].group == group:
                    y.last_w.append(op)
                else:
                    y.last_w = [op]
                y.readers = {}
        best = {}
        for o in deps.values():
            if o.is_dma or o.own:
                best[("d", id(o))] = o
            else:
                if eng == "tensor" and o.eng == "tensor":
                    continue
                k = o.eng
                if k not in best or best[k].pos < o.pos:
                    best[k] = o
        op.deps = list(best.values())
        for o in op.deps:
            o.signal = True
        self.ops.append(op)
        return op

    def emit(self, nc, es):
        sems = {e: es.enter_context(nc.semaphore("c_" + e)) for e in self.ENGS}
        dq = {e: [es.enter_context(nc.semaphore(f"dq_{e}_{i}")) for i in range(self.NDQ)]
              for e in ("sync", "gpsimd", "scalar")}
        streams = {e: [] for e in self.ENGS}
        ticks = {e: 0 for e in self.ENGS}
        ndma = {e: 0 for e in self.ENGS}
        for op in self.ops:
            streams[op.eng].append(op)
            if op.is_dma:
                i = ndma[op.eng]
                ndma[op.eng] += 1
                op.sem = dq[op.eng][i % self.NDQ]
                op.val = 16 * (i // self.NDQ + 1)
                op.signal = True
            elif op.own:
                op.sem = es.enter_context(nc.semaphore(f"own_{op.gidx}"))
                op.val = 1
                op.signal = True
            elif op.signal:
                ticks[op.eng] += 1
                op.sem = sems[op.eng]
                op.val = ticks[op.eng]
        for e in self.ENGS:
            seen = {}
            for op in streams[e]:
                w = []
                if op.is_dma and op.val > 16:
                    w.append((op.sem, op.val - 16))
                for d in op.deps:
                    w.append((d.sem, d.val))
                for (s, v) in w:
                    if seen.get(id(s), 0) >= v:
                        continue
                    seen[id(s)] = v
                    op.waits.append((s, v))
        block = es.enter_context(nc.Block())

        def run(e, stream):
            for op in stream:
                for (s, v) in op.waits:
                    e.wait_ge(s, v)
                ins = getattr(e, op.fn[0])(**op.fn[1])
                if op.signal:
                    ins.then_inc(op.sem, 16 if op.is_dma else 1)

        @block.sync
        def _(e):
            run(e, streams["sync"])

        @block.gpsimd
        def _(e):
            run(e, streams["gpsimd"])

        @block.scalar
        def _(e):
            run(e, streams["scalar"])

        @block.vector
        def _(e):
            run(e, streams["vector"])

        @block.tensor
        def _(e):
            run(e, streams["tensor"])


class SbT:
    def __init__(self, S, arena, name, off, ncell, clen, dt):
        self.esz = 4 if dt == F32 else 2
        self.ncell, self.clen, self.dt = ncell, clen, dt
        self.off = off
        self.nbytes = ncell * clen * self.esz
        assert off % 4 == 0 and self.nbytes % 4 == 0
        w0, w1 = off // 4, (off + self.nbytes) // 4
        v = arena[:, w0:w1]
        if dt != F32:
            v = v.bitcast(dt)
        self.flat = v
        self.v = v.rearrange("p (a b) -> p a b", b=clen)
        self.cells = [S.sb_cell(f"{name}{i}", off + i * clen * self.esz, clen * self.esz) for i in range(ncell)]

    def ap(self, i):
        return self.v[:, i, :]

    def c(self, i):
        return self.cells[i]


class Arena:
    def __init__(self, S, arena, total):
        self.S, self.arena, self.total = S, arena, total
        self.top = 0

    def alloc(self, name, ncell, clen, dt, at=None):
        esz = 4 if dt == F32 else 2
        nb = ncell * clen * esz
        nb4 = (nb + 31) // 32 * 32
        if at is None:
            at = self.top
            self.top += nb4
        assert at + nb <= self.total, (name, at, nb, self.total)
        return SbT(self.S, self.arena, name, at, ncell, clen, dt)


ARENA_BYTES = 207 * 1024


class Pool:
    def __init__(self, items):
        self.free = list(items)

    def get(self):
        return self.free.pop(0) if self.free else None

    def put(self, x):
        self.free.append(x)


def run_tasks(tasks, width):
    done, started, active = set(), set(), []
    while len(done) < len(tasks):
        for i in range(len(tasks)):
            if len(active) >= width:
                break
            if i in started:
                continue
            if all(d in done for d in tasks[i][1]):
                started.add(i)
                active.append((i, tasks[i][0]()))
        assert active, "task deadlock"
        for (i, g) in list(active):
            try:
                next(g)
            except StopIteration:
                active.remove((i, g))
                done.add(i)


def build_program(stages=("ffn1", "mixer", "xattn", "ffn2", "final")):
    stages = set(stages)
    nc = bass.Bass("TRN2", target_bir_lowering=False)
    S = Sched()
    es = ExitStack()

    def din(name, shape, dt=F32):
        return nc.dram_tensor(name, shape, dt, kind="ExternalInput").ap()

    xT = din("xT", [D, T])
    gains = din("gains", [128, 6 * KC])
    if "ffn1" in stages:
        wg1 = din("ffn1_w_gate", [D, DFF]); wu1 = din("ffn1_w_up", [D, DFF]); wd1 = din("ffn1_w_down", [DFF, D])
    if "ffn2" in stages:
        wg2 = din("ffn2_w_gate", [D, DFF]); wu2 = din("ffn2_w_up", [D, DFF]); wd2 = din("ffn2_w_down", [DFF, D])
    if "mixer" in stages:
        w_in = din("w_in_p", [D, 42 * 128])
        w_out = din("w_out_p", [D, D])
        ropeR_d = din("ropeR", [128, 2 * T])
        ropeS_d = din("ropeS", [128, 2 * T])
        mixc_d = din("mixc", [128, MIXC_W])
        coef_d = din("coef", [128, 40])
        mask0_d = din("mask0", [128, 256])
        ident_d = din("ident", [128, 128])
        hspill = nc.dram_tensor("hspill", [D, T], F32).ap()
        send_t = nc.dram_tensor("xsend", [10 * 128, 128], F32)
        recv_t = nc.dram_tensor("xrecv", [4 * 10 * 128, 128], F32)
    if "xattn" in stages:
        memT = din("memT", [D, 256])
        wq_d = din("xa_wq", [D, D]); wkv_d = din("xa_wkv", [D, 2 * D]); wo_d = din("xa_wo", [D, D])
        ident_d2 = ident_d if "mixer" in stages else din("ident", [128, 128])
    outT = nc.dram_tensor("outT", [D, T], F32, kind="ExternalOutput").ap()

    arena_t = es.enter_context(nc.sbuf_tensor("arena", [128, ARENA_BYTES // 4], F32))
    A = Arena(S, arena_t, ARENA_BYTES)
    psum_all = es.enter_context(nc.psum_tensor("psall", [128, 8 * 512], F32))
    psum = [psum_all[:, i * 512:(i + 1) * 512] for i in range(8)]
    pcell = [S.cell(f"ps{i}") for i in range(8)]

    hT = A.alloc("hT", KC * 2, TH, F32)
    nT = A.alloc("nT", KC * 2, TH, BF16)
    wslot = [A.alloc(f"ws{i}", 2, 4096, BF16) for i in range(2)]
    gn = A.alloc("gn", 1, 6 * KC, F32)
    ones_b = A.alloc("ones_b", 1, 128, BF16)
    ones_f = A.alloc("ones_f", 1, 128, F32)
    identb = A.alloc("identb", 1, 128, BF16)
    epsc = A.alloc("epsc", 1, 8, F32)
    phase_base = A.top

    psi = [0]

    def next_ps():
        i = psi[0] % 6
        psi[0] += 1
        return i

    def next_ps_pair():
        if psi[0] % 2:
            psi[0] += 1
        i = psi[0] % 6
        psi[0] += 2
        return i

    hpsi = [0]

    def hold_ps():
        i = 6 + hpsi[0] % 2
        hpsi[0] += 1
        return i

    wsi = [0]

    def next_wcell():
        i = wsi[0] % 4
        wsi[0] += 1
        return wslot[i // 2], i % 2

    def next_ws():
        if wsi[0] % 2:
            wsi[0] += 1
        i = (wsi[0] // 2) % 2
        wsi[0] += 2
        return wslot[i]

    def H(kc, half):
        return kc * 2 + half

    gid = [0]

    def wdma(dst3, src2, nk, cells):
        gid[0] += 1
        for k0 in range(0, nk, 4):
            k1 = min(nk, k0 + 4)
            S.add("gpsimd", "dma_start", dict(out=dst3[:, k0:k1, :],
                                               in_=src2[k0 * 128:k1 * 128, :].rearrange("(k p) c -> p k c", p=128)),
                  writes=cells, dma=True, group=gid[0])

    def V(eng, meth, reads, writes, **kw):
        return S.add(eng, meth, kw, reads=reads, writes=writes)

    def MM(out, lhsT, rhs, start, stop, reads, writes):
        return S.add("tensor", "matmul", dict(out=out, lhsT=lhsT, rhs=rhs, start=start, stop=stop), reads=reads, writes=writes)

    def LD(out, in_, writes, reads=(), eng="sync"):
        return S.add(eng, "dma_start", dict(out=out, in_=in_), reads=reads, writes=writes, dma=True)

    V("vector", "memset", [], [ones_b.c(0)], ap=ones_b.ap(0), constant=1.0)
    V("vector", "memset", [], [ones_f.c(0)], ap=ones_f.ap(0), constant=1.0)
    V("vector", "memset", [], [epsc.c(0)], ap=epsc.ap(0), constant=EPS)
    LD(gn.ap(0), gains, [gn.c(0)])
    for half in range(2):
        for kc in range(KC):
            LD(hT.ap(H(kc, half)), xT[kc * 128:(kc + 1) * 128, half * TH:(half + 1) * TH], [hT.c(H(kc, half))])

    def rmsnorm(gi, tmp_base, to_out=False):
        At = Arena(S, arena_t, ARENA_BYTES)
        At.top = tmp_base
        sq = At.alloc("sq", 4, TH, BF16)
        rstd = At.alloc("rstd", 2, TH, F32)
        ot = At.alloc("ot", 4, TH, F32) if to_out else None
        fin = []
        for half in range(2):
            pb = next_ps()
            for kc in range(KC):
                s = kc % 4
                hc = H(kc, half)
                if kc % 2 == 0:
                    V("scalar", "activation", [hT.c(hc)], [sq.c(s)], out=sq.ap(s), in_=hT.ap(hc), func=AF.Square)
                else:
                    V("vector", "tensor_tensor", [hT.c(hc)], [sq.c(s)], out=sq.ap(s), in0=hT.ap(hc), in1=hT.ap(hc), op=ALU.mult)
                MM(psum[pb][:, :], ones_b.ap(0), sq.ap(s), kc == 0, kc == KC - 1, [sq.c(s), ones_b.c(0)], [pcell[pb]])
            V("vector", "tensor_scalar", [pcell[pb]], [rstd.c(half)], out=rstd.ap(half), in0=psum[pb][:, :],
              scalar1=1.0 / D, scalar2=EPS, op0=ALU.mult, op1=ALU.add)
            V("scalar", "activation", [rstd.c(half)], [rstd.c(half)], out=rstd.ap(half), in_=rstd.ap(half), func=AF.Sqrt)
            V("vector", "reciprocal", [rstd.c(half)], [rstd.c(half)], out=rstd.ap(half), in_=rstd.ap(half))
            for kc in range(KC):
                hc = H(kc, half)
                gsc = gn.ap(0)[:, gi * KC + kc:gi * KC + kc + 1]
                if not to_out:
                    V("vector", "scalar_tensor_tensor", [hT.c(hc), gn.c(0), rstd.c(half)], [nT.c(hc)],
                      out=nT.ap(hc), in0=hT.ap(hc), scalar=gsc, in1=rstd.ap(half), op0=ALU.mult, op1=ALU.mult)
                else:
                    o = kc % 4
                    V("vector", "scalar_tensor_tensor", [hT.c(hc), gn.c(0), rstd.c(half)], [ot.c(o)],
                      out=ot.ap(o), in0=hT.ap(hc), scalar=gsc, in1=rstd.ap(half), op0=ALU.mult, op1=ALU.mult)
                    fin.append(LD(outT[kc * 128:(kc + 1) * 128, half * TH:(half + 1) * TH], ot.ap(o), [], reads=[ot.c(o)]))
        return fin

    def ffn(wg, wu, wd, tmp_base):
        At = Arena(S, arena_t, ARENA_BYTES)
        At.top = tmp_base
        act = At.alloc("act", 22 * 2, TH, BF16)
        sg = At.alloc("sg", 2, TH, F32)
        sgi = 0
        for ffh in range(2):
            for p in range(11):
                f0 = (ffh * 22 + 2 * p) * 128
                wt = next_ws()
                wv = wt.flat.rearrange("p (g k c) -> p g k c", g=2, k=KC)
                wdma(wv[:, 0], wg[:, f0:f0 + 256], KC, [wt.c(0)])
                wdma(wv[:, 1], wu[:, f0:f0 + 256], KC, [wt.c(1)])
                for j in range(2):
                    fl = 2 * p + j
                    for half in range(2):
                        pg, pu = next_ps(), next_ps()
                        for kc in range(KC):
                            MM(psum[pg][:, :], wv[:, 0, kc, j * 128:(j + 1) * 128], nT.ap(H(kc, half)), kc == 0, kc == KC - 1,
                               [wt.c(0), nT.c(H(kc, half))], [pcell[pg]])
                        for kc in range(KC):
                            MM(psum[pu][:, :], wv[:, 1, kc, j * 128:(j + 1) * 128], nT.ap(H(kc, half)), kc == 0, kc == KC - 1,
                               [wt.c(1), nT.c(H(kc, half))], [pcell[pu]])
                        si = sgi % 2
                        sgi += 1
                        V("scalar", "activation", [pcell[pg]], [sg.c(si)], out=sg.ap(si), in_=psum[pg][:, :], func=AF.Silu)
                        V("vector", "tensor_tensor", [pcell[pu], sg.c(si)], [act.c(fl * 2 + half)],
                          out=act.ap(fl * 2 + half), in0=psum[pu][:, :], in1=sg.ap(si), op=ALU.mult)
            for dblk in range(8):
                wt = next_ws()
                wv = wt.flat[:, 0:22 * 256].rearrange("p (k c) -> p k c", k=22)
                r0 = ffh * 22 * 128
                wdma(wv, wd[r0:r0 + 22 * 128, dblk * 256:(dblk + 1) * 256], 22, [wt.c(0), wt.c(1)])
                for j in range(2):
                    dc = dblk * 2 + j
                    for half in range(2):
                        pb = next_ps()
                        for fl in range(22):
                            MM(psum[pb][:, :], wv[:, fl, j * 128:(j + 1) * 128], act.ap(fl * 2 + half), fl == 0, fl == 21,
                               [wt.c(0), wt.c(1), act.c(fl * 2 + half)], [pcell[pb]])
                        V("vector", "scalar_tensor_tensor", [pcell[pb], hT.c(H(dc, half))], [hT.c(H(dc, half))],
                          out=hT.ap(H(dc, half)), in0=psum[pb][:, :], scalar=0.5, in1=hT.ap(H(dc, half)), op0=ALU.mult, op1=ALU.add)

    def proj_residual(w_d, src, reload=None):
        for dblk in range(8):
            wt, ci = next_wcell()
            wv = wt.v[:, ci, :].rearrange("p (k c) -> p k c", k=KC)
            wdma(wv, w_d[:, dblk * 256:(dblk + 1) * 256], KC, [wt.c(ci)])
            for j in range(2):
                dc = dblk * 2 + j
                for half in range(2):
                    pb = next_ps()
                    for ac in range(KC):
                        MM(psum[pb][:, :], wv[:, ac, j * 128:(j + 1) * 128], src.ap(H(ac, half)), ac == 0, ac == KC - 1,
                           [wt.c(ci), src.c(H(ac, half))], [pcell[pb]])
                    hc = H(dc, half)
                    if reload is not None:
                        LD(hT.ap(hc), reload[dc * 128:(dc + 1) * 128, half * TH:(half + 1) * TH], [hT.c(hc)])
                    V("vector", "tensor_tensor", [pcell[pb], hT.c(hc)], [hT.c(hc)],
                      out=hT.ap(hc), in0=psum[pb][:, :], in1=hT.ap(hc), op=ALU.add)

    def rope(pb, tab, half, dh, out_ap, out_cell, t1, t2):
        V("vector", "tensor_tensor", [pcell[pb], tab.c(half)], [t1.c(0)], out=t1.ap(0), in0=psum[pb][:, :], in1=tab.ap(half), op=ALU.mult)
        hd = dh // 2
        for base in range(0, 128, dh):
            for (dst, src) in ((base, base + hd), (base + hd, base)):
                V("vector", "tensor_tensor", [pcell[pb], tab.c(2 + half)], [t2.c(0)],
                  out=t2.ap(0)[dst:dst + hd, :], in0=psum[pb][src:src + hd, :], in1=tab.ap(2 + half)[dst:dst + hd, :], op=ALU.mult)
        if isinstance(out_ap, tuple):
            for (lo, oap) in ((0, out_ap[0]), (64, out_ap[1])):
                V("vector", "tensor_tensor", [t1.c(0), t2.c(0)], out_cell, out=oap[lo:lo + 64, :], in0=t1.ap(0)[lo:lo + 64, :], in1=t2.ap(0)[lo:lo + 64, :], op=ALU.add)
        else:
            V("vector", "tensor_tensor", [t1.c(0), t2.c(0)], out_cell if isinstance(out_cell, list) else [out_cell], out=out_ap, in0=t1.ap(0), in1=t2.ap(0), op=ALU.add)

    def mixer():
        rmsnorm(1, phase_base)
        for kc in range(KC):
            for half in range(2):
                S.add("sync", "dma_start", dict(out=hspill[kc * 128:(kc + 1) * 128, half * TH:(half + 1) * TH], in_=hT.ap(H(kc, half))),
                      reads=[hT.c(H(kc, half))], writes=[hsp_cell[H(kc, half)]], dma=True)
        R = Arena(S, arena_t, ARENA_BYTES)
        R.top = hT.off
        kT = R.alloc("kT", 16, TH, BF16)
        vtok = R.alloc("vtok", 64, 128, BF16)
        Sloc = R.alloc("Sloc", 64, 128, BF16)
        ropeR = R.alloc("ropeR", 4, TH, F32)
        ropeS = R.alloc("ropeS", 4, TH, F32)
        assert R.top <= hT.off + hT.nbytes
        P = Arena(S, arena_t, ARENA_BYTES)
        P.top = phase_base
        aT = P.alloc("aT", 32, TH, BF16)
        mixc = P.alloc("mixc", 1, MIXC_W, F32)
        coef = P.alloc("coef", 1, 40, F32)
        mask0 = P.alloc("mask0", 1, 256, F32)
        skA = P.alloc("skA", 9, 128, BF16)
        skB = P.alloc("skB", 9, 128, BF16)
        svt = P.alloc("svt", 9, 128, BF16)
        t1 = P.alloc("t1", 1, TH, F32)
        t2 = P.alloc("t2", 1, TH, F32)
        common_top = P.top
        mc = mixc.ap(0)
        dmat = mc[:, MC_DMAT:MC_DMAT + 1024].rearrange("p (h c) -> p h c", h=8)
        xi = mc[:, MC_XI:MC_XI + 1024].rearrange("p (h c) -> p h c", h=8)
        zeta = mc[:, MC_ZETA:MC_ZETA + 8]
        gng = mc[:, MC_GNG:MC_GNG + 8]
        sinks = mc[:, MC_SINK:MC_SINK + 16]
        maskg = mc[:, MC_MASK:MC_MASK + 256]
        V("vector", "memset", [], [skA.c(i) for i in range(9)], ap=skA.flat, constant=0.0)
        V("vector", "memset", [], [skB.c(i) for i in range(9)], ap=skB.flat, constant=0.0)
        LD(mixc.ap(0), mixc_d, [mixc.c(0)])
        LD(coef.ap(0), coef_d, [coef.c(0)])
        LD(mask0.ap(0), mask0_d, [mask0.c(0)])
        for i in range(4):
            LD(ropeR.ap(i), ropeR_d[:, i * TH:(i + 1) * TH], [ropeR.c(i)])
            LD(ropeS.ap(i), ropeS_d[:, i * TH:(i + 1) * TH], [ropeS.c(i)])
        S.add("gpsimd", "dma_start", dict(out=identb.ap(0), in_=ident_d), writes=[identb.c(0)], dma=True)

        def wblock(b):
            wt, ci = next_wcell()
            wv = wt.v[:, ci, :].rearrange("p (k c) -> p k c", k=KC)
            wdma(wv, w_in[:, b * 256:(b + 1) * 256], KC, [wt.c(ci)])
            return wv, wt.c(ci)

        def proj_fm(wv, wc, j, half):
            pb = next_ps()
            for kc in range(KC):
                MM(psum[pb][:, :], wv[:, kc, j * 128:(j + 1) * 128], nT.ap(H(kc, half)), kc == 0, kc == KC - 1,
                   [wc, nT.c(H(kc, half))], [pcell[pb]])
            return pb

        P1 = Arena(S, arena_t, ARENA_BYTES)
        P1.top = common_top
        kz = P1.alloc("kz", 2, 1024, BF16)
        Rst = P1.alloc("Rst", 2, 128, F32)
        Lst = P1.alloc("Lst", 10, 128, F32)
        for hp in range(4):
            wk, wkc = wblock(hp)
            wvv, wvc = wblock(4 + hp)
            for j in range(2):
                h = 2 * hp + j
                for half in range(2):
                    pb = proj_fm(wk, wkc, j, half)
                    rope(pb, ropeR, half, 128, kT.ap(h * 2 + half), kT.c(h * 2 + half), t1, t2)
            for n in range(8):
                pb = next_ps()
                half, o = n // 4, (n % 4) * 128
                for kc in range(KC):
                    MM(psum[pb][:, 0:256], nT.ap(H(kc, half))[:, o:o + 128], wvv[:, kc, :], kc == 0, kc == KC - 1,
                       [wvc, nT.c(H(kc, half))], [pcell[pb]])
                for j in range(2):
                    h = 2 * hp + j
                    V("scalar", "activation", [pcell[pb]], [vtok.c(n * 8 + h)], out=vtok.ap(n * 8 + h), in_=psum[pb][:, j * 128:(j + 1) * 128], func=AF.Copy)
            for j in range(2):
                h = 2 * hp + j
                gC = float(GAMMA[h] ** 128)
                pb = next_ps()
                pbv = psum[pb][:, :].bitcast(BF16)
                for n in range(8):
                    half, o = n // 4, (n % 4) * 128
                    S.add("tensor", "transpose", dict(out=pbv[:, n * 128:(n + 1) * 128], in_=kT.ap(h * 2 + half)[:, o:o + 128], identity=identb.ap(0)),
                          reads=[kT.c(h * 2 + half), identb.c(0)], writes=[pcell[pb]])
                kzi = h % 2
                V("vector", "tensor_scalar", [pcell[pb], mixc.c(0)], [kz.c(kzi)], out=kz.ap(kzi), in0=pbv, scalar1=zeta[:, h:h + 1], scalar2=None, op0=ALU.mult)
                pbs = [next_ps(), next_ps()]
                for n in range(8):
                    pb2 = pbs[n // 4]
                    o = (n % 4) * 128
                    MM(psum[pb2][:, o:o + 128], kz.ap(kzi)[:, n * 128:(n + 1) * 128], vtok.ap(n * 8 + h), True, True,
                       [kz.c(kzi), vtok.c(n * 8 + h)], [pcell[pb2]])
                ri = 0
                for n in range(8):
                    pb2 = pbs[n // 4]
                    o = (n % 4) * 128
                    dst_ap, dst_c = (Rst.ap(1 - ri), Rst.c(1 - ri)) if n < 7 else (Lst.ap(h), Lst.c(h))
                    if n == 0:
                        V("vector", "tensor_copy", [pcell[pb2]], [dst_c], out=dst_ap, in_=psum[pb2][:, o:o + 128])
                    else:
                        V("vector", "scalar_tensor_tensor", [pcell[pb2], Rst.c(ri)], [dst_c], out=dst_ap, in0=Rst.ap(ri), scalar=gC,
                          in1=psum[pb2][:, o:o + 128], op0=ALU.mult, op1=ALU.add)
                    ri = 1 - ri
                    if n < 7:
                        V("scalar", "activation", [Rst.c(ri)], [Sloc.c(h * 8 + n + 1)], out=Sloc.ap(h * 8 + n + 1), in_=Rst.ap(ri), func=AF.Copy)
        wsk, wskc = wblock(8)
        for half in range(2):
            pb = proj_fm(wsk, wskc, 0, half)
            rope(pb, ropeS, half, 64, (skA.flat[:, 128 + half * TH:128 + (half + 1) * TH], skB.flat[:, 128 + half * TH:128 + (half + 1) * TH]),
                 [skA.c(1 + half * 4 + nn) for nn in range(4)] + [skB.c(1 + half * 4 + nn) for nn in range(4)], t1, t2)
        for n in range(8):
            pb = next_ps()
            half, o = n // 4, (n % 4) * 128
            for kc in range(KC):
                MM(psum[pb][:, 0:128], nT.ap(H(kc, half))[:, o:o + 128], wsk[:, kc, 128:256], kc == 0, kc == KC - 1,
                   [wskc, nT.c(H(kc, half))], [pcell[pb]])
            V("scalar", "activation", [pcell[pb]], [svt.c(1 + n)], out=svt.ap(1 + n), in_=psum[pb][:, 0:128], func=AF.Copy)
        V("vector", "tensor_copy", [skA.c(8)], [Lst.c(8)], out=Lst.ap(8)[0:64, :], in_=skA.ap(8)[0:64, :])
        V("vector", "tensor_copy", [skB.c(8)], [Lst.c(8)], out=Lst.ap(8)[64:128, :], in_=skB.ap(8)[64:128, :])
        V("vector", "tensor_copy", [svt.c(8)], [Lst.c(9)], out=Lst.ap(9), in_=svt.ap(8))
        snd = S.add("gpsimd", "dma_start", dict(out=send_t.ap().rearrange("(a p) e -> p a e", p=128), in_=Lst.v),
                    reads=[Lst.c(i) for i in range(10)], writes=[xs_cell], dma=True)
        S.add("gpsimd", "collective_compute", dict(kind="AllGather", op=ALU.bypass, replica_groups=[[0, 1, 2, 3], [4, 5, 6, 7]],
                                                    ins=[send_t.ap().opt()], outs=[recv_t.ap().opt()]),
              reads=[xs_cell], writes=[xr_cell], own_sem=True)
        recv4 = recv_t.ap().rearrange("(r a p) e -> a p r e", r=4, a=10)
        P2 = Arena(S, arena_t, ARENA_BYTES)
        P2.top = common_top
        Sin32 = P2.alloc("Sin32", 8, 128, F32)
        p2_top = P2.top
        rcv = P2.alloc("rcv", 2, 512, F32)
        acc = P2.alloc("acc", 2, 128, F32)
        cf = coef.ap(0)
        for piece in list(range(8)) + [8, 9]:
            ri = piece % 2
            LD(rcv.ap(ri).rearrange("p (r e) -> p r e", r=4), recv4[piece], [rcv.c(ri)], reads=[xr_cell])
            rv3 = rcv.ap(ri).rearrange("p (r e) -> p r e", r=4)
            for r in range(4):
                csc = cf[:, r * 8 + piece:r * 8 + piece + 1] if piece < 8 else cf[:, 32 + r:33 + r]
                if r == 0:
                    V("vector", "tensor_scalar", [rcv.c(ri), coef.c(0)], [acc.c(ri)], out=acc.ap(ri), in0=rv3[:, 0, :], scalar1=csc, scalar2=None, op0=ALU.mult)
                else:
                    V("vector", "scalar_tensor_tensor", [rcv.c(ri), coef.c(0), acc.c(ri)], [acc.c(ri)], out=acc.ap(ri), in0=rv3[:, r, :], scalar=csc,
                      in1=acc.ap(ri), op0=ALU.mult, op1=ALU.add)
            if piece < 8:
                V("vector", "tensor_copy", [acc.c(ri)], [Sin32.c(piece)], out=Sin32.ap(piece), in_=acc.ap(ri))
            elif piece == 8:
                V("vector", "tensor_copy", [acc.c(ri)], [skA.c(0)], out=skA.ap(0)[0:64, :], in_=acc.ap(ri)[0:64, :])
                V("vector", "tensor_copy", [acc.c(ri)], [skB.c(0)], out=skB.ap(0)[64:128, :], in_=acc.ap(ri)[64:128, :])
            else:
                V("vector", "tensor_copy", [acc.c(ri)], [svt.c(0)], out=svt.ap(0), in_=acc.ap(ri))

        W2 = Arena(S, arena_t, ARENA_BYTES)
        W2.top = p2_top
        NB = 3
        sqT = W2.alloc("sqT", 4, TH, BF16)
        Ss = W2.alloc("Ss", NB, 512, F32)
        Pb = W2.alloc("Pb", NB, 512, BF16)
        Pn = W2.alloc("Pn", NB, 512, BF16)
        PT = W2.alloc("PT", NB, 512, BF16)
        st = W2.alloc("st", NB, 16, F32)
        pp = Pool(range(6))
        hp = Pool([6, 7])
        bp = Pool(range(NB))
        wq_state = {}
        po_bank = {}

        def prep(c):
            def g():
                if c % 2 == 0:
                    wq_state["w"] = wblock(9 + c // 2)
                wsq, wsqc = wq_state["w"]
                for half in range(2):
                    pb = pp.get()
                    while pb is None:
                        yield
                        pb = pp.get()
                    for kc in range(KC):
                        MM(psum[pb][:, :], wsq[:, kc, (c % 2) * 128:(c % 2 + 1) * 128], nT.ap(H(kc, half)), kc == 0, kc == KC - 1,
                           [wsqc, nT.c(H(kc, half))], [pcell[pb]])
                    yield
                    rope(pb, ropeS, half, 64, sqT.ap((c % 2) * 2 + half), sqT.c((c % 2) * 2 + half), t1, t2)
                    pp.put(pb)
                    yield
            return g

        def block(c, half, nb):
            def g():
                n = half * 4 + nb
                sq_ = sqT.ap((c % 2) * 2 + half)
                sqc = sqT.c((c % 2) * 2 + half)
                it = bp.get()
                while it is None:
                    yield
                    it = bp.get()
                if (c, half) not in po_bank:
                    po = hp.get()
                    while po is None:
                        yield
                        po = hp.get()
                    po_bank[(c, half)] = po
                po = po_bank[(c, half)]
                ps_ = pp.get()
                while ps_ is None:
                    yield
                    ps_ = pp.get()
                ps3 = psum[ps_][:, :].rearrange("p (i k) -> p i k", i=2)
                for i, skx in enumerate((skA, skB)):
                    MM(ps3[:, i, :], sq_[:, nb * 128:(nb + 1) * 128], skx.flat[:, n * 128:n * 128 + 256], True, True,
                       [sqc, skx.c(n), skx.c(n + 1)], [pcell[ps_]])
                yield
                msk = (mask0.ap(0) if n == 0 else maskg)
                ss3 = Ss.ap(it).rearrange("p (i k) -> p i k", i=2)
                sv_ = st.ap(it)
                V("vector", "scalar_tensor_tensor", [pcell[ps_], mask0.c(0), mixc.c(0)], [Ss.c(it)], out=ss3, in0=ps3, scalar=0.125,
                  in1=msk.unsqueeze(1).to_broadcast([128, 2, 256]), op0=ALU.mult, op1=ALU.add)
                pp.put(ps_)
                V("vector", "tensor_reduce", [Ss.c(it)], [st.c(it)], out=sv_[:, 0:2], in_=ss3, axis=AX.X, op=ALU.max)
                V("vector", "tensor_tensor", [st.c(it), mixc.c(0)], [st.c(it)], out=sv_[:, 2:4], in0=sv_[:, 0:2], in1=sinks[:, 2 * c:2 * c + 2], op=ALU.max)
                V("vector", "tensor_scalar", [st.c(it)], [st.c(it)], out=sv_[:, 4:6], in0=sv_[:, 2:4], scalar1=-1.0, scalar2=None, op0=ALU.mult)
                V("vector", "tensor_tensor", [st.c(it), mixc.c(0)], [st.c(it)], out=sv_[:, 8:10], in0=sinks[:, 2 * c:2 * c + 2], in1=sv_[:, 2:4], op=ALU.subtract)
                yield
                pb3 = Pb.ap(it).rearrange("p (i k) -> p i k", i=2)
                for i in range(2):
                    V("scalar", "activation", [Ss.c(it), st.c(it)], [Pb.c(it), st.c(it)], out=pb3[:, i, :], in_=ss3[:, i, :], func=AF.Exp,
                      bias=sv_[:, 4 + i:5 + i], scale=1.0, accum_out=sv_[:, 6 + i:7 + i])
                V("scalar", "activation", [st.c(it)], [st.c(it)], out=sv_[:, 8:10], in_=sv_[:, 8:10], func=AF.Exp)
                yield
                V("vector", "tensor_tensor", [st.c(it)], [st.c(it)], out=sv_[:, 10:12], in0=sv_[:, 6:8], in1=sv_[:, 8:10], op=ALU.add)
                V("vector", "reciprocal", [st.c(it)], [st.c(it)], out=sv_[:, 12:14], in_=sv_[:, 10:12])
                pn3 = Pn.ap(it).rearrange("p (i k) -> p i k", i=2)
                for i in range(2):
                    V("vector", "tensor_scalar", [Pb.c(it), st.c(it)], [Pn.c(it)], out=pn3[:, i, :], in0=pb3[:, i, :], scalar1=sv_[:, 12 + i:13 + i],
                      scalar2=None, op0=ALU.mult)
                yield
                pt_ = pp.get()
                while pt_ is None:
                    yield
                    pt_ = pp.get()
                ptv = psum[pt_][:, :].bitcast(BF16)[:, 0:512]
                for i in range(2):
                    for kt in range(2):
                        S.add("tensor", "transpose", dict(out=ptv[:, (i * 2 + kt) * 128:(i * 2 + kt + 1) * 128],
                                                           in_=pn3[:, i, kt * 128:(kt + 1) * 128], identity=identb.ap(0)),
                              reads=[Pn.c(it), identb.c(0)], writes=[pcell[pt_]])
                yield
                V("scalar", "activation", [pcell[pt_]], [PT.c(it)], out=PT.ap(it), in_=ptv, func=AF.Copy)
                pp.put(pt_)
                yield
                for i in range(2):
                    for kt in range(2):
                        MM(psum[po][64 * i:64 * i + 64, nb * 128:(nb + 1) * 128], svt.ap(n + kt)[:, 64 * i:64 * i + 64],
                           PT.ap(it)[:, (i * 2 + kt) * 128:(i * 2 + kt + 1) * 128], kt == 0, kt == 1,
                           [svt.c(n + kt), PT.c(it)], [pcell[po]])
                bp.put(it)
            return g

        def finish(c, half):
            def g():
                po = po_bank[(c, half)]
                V("scalar", "activation", [pcell[po]], [aT.c((8 + c) * 2 + half)], out=aT.ap((8 + c) * 2 + half), in_=psum[po][:, :], func=AF.Copy)
                hp.put(po)
                yield
            return g

        tasks = []
        prep_id, fin_ids = {}, {}
        for c in range(8):
            deps = [prep_id[c - 1]] if c >= 1 else []
            if c >= 2:
                deps += fin_ids[c - 2]
            prep_id[c] = len(tasks)
            tasks.append((prep(c), deps))
            fin_ids[c] = []
            for half in range(2):
                bl = []
                for nb in range(4):
                    bl.append(len(tasks))
                    tasks.append((block(c, half, nb), [prep_id[c]]))
                fin_ids[c].append(len(tasks))
                tasks.append((finish(c, half), bl))
        run_tasks(tasks, 4)

        W3 = Arena(S, arena_t, ARENA_BYTES)
        W3.top = p2_top
        Ra = Arena(S, arena_t, ARENA_BYTES)
        Ra.top = ropeS.off
        Rb = Arena(S, arena_t, ARENA_BYTES)
        Rb.top = skA.off
        hb = []
        for k_ in range(2):
            A1 = W3 if k_ == 0 else Ra
            A2 = W3 if k_ == 0 else Rb
            d_ = dict(qT=A1.alloc(f"qT{k_}", 2, TH, BF16), qxi=A1.alloc(f"qxi{k_}", 2, TH, BF16), q32=A1.alloc(f"q32{k_}", 1, TH, F32),
                      rstd=A1.alloc(f"rstdg{k_}", 1, TH, F32), sgt=A2.alloc(f"sgt{k_}", 2, TH, F32), sTm=A2.alloc(f"sTm{k_}", 2, TH, BF16),
                      SinS=W3.alloc(f"SinS{k_}", 8, 128, BF16))
            hb.append(d_)
        assert Ra.top <= ropeS.off + ropeS.nbytes and Rb.top <= svt.off + svt.nbytes
        r2 = t1
        mean = t2
        pp = Pool(range(6))
        hbp = Pool(range(2))

        def getbank():
            b_ = pp.get()
            while b_ is None:
                yield None
                b_ = pp.get()
            yield b_

        def head(h):
            def g():
                k_ = hbp.get()
                while k_ is None:
                    yield
                    k_ = hbp.get()
                B_ = hb[k_]
                qT, qxi, q32, rstd, sgt, sTm, SinS = B_["qT"], B_["qxi"], B_["q32"], B_["rstd"], B_["sgt"], B_["sTm"], B_["SinS"]
                r32 = q32
                wq_, wqc = wblock(13 + h)
                for n in range(8):
                    V("scalar", "mul", [Sin32.c(h)], [SinS.c(n)], out=SinS.ap(n), in_=Sin32.ap(h), mul=float(GAMMA[h] ** (128 * n)))
                for half in range(2):
                    pb = None
                    while pb is None:
                        pb = pp.get()
                        if pb is None:
                            yield
                    for kc in range(KC):
                        MM(psum[pb][:, :], wq_[:, kc, 0:128], nT.ap(H(kc, half)), kc == 0, kc == KC - 1, [wqc, nT.c(H(kc, half))], [pcell[pb]])
                    yield
                    rope(pb, ropeR, half, 128, q32.ap(0), q32.c(0), t1, t2)
                    pp.put(pb)
                    V("scalar", "activation", [q32.c(0)], [qT.c(half)], out=qT.ap(half), in_=q32.ap(0), func=AF.Copy)
                    V("vector", "tensor_tensor", [q32.c(0), mixc.c(0)], [qxi.c(half)], out=qxi.ap(half).rearrange("p (n c) -> p n c", n=4),
                      in0=q32.ap(0).rearrange("p (n c) -> p n c", n=4), in1=xi[:, h, :].unsqueeze(1).to_broadcast([128, 4, 128]), op=ALU.mult)
                    yield
                    pg = None
                    while pg is None:
                        pg = pp.get()
                        if pg is None:
                            yield
                    for kc in range(KC):
                        MM(psum[pg][:, :], wq_[:, kc, 128:256], nT.ap(H(kc, half)), kc == 0, kc == KC - 1, [wqc, nT.c(H(kc, half))], [pcell[pg]])
                    yield
                    V("scalar", "activation", [pcell[pg]], [sgt.c(half)], out=sgt.ap(half), in_=psum[pg][:, :], func=AF.Silu)
                    pp.put(pg)
                    yield
                for half in range(2):
                    pss = None
                    while pss is None:
                        pss = pp.get()
                        if pss is None:
                            yield
                    for nb in range(4):
                        MM(psum[pss][:, nb * 128:(nb + 1) * 128], kT.ap(h * 2 + half)[:, nb * 128:(nb + 1) * 128], qT.ap(half)[:, nb * 128:(nb + 1) * 128],
                           True, True, [kT.c(h * 2 + half), qT.c(half)], [pcell[pss]])
                    yield
                    V("vector", "tensor_tensor", [pcell[pss], mixc.c(0)], [sTm.c(half)], out=sTm.ap(half).rearrange("p (n c) -> p n c", n=4),
                      in0=psum[pss][:, :].rearrange("p (n c) -> p n c", n=4), in1=dmat[:, h, :].unsqueeze(1).to_broadcast([128, 4, 128]), op=ALU.mult)
                    pp.put(pss)
                    yield
                    pr = None
                    while pr is None:
                        pr = pp.get()
                        if pr is None:
                            yield
                    for nb in range(4):
                        n = half * 4 + nb
                        oap = psum[pr][:, nb * 128:(nb + 1) * 128]
                        qx = qxi.ap(half)[:, nb * 128:(nb + 1) * 128]
                        MM(oap, vtok.ap(n * 8 + h), sTm.ap(half)[:, nb * 128:(nb + 1) * 128], True, False,
                           [vtok.c(n * 8 + h), sTm.c(half)], [pcell[pr]])
                        if n > 0:
                            MM(oap, Sloc.ap(h * 8 + n), qx, False, False, [Sloc.c(h * 8 + n), qxi.c(half)], [pcell[pr]])
                        MM(oap, SinS.ap(n), qx, False, True, [SinS.c(n), qxi.c(half)], [pcell[pr]])
                    yield
                    p1 = None
                    while p1 is None:
                        p1 = pp.get()
                        if p1 is None:
                            yield
                    p2 = None
                    while p2 is None:
                        p2 = pp.get()
                        if p2 is None:
                            yield
                    V("scalar", "activation", [pcell[pr]], [r32.c(0)], out=r32.ap(0), in_=psum[pr][:, :], func=AF.Copy)
                    V("scalar", "activation", [pcell[pr]], [r2.c(0)], out=r2.ap(0), in_=psum[pr][:, :], func=AF.Square)
                    pp.put(pr)
                    MM(psum[p1][:, :], ones_f.ap(0), r32.ap(0), True, True, [ones_f.c(0), r32.c(0)], [pcell[p1]])
                    MM(psum[p2][:, :], ones_f.ap(0), r2.ap(0), True, True, [ones_f.c(0), r2.c(0)], [pcell[p2]])
                    V("vector", "tensor_scalar", [pcell[p1]], [mean.c(0)], out=mean.ap(0), in0=psum[p1][:, :], scalar1=1.0 / 128, scalar2=None, op0=ALU.mult)
                    V("vector", "tensor_tensor", [mean.c(0)], [rstd.c(0)], out=rstd.ap(0), in0=mean.ap(0), in1=mean.ap(0), op=ALU.mult)
                    V("vector", "scalar_tensor_tensor", [pcell[p2], rstd.c(0)], [rstd.c(0)], out=rstd.ap(0), in0=psum[p2][:, :], scalar=1.0 / 128,
                      in1=rstd.ap(0), op0=ALU.mult, op1=ALU.subtract)
                    pp.put(p1)
                    pp.put(p2)
                    V("scalar", "activation", [rstd.c(0), epsc.c(0)], [rstd.c(0)], out=rstd.ap(0), in_=rstd.ap(0), func=AF.Sqrt, bias=epsc.ap(0)[:, 0:1], scale=1.0)
                    V("vector", "reciprocal", [rstd.c(0)], [rstd.c(0)], out=rstd.ap(0), in_=rstd.ap(0))
                    V("vector", "tensor_tensor", [r32.c(0), mean.c(0)], [r32.c(0)], out=r32.ap(0), in0=r32.ap(0), in1=mean.ap(0), op=ALU.subtract)
                    V("vector", "tensor_tensor", [r32.c(0), rstd.c(0)], [r32.c(0)], out=r32.ap(0), in0=r32.ap(0), in1=rstd.ap(0), op=ALU.mult)
                    V("vector", "scalar_tensor_tensor", [r32.c(0), mixc.c(0), sgt.c(half)], [aT.c(h * 2 + half)], out=aT.ap(h * 2 + half), in0=r32.ap(0),
                      scalar=gng[:, h:h + 1], in1=sgt.ap(half), op0=ALU.mult, op1=ALU.mult)
                    yield
                hbp.put(k_)
            return g

        run_tasks([(head(h), []) for h in range(8)], 2)

        if DEBUG_A:
            for kc in range(KC):
                for half in range(2):
                    V("vector", "tensor_copy", [aT.c(H(kc, half))], [hT.c(H(kc, half))], out=hT.ap(H(kc, half)), in_=aT.ap(H(kc, half)))
        else:
            proj_residual(w_out, aT, reload=hspill)

    def xattn():
        rmsnorm(2, phase_base)
        X = Arena(S, arena_t, ARENA_BYTES)
        X.top = phase_base
        qo = X.alloc("qo", 32, TH, BF16)
        kmT = X.alloc("kmT", 16, 256, BF16)
        vm = X.alloc("vm", 2 * 8, 256, BF16)
        mnT = X.alloc("mnT", 16, 256, BF16)
        xtop = X.top
        m32 = X.alloc("m32", 16, 256, F32)
        msq = X.alloc("msq", 2, 256, BF16)
        mrs = X.alloc("mrs", 1, 256, F32)
        if "mixer" not in stages:
            S.add("gpsimd", "dma_start", dict(out=identb.ap(0), in_=ident_d2), writes=[identb.c(0)], dma=True)
        for kc in range(KC):
            LD(m32.ap(kc), memT[kc * 128:(kc + 1) * 128, :], [m32.c(kc)])
        pb = next_ps()
        for kc in range(KC):
            s_ = kc % 2
            V("vector", "tensor_tensor", [m32.c(kc)], [msq.c(s_)], out=msq.ap(s_), in0=m32.ap(kc), in1=m32.ap(kc), op=ALU.mult)
            MM(psum[pb][:, 0:256], ones_b.ap(0), msq.ap(s_), kc == 0, kc == KC - 1, [ones_b.c(0), msq.c(s_)], [pcell[pb]])
        V("vector", "tensor_scalar", [pcell[pb]], [mrs.c(0)], out=mrs.ap(0), in0=psum[pb][:, 0:256], scalar1=1.0 / D, scalar2=EPS, op0=ALU.mult, op1=ALU.add)
        V("scalar", "activation", [mrs.c(0)], [mrs.c(0)], out=mrs.ap(0), in_=mrs.ap(0), func=AF.Sqrt)
        V("vector", "reciprocal", [mrs.c(0)], [mrs.c(0)], out=mrs.ap(0), in_=mrs.ap(0))
        for kc in range(KC):
            V("vector", "scalar_tensor_tensor", [m32.c(kc), gn.c(0), mrs.c(0)], [mnT.c(kc)], out=mnT.ap(kc), in0=m32.ap(kc),
              scalar=gn.ap(0)[:, 3 * KC + kc:3 * KC + kc + 1], in1=mrs.ap(0), op0=ALU.mult, op1=ALU.mult)
        for blk in range(8):
            wt, ci = next_wcell()
            wv = wt.v[:, ci, :].rearrange("p (k c) -> p k c", k=KC)
            wdma(wv, wkv_d[:, blk * 256:(blk + 1) * 256], KC, [wt.c(ci)])
            for j in range(2):
                pb = next_ps()
                for kc in range(KC):
                    MM(psum[pb][:, 0:256], wv[:, kc, j * 128:(j + 1) * 128], mnT.ap(kc), kc == 0, kc == KC - 1, [wt.c(ci), mnT.c(kc)], [pcell[pb]])
                V("scalar", "activation", [pcell[pb]], [kmT.c(blk * 2 + j)], out=kmT.ap(blk * 2 + j), in_=psum[pb][:, 0:256], func=AF.Copy)
        for blk in range(8):
            wt, ci = next_wcell()
            wv = wt.v[:, ci, :].rearrange("p (k c) -> p k c", k=KC)
            wdma(wv, wkv_d[:, D + blk * 256:D + (blk + 1) * 256], KC, [wt.c(ci)])
            for mt in range(2):
                pb = next_ps()
                for kc in range(KC):
                    MM(psum[pb][:, 0:256], mnT.ap(kc)[:, mt * 128:(mt + 1) * 128], wv[:, kc, :], kc == 0, kc == KC - 1, [wt.c(ci), mnT.c(kc)], [pcell[pb]])
                V("scalar", "activation", [pcell[pb]], [vm.c(mt * 8 + blk)], out=vm.ap(mt * 8 + blk), in_=psum[pb][:, 0:256], func=AF.Copy)
        for blk in range(8):
            wt, ci = next_wcell()
            wv = wt.v[:, ci, :].rearrange("p (k c) -> p k c", k=KC)
            wdma(wv, wq_d[:, blk * 256:(blk + 1) * 256], KC, [wt.c(ci)])
            for j in range(2):
                for half in range(2):
                    pb = next_ps()
                    for kc in range(KC):
                        MM(psum[pb][:, :], wv[:, kc, j * 128:(j + 1) * 128], nT.ap(H(kc, half)), kc == 0, kc == KC - 1, [wt.c(ci), nT.c(H(kc, half))], [pcell[pb]])
                    qc = H(blk * 2 + j, half)
                    V("scalar", "activation", [pcell[pb]], [qo.c(qc)], out=qo.ap(qc), in_=psum[pb][:, :], func=AF.Copy)
        X2 = Arena(S, arena_t, ARENA_BYTES)
        X2.top = xtop
        NX = 3
        Px = X2.alloc("Px", NX, 256, BF16)
        Pnx = X2.alloc("Pnx", NX, 256, BF16)
        stx = X2.alloc("stx", NX, 8, F32)
        PTx = X2.alloc("PTx", 2, 1024, BF16)
        SC = float(512 ** -0.5)
        pp = Pool(range(6))
        xbp = Pool(range(NX))
        ptp = Pool(range(2))
        pt_slot = {}

        def xtile(hd, half, tq):
            def g():
                it = xbp.get()
                while it is None:
                    yield
                    it = xbp.get()
                if (hd, half) not in pt_slot:
                    sl = ptp.get()
                    while sl is None:
                        yield
                        sl = ptp.get()
                    pt_slot[(hd, half)] = sl
                sl = pt_slot[(hd, half)]
                ptx3 = PTx.ap(sl).rearrange("p (m q) -> p m q", m=2)
                ps_ = pp.get()
                while ps_ is None:
                    yield
                    ps_ = pp.get()
                for cc in range(4):
                    MM(psum[ps_][:, 0:256], qo.ap(H(hd * 4 + cc, half))[:, tq * 128:(tq + 1) * 128], kmT.ap(hd * 4 + cc), cc == 0, cc == 3,
                       [qo.c(H(hd * 4 + cc, half)), kmT.c(hd * 4 + cc)], [pcell[ps_]])
                yield
                sv_ = stx.ap(it)
                V("vector", "tensor_reduce", [pcell[ps_]], [stx.c(it)], out=sv_[:, 0:1], in_=psum[ps_][:, 0:256], axis=AX.X, op=ALU.max)
                V("vector", "tensor_scalar", [stx.c(it)], [stx.c(it)], out=sv_[:, 1:2], in0=sv_[:, 0:1], scalar1=-SC, scalar2=None, op0=ALU.mult)
                yield
                V("scalar", "activation", [pcell[ps_], stx.c(it)], [Px.c(it), stx.c(it)], out=Px.ap(it), in_=psum[ps_][:, 0:256], func=AF.Exp,
                  bias=sv_[:, 1:2], scale=SC, accum_out=sv_[:, 2:3])
                pp.put(ps_)
                yield
                V("vector", "reciprocal", [stx.c(it)], [stx.c(it)], out=sv_[:, 3:4], in_=sv_[:, 2:3])
                V("vector", "tensor_scalar", [Px.c(it), stx.c(it)], [Pnx.c(it)], out=Pnx.ap(it), in0=Px.ap(it), scalar1=sv_[:, 3:4], scalar2=None, op0=ALU.mult)
                yield
                pt_ = pp.get()
                while pt_ is None:
                    yield
                    pt_ = pp.get()
                ptv = psum[pt_][:, :].bitcast(BF16)[:, 0:256]
                for mt in range(2):
                    S.add("tensor", "transpose", dict(out=ptv[:, mt * 128:(mt + 1) * 128], in_=Pnx.ap(it)[:, mt * 128:(mt + 1) * 128], identity=identb.ap(0)),
                          reads=[Pnx.c(it), identb.c(0)], writes=[pcell[pt_]])
                yield
                V("scalar", "activation", [pcell[pt_]], [PTx.c(sl)], out=ptx3[:, :, tq * 128:(tq + 1) * 128],
                  in_=ptv.rearrange("p (m q) -> p m q", m=2), func=AF.Copy)
                pp.put(pt_)
                xbp.put(it)
            return g

        def xfin(hd, half):
            def g():
                sl = pt_slot[(hd, half)]
                ptx3 = PTx.ap(sl).rearrange("p (m q) -> p m q", m=2)
                for cc in range(4):
                    dchunk = hd * 4 + cc
                    po = pp.get()
                    while po is None:
                        yield
                        po = pp.get()
                    for mt in range(2):
                        MM(psum[po][:, :], vm.ap(mt * 8 + dchunk // 2)[:, (dchunk % 2) * 128:(dchunk % 2 + 1) * 128], ptx3[:, mt, :], mt == 0, mt == 1,
                           [vm.c(mt * 8 + dchunk // 2), PTx.c(sl)], [pcell[po]])
                    yield
                    V("scalar", "activation", [pcell[po]], [qo.c(H(dchunk, half))], out=qo.ap(H(dchunk, half)), in_=psum[po][:, :], func=AF.Copy)
                    pp.put(po)
                ptp.put(sl)
            return g

        xt = []
        for hd in range(4):
            for half in range(2):
                ids = []
                for tq in range(4):
                    ids.append(len(xt))
                    xt.append((xtile(hd, half, tq), []))
                xt.append((xfin(hd, half), ids))
        run_tasks(xt, 4)
        proj_residual(wo_d, qo)

    hsp_cell = [S.cell(f"hsp{i}") for i in range(32)]
    xs_cell = S.cell("xsend")
    xr_cell = S.cell("xrecv")
    if "ffn1" in stages:
        rmsnorm(0, phase_base)
        ffn(wg1, wu1, wd1, phase_base)
    if "mixer" in stages:
        mixer()
    if "xattn" in stages:
        xattn()
    if "ffn2" in stages:
        rmsnorm(4, phase_base)
        ffn(wg2, wu2, wd2, phase_base)
    if "final" in stages:
        final = rmsnorm(5, phase_base, to_out=True)
    else:
        final = []
        for kc in range(KC):
            for half in range(2):
                final.append(LD(outT[kc * 128:(kc + 1) * 128, half * TH:(half + 1) * TH], hT.ap(H(kc, half)), [], reads=[hT.c(H(kc, half))]))
    S.add("sync", "nop", dict(), after=final)
    S.emit(nc, es)
    es.close()
    return nc


_CACHE = {}


def _gain_cols(v):
    return np.ascontiguousarray(np.asarray(v, np.float32).reshape(KC, 128).T)


def _rope_table(d, pos):
    f32 = np.float32
    inv = (f32(10000.0) ** (-np.arange(0, d, 2, dtype=f32) / f32(d))).astype(f32)
    ang = (pos.astype(f32)[None, :] * inv[:, None]).astype(f32)
    cos = np.cos(ang.astype(np.float64)).astype(f32)
    sin = np.sin(ang.astype(np.float64)).astype(f32)
    p = np.arange(128)
    pp = p % d
    fi = pp % (d // 2)
    sign = np.where(pp < d // 2, -1.0, 1.0).astype(f32)
    return np.ascontiguousarray(np.concatenate([cos[fi], sin[fi] * sign[:, None]], axis=1), f32)


def _mix_consts(ret_gn_gain, swa_sinks):
    g = np.array(GAMMA, np.float64)
    j = np.arange(128)[:, None]
    c = np.arange(128)[None, :]
    m = np.zeros((128, MIXC_W), np.float32)
    sc = 128.0 ** -0.5
    for h in range(8):
        dm = np.where(c >= j, sc * g[h] ** np.maximum(c - j, 0), 0.0)
        m[:, MC_DMAT + h * 128:MC_DMAT + (h + 1) * 128] = dm
        m[:, MC_XI + h * 128:MC_XI + (h + 1) * 128] = (g[h] ** (np.arange(128) + 1))[None, :]
        m[:, MC_ZETA + h] = sc * g[h] ** (127 - np.arange(128))
    m[:, MC_GNG:MC_GNG + 8] = np.asarray(ret_gn_gain, np.float32).reshape(8, 128).T
    sk = np.asarray(swa_sinks, np.float32).reshape(16)
    order = [cc + 8 * i for cc in range(8) for i in range(2)]
    m[:, MC_SINK:MC_SINK + 16] = sk[order][None, :]
    i = np.arange(128)[:, None]
    kk = np.arange(256)[None, :]
    m[:, MC_MASK:MC_MASK + 256] = np.where((kk > i) & (kk <= i + 128), 0.0, NEG)
    return m


def _w_in_perm():
    cols = []
    for h in range(8):
        cols += list(range(1024 + h * 128, 1024 + (h + 1) * 128))
    for h in range(8):
        cols += list(range(2048 + h * 128, 2048 + (h + 1) * 128))
    cols += list(range(5120, 5376))
    for c in range(8):
        cols += list(range(4096 + c * 64, 4096 + (c + 1) * 64)) + list(range(4096 + (c + 8) * 64, 4096 + (c + 9) * 64))
    for h in range(8):
        cols += list(range(h * 128, (h + 1) * 128)) + list(range(3072 + h * 128, 3072 + (h + 1) * 128))
    return np.array(cols)


def _w_out_perm():
    rows = list(range(1024))
    for c in range(8):
        rows += list(range(1024 + c * 64, 1024 + (c + 1) * 64)) + list(range(1024 + (c + 8) * 64, 1024 + (c + 9) * 64))
    return np.array(rows)


def kernel(x, mem, ffn1_norm, ffn1_w_gate, ffn1_w_up, ffn1_w_down, mix_norm, w_in, ret_gn_gain,
           swa_sinks, w_out, xa_norm, mem_norm, xa_wq, xa_wkv, xa_wo, ffn2_norm, ffn2_w_gate,
           ffn2_w_up, ffn2_w_down, final_norm):
    st = tuple(STAGES)
    x = np.asarray(x, np.float32)
    mem = np.asarray(mem, np.float32)
    key = ("nc", st)
    if key not in _CACHE:
        _CACHE[key] = build_program(st)
    nc = _CACHE[key]
    f = lambda a: np.ascontiguousarray(np.asarray(a, np.float32)[0])
    gains = np.concatenate([_gain_cols(np.asarray(g).reshape(-1)) for g in
                            (ffn1_norm, mix_norm, xa_norm, mem_norm, ffn2_norm, final_norm)], axis=1)
    shared = {"gains": np.ascontiguousarray(gains, np.float32)}
    if "ffn1" in st:
        shared.update(ffn1_w_gate=f(ffn1_w_gate), ffn1_w_up=f(ffn1_w_up), ffn1_w_down=f(ffn1_w_down))
    if "ffn2" in st:
        shared.update(ffn2_w_gate=f(ffn2_w_gate), ffn2_w_up=f(ffn2_w_up), ffn2_w_down=f(ffn2_w_down))
    if "mixer" in st:
        shared["w_in_p"] = np.ascontiguousarray(f(w_in)[:, _w_in_perm()])
        shared["w_out_p"] = np.ascontiguousarray(f(w_out)[_w_out_perm(), :])
        shared["mixc"] = _mix_consts(np.asarray(ret_gn_gain).reshape(-1), np.asarray(swa_sinks).reshape(-1))
    if "mixer" in st or "xattn" in st:
        shared["ident"] = np.eye(128, dtype=np.float32)
    if "xattn" in st:
        shared.update(xa_wq=f(xa_wq), xa_wkv=f(xa_wkv), xa_wo=f(xa_wo))
    in_maps = []
    g64 = np.array(GAMMA, np.float64)
    for c in range(NCORES):
        b, q = c // 4, c % 4
        m = dict(shared)
        m["xT"] = np.ascontiguousarray(x[b, q * T:(q + 1) * T, :].T)
        if "mixer" in st:
            pos = np.arange(q * T, (q + 1) * T)
            m["ropeR"] = _rope_table(128, pos)
            m["ropeS"] = _rope_table(64, pos)
            cf = np.zeros((128, 40), np.float32)
            for r in range(4):
                if r < q:
                    cf[:, r * 8:(r + 1) * 8] = (g64 ** (1024 * (q - 1 - r)))[None, :]
                if r == q - 1:
                    cf[:, 32 + r] = 1.0
            m["coef"] = cf
            mk = shared["mixc"][:, MC_MASK:MC_MASK + 256].copy()
            if q == 0:
                mk[:, 0:128] = NEG
            m["mask0"] = np.ascontiguousarray(mk)
        if "xattn" in st:
            m["memT"] = np.ascontiguousarray(mem[b].T)
        in_maps.append(m)
    res = run_bass_kernel_spmd(nc, in_maps, core_ids=list(range(NCORES)))
    out = np.empty((2, 4096, D), np.float32)
    for c in range(NCORES):
        b, q = c // 4, c % 4
        out[b, q * T:(q + 1) * T, :] = res.results[c]["outT"].T
    return out
```

```python
import numpy as np
from contextlib import ExitStack
import concourse.bass as bass
import concourse.mybir as mybir
from concourse.bass_utils import run_bass_kernel_spmd

F32 = mybir.dt.float32
BF16 = mybir.dt.bfloat16
ALU = mybir.AluOpType
AF = mybir.ActivationFunctionType
AX = mybir.AxisListType

D = 2048
KC = 16
T = 1024
TH = 512
DFF = 5632
FC = 44
EPS = 1e-6
NCORES = 8

STAGES = ("ffn1", "mixer", "xattn", "ffn2", "final")
GAMMA = [1.0 - 2.0 ** (-5 - h) for h in range(8)]
MC_DMAT, MC_XI, MC_ZETA, MC_GNG, MC_SINK, MC_MASK = 0, 1024, 2048, 2056, 2064, 2080
MIXC_W = 2336
NEG = -30000.0
DEBUG_A = False


class Cell:
    __slots__ = ("name", "space", "off", "size", "last_w", "readers", "ov")

    def __init__(self, name, space, off=0, size=0):
        self.name, self.space, self.off, self.size = name, space, off, size
        self.last_w = []
        self.readers = {}
        self.ov = []


class Op:
    __slots__ = ("eng", "fn", "deps", "pos", "signal", "tick", "is_dma", "sem", "val", "waits", "gidx", "group", "own")

    def __init__(self, eng, fn, is_dma):
        self.eng, self.fn, self.is_dma = eng, fn, is_dma
        self.deps = []
        self.signal = False
        self.tick = None
        self.sem = None
        self.val = None
        self.waits = []


class Sched:
    ENGS = ["sync", "gpsimd", "scalar", "vector", "tensor"]
    NDQ = 8

    def __init__(self):
        self.ops = []
        self.sb_cells = []
        self.count = {e: 0 for e in self.ENGS}

    def sb_cell(self, name, off, size):
        c = Cell(name, "sb", off, size)
        for o in self.sb_cells:
            if o.off < off + size and off < o.off + o.size:
                o.ov.append(c)
                c.ov.append(o)
        self.sb_cells.append(c)
        return c

    def cell(self, name):
        return Cell(name, "x")

    def add(self, eng, meth, kw, reads=(), writes=(), dma=False, after=(), group=None, own_sem=False):
        op = Op(eng, (meth, kw), dma)
        op.group = group
        op.own = own_sem
        op.pos = self.count[eng]
        self.count[eng] += 1
        op.gidx = len(self.ops)
        deps = {}

        def dep(o):
            if o is not None and o is not op and not (group is not None and o.group == group):
                deps[id(o)] = o

        for o in after:
            dep(o)
        for c in reads:
            for y in [c] + c.ov:
                for w in y.last_w:
                    dep(w)
        for c in writes:
            for y in [c] + c.ov:
                for w in y.last_w:
                    dep(w)
                for r in y.readers.values():
                    dep(r)
        for c in reads:
            for y in [c] + c.ov:
                key = ("d", op.gidx) if dma else eng
                y.readers[key] = op
        for c in writes:
            for y in [c] + c.ov:
                if group is not None and y.last_w and y.last_w[0].group == group:
                    y.last_w.append(op)
                else:
                    y.last_w = [op]
                y.readers = {}
        best = {}
        for o in deps.values():
            if o.is_dma or o.own:
                best[("d", id(o))] = o
            else:
                if eng == "tensor" and o.eng == "tensor":
                    continue
                k = o.eng
                if k not in best or best[k].pos < o.pos:
                    best[k] = o
        op.deps = list(best.values())
        for o in op.deps:
            o.signal = True
        self.ops.append(op)
        return op

    def emit(self, nc, es):
        sems = {e: es.enter_context(nc.semaphore("c_" + e)) for e in self.ENGS}
        dq = {e: [es.enter_context(nc.semaphore(f"dq_{e}_{i}")) for i in range(self.NDQ)]
              for e in ("sync", "gpsimd", "scalar")}
        streams = {e: [] for e in self.ENGS}
        ticks = {e: 0 for e in self.ENGS}
        ndma = {e: 0 for e in self.ENGS}
        for op in self.ops:
            streams[op.eng].append(op)
            if op.is_dma:
                i = ndma[op.eng]
                ndma[op.eng] += 1
                op.sem = dq[op.eng][i % self.NDQ]
                op.val = 16 * (i // self.NDQ + 1)
                op.signal = True
            elif op.own:
                op.sem = es.enter_context(nc.semaphore(f"own_{op.gidx}"))
                op.val = 1
                op.signal = True
            elif op.signal:
                ticks[op.eng] += 1
                op.sem = sems[op.eng]
                op.val = ticks[op.eng]
        for e in self.ENGS:
            seen = {}
            for op in streams[e]:
                w = []
                if op.is_dma and op.val > 16:
                    w.append((op.sem, op.val - 16))
                for d in op.deps:
                    w.append((d.sem, d.val))
                for (s, v) in w:
                    if seen.get(id(s), 0) >= v:
                        continue
                    seen[id(s)] = v
                    op.waits.append((s, v))
        block = es.enter_context(nc.Block())

        def run(e, stream):
            for op in stream:
                for (s, v) in op.waits:
                    e.wait_ge(s, v)
                ins = getattr(e, op.fn[0])(**op.fn[1])
                if op.signal:
                    ins.then_inc(op.sem, 16 if op.is_dma else 1)

        @block.sync
        def _(e):
            run(e, streams["sync"])

        @block.gpsimd
        def _(e):
            run(e, streams["gpsimd"])

        @block.scalar
        def _(e):
            run(e, streams["scalar"])

        @block.vector
        def _(e):
            run(e, streams["vector"])

        @block.tensor
        def _(e):
            run(e, streams["tensor"])


class SbT:
    def __init__(self, S, arena, name, off, ncell, clen, dt):
        self.esz = 4 if dt == F32 else 2
        self.ncell, self.clen, self.dt = ncell, clen, dt
        self.off = off
        self.nbytes = ncell * clen * self.esz
        assert off % 4 == 0 and self.nbytes % 4 == 0
        w0, w1 = off // 4, (off + self.nbytes) // 4
        v = arena[:, w0:w1]
        if dt != F32:
            v = v.bitcast(dt)
        self.flat = v
        self.v = v.rearrange("p (a b) -> p a b", b=clen)
        self.cells = [S.sb_cell(f"{name}{i}", off + i * clen * self.esz, clen * self.esz) for i in range(ncell)]

    def ap(self, i):
        return self.v[:, i, :]

    def c(self, i):
        return self.cells[i]


class Arena:
    def __init__(self, S, arena, total):
        self.S, self.arena, self.total = S, arena, total
        self.top = 0

    def alloc(self, name, ncell, clen, dt, at=None):
        esz = 4 if dt == F32 else 2
        nb = ncell * clen * esz
        nb4 = (nb + 31) // 32 * 32
        if at is None:
            at = self.top
            self.top += nb4
        assert at + nb <= self.total, (name, at, nb, self.total)
        return SbT(self.S, self.arena, name, at, ncell, clen, dt)


ARENA_BYTES = 207 * 1024


class Pool:
    def __init__(self, items):
        self.free = list(items)

    def get(self):
        return self.free.pop(0) if self.free else None

    def put(self, x):
        self.free.append(x)


def run_tasks(tasks, width):
    done, started, active = set(), set(), []
    while len(done) < len(tasks):
        for i in range(len(tasks)):
            if len(active) >= width:
                break
            if i in started:
                continue
            if all(d in done for d in tasks[i][1]):
                started.add(i)
                active.append((i, tasks[i][0]()))
        assert active, "task deadlock"
        for (i, g) in list(active):
            try:
                next(g)
            except StopIteration:
                active.remove((i, g))
                done.add(i)


def build_program(stages=("ffn1", "mixer", "xattn", "ffn2", "final")):
    stages = set(stages)
    nc = bass.Bass("TRN2", target_bir_lowering=False)
    S = Sched()
    es = ExitStack()

    def din(name, shape, dt=F32):
        return nc.dram_tensor(name, shape, dt, kind="ExternalInput").ap()

    xT = din("xT", [D, T])
    gains = din("gains", [128, 6 * KC])
    if "ffn1" in stages:
        wg1 = din("ffn1_w_gate", [D, DFF]); wu1 = din("ffn1_w_up", [D, DFF]); wd1 = din("ffn1_w_down", [DFF, D])
    if "ffn2" in stages:
        wg2 = din("ffn2_w_gate", [D, DFF]); wu2 = din("ffn2_w_up", [D, DFF]); wd2 = din("ffn2_w_down", [DFF, D])
    if "mixer" in stages:
        w_in = din("w_in_p", [D, 42 * 128])
        w_out = din("w_out_p", [D, D])
        ropeR_d = din("ropeR", [128, 2 * T])
        ropeS_d = din("ropeS", [128, 2 * T])
        mixc_d = din("mixc", [128, MIXC_W])
        coef_d = din("coef", [128, 40])
        mask0_d = din("mask0", [128, 256])
        ident_d = din("ident", [128, 128])
        hspill = nc.dram_tensor("hspill", [D, T], F32).ap()
        send_t = nc.dram_tensor("xsend", [10 * 128, 128], F32)
        recv_t = nc.dram_tensor("xrecv", [4 * 10 * 128, 128], F32)
    if "xattn" in stages:
        memT = din("memT", [D, 256])
        wq_d = din("xa_wq", [D, D]); wkv_d = din("xa_wkv", [D, 2 * D]); wo_d = din("xa_wo", [D, D])
        ident_d2 = ident_d if "mixer" in stages else din("ident", [128, 128])
    outT = nc.dram_tensor("outT", [D, T], F32, kind="ExternalOutput").ap()

    arena_t = es.enter_context(nc.sbuf_tensor("arena", [128, ARENA_BYTES // 4], F32))
    A = Arena(S, arena_t, ARENA_BYTES)
    psum_all = es.enter_context(nc.psum_tensor("psall", [128, 8 * 512], F32))
    psum = [psum_all[:, i * 512:(i + 1) * 512] for i in range(8)]
    pcell = [S.cell(f"ps{i}") for i in range(8)]

    hT = A.alloc("hT", KC * 2, TH, F32)
    nT = A.alloc("nT", KC * 2, TH, BF16)
    wslot = [A.alloc(f"ws{i}", 2, 4096, BF16) for i in range(2)]
    gn = A.alloc("gn", 1, 6 * KC, F32)
    ones_b = A.alloc("ones_b", 1, 128, BF16)
    ones_f = A.alloc("ones_f", 1, 128, F32)
    identb = A.alloc("identb", 1, 128, BF16)
    epsc = A.alloc("epsc", 1, 8, F32)
    phase_base = A.top

    psi = [0]

    def next_ps():
        i = psi[0] % 6
        psi[0] += 1
        return i

    def next_ps_pair():
        if psi[0] % 2:
            psi[0] += 1
        i = psi[0] % 6
        psi[0] += 2
        return i

    hpsi = [0]

    def hold_ps():
        i = 6 + hpsi[0] % 2
        hpsi[0] += 1
        return i

    wsi = [0]

    def next_wcell():
        i = wsi[0] % 4
        wsi[0] += 1
        return wslot[i // 2], i % 2

    def next_ws():
        if wsi[0] % 2:
            wsi[0] += 1
        i = (wsi[0] // 2) % 2
        wsi[0] += 2
        return wslot[i]

    def H(kc, half):
        return kc * 2 + half

    gid = [0]

    def wdma(dst3, src2, nk, cells):
        gid[0] += 1
        for k0 in range(0, nk, 4):
            k1 = min(nk, k0 + 4)
            S.add("gpsimd", "dma_start", dict(out=dst3[:, k0:k1, :],
                                               in_=src2[k0 * 128:k1 * 128, :].rearrange("(k p) c -> p k c", p=128)),
                  writes=cells, dma=True, group=gid[0])

    def V(eng, meth, reads, writes, **kw):
        return S.add(eng, meth, kw, reads=reads, writes=writes)

    def MM(out, lhsT, rhs, start, stop, reads, writes):
        return S.add("tensor", "matmul", dict(out=out, lhsT=lhsT, rhs=rhs, start=start, stop=stop), reads=reads, writes=writes)

    def LD(out, in_, writes, reads=(), eng="sync"):
        return S.add(eng, "dma_start", dict(out=out, in_=in_), reads=reads, writes=writes, dma=True)

    V("vector", "memset", [], [ones_b.c(0)], ap=ones_b.ap(0), constant=1.0)
    V("vector", "memset", [], [ones_f.c(0)], ap=ones_f.ap(0), constant=1.0)
    V("vector", "memset", [], [epsc.c(0)], ap=epsc.ap(0), constant=EPS)
    LD(gn.ap(0), gains, [gn.c(0)])
    for half in range(2):
        for kc in range(KC):
            LD(hT.ap(H(kc, half)), xT[kc * 128:(kc + 1) * 128, half * TH:(half + 1) * TH], [hT.c(H(kc, half))])

    def rmsnorm(gi, tmp_base, to_out=False):
        At = Arena(S, arena_t, ARENA_BYTES)
        At.top = tmp_base
        sq = At.alloc("sq", 4, TH, BF16)
        rstd = At.alloc("rstd", 2, TH, F32)
        ot = At.alloc("ot", 4, TH, F32) if to_out else None
        fin = []
        for half in range(2):
            pb = next_ps()
            for kc in range(KC):
                s = kc % 4
                hc = H(kc, half)
                if kc % 2 == 0:
                    V("scalar", "activation", [hT.c(hc)], [sq.c(s)], out=sq.ap(s), in_=hT.ap(hc), func=AF.Square)
                else:
                    V("vector", "tensor_tensor", [hT.c(hc)], [sq.c(s)], out=sq.ap(s), in0=hT.ap(hc), in1=hT.ap(hc), op=ALU.mult)
                MM(psum[pb][:, :], ones_b.ap(0), sq.ap(s), kc == 0, kc == KC - 1, [sq.c(s), ones_b.c(0)], [pcell[pb]])
            V("vector", "tensor_scalar", [pcell[pb]], [rstd.c(half)], out=rstd.ap(half), in0=psum[pb][:, :],
              scalar1=1.0 / D, scalar2=EPS, op0=ALU.mult, op1=ALU.add)
            V("scalar", "activation", [rstd.c(half)], [rstd.c(half)], out=rstd.ap(half), in_=rstd.ap(half), func=AF.Sqrt)
            V("vector", "reciprocal", [rstd.c(half)], [rstd.c(half)], out=rstd.ap(half), in_=rstd.ap(half))
            for kc in range(KC):
                hc = H(kc, half)
                gsc = gn.ap(0)[:, gi * KC + kc:gi * KC + kc + 1]
                if not to_out:
                    V("vector", "scalar_tensor_tensor", [hT.c(hc), gn.c(0), rstd.c(half)], [nT.c(hc)],
                      out=nT.ap(hc), in0=hT.ap(hc), scalar=gsc, in1=rstd.ap(half), op0=ALU.mult, op1=ALU.mult)
                else:
                    o = kc % 4
                    V("vector", "scalar_tensor_tensor", [hT.c(hc), gn.c(0), rstd.c(half)], [ot.c(o)],
                      out=ot.ap(o), in0=hT.ap(hc), scalar=gsc, in1=rstd.ap(half), op0=ALU.mult, op1=ALU.mult)
                    fin.append(LD(outT[kc * 128:(kc + 1) * 128, half * TH:(half + 1) * TH], ot.ap(o), [], reads=[ot.c(o)]))
        return fin

    def ffn(wg, wu, wd, tmp_base):
        At = Arena(S, arena_t, ARENA_BYTES)
        At.top = tmp_base
        act = At.alloc("act", 22 * 2, TH, BF16)
        sg = At.alloc("sg", 2, TH, F32)
        sgi = 0
        for ffh in range(2):
            for p in range(11):
                f0 = (ffh * 22 + 2 * p) * 128
                wt = next_ws()
                wv = wt.flat.rearrange("p (g k c) -> p g k c", g=2, k=KC)
                wdma(wv[:, 0], wg[:, f0:f0 + 256], KC, [wt.c(0)])
                wdma(wv[:, 1], wu[:, f0:f0 + 256], KC, [wt.c(1)])
                for j in range(2):
                    fl = 2 * p + j
                    for half in range(2):
                        pg, pu = next_ps(), next_ps()
                        for kc in range(KC):
                            MM(psum[pg][:, :], wv[:, 0, kc, j * 128:(j + 1) * 128], nT.ap(H(kc, half)), kc == 0, kc == KC - 1,
                               [wt.c(0), nT.c(H(kc, half))], [pcell[pg]])
                        for kc in range(KC):
                            MM(psum[pu][:, :], wv[:, 1, kc, j * 128:(j + 1) * 128], nT.ap(H(kc, half)), kc == 0, kc == KC - 1,
                               [wt.c(1), nT.c(H(kc, half))], [pcell[pu]])
                        si = sgi % 2
                        sgi += 1
                        V("scalar", "activation", [pcell[pg]], [sg.c(si)], out=sg.ap(si), in_=psum[pg][:, :], func=AF.Silu)
                        V("vector", "tensor_tensor", [pcell[pu], sg.c(si)], [act.c(fl * 2 + half)],
                          out=act.ap(fl * 2 + half), in0=psum[pu][:, :], in1=sg.ap(si), op=ALU.mult)
            for dblk in range(8):
                wt = next_ws()
                wv = wt.flat[:, 0:22 * 256].rearrange("p (k c) -> p k c", k=22)
                r0 = ffh * 22 * 128
                wdma(wv, wd[r0:r0 + 22 * 128, dblk * 256:(dblk + 1) * 256], 22, [wt.c(0), wt.c(1)])
                for j in range(2):
                    dc = dblk * 2 + j
                    for half in range(2):
                        pb = next_ps()
                        for fl in range(22):
                            MM(psum[pb][:, :], wv[:, fl, j * 128:(j + 1) * 128], act.ap(fl * 2 + half), fl == 0, fl == 21,
                               [wt.c(0), wt.c(1), act.c(fl * 2 + half)], [pcell[pb]])
                        V("vector", "scalar_tensor_tensor", [pcell[pb], hT.c(H(dc, half))], [hT.c(H(dc, half))],
                          out=hT.ap(H(dc, half)), in0=psum[pb][:, :], scalar=0.5, in1=hT.ap(H(dc, half)), op0=ALU.mult, op1=ALU.add)

    def proj_residual(w_d, src, reload=None):
        for dblk in range(8):
            wt, ci = next_wcell()
            wv = wt.v[:, ci, :].rearrange("p (k c) -> p k c", k=KC)
            wdma(wv, w_d[:, dblk * 256:(dblk + 1) * 256], KC, [wt.c(ci)])
            for j in range(2):
                dc = dblk * 2 + j
                for half in range(2):
                    pb = next_ps()
                    for ac in range(KC):
                        MM(psum[pb][:, :], wv[:, ac, j * 128:(j + 1) * 128], src.ap(H(ac, half)), ac == 0, ac == KC - 1,
                           [wt.c(ci), src.c(H(ac, half))], [pcell[pb]])
                    hc = H(dc, half)
                    if reload is not None:
                        LD(hT.ap(hc), reload[dc * 128:(dc + 1) * 128, half * TH:(half + 1) * TH], [hT.c(hc)])
                    V("vector", "tensor_tensor", [pcell[pb], hT.c(hc)], [hT.c(hc)],
                      out=hT.ap(hc), in0=psum[pb][:, :], in1=hT.ap(hc), op=ALU.add)

    def rope(pb, tab, half, dh, out_ap, out_cell, t1, t2):
        V("vector", "tensor_tensor", [pcell[pb], tab.c(half)], [t1.c(0)], out=t1.ap(0), in0=psum[pb][:, :], in1=tab.ap(half), op=ALU.mult)
        hd = dh // 2
        for base in range(0, 128, dh):
            for (dst, src) in ((base, base + hd), (base + hd, base)):
                V("vector", "tensor_tensor", [pcell[pb], tab.c(2 + half)], [t2.c(0)],
                  out=t2.ap(0)[dst:dst + hd, :], in0=psum[pb][src:src + hd, :], in1=tab.ap(2 + half)[dst:dst + hd, :], op=ALU.mult)
        if isinstance(out_ap, tuple):
            for (lo, oap) in ((0, out_ap[0]), (64, out_ap[1])):
                V("vector", "tensor_tensor", [t1.c(0), t2.c(0)], out_cell, out=oap[lo:lo + 64, :], in0=t1.ap(0)[lo:lo + 64, :], in1=t2.ap(0)[lo:lo + 64, :], op=ALU.add)
        else:
            V("vector", "tensor_tensor", [t1.c(0), t2.c(0)], out_cell if isinstance(out_cell, list) else [out_cell], out=out_ap, in0=t1.ap(0), in1=t2.ap(0), op=ALU.add)

    def mixer():
        rmsnorm(1, phase_base)
        for kc in range(KC):
            for half in range(2):
                S.add("sync", "dma_start", dict(out=hspill[kc * 128:(kc + 1) * 128, half * TH:(half + 1) * TH], in_=hT.ap(H(kc, half))),
                      reads=[hT.c(H(kc, half))], writes=[hsp_cell[H(kc, half)]], dma=True)
        R = Arena(S, arena_t, ARENA_BYTES)
        R.top = hT.off
        kT = R.alloc("kT", 16, TH, BF16)
        vtok = R.alloc("vtok", 64, 128, BF16)
        Sloc = R.alloc("Sloc", 64, 128, BF16)
        ropeR = R.alloc("ropeR", 4, TH, F32)
        ropeS = R.alloc("ropeS", 4, TH, F32)
        assert R.top <= hT.off + hT.nbytes
        P = Arena(S, arena_t, ARENA_BYTES)
        P.top = phase_base
        aT = P.alloc("aT", 32, TH, BF16)
        mixc = P.alloc("mixc", 1, MIXC_W, F32)
        coef = P.alloc("coef", 1, 40, F32)
        mask0 = P.alloc("mask0", 1, 256, F32)
        skA = P.alloc("skA", 9, 128, BF16)
        skB = P.alloc("skB", 9, 128, BF16)
        svt = P.alloc("svt", 9, 128, BF16)
        t1 = P.alloc("t1", 1, TH, F32)
        t2 = P.alloc("t2", 1, TH, F32)
        common_top = P.top
        mc = mixc.ap(0)
        dmat = mc[:, MC_DMAT:MC_DMAT + 1024].rearrange("p (h c) -> p h c", h=8)
        xi = mc[:, MC_XI:MC_XI + 1024].rearrange("p (h c) -> p h c", h=8)
        zeta = mc[:, MC_ZETA:MC_ZETA + 8]
        gng = mc[:, MC_GNG:MC_GNG + 8]
        sinks = mc[:, MC_SINK:MC_SINK + 16]
        maskg = mc[:, MC_MASK:MC_MASK + 256]
        V("vector", "memset", [], [skA.c(i) for i in range(9)], ap=skA.flat, constant=0.0)
        V("vector", "memset", [], [skB.c(i) for i in range(9)], ap=skB.flat, constant=0.0)
        LD(mixc.ap(0), mixc_d, [mixc.c(0)])
        LD(coef.ap(0), coef_d, [coef.c(0)])
        LD(mask0.ap(0), mask0_d, [mask0.c(0)])
        for i in range(4):
            LD(ropeR.ap(i), ropeR_d[:, i * TH:(i + 1) * TH], [ropeR.c(i)])
            LD(ropeS.ap(i), ropeS_d[:, i * TH:(i + 1) * TH], [ropeS.c(i)])
        S.add("gpsimd", "dma_start", dict(out=identb.ap(0), in_=ident_d), writes=[identb.c(0)], dma=True)

        def wblock(b):
            wt, ci = next_wcell()
            wv = wt.v[:, ci, :].rearrange("p (k c) -> p k c", k=KC)
            wdma(wv, w_in[:, b * 256:(b + 1) * 256], KC, [wt.c(ci)])
            return wv, wt.c(ci)

        def proj_fm(wv, wc, j, half):
            pb = next_ps()
            for kc in range(KC):
                MM(psum[pb][:, :], wv[:, kc, j * 128:(j + 1) * 128], nT.ap(H(kc, half)), kc == 0, kc == KC - 1,
                   [wc, nT.c(H(kc, half))], [pcell[pb]])
            return pb

        P1 = Arena(S, arena_t, ARENA_BYTES)
        P1.top = common_top
        kz = P1.alloc("kz", 2, 1024, BF16)
        Rst = P1.alloc("Rst", 2, 128, F32)
        Lst = P1.alloc("Lst", 10, 128, F32)
        wsk, wskc = wblock(8)
        for half in range(2):
            pb = proj_fm(wsk, wskc, 0, half)
            rope(pb, ropeS, half, 64, (skA.flat[:, 128 + half * TH:128 + (half + 1) * TH], skB.flat[:, 128 + half * TH:128 + (half + 1) * TH]),
                 [skA.c(1 + half * 4 + nn) for nn in range(4)] + [skB.c(1 + half * 4 + nn) for nn in range(4)], t1, t2)
        for n in range(8):
            pb = next_ps()
            half, o = n // 4, (n % 4) * 128
            for kc in range(KC):
                MM(psum[pb][:, 0:128], nT.ap(H(kc, half))[:, o:o + 128], wsk[:, kc, 128:256], kc == 0, kc == KC - 1,
                   [wskc, nT.c(H(kc, half))], [pcell[pb]])
            V("scalar", "activation", [pcell[pb]], [svt.c(1 + n)], out=svt.ap(1 + n), in_=psum[pb][:, 0:128], func=AF.Copy)
        for hp in range(4):
            wk, wkc = wblock(hp)
            wvv, wvc = wblock(4 + hp)
            for j in range(2):
                h = 2 * hp + j
                for half in range(2):
                    pb = proj_fm(wk, wkc, j, half)
                    rope(pb, ropeR, half, 128, kT.ap(h * 2 + half), kT.c(h * 2 + half), t1, t2)
            for n in range(8):
                pb = next_ps()
                half, o = n // 4, (n % 4) * 128
                for kc in range(KC):
                    MM(psum[pb][:, 0:256], nT.ap(H(kc, half))[:, o:o + 128], wvv[:, kc, :], kc == 0, kc == KC - 1,
                       [wvc, nT.c(H(kc, half))], [pcell[pb]])
                for j in range(2):
                    h = 2 * hp + j
                    V("scalar", "activation", [pcell[pb]], [vtok.c(n * 8 + h)], out=vtok.ap(n * 8 + h), in_=psum[pb][:, j * 128:(j + 1) * 128], func=AF.Copy)
            for j in range(2):
                h = 2 * hp + j
                gC = float(GAMMA[h] ** 128)
                pb = next_ps()
                pbv = psum[pb][:, :].bitcast(BF16)
                for n in range(8):
                    half, o = n // 4, (n % 4) * 128
                    S.add("tensor", "transpose", dict(out=pbv[:, n * 128:(n + 1) * 128], in_=kT.ap(h * 2 + half)[:, o:o + 128], identity=identb.ap(0)),
                          reads=[kT.c(h * 2 + half), identb.c(0)], writes=[pcell[pb]])
                kzi = h % 2
                V("vector", "tensor_scalar", [pcell[pb], mixc.c(0)], [kz.c(kzi)], out=kz.ap(kzi), in0=pbv, scalar1=zeta[:, h:h + 1], scalar2=None, op0=ALU.mult)
                pbs = [next_ps(), next_ps()]
                for n in range(8):
                    pb2 = pbs[n // 4]
                    o = (n % 4) * 128
                    MM(psum[pb2][:, o:o + 128], kz.ap(kzi)[:, n * 128:(n + 1) * 128], vtok.ap(n * 8 + h), True, True,
                       [kz.c(kzi), vtok.c(n * 8 + h)], [pcell[pb2]])
                ri = 0
                for n in range(8):
                    pb2 = pbs[n // 4]
                    o = (n % 4) * 128
                    dst_ap, dst_c = (Rst.ap(1 - ri), Rst.c(1 - ri)) if n < 7 else (Lst.ap(h), Lst.c(h))
                    if n == 0:
                        V("vector", "tensor_copy", [pcell[pb2]], [dst_c], out=dst_ap, in_=psum[pb2][:, o:o + 128])
                    else:
                        V("vector", "scalar_tensor_tensor", [pcell[pb2], Rst.c(ri)], [dst_c], out=dst_ap, in0=Rst.ap(ri), scalar=gC,
                          in1=psum[pb2][:, o:o + 128], op0=ALU.mult, op1=ALU.add)
                    ri = 1 - ri
                    if n < 7:
                        V("scalar", "activation", [Rst.c(ri)], [Sloc.c(h * 8 + n + 1)], out=Sloc.ap(h * 8 + n + 1), in_=Rst.ap(ri), func=AF.Copy)
        V("vector", "tensor_copy", [skA.c(8)], [Lst.c(8)], out=Lst.ap(8)[0:64, :], in_=skA.ap(8)[0:64, :])
        V("vector", "tensor_copy", [skB.c(8)], [Lst.c(8)], out=Lst.ap(8)[64:128, :], in_=skB.ap(8)[64:128, :])
        V("vector", "tensor_copy", [svt.c(8)], [Lst.c(9)], out=Lst.ap(9), in_=svt.ap(8))
        snd = S.add("gpsimd", "dma_start", dict(out=send_t.ap().rearrange("(a p) e -> p a e", p=128), in_=Lst.v),
                    reads=[Lst.c(i) for i in range(10)], writes=[xs_cell], dma=True)
        S.add("gpsimd", "collective_compute", dict(kind="AllGather", op=ALU.bypass, replica_groups=[[0, 1, 2, 3], [4, 5, 6, 7]],
                                                    ins=[send_t.ap().opt()], outs=[recv_t.ap().opt()]),
              reads=[xs_cell], writes=[xr_cell], own_sem=True)
        recv4 = recv_t.ap().rearrange("(r a p) e -> a p r e", r=4, a=10)
        P2 = Arena(S, arena_t, ARENA_BYTES)
        P2.top = common_top
        Sin32 = P2.alloc("Sin32", 8, 128, F32)
        p2_top = P2.top
        rcv = P2.alloc("rcv", 2, 512, F32)
        acc = P2.alloc("acc", 2, 128, F32)
        cf = coef.ap(0)
        for piece in list(range(8)) + [8, 9]:
            ri = piece % 2
            LD(rcv.ap(ri).rearrange("p (r e) -> p r e", r=4), recv4[piece], [rcv.c(ri)], reads=[xr_cell])
            rv3 = rcv.ap(ri).rearrange("p (r e) -> p r e", r=4)
            for r in range(4):
                csc = cf[:, r * 8 + piece:r * 8 + piece + 1] if piece < 8 else cf[:, 32 + r:33 + r]
                if r == 0:
                    V("vector", "tensor_scalar", [rcv.c(ri), coef.c(0)], [acc.c(ri)], out=acc.ap(ri), in0=rv3[:, 0, :], scalar1=csc, scalar2=None, op0=ALU.mult)
                else:
                    V("vector", "scalar_tensor_tensor", [rcv.c(ri), coef.c(0), acc.c(ri)], [acc.c(ri)], out=acc.ap(ri), in0=rv3[:, r, :], scalar=csc,
                      in1=acc.ap(ri), op0=ALU.mult, op1=ALU.add)
            if piece < 8:
                V("vector", "tensor_copy", [acc.c(ri)], [Sin32.c(piece)], out=Sin32.ap(piece), in_=acc.ap(ri))
            elif piece == 8:
                V("vector", "tensor_copy", [acc.c(ri)], [skA.c(0)], out=skA.ap(0)[0:64, :], in_=acc.ap(ri)[0:64, :])
                V("vector", "tensor_copy", [acc.c(ri)], [skB.c(0)], out=skB.ap(0)[64:128, :], in_=acc.ap(ri)[64:128, :])
            else:
                V("vector", "tensor_copy", [acc.c(ri)], [svt.c(0)], out=svt.ap(0), in_=acc.ap(ri))

        W2 = Arena(S, arena_t, ARENA_BYTES)
        W2.top = p2_top
        NB = 3
        sqT = W2.alloc("sqT", 4, TH, BF16)
        Ss = W2.alloc("Ss", NB, 512, F32)
        Pb = W2.alloc("Pb", NB, 512, BF16)
        Pn = W2.alloc("Pn", NB, 512, BF16)
        PT = W2.alloc("PT", NB, 512, BF16)
        st = W2.alloc("st", NB, 16, F32)
        pp = Pool(range(6))
        hp = Pool([6, 7])
        bp = Pool(range(NB))
        wq_state = {}
        po_bank = {}

        def prep(c):
            def g():
                if c % 2 == 0:
                    wq_state["w"] = wblock(9 + c // 2)
                wsq, wsqc = wq_state["w"]
                for half in range(2):
                    pb = pp.get()
                    while pb is None:
                        yield
                        pb = pp.get()
                    for kc in range(KC):
                        MM(psum[pb][:, :], wsq[:, kc, (c % 2) * 128:(c % 2 + 1) * 128], nT.ap(H(kc, half)), kc == 0, kc == KC - 1,
                           [wsqc, nT.c(H(kc, half))], [pcell[pb]])
                    yield
                    rope(pb, ropeS, half, 64, sqT.ap((c % 2) * 2 + half), sqT.c((c % 2) * 2 + half), t1, t2)
                    pp.put(pb)
                    yield
            return g

        def block(c, half, nb):
            def g():
                n = half * 4 + nb
                sq_ = sqT.ap((c % 2) * 2 + half)
                sqc = sqT.c((c % 2) * 2 + half)
                it = bp.get()
                while it is None:
                    yield
                    it = bp.get()
                if (c, half) not in po_bank:
                    po = hp.get()
                    while po is None:
                        yield
                        po = hp.get()
                    po_bank[(c, half)] = po
                po = po_bank[(c, half)]
                ps_ = pp.get()
                while ps_ is None:
                    yield
                    ps_ = pp.get()
                ps3 = psum[ps_][:, :].rearrange("p (i k) -> p i k", i=2)
                for i, skx in enumerate((skA, skB)):
                    MM(ps3[:, i, :], sq_[:, nb * 128:(nb + 1) * 128], skx.flat[:, n * 128:n * 128 + 256], True, True,
                       [sqc, skx.c(n), skx.c(n + 1)], [pcell[ps_]])
                yield
                msk = (mask0.ap(0) if n == 0 else maskg)
                ss3 = Ss.ap(it).rearrange("p (i k) -> p i k", i=2)
                sv_ = st.ap(it)
                V("vector", "scalar_tensor_tensor", [pcell[ps_], mask0.c(0), mixc.c(0)], [Ss.c(it)], out=ss3, in0=ps3, scalar=0.125,
                  in1=msk.unsqueeze(1).to_broadcast([128, 2, 256]), op0=ALU.mult, op1=ALU.add)
                pp.put(ps_)
                V("vector", "tensor_reduce", [Ss.c(it)], [st.c(it)], out=sv_[:, 0:2], in_=ss3, axis=AX.X, op=ALU.max)
                V("vector", "tensor_tensor", [st.c(it), mixc.c(0)], [st.c(it)], out=sv_[:, 2:4], in0=sv_[:, 0:2], in1=sinks[:, 2 * c:2 * c + 2], op=ALU.max)
                V("vector", "tensor_scalar", [st.c(it)], [st.c(it)], out=sv_[:, 4:6], in0=sv_[:, 2:4], scalar1=-1.0, scalar2=None, op0=ALU.mult)
                V("vector", "tensor_tensor", [st.c(it), mixc.c(0)], [st.c(it)], out=sv_[:, 8:10], in0=sinks[:, 2 * c:2 * c + 2], in1=sv_[:, 2:4], op=ALU.subtract)
                yield
                pb3 = Pb.ap(it).rearrange("p (i k) -> p i k", i=2)
                for i in range(2):
                    V("scalar", "activation", [Ss.c(it), st.c(it)], [Pb.c(it), st.c(it)], out=pb3[:, i, :], in_=ss3[:, i, :], func=AF.Exp,
                      bias=sv_[:, 4 + i:5 + i], scale=1.0, accum_out=sv_[:, 6 + i:7 + i])
                V("scalar", "activation", [st.c(it)], [st.c(it)], out=sv_[:, 8:10], in_=sv_[:, 8:10], func=AF.Exp)
                yield
                V("vector", "tensor_tensor", [st.c(it)], [st.c(it)], out=sv_[:, 10:12], in0=sv_[:, 6:8], in1=sv_[:, 8:10], op=ALU.add)
                V("vector", "reciprocal", [st.c(it)], [st.c(it)], out=sv_[:, 12:14], in_=sv_[:, 10:12])
                pn3 = Pn.ap(it).rearrange("p (i k) -> p i k", i=2)
                for i in range(2):
                    V("vector", "tensor_scalar", [Pb.c(it), st.c(it)], [Pn.c(it)], out=pn3[:, i, :], in0=pb3[:, i, :], scalar1=sv_[:, 12 + i:13 + i],
                      scalar2=None, op0=ALU.mult)
                yield
                pt_ = pp.get()
                while pt_ is None:
                    yield
                    pt_ = pp.get()
                ptv = psum[pt_][:, :].bitcast(BF16)[:, 0:512]
                for i in range(2):
                    for kt in range(2):
                        S.add("tensor", "transpose", dict(out=ptv[:, (i * 2 + kt) * 128:(i * 2 + kt + 1) * 128],
                                                           in_=pn3[:, i, kt * 128:(kt + 1) * 128], identity=identb.ap(0)),
                              reads=[Pn.c(it), identb.c(0)], writes=[pcell[pt_]])
                yield
                V("scalar", "activation", [pcell[pt_]], [PT.c(it)], out=PT.ap(it), in_=ptv, func=AF.Copy)
                pp.put(pt_)
                yield
                for i in range(2):
                    for kt in range(2):
                        MM(psum[po][64 * i:64 * i + 64, nb * 128:(nb + 1) * 128], svt.ap(n + kt)[:, 64 * i:64 * i + 64],
                           PT.ap(it)[:, (i * 2 + kt) * 128:(i * 2 + kt + 1) * 128], kt == 0, kt == 1,
                           [svt.c(n + kt), PT.c(it)], [pcell[po]])
                bp.put(it)
            return g

        def finish(c, half):
            def g():
                po = po_bank[(c, half)]
                V("scalar", "activation", [pcell[po]], [aT.c((8 + c) * 2 + half)], out=aT.ap((8 + c) * 2 + half), in_=psum[po][:, :], func=AF.Copy)
                hp.put(po)
                yield
            return g

        tasks = []
        prep_id, fin_ids = {}, {}
        for c in range(8):
            deps = [prep_id[c - 1]] if c >= 1 else []
            if c >= 2:
                deps += fin_ids[c - 2]
            prep_id[c] = len(tasks)
            tasks.append((prep(c), deps))
            fin_ids[c] = []
            for half in range(2):
                bl = []
                for nb in range(4):
                    bl.append(len(tasks))
                    tasks.append((block(c, half, nb), [prep_id[c]]))
                fin_ids[c].append(len(tasks))
                tasks.append((finish(c, half), bl))
        run_tasks(tasks, 4)

        W3 = Arena(S, arena_t, ARENA_BYTES)
        W3.top = p2_top
        Ra = Arena(S, arena_t, ARENA_BYTES)
        Ra.top = ropeS.off
        Rb = Arena(S, arena_t, ARENA_BYTES)
        Rb.top = skA.off
        hb = []
        for k_ in range(2):
            A1 = W3 if k_ == 0 else Ra
            A2 = W3 if k_ == 0 else Rb
            d_ = dict(qT=A1.alloc(f"qT{k_}", 2, TH, BF16), qxi=A1.alloc(f"qxi{k_}", 2, TH, BF16), q32=A1.alloc(f"q32{k_}", 1, TH, F32),
                      rstd=A1.alloc(f"rstdg{k_}", 1, TH, F32), sgt=A2.alloc(f"sgt{k_}", 2, TH, F32), sTm=A2.alloc(f"sTm{k_}", 2, TH, BF16),
                      SinS=W3.alloc(f"SinS{k_}", 8, 128, BF16))
            hb.append(d_)
        assert Ra.top <= ropeS.off + ropeS.nbytes and Rb.top <= svt.off + svt.nbytes
        r2 = t1
        mean = t2
        pp = Pool(range(6))
        hbp = Pool(range(2))

        def getbank():
            b_ = pp.get()
            while b_ is None:
                yield None
                b_ = pp.get()
            yield b_

        def head(h):
            def g():
                k_ = hbp.get()
                while k_ is None:
                    yield
                    k_ = hbp.get()
                B_ = hb[k_]
                qT, qxi, q32, rstd, sgt, sTm, SinS = B_["qT"], B_["qxi"], B_["q32"], B_["rstd"], B_["sgt"], B_["sTm"], B_["SinS"]
                r32 = q32
                wq_, wqc = wblock(13 + h)
                for n in range(8):
                    V("scalar", "mul", [Sin32.c(h)], [SinS.c(n)], out=SinS.ap(n), in_=Sin32.ap(h), mul=float(GAMMA[h] ** (128 * n)))
                for half in range(2):
                    pb = None
                    while pb is None:
                        pb = pp.get()
                        if pb is None:
                            yield
                    for kc in range(KC):
                        MM(psum[pb][:, :], wq_[:, kc, 0:128], nT.ap(H(kc, half)), kc == 0, kc == KC - 1, [wqc, nT.c(H(kc, half))], [pcell[pb]])
                    yield
                    rope(pb, ropeR, half, 128, q32.ap(0), q32.c(0), t1, t2)
                    pp.put(pb)
                    V("scalar", "activation", [q32.c(0)], [qT.c(half)], out=qT.ap(half), in_=q32.ap(0), func=AF.Copy)
                    V("vector", "tensor_tensor", [q32.c(0), mixc.c(0)], [qxi.c(half)], out=qxi.ap(half).rearrange("p (n c) -> p n c", n=4),
                      in0=q32.ap(0).rearrange("p (n c) -> p n c", n=4), in1=xi[:, h, :].unsqueeze(1).to_broadcast([128, 4, 128]), op=ALU.mult)
                    yield
                    pg = None
                    while pg is None:
                        pg = pp.get()
                        if pg is None:
                            yield
                    for kc in range(KC):
                        MM(psum[pg][:, :], wq_[:, kc, 128:256], nT.ap(H(kc, half)), kc == 0, kc == KC - 1, [wqc, nT.c(H(kc, half))], [pcell[pg]])
                    yield
                    V("scalar", "activation", [pcell[pg]], [sgt.c(half)], out=sgt.ap(half), in_=psum[pg][:, :], func=AF.Silu)
                    pp.put(pg)
                    yield
                for half in range(2):
                    pss = None
                    while pss is None:
                        pss = pp.get()
                        if pss is None:
                            yield
                    for nb in range(4):
                        MM(psum[pss][:, nb * 128:(nb + 1) * 128], kT.ap(h * 2 + half)[:, nb * 128:(nb + 1) * 128], qT.ap(half)[:, nb * 128:(nb + 1) * 128],
                           True, True, [kT.c(h * 2 + half), qT.c(half)], [pcell[pss]])
                    yield
                    V("vector", "tensor_tensor", [pcell[pss], mixc.c(0)], [sTm.c(half)], out=sTm.ap(half).rearrange("p (n c) -> p n c", n=4),
                      in0=psum[pss][:, :].rearrange("p (n c) -> p n c", n=4), in1=dmat[:, h, :].unsqueeze(1).to_broadcast([128, 4, 128]), op=ALU.mult)
                    pp.put(pss)
                    yield
                    pr = None
                    while pr is None:
                        pr = pp.get()
                        if pr is None:
                            yield
                    for nb in range(4):
                        n = half * 4 + nb
                        oap = psum[pr][:, nb * 128:(nb + 1) * 128]
                        qx = qxi.ap(half)[:, nb * 128:(nb + 1) * 128]
                        MM(oap, vtok.ap(n * 8 + h), sTm.ap(half)[:, nb * 128:(nb + 1) * 128], True, False,
                           [vtok.c(n * 8 + h), sTm.c(half)], [pcell[pr]])
                        if n > 0:
                            MM(oap, Sloc.ap(h * 8 + n), qx, False, False, [Sloc.c(h * 8 + n), qxi.c(half)], [pcell[pr]])
                        MM(oap, SinS.ap(n), qx, False, True, [SinS.c(n), qxi.c(half)], [pcell[pr]])
                    yield
                    p1 = None
                    while p1 is None:
                        p1 = pp.get()
                        if p1 is None:
                            yield
                    p2 = None
                    while p2 is None:
                        p2 = pp.get()
                        if p2 is None:
                            yield
                    V("scalar", "activation", [pcell[pr]], [r32.c(0)], out=r32.ap(0), in_=psum[pr][:, :], func=AF.Copy)
                    V("scalar", "activation", [pcell[pr]], [r2.c(0)], out=r2.ap(0), in_=psum[pr][:, :], func=AF.Square)
                    pp.put(pr)
                    MM(psum[p1][:, :], ones_f.ap(0), r32.ap(0), True, True, [ones_f.c(0), r32.c(0)], [pcell[p1]])
                    MM(psum[p2][:, :], ones_f.ap(0), r2.ap(0), True, True, [ones_f.c(0), r2.c(0)], [pcell[p2]])
                    V("vector", "tensor_scalar", [pcell[p1]], [mean.c(0)], out=mean.ap(0), in0=psum[p1][:, :], scalar1=1.0 / 128, scalar2=None, op0=ALU.mult)
                    V("vector", "tensor_tensor", [mean.c(0)], [rstd.c(0)], out=rstd.ap(0), in0=mean.ap(0), in1=mean.ap(0), op=ALU.mult)
                    V("vector", "scalar_tensor_tensor", [pcell[p2], rstd.c(0)], [rstd.c(0)], out=rstd.ap(0), in0=psum[p2][:, :], scalar=1.0 / 128,
                      in1=rstd.ap(0), op0=ALU.mult, op1=ALU.subtract)
                    pp.put(p1)
                    pp.put(p2)
                    V("scalar", "activation", [rstd.c(0), epsc.c(0)], [rstd.c(0)], out=rstd.ap(0), in_=rstd.ap(0), func=AF.Sqrt, bias=epsc.ap(0)[:, 0:1], scale=1.0)
                    V("vector", "reciprocal", [rstd.c(0)], [rstd.c(0)], out=rstd.ap(0), in_=rstd.ap(0))
                    V("vector", "tensor_tensor", [r32.c(0), mean.c(0)], [r32.c(0)], out=r32.ap(0), in0=r32.ap(0), in1=mean.ap(0), op=ALU.subtract)
                    V("vector", "tensor_tensor", [r32.c(0), rstd.c(0)], [r32.c(0)], out=r32.ap(0), in0=r32.ap(0), in1=rstd.ap(0), op=ALU.mult)
                    V("vector", "scalar_tensor_tensor", [r32.c(0), mixc.c(0), sgt.c(half)], [aT.c(h * 2 + half)], out=aT.ap(h * 2 + half), in0=r32.ap(0),
                      scalar=gng[:, h:h + 1], in1=sgt.ap(half), op0=ALU.mult, op1=ALU.mult)
                    yield
                hbp.put(k_)
            return g

        run_tasks([(head(h), []) for h in range(8)], 2)

        if DEBUG_A:
            for kc in range(KC):
                for half in range(2):
                    V("vector", "tensor_copy", [aT.c(H(kc, half))], [hT.c(H(kc, half))], out=hT.ap(H(kc, half)), in_=aT.ap(H(kc, half)))
        else:
            proj_residual(w_out, aT, reload=hspill)

    def xattn():
        X = Arena(S, arena_t, ARENA_BYTES)
        X.top = phase_base
        qo = X.alloc("qo", 32, TH, BF16)
        kmT = X.alloc("kmT", 16, 256, BF16)
        vm = X.alloc("vm", 2 * 8, 256, BF16)
        mnT = X.alloc("mnT", 16, 256, BF16)
        xtop = X.top
        m32 = X.alloc("m32", 16, 256, F32)
        msq = X.alloc("msq", 2, 256, BF16)
        mrs = X.alloc("mrs", 1, 256, F32)
        if "mixer" not in stages:
            S.add("gpsimd", "dma_start", dict(out=identb.ap(0), in_=ident_d2), writes=[identb.c(0)], dma=True)
        for kc in range(KC):
            LD(m32.ap(kc), memT[kc * 128:(kc + 1) * 128, :], [m32.c(kc)])
        pb = next_ps()
        for kc in range(KC):
            s_ = kc % 2
            V("vector", "tensor_tensor", [m32.c(kc)], [msq.c(s_)], out=msq.ap(s_), in0=m32.ap(kc), in1=m32.ap(kc), op=ALU.mult)
            MM(psum[pb][:, 0:256], ones_b.ap(0), msq.ap(s_), kc == 0, kc == KC - 1, [ones_b.c(0), msq.c(s_)], [pcell[pb]])
        V("vector", "tensor_scalar", [pcell[pb]], [mrs.c(0)], out=mrs.ap(0), in0=psum[pb][:, 0:256], scalar1=1.0 / D, scalar2=EPS, op0=ALU.mult, op1=ALU.add)
        V("scalar", "activation", [mrs.c(0)], [mrs.c(0)], out=mrs.ap(0), in_=mrs.ap(0), func=AF.Sqrt)
        V("vector", "reciprocal", [mrs.c(0)], [mrs.c(0)], out=mrs.ap(0), in_=mrs.ap(0))
        for kc in range(KC):
            V("vector", "scalar_tensor_tensor", [m32.c(kc), gn.c(0), mrs.c(0)], [mnT.c(kc)], out=mnT.ap(kc), in0=m32.ap(kc),
              scalar=gn.ap(0)[:, 3 * KC + kc:3 * KC + kc + 1], in1=mrs.ap(0), op0=ALU.mult, op1=ALU.mult)
        for blk in range(8):
            wt, ci = next_wcell()
            wv = wt.v[:, ci, :].rearrange("p (k c) -> p k c", k=KC)
            wdma(wv, wkv_d[:, blk * 256:(blk + 1) * 256], KC, [wt.c(ci)])
            for j in range(2):
                pb = next_ps()
                for kc in range(KC):
                    MM(psum[pb][:, 0:256], wv[:, kc, j * 128:(j + 1) * 128], mnT.ap(kc), kc == 0, kc == KC - 1, [wt.c(ci), mnT.c(kc)], [pcell[pb]])
                V("scalar", "activation", [pcell[pb]], [kmT.c(blk * 2 + j)], out=kmT.ap(blk * 2 + j), in_=psum[pb][:, 0:256], func=AF.Copy)
        for blk in range(8):
            wt, ci = next_wcell()
            wv = wt.v[:, ci, :].rearrange("p (k c) -> p k c", k=KC)
            wdma(wv, wkv_d[:, D + blk * 256:D + (blk + 1) * 256], KC, [wt.c(ci)])
            for mt in range(2):
                pb = next_ps()
                for kc in range(KC):
                    MM(psum[pb][:, 0:256], mnT.ap(kc)[:, mt * 128:(mt + 1) * 128], wv[:, kc, :], kc == 0, kc == KC - 1, [wt.c(ci), mnT.c(kc)], [pcell[pb]])
                V("scalar", "activation", [pcell[pb]], [vm.c(mt * 8 + blk)], out=vm.ap(mt * 8 + blk), in_=psum[pb][:, 0:256], func=AF.Copy)
        rmsnorm(2, phase_base)
        for blk in range(8):
            wt, ci = next_wcell()
            wv = wt.v[:, ci, :].rearrange("p (k c) -> p k c", k=KC)
            wdma(wv, wq_d[:, blk * 256:(blk + 1) * 256], KC, [wt.c(ci)])
            for j in range(2):
                for half in range(2):
                    pb = next_ps()
                    for kc in range(KC):
                        MM(psum[pb][:, :], wv[:, kc, j * 128:(j + 1) * 128], nT.ap(H(kc, half)), kc == 0, kc == KC - 1, [wt.c(ci), nT.c(H(kc, half))], [pcell[pb]])
                    qc = H(blk * 2 + j, half)
                    V("scalar", "activation", [pcell[pb]], [qo.c(qc)], out=qo.ap(qc), in_=psum[pb][:, :], func=AF.Copy)
        X2 = Arena(S, arena_t, ARENA_BYTES)
        X2.top = xtop
        NX = 3
        Px = X2.alloc("Px", NX, 256, BF16)
        Pnx = X2.alloc("Pnx", NX, 256, BF16)
        stx = X2.alloc("stx", NX, 8, F32)
        PTx = X2.alloc("PTx", 2, 1024, BF16)
        SC = float(512 ** -0.5)
        pp = Pool(range(6))
        xbp = Pool(range(NX))
        ptp = Pool(range(2))
        pt_slot = {}

        def xtile(hd, half, tq):
            def g():
                it = xbp.get()
                while it is None:
                    yield
                    it = xbp.get()
                if (hd, half) not in pt_slot:
                    sl = ptp.get()
                    while sl is None:
                        yield
                        sl = ptp.get()
                    pt_slot[(hd, half)] = sl
                sl = pt_slot[(hd, half)]
                ptx3 = PTx.ap(sl).rearrange("p (m q) -> p m q", m=2)
                ps_ = pp.get()
                while ps_ is None:
                    yield
                    ps_ = pp.get()
                for cc in range(4):
                    MM(psum[ps_][:, 0:256], qo.ap(H(hd * 4 + cc, half))[:, tq * 128:(tq + 1) * 128], kmT.ap(hd * 4 + cc), cc == 0, cc == 3,
                       [qo.c(H(hd * 4 + cc, half)), kmT.c(hd * 4 + cc)], [pcell[ps_]])
                yield
                sv_ = stx.ap(it)
                V("vector", "tensor_reduce", [pcell[ps_]], [stx.c(it)], out=sv_[:, 0:1], in_=psum[ps_][:, 0:256], axis=AX.X, op=ALU.max)
                V("vector", "tensor_scalar", [stx.c(it)], [stx.c(it)], out=sv_[:, 1:2], in0=sv_[:, 0:1], scalar1=-SC, scalar2=None, op0=ALU.mult)
                yield
                V("scalar", "activation", [pcell[ps_], stx.c(it)], [Px.c(it), stx.c(it)], out=Px.ap(it), in_=psum[ps_][:, 0:256], func=AF.Exp,
                  bias=sv_[:, 1:2], scale=SC, accum_out=sv_[:, 2:3])
                pp.put(ps_)
                yield
                V("vector", "reciprocal", [stx.c(it)], [stx.c(it)], out=sv_[:, 3:4], in_=sv_[:, 2:3])
                V("vector", "tensor_scalar", [Px.c(it), stx.c(it)], [Pnx.c(it)], out=Pnx.ap(it), in0=Px.ap(it), scalar1=sv_[:, 3:4], scalar2=None, op0=ALU.mult)
                yield
                pt_ = pp.get()
                while pt_ is None:
                    yield
                    pt_ = pp.get()
                ptv = psum[pt_][:, :].bitcast(BF16)[:, 0:256]
                for mt in range(2):
                    S.add("tensor", "transpose", dict(out=ptv[:, mt * 128:(mt + 1) * 128], in_=Pnx.ap(it)[:, mt * 128:(mt + 1) * 128], identity=identb.ap(0)),
                          reads=[Pnx.c(it), identb.c(0)], writes=[pcell[pt_]])
                yield
                V("scalar", "activation", [pcell[pt_]], [PTx.c(sl)], out=ptx3[:, :, tq * 128:(tq + 1) * 128],
                  in_=ptv.rearrange("p (m q) -> p m q", m=2), func=AF.Copy)
                pp.put(pt_)
                xbp.put(it)
            return g

        def xfin(hd, half):
            def g():
                sl = pt_slot[(hd, half)]
                ptx3 = PTx.ap(sl).rearrange("p (m q) -> p m q", m=2)
                for cc in range(4):
                    dchunk = hd * 4 + cc
                    po = pp.get()
                    while po is None:
                        yield
                        po = pp.get()
                    for mt in range(2):
                        MM(psum[po][:, :], vm.ap(mt * 8 + dchunk // 2)[:, (dchunk % 2) * 128:(dchunk % 2 + 1) * 128], ptx3[:, mt, :], mt == 0, mt == 1,
                           [vm.c(mt * 8 + dchunk // 2), PTx.c(sl)], [pcell[po]])
                    yield
                    V("scalar", "activation", [pcell[po]], [qo.c(H(dchunk, half))], out=qo.ap(H(dchunk, half)), in_=psum[po][:, :], func=AF.Copy)
                    pp.put(po)
                ptp.put(sl)
            return g

        xt = []
        for hd in range(4):
            for half in range(2):
                ids = []
                for tq in range(4):
                    ids.append(len(xt))
                    xt.append((xtile(hd, half, tq), []))
                xt.append((xfin(hd, half), ids))
        run_tasks(xt, 4)
        proj_residual(wo_d, qo)

    hsp_cell = [S.cell(f"hsp{i}") for i in range(32)]
    xs_cell = S.cell("xsend")
    xr_cell = S.cell("xrecv")
    if "ffn1" in stages:
        rmsnorm(0, phase_base)
        ffn(wg1, wu1, wd1, phase_base)
    if "mixer" in stages:
        mixer()
    if "xattn" in stages:
        xattn()
    if "ffn2" in stages:
        rmsnorm(4, phase_base)
        ffn(wg2, wu2, wd2, phase_base)
    if "final" in stages:
        final = rmsnorm(5, phase_base, to_out=True)
    else:
        final = []
        for kc in range(KC):
            for half in range(2):
                final.append(LD(outT[kc * 128:(kc + 1) * 128, half * TH:(half + 1) * TH], hT.ap(H(kc, half)), [], reads=[hT.c(H(kc, half))]))
    S.add("sync", "nop", dict(), after=final)
    S.emit(nc, es)
    es.close()
    return nc


_CACHE = {}


def _gain_cols(v):
    return np.ascontiguousarray(np.asarray(v, np.float32).reshape(KC, 128).T)


def _rope_table(d, pos):
    f32 = np.float32
    inv = (f32(10000.0) ** (-np.arange(0, d, 2, dtype=f32) / f32(d))).astype(f32)
    ang = (pos.astype(f32)[None, :] * inv[:, None]).astype(f32)
    cos = np.cos(ang.astype(np.float64)).astype(f32)
    sin = np.sin(ang.astype(np.float64)).astype(f32)
    p = np.arange(128)
    pp = p % d
    fi = pp % (d // 2)
    sign = np.where(pp < d // 2, -1.0, 1.0).astype(f32)
    return np.ascontiguousarray(np.concatenate([cos[fi], sin[fi] * sign[:, None]], axis=1), f32)


def _mix_consts(ret_gn_gain, swa_sinks):
    g = np.array(GAMMA, np.float64)
    j = np.arange(128)[:, None]
    c = np.arange(128)[None, :]
    m = np.zeros((128, MIXC_W), np.float32)
    sc = 128.0 ** -0.5
    for h in range(8):
        dm = np.where(c >= j, sc * g[h] ** np.maximum(c - j, 0), 0.0)
        m[:, MC_DMAT + h * 128:MC_DMAT + (h + 1) * 128] = dm
        m[:, MC_XI + h * 128:MC_XI + (h + 1) * 128] = (g[h] ** (np.arange(128) + 1))[None, :]
        m[:, MC_ZETA + h] = sc * g[h] ** (127 - np.arange(128))
    m[:, MC_GNG:MC_GNG + 8] = np.asarray(ret_gn_gain, np.float32).reshape(8, 128).T
    sk = np.asarray(swa_sinks, np.float32).reshape(16)
    order = [cc + 8 * i for cc in range(8) for i in range(2)]
    m[:, MC_SINK:MC_SINK + 16] = sk[order][None, :]
    i = np.arange(128)[:, None]
    kk = np.arange(256)[None, :]
    m[:, MC_MASK:MC_MASK + 256] = np.where((kk > i) & (kk <= i + 128), 0.0, NEG)
    return m


def _w_in_perm():
    cols = []
    for h in range(8):
        cols += list(range(1024 + h * 128, 1024 + (h + 1) * 128))
    for h in range(8):
        cols += list(range(2048 + h * 128, 2048 + (h + 1) * 128))
    cols += list(range(5120, 5376))
    for c in range(8):
        cols += list(range(4096 + c * 64, 4096 + (c + 1) * 64)) + list(range(4096 + (c + 8) * 64, 4096 + (c + 9) * 64))
    for h in range(8):
        cols += list(range(h * 128, (h + 1) * 128)) + list(range(3072 + h * 128, 3072 + (h + 1) * 128))
    return np.array(cols)


def _w_out_perm():
    rows = list(range(1024))
    for c in range(8):
        rows += list(range(1024 + c * 64, 1024 + (c + 1) * 64)) + list(range(1024 + (c + 8) * 64, 1024 + (c + 9) * 64))
    return np.array(rows)


def kernel(x, mem, ffn1_norm, ffn1_w_gate, ffn1_w_up, ffn1_w_down, mix_norm, w_in, ret_gn_gain,
           swa_sinks, w_out, xa_norm, mem_norm, xa_wq, xa_wkv, xa_wo, ffn2_norm, ffn2_w_gate,
           ffn2_w_up, ffn2_w_down, final_norm):
    st = tuple(STAGES)
    x = np.asarray(x, np.float32)
    mem = np.asarray(mem, np.float32)
    key = ("nc", st)
    if key not in _CACHE:
        _CACHE[key] = build_program(st)
    nc = _CACHE[key]
    f = lambda a: np.ascontiguousarray(np.asarray(a, np.float32)[0])
    gains = np.concatenate([_gain_cols(np.asarray(g).reshape(-1)) for g in
                            (ffn1_norm, mix_norm, xa_norm, mem_norm, ffn2_norm, final_norm)], axis=1)
    shared = {"gains": np.ascontiguousarray(gains, np.float32)}
    if "ffn1" in st:
        shared.update(ffn1_w_gate=f(ffn1_w_gate), ffn1_w_up=f(ffn1_w_up), ffn1_w_down=f(ffn1_w_down))
    if "ffn2" in st:
        shared.update(ffn2_w_gate=f(ffn2_w_gate), ffn2_w_up=f(ffn2_w_up), ffn2_w_down=f(ffn2_w_down))
    if "mixer" in st:
        shared["w_in_p"] = np.ascontiguousarray(f(w_in)[:, _w_in_perm()])
        shared["w_out_p"] = np.ascontiguousarray(f(w_out)[_w_out_perm(), :])
        shared["mixc"] = _mix_consts(np.asarray(ret_gn_gain).reshape(-1), np.asarray(swa_sinks).reshape(-1))
    if "mixer" in st or "xattn" in st:
        shared["ident"] = np.eye(128, dtype=np.float32)
    if "xattn" in st:
        shared.update(xa_wq=f(xa_wq), xa_wkv=f(xa_wkv), xa_wo=f(xa_wo))
    in_maps = []
    g64 = np.array(GAMMA, np.float64)
    for c in range(NCORES):
        b, q = c // 4, c % 4
        m = dict(shared)
        m["xT"] = np.ascontiguousarray(x[b, q * T:(q + 1) * T, :].T)
        if "mixer" in st:
            pos = np.arange(q * T, (q + 1) * T)
            m["ropeR"] = _rope_table(128, pos)
            m["ropeS"] = _rope_table(64, pos)
            cf = np.zeros((128, 40), np.float32)
            for r in range(4):
                if r < q:
                    cf[:, r * 8:(r + 1) * 8] = (g64 ** (1024 * (q - 1 - r)))[None, :]
                if r == q - 1:
                    cf[:, 32 + r] = 1.0
            m["coef"] = cf
            mk = shared["mixc"][:, MC_MASK:MC_MASK + 256].copy()
            if q == 0:
                mk[:, 0:128] = NEG
            m["mask0"] = np.ascontiguousarray(mk)
        if "xattn" in st:
            m["memT"] = np.ascontiguousarray(mem[b].T)
        in_maps.append(m)
    res = run_bass_kernel_spmd(nc, in_maps, core_ids=list(range(NCORES)))
    out = np.empty((2, 4096, D), np.float32)
    for c in range(NCORES):
        b, q = c // 4, c % 4
        out[b, q * T:(q + 1) * T, :] = res.results[c]["outT"].T
    return out
```

```python
import numpy as np
from contextlib import ExitStack
import concourse.bass as bass
import concourse.mybir as mybir
from concourse.bass_utils import run_bass_kernel_spmd

F32 = mybir.dt.float32
BF16 = mybir.dt.bfloat16
ALU = mybir.AluOpType
AF = mybir.ActivationFunctionType
AX = mybir.AxisListType

D = 2048
KC = 16
T = 1024
TH = 512
DFF = 5632
FC = 44
EPS = 1e-6
NCORES = 8

STAGES = ("ffn1", "mixer", "xattn", "ffn2", "final")
GAMMA = [1.0 - 2.0 ** (-5 - h) for h in range(8)]
MC_DMAT, MC_XI, MC_ZETA, MC_GNG, MC_SINK, MC_MASK = 0, 1024, 2048, 2056, 2064, 2080
MIXC_W = 2336
NEG = -30000.0
DEBUG_A = False


class Cell:
    __slots__ = ("name", "space", "off", "size", "last_w", "readers", "ov")

    def __init__(self, name, space, off=0, size=0):
        self.name, self.space, self.off, self.size = name, space, off, size
        self.last_w = []
        self.readers = {}
        self.ov = []


class Op:
    __slots__ = ("eng", "fn", "deps", "pos", "signal", "tick", "is_dma", "sem", "val", "waits", "gidx", "group", "own")

    def __init__(self, eng, fn, is_dma):
        self.eng, self.fn, self.is_dma = eng, fn, is_dma
        self.deps = []
        self.signal = False
        self.tick = None
        self.sem = None
        self.val = None
        self.waits = []


class Sched:
    ENGS = ["sync", "gpsimd", "scalar", "vector", "tensor"]
    NDQ = 8

    def __init__(self):
        self.ops = []
        self.sb_cells = []
        self.count = {e: 0 for e in self.ENGS}

    def sb_cell(self, name, off, size):
        c = Cell(name, "sb", off, size)
        for o in self.sb_cells:
            if o.off < off + size and off < o.off + o.size:
                o.ov.append(c)
                c.ov.append(o)
        self.sb_cells.append(c)
        return c

    def cell(self, name):
        return Cell(name, "x")

    def add(self, eng, meth, kw, reads=(), writes=(), dma=False, after=(), group=None, own_sem=False):
        op = Op(eng, (meth, kw), dma)
        op.group = group
        op.own = own_sem
        op.pos = self.count[eng]
        self.count[eng] += 1
        op.gidx = len(self.ops)
        deps = {}

        def dep(o):
            if o is not None and o is not op and not (group is not None and o.group == group):
                deps[id(o)] = o

        for o in after:
            dep(o)
        for c in reads:
            for y in [c] + c.ov:
                for w in y.last_w:
                    dep(w)
        for c in writes:
            for y in [c] + c.ov:
                for w in y.last_w:
                    dep(w)
                for r in y.readers.values():
                    dep(r)
        for c in reads:
            for y in [c] + c.ov:
                key = ("d", op.gidx) if dma else eng
                y.readers[key] = op
        for c in writes:
            for y in [c] + c.ov:
                if group is not None and y.last_w and y.last_w[0].group == group:
                    y.last_w.append(op)
                else:
                    y.last_w = [op]
                y.readers = {}
        best = {}
        for o in deps.values():
            if o.is_dma or o.own:
                best[("d", id(o))] = o
            else:
                if eng == "tensor" and o.eng == "tensor":
                    continue
                k = o.eng
                if k not in best or best[k].pos < o.pos:
                    best[k] = o
        op.deps = list(best.values())
        for o in op.deps:
            o.signal = True
        self.ops.append(op)
        return op

    def emit(self, nc, es):
        sems = {e: es.enter_context(nc.semaphore("c_" + e)) for e in self.ENGS}
        dq = {e: [es.enter_context(nc.semaphore(f"dq_{e}_{i}")) for i in range(self.NDQ)]
              for e in ("sync", "gpsimd", "scalar")}
        streams = {e: [] for e in self.ENGS}
        ticks = {e: 0 for e in self.ENGS}
        ndma = {e: 0 for e in self.ENGS}
        for op in self.ops:
            streams[op.eng].append(op)
            if op.is_dma:
                i = ndma[op.eng]
                ndma[op.eng] += 1
                op.sem = dq[op.eng][i % self.NDQ]
                op.val = 16 * (i // self.NDQ + 1)
                op.signal = True
            elif op.own:
                op.sem = es.enter_context(nc.semaphore(f"own_{op.gidx}"))
                op.val = 1
                op.signal = True
            elif op.signal:
                ticks[op.eng] += 1
                op.sem = sems[op.eng]
                op.val = ticks[op.eng]
        for e in self.ENGS:
            seen = {}
            for op in streams[e]:
                w = []
                if op.is_dma and op.val > 16:
                    w.append((op.sem, op.val - 16))
                for d in op.deps:
                    w.append((d.sem, d.val))
                for (s, v) in w:
                    if seen.get(id(s), 0) >= v:
                        continue
                    seen[id(s)] = v
                    op.waits.append((s, v))
        block = es.enter_context(nc.Block())

        def run(e, stream):
            for op in stream:
                for (s, v) in op.waits:
                    e.wait_ge(s, v)
                ins = getattr(e, op.fn[0])(**op.fn[1])
                if op.signal:
                    ins.then_inc(op.sem, 16 if op.is_dma else 1)

        @block.sync
        def _(e):
            run(e, streams["sync"])

        @block.gpsimd
        def _(e):
            run(e, streams["gpsimd"])

        @block.scalar
        def _(e):
            run(e, streams["scalar"])

        @block.vector
        def _(e):
            run(e, streams["vector"])

        @block.tensor
        def _(e):
            run(e, streams["tensor"])


class SbT:
    def __init__(self, S, arena, name, off, ncell, clen, dt):
        self.esz = 4 if dt == F32 else 2
        self.ncell, self.clen, self.dt = ncell, clen, dt
        self.off = off
        self.nbytes = ncell * clen * self.esz
        assert off % 4 == 0 and self.nbytes % 4 == 0
        w0, w1 = off // 4, (off + self.nbytes) // 4
        v = arena[:, w0:w1]
        if dt != F32:
            v = v.bitcast(dt)
        self.flat = v
        self.v = v.rearrange("p (a b) -> p a b", b=clen)
        self.cells = [S.sb_cell(f"{name}{i}", off + i * clen * self.esz, clen * self.esz) for i in range(ncell)]

    def ap(self, i):
        return self.v[:, i, :]

    def c(self, i):
        return self.cells[i]


class Arena:
    def __init__(self, S, arena, total):
        self.S, self.arena, self.total = S, arena, total
        self.top = 0

    def alloc(self, name, ncell, clen, dt, at=None):
        esz = 4 if dt == F32 else 2
        nb = ncell * clen * esz
        nb4 = (nb + 31) // 32 * 32
        if at is None:
            at = self.top
            self.top += nb4
        assert at + nb <= self.total, (name, at, nb, self.total)
        return SbT(self.S, self.arena, name, at, ncell, clen, dt)


ARENA_BYTES = 207 * 1024


class Pool:
    def __init__(self, items):
        self.free = list(items)

    def get(self):
        return self.free.pop(0) if self.free else None

    def put(self, x):
        self.free.append(x)


def run_tasks(tasks, width):
    done, started, active = set(), set(), []
    while len(done) < len(tasks):
        for i in range(len(tasks)):
            if len(active) >= width:
                break
            if i in started:
                continue
            if all(d in done for d in tasks[i][1]):
                started.add(i)
                active.append((i, tasks[i][0]()))
        assert active, "task deadlock"
        for (i, g) in list(active):
            try:
                next(g)
            except StopIteration:
                active.remove((i, g))
                done.add(i)


def build_program(stages=("ffn1", "mixer", "xattn", "ffn2", "final")):
    stages = set(stages)
    nc = bass.Bass("TRN2", target_bir_lowering=False)
    S = Sched()
    es = ExitStack()

    def din(name, shape, dt=F32):
        return nc.dram_tensor(name, shape, dt, kind="ExternalInput").ap()

    xT = din("xT", [D, T])
    gains = din("gains", [128, 6 * KC])
    if "ffn1" in stages:
        wg1 = din("ffn1_w_gate", [D, DFF]); wu1 = din("ffn1_w_up", [D, DFF]); wd1 = din("ffn1_w_down", [DFF, D])
    if "ffn2" in stages:
        wg2 = din("ffn2_w_gate", [D, DFF]); wu2 = din("ffn2_w_up", [D, DFF]); wd2 = din("ffn2_w_down", [DFF, D])
    if "mixer" in stages:
        w_in = din("w_in_p", [D, 42 * 128])
        w_out = din("w_out_p", [D, D])
        ropeR_d = din("ropeR", [128, 2 * T])
        ropeS_d = din("ropeS", [128, 2 * T])
        mixc_d = din("mixc", [128, MIXC_W])
        coef_d = din("coef", [128, 40])
        mask0_d = din("mask0", [128, 256])
        ident_d = din("ident", [128, 128])
        hspill = nc.dram_tensor("hspill", [D, T], F32).ap()
        send_t = nc.dram_tensor("xsend", [8 * 128, 128], F32)
        recv_t = nc.dram_tensor("xrecv", [4 * 8 * 128, 128], F32)
        send1_t = nc.dram_tensor("xsend1", [2 * 128, 128], F32)
        recv1_t = nc.dram_tensor("xrecv1", [4 * 2 * 128, 128], F32)
    if "xattn" in stages:
        memT = din("memT", [D, 256])
        wq_d = din("xa_wq", [D, D]); wkv_d = din("xa_wkv", [D, 2 * D]); wo_d = din("xa_wo", [D, D])
        ident_d2 = ident_d if "mixer" in stages else din("ident", [128, 128])
    outT = nc.dram_tensor("outT", [D, T], F32, kind="ExternalOutput").ap()

    arena_t = es.enter_context(nc.sbuf_tensor("arena", [128, ARENA_BYTES // 4], F32))
    A = Arena(S, arena_t, ARENA_BYTES)
    psum_all = es.enter_context(nc.psum_tensor("psall", [128, 8 * 512], F32))
    psum = [psum_all[:, i * 512:(i + 1) * 512] for i in range(8)]
    pcell = [S.cell(f"ps{i}") for i in range(8)]

    hT = A.alloc("hT", KC * 2, TH, F32)
    nT = A.alloc("nT", KC * 2, TH, BF16)
    wslot = [A.alloc(f"ws{i}", 2, 4096, BF16) for i in range(2)]
    gn = A.alloc("gn", 1, 6 * KC, F32)
    ones_b = A.alloc("ones_b", 1, 128, BF16)
    ones_f = A.alloc("ones_f", 1, 128, F32)
    identb = A.alloc("identb", 1, 128, BF16)
    epsc = A.alloc("epsc", 1, 8, F32)
    phase_base = A.top

    psi = [0]

    def next_ps():
        i = psi[0] % 6
        psi[0] += 1
        return i

    def next_ps_pair():
        if psi[0] % 2:
            psi[0] += 1
        i = psi[0] % 6
        psi[0] += 2
        return i

    hpsi = [0]

    def hold_ps():
        i = 6 + hpsi[0] % 2
        hpsi[0] += 1
        return i

    wsi = [0]

    def next_wcell():
        i = wsi[0] % 4
        wsi[0] += 1
        return wslot[i // 2], i % 2

    def next_ws():
        if wsi[0] % 2:
            wsi[0] += 1
        i = (wsi[0] // 2) % 2
        wsi[0] += 2
        return wslot[i]

    def H(kc, half):
        return kc * 2 + half

    gid = [0]

    def wdma(dst3, src2, nk, cells):
        gid[0] += 1
        for k0 in range(0, nk, 4):
            k1 = min(nk, k0 + 4)
            S.add("gpsimd", "dma_start", dict(out=dst3[:, k0:k1, :],
                                               in_=src2[k0 * 128:k1 * 128, :].rearrange("(k p) c -> p k c", p=128)),
                  writes=cells, dma=True, group=gid[0])

    def V(eng, meth, reads, writes, **kw):
        return S.add(eng, meth, kw, reads=reads, writes=writes)

    def MM(out, lhsT, rhs, start, stop, reads, writes):
        return S.add("tensor", "matmul", dict(out=out, lhsT=lhsT, rhs=rhs, start=start, stop=stop), reads=reads, writes=writes)

    def LD(out, in_, writes, reads=(), eng="sync"):
        return S.add(eng, "dma_start", dict(out=out, in_=in_), reads=reads, writes=writes, dma=True)

    V("vector", "memset", [], [ones_b.c(0)], ap=ones_b.ap(0), constant=1.0)
    V("vector", "memset", [], [ones_f.c(0)], ap=ones_f.ap(0), constant=1.0)
    V("vector", "memset", [], [epsc.c(0)], ap=epsc.ap(0), constant=EPS)
    LD(gn.ap(0), gains, [gn.c(0)])
    for half in range(2):
        for kc in range(KC):
            LD(hT.ap(H(kc, half)), xT[kc * 128:(kc + 1) * 128, half * TH:(half + 1) * TH], [hT.c(H(kc, half))])

    def rmsnorm(gi, tmp_base, to_out=False):
        At = Arena(S, arena_t, ARENA_BYTES)
        At.top = tmp_base
        sq = At.alloc("sq", 4, TH, BF16)
        rstd = At.alloc("rstd", 2, TH, F32)
        ot = At.alloc("ot", 4, TH, F32) if to_out else None
        fin = []
        for half in range(2):
            pb = next_ps()
            for kc in range(KC):
                s = kc % 4
                hc = H(kc, half)
                if kc % 2 == 0:
                    V("scalar", "activation", [hT.c(hc)], [sq.c(s)], out=sq.ap(s), in_=hT.ap(hc), func=AF.Square)
                else:
                    V("vector", "tensor_tensor", [hT.c(hc)], [sq.c(s)], out=sq.ap(s), in0=hT.ap(hc), in1=hT.ap(hc), op=ALU.mult)
                MM(psum[pb][:, :], ones_b.ap(0), sq.ap(s), kc == 0, kc == KC - 1, [sq.c(s), ones_b.c(0)], [pcell[pb]])
            V("vector", "tensor_scalar", [pcell[pb]], [rstd.c(half)], out=rstd.ap(half), in0=psum[pb][:, :],
              scalar1=1.0 / D, scalar2=EPS, op0=ALU.mult, op1=ALU.add)
            V("scalar", "activation", [rstd.c(half)], [rstd.c(half)], out=rstd.ap(half), in_=rstd.ap(half), func=AF.Sqrt)
            V("vector", "reciprocal", [rstd.c(half)], [rstd.c(half)], out=rstd.ap(half), in_=rstd.ap(half))
            for kc in range(KC):
                hc = H(kc, half)
                gsc = gn.ap(0)[:, gi * KC + kc:gi * KC + kc + 1]
                if not to_out:
                    V("vector", "scalar_tensor_tensor", [hT.c(hc), gn.c(0), rstd.c(half)], [nT.c(hc)],
                      out=nT.ap(hc), in0=hT.ap(hc), scalar=gsc, in1=rstd.ap(half), op0=ALU.mult, op1=ALU.mult)
                else:
                    o = kc % 4
                    V("vector", "scalar_tensor_tensor", [hT.c(hc), gn.c(0), rstd.c(half)], [ot.c(o)],
                      out=ot.ap(o), in0=hT.ap(hc), scalar=gsc, in1=rstd.ap(half), op0=ALU.mult, op1=ALU.mult)
                    fin.append(LD(outT[kc * 128:(kc + 1) * 128, half * TH:(half + 1) * TH], ot.ap(o), [], reads=[ot.c(o)]))
        return fin

    def ffn(wg, wu, wd, tmp_base):
        At = Arena(S, arena_t, ARENA_BYTES)
        At.top = tmp_base
        act = At.alloc("act", 22 * 2, TH, BF16)
        sg = At.alloc("sg", 2, TH, F32)
        sgi = 0
        for ffh in range(2):
            for p in range(11):
                f0 = (ffh * 22 + 2 * p) * 128
                wt = next_ws()
                wv = wt.flat.rearrange("p (g k c) -> p g k c", g=2, k=KC)
                wdma(wv[:, 0], wg[:, f0:f0 + 256], KC, [wt.c(0)])
                wdma(wv[:, 1], wu[:, f0:f0 + 256], KC, [wt.c(1)])
                for j in range(2):
                    fl = 2 * p + j
                    for half in range(2):
                        pg, pu = next_ps(), next_ps()
                        for kc in range(KC):
                            MM(psum[pg][:, :], wv[:, 0, kc, j * 128:(j + 1) * 128], nT.ap(H(kc, half)), kc == 0, kc == KC - 1,
                               [wt.c(0), nT.c(H(kc, half))], [pcell[pg]])
                        for kc in range(KC):
                            MM(psum[pu][:, :], wv[:, 1, kc, j * 128:(j + 1) * 128], nT.ap(H(kc, half)), kc == 0, kc == KC - 1,
                               [wt.c(1), nT.c(H(kc, half))], [pcell[pu]])
                        si = sgi % 2
                        sgi += 1
                        V("scalar", "activation", [pcell[pg]], [sg.c(si)], out=sg.ap(si), in_=psum[pg][:, :], func=AF.Silu)
                        V("vector", "tensor_tensor", [pcell[pu], sg.c(si)], [act.c(fl * 2 + half)],
                          out=act.ap(fl * 2 + half), in0=psum[pu][:, :], in1=sg.ap(si), op=ALU.mult)
            for dblk in range(8):
                wt = next_ws()
                wv = wt.flat[:, 0:22 * 256].rearrange("p (k c) -> p k c", k=22)
                r0 = ffh * 22 * 128
                wdma(wv, wd[r0:r0 + 22 * 128, dblk * 256:(dblk + 1) * 256], 22, [wt.c(0), wt.c(1)])
                for j in range(2):
                    dc = dblk * 2 + j
                    for half in range(2):
                        pb = next_ps()
                        for fl in range(22):
                            MM(psum[pb][:, :], wv[:, fl, j * 128:(j + 1) * 128], act.ap(fl * 2 + half), fl == 0, fl == 21,
                               [wt.c(0), wt.c(1), act.c(fl * 2 + half)], [pcell[pb]])
                        V("vector", "scalar_tensor_tensor", [pcell[pb], hT.c(H(dc, half))], [hT.c(H(dc, half))],
                          out=hT.ap(H(dc, half)), in0=psum[pb][:, :], scalar=0.5, in1=hT.ap(H(dc, half)), op0=ALU.mult, op1=ALU.add)

    def proj_residual(w_d, src, reload=None):
        for dblk in range(8):
            wt, ci = next_wcell()
            wv = wt.v[:, ci, :].rearrange("p (k c) -> p k c", k=KC)
            wdma(wv, w_d[:, dblk * 256:(dblk + 1) * 256], KC, [wt.c(ci)])
            for j in range(2):
                dc = dblk * 2 + j
                for half in range(2):
                    pb = next_ps()
                    for ac in range(KC):
                        MM(psum[pb][:, :], wv[:, ac, j * 128:(j + 1) * 128], src.ap(H(ac, half)), ac == 0, ac == KC - 1,
                           [wt.c(ci), src.c(H(ac, half))], [pcell[pb]])
                    hc = H(dc, half)
                    if reload is not None:
                        LD(hT.ap(hc), reload[dc * 128:(dc + 1) * 128, half * TH:(half + 1) * TH], [hT.c(hc)])
                    V("vector", "tensor_tensor", [pcell[pb], hT.c(hc)], [hT.c(hc)],
                      out=hT.ap(hc), in0=psum[pb][:, :], in1=hT.ap(hc), op=ALU.add)

    def rope(pb, tab, half, dh, out_ap, out_cell, t1, t2):
        V("vector", "tensor_tensor", [pcell[pb], tab.c(half)], [t1.c(0)], out=t1.ap(0), in0=psum[pb][:, :], in1=tab.ap(half), op=ALU.mult)
        hd = dh // 2
        for base in range(0, 128, dh):
            for (dst, src) in ((base, base + hd), (base + hd, base)):
                V("vector", "tensor_tensor", [pcell[pb], tab.c(2 + half)], [t2.c(0)],
                  out=t2.ap(0)[dst:dst + hd, :], in0=psum[pb][src:src + hd, :], in1=tab.ap(2 + half)[dst:dst + hd, :], op=ALU.mult)
        if isinstance(out_ap, tuple):
            for (lo, oap) in ((0, out_ap[0]), (64, out_ap[1])):
                V("vector", "tensor_tensor", [t1.c(0), t2.c(0)], out_cell, out=oap[lo:lo + 64, :], in0=t1.ap(0)[lo:lo + 64, :], in1=t2.ap(0)[lo:lo + 64, :], op=ALU.add)
        else:
            V("vector", "tensor_tensor", [t1.c(0), t2.c(0)], out_cell if isinstance(out_cell, list) else [out_cell], out=out_ap, in0=t1.ap(0), in1=t2.ap(0), op=ALU.add)

    def mixer():
        rmsnorm(1, phase_base)
        for kc in range(KC):
            for half in range(2):
                S.add("sync", "dma_start", dict(out=hspill[kc * 128:(kc + 1) * 128, half * TH:(half + 1) * TH], in_=hT.ap(H(kc, half))),
                      reads=[hT.c(H(kc, half))], writes=[hsp_cell[H(kc, half)]], dma=True)
        R = Arena(S, arena_t, ARENA_BYTES)
        R.top = hT.off
        kT = R.alloc("kT", 16, TH, BF16)
        vtok = R.alloc("vtok", 64, 128, BF16)
        Sloc = R.alloc("Sloc", 64, 128, BF16)
        ropeR = R.alloc("ropeR", 4, TH, F32)
        ropeS = R.alloc("ropeS", 4, TH, F32)
        assert R.top <= hT.off + hT.nbytes
        P = Arena(S, arena_t, ARENA_BYTES)
        P.top = phase_base
        aT = P.alloc("aT", 32, TH, BF16)
        mixc = P.alloc("mixc", 1, MIXC_W, F32)
        coef = P.alloc("coef", 1, 40, F32)
        mask0 = P.alloc("mask0", 1, 256, F32)
        skA = P.alloc("skA", 9, 128, BF16)
        skB = P.alloc("skB", 9, 128, BF16)
        svt = P.alloc("svt", 9, 128, BF16)
        t1 = P.alloc("t1", 1, TH, F32)
        t2 = P.alloc("t2", 1, TH, F32)
        common_top = P.top
        mc = mixc.ap(0)
        dmat = mc[:, MC_DMAT:MC_DMAT + 1024].rearrange("p (h c) -> p h c", h=8)
        xi = mc[:, MC_XI:MC_XI + 1024].rearrange("p (h c) -> p h c", h=8)
        zeta = mc[:, MC_ZETA:MC_ZETA + 8]
        gng = mc[:, MC_GNG:MC_GNG + 8]
        sinks = mc[:, MC_SINK:MC_SINK + 16]
        maskg = mc[:, MC_MASK:MC_MASK + 256]
        V("vector", "memset", [], [skA.c(i) for i in range(9)], ap=skA.flat, constant=0.0)
        V("vector", "memset", [], [skB.c(i) for i in range(9)], ap=skB.flat, constant=0.0)
        LD(mixc.ap(0), mixc_d, [mixc.c(0)])
        LD(coef.ap(0), coef_d, [coef.c(0)])
        LD(mask0.ap(0), mask0_d, [mask0.c(0)])
        for i in range(4):
            LD(ropeR.ap(i), ropeR_d[:, i * TH:(i + 1) * TH], [ropeR.c(i)])
            LD(ropeS.ap(i), ropeS_d[:, i * TH:(i + 1) * TH], [ropeS.c(i)])
        S.add("gpsimd", "dma_start", dict(out=identb.ap(0), in_=ident_d), writes=[identb.c(0)], dma=True)

        def wblock(b):
            wt, ci = next_wcell()
            wv = wt.v[:, ci, :].rearrange("p (k c) -> p k c", k=KC)
            wdma(wv, w_in[:, b * 256:(b + 1) * 256], KC, [wt.c(ci)])
            return wv, wt.c(ci)

        def proj_fm(wv, wc, j, half):
            pb = next_ps()
            for kc in range(KC):
                MM(psum[pb][:, :], wv[:, kc, j * 128:(j + 1) * 128], nT.ap(H(kc, half)), kc == 0, kc == KC - 1,
                   [wc, nT.c(H(kc, half))], [pcell[pb]])
            return pb

        P1 = Arena(S, arena_t, ARENA_BYTES)
        P1.top = common_top
        kz = P1.alloc("kz", 2, 1024, BF16)
        Rst = P1.alloc("Rst", 2, 128, F32)
        Lst = P1.alloc("Lst", 10, 128, F32)
        wsk, wskc = wblock(8)
        for half in range(2):
            pb = proj_fm(wsk, wskc, 0, half)
            rope(pb, ropeS, half, 64, (skA.flat[:, 128 + half * TH:128 + (half + 1) * TH], skB.flat[:, 128 + half * TH:128 + (half + 1) * TH]),
                 [skA.c(1 + half * 4 + nn) for nn in range(4)] + [skB.c(1 + half * 4 + nn) for nn in range(4)], t1, t2)
        for n in range(8):
            pb = next_ps()
            half, o = n // 4, (n % 4) * 128
            for kc in range(KC):
                MM(psum[pb][:, 0:128], nT.ap(H(kc, half))[:, o:o + 128], wsk[:, kc, 128:256], kc == 0, kc == KC - 1,
                   [wskc, nT.c(H(kc, half))], [pcell[pb]])
            V("scalar", "activation", [pcell[pb]], [svt.c(1 + n)], out=svt.ap(1 + n), in_=psum[pb][:, 0:128], func=AF.Copy)
        V("vector", "tensor_copy", [skA.c(8)], [Lst.c(8)], out=Lst.ap(8)[0:64, :], in_=skA.ap(8)[0:64, :])
        V("vector", "tensor_copy", [skB.c(8)], [Lst.c(8)], out=Lst.ap(8)[64:128, :], in_=skB.ap(8)[64:128, :])
        V("vector", "tensor_copy", [svt.c(8)], [Lst.c(9)], out=Lst.ap(9), in_=svt.ap(8))
        S.add("gpsimd", "dma_start", dict(out=send1_t.ap().rearrange("(a p) e -> p a e", p=128), in_=Lst.v[:, 8:10, :]),
              reads=[Lst.c(8), Lst.c(9)], writes=[xs1_cell], dma=True)
        S.add("gpsimd", "collective_compute", dict(kind="AllGather", op=ALU.bypass, replica_groups=[[0, 1, 2, 3], [4, 5, 6, 7]],
                                                    ins=[send1_t.ap().opt()], outs=[recv1_t.ap().opt()]),
              reads=[xs1_cell], writes=[xr1_cell], own_sem=True)
        for hp in range(4):
            wk, wkc = wblock(hp)
            wvv, wvc = wblock(4 + hp)
            for j in range(2):
                h = 2 * hp + j
                for half in range(2):
                    pb = proj_fm(wk, wkc, j, half)
                    rope(pb, ropeR, half, 128, kT.ap(h * 2 + half), kT.c(h * 2 + half), t1, t2)
            for n in range(8):
                pb = next_ps()
                half, o = n // 4, (n % 4) * 128
                for kc in range(KC):
                    MM(psum[pb][:, 0:256], nT.ap(H(kc, half))[:, o:o + 128], wvv[:, kc, :], kc == 0, kc == KC - 1,
                       [wvc, nT.c(H(kc, half))], [pcell[pb]])
                for j in range(2):
                    h = 2 * hp + j
                    V("scalar", "activation", [pcell[pb]], [vtok.c(n * 8 + h)], out=vtok.ap(n * 8 + h), in_=psum[pb][:, j * 128:(j + 1) * 128], func=AF.Copy)
            for j in range(2):
                h = 2 * hp + j
                gC = float(GAMMA[h] ** 128)
                pb = next_ps()
                pbv = psum[pb][:, :].bitcast(BF16)
                for n in range(8):
                    half, o = n // 4, (n % 4) * 128
                    S.add("tensor", "transpose", dict(out=pbv[:, n * 128:(n + 1) * 128], in_=kT.ap(h * 2 + half)[:, o:o + 128], identity=identb.ap(0)),
                          reads=[kT.c(h * 2 + half), identb.c(0)], writes=[pcell[pb]])
                kzi = h % 2
                V("vector", "tensor_scalar", [pcell[pb], mixc.c(0)], [kz.c(kzi)], out=kz.ap(kzi), in0=pbv, scalar1=zeta[:, h:h + 1], scalar2=None, op0=ALU.mult)
                pbs = [next_ps(), next_ps()]
                for n in range(8):
                    pb2 = pbs[n // 4]
                    o = (n % 4) * 128
                    MM(psum[pb2][:, o:o + 128], kz.ap(kzi)[:, n * 128:(n + 1) * 128], vtok.ap(n * 8 + h), True, True,
                       [kz.c(kzi), vtok.c(n * 8 + h)], [pcell[pb2]])
                ri = 0
                for n in range(8):
                    pb2 = pbs[n // 4]
                    o = (n % 4) * 128
                    dst_ap, dst_c = (Rst.ap(1 - ri), Rst.c(1 - ri)) if n < 7 else (Lst.ap(h), Lst.c(h))
                    if n == 0:
                        V("vector", "tensor_copy", [pcell[pb2]], [dst_c], out=dst_ap, in_=psum[pb2][:, o:o + 128])
                    else:
                        V("vector", "scalar_tensor_tensor", [pcell[pb2], Rst.c(ri)], [dst_c], out=dst_ap, in0=Rst.ap(ri), scalar=gC,
                          in1=psum[pb2][:, o:o + 128], op0=ALU.mult, op1=ALU.add)
                    ri = 1 - ri
                    if n < 7:
                        V("scalar", "activation", [Rst.c(ri)], [Sloc.c(h * 8 + n + 1)], out=Sloc.ap(h * 8 + n + 1), in_=Rst.ap(ri), func=AF.Copy)
        S.add("gpsimd", "dma_start", dict(out=send_t.ap().rearrange("(a p) e -> p a e", p=128), in_=Lst.v[:, 0:8, :]),
              reads=[Lst.c(i) for i in range(8)], writes=[xs_cell], dma=True)
        S.add("gpsimd", "collective_compute", dict(kind="AllGather", op=ALU.bypass, replica_groups=[[0, 1, 2, 3], [4, 5, 6, 7]],
                                                    ins=[send_t.ap().opt()], outs=[recv_t.ap().opt()]),
              reads=[xs_cell], writes=[xr_cell], own_sem=True)
        recv4 = recv_t.ap().rearrange("(r a p) e -> a p r e", r=4, a=8)
        recv1 = recv1_t.ap().rearrange("(r a p) e -> a p r e", r=4, a=2)
        P2 = Arena(S, arena_t, ARENA_BYTES)
        P2.top = common_top
        Sin32 = P2.alloc("Sin32", 8, 128, F32)
        p2_top = P2.top
        rcv = P2.alloc("rcv", 2, 512, F32)
        acc = P2.alloc("acc", 2, 128, F32)
        cf = coef.ap(0)

        def receive(pieces):
            for piece in pieces:
                ri = piece % 2
                src = recv4[piece] if piece < 8 else recv1[piece - 8]
                LD(rcv.ap(ri).rearrange("p (r e) -> p r e", r=4), src, [rcv.c(ri)], reads=[xr_cell if piece < 8 else xr1_cell])
                rv3 = rcv.ap(ri).rearrange("p (r e) -> p r e", r=4)
                for r in range(4):
                    csc = cf[:, r * 8 + piece:r * 8 + piece + 1] if piece < 8 else cf[:, 32 + r:33 + r]
                    if r == 0:
                        V("vector", "tensor_scalar", [rcv.c(ri), coef.c(0)], [acc.c(ri)], out=acc.ap(ri), in0=rv3[:, 0, :], scalar1=csc, scalar2=None, op0=ALU.mult)
                    else:
                        V("vector", "scalar_tensor_tensor", [rcv.c(ri), coef.c(0), acc.c(ri)], [acc.c(ri)], out=acc.ap(ri), in0=rv3[:, r, :], scalar=csc,
                          in1=acc.ap(ri), op0=ALU.mult, op1=ALU.add)
                if piece < 8:
                    V("vector", "tensor_copy", [acc.c(ri)], [Sin32.c(piece)], out=Sin32.ap(piece), in_=acc.ap(ri))
                elif piece == 8:
                    V("vector", "tensor_copy", [acc.c(ri)], [skA.c(0)], out=skA.ap(0)[0:64, :], in_=acc.ap(ri)[0:64, :])
                    V("vector", "tensor_copy", [acc.c(ri)], [skB.c(0)], out=skB.ap(0)[64:128, :], in_=acc.ap(ri)[64:128, :])
                else:
                    V("vector", "tensor_copy", [acc.c(ri)], [svt.c(0)], out=svt.ap(0), in_=acc.ap(ri))

        receive([8, 9])

        W2 = Arena(S, arena_t, ARENA_BYTES)
        W2.top = p2_top
        NB = 3
        sqT = W2.alloc("sqT", 4, TH, BF16)
        Ss = W2.alloc("Ss", NB, 512, F32)
        Pb = W2.alloc("Pb", NB, 512, BF16)
        Pn = W2.alloc("Pn", NB, 512, BF16)
        PT = W2.alloc("PT", NB, 512, BF16)
        st = W2.alloc("st", NB, 16, F32)
        pp = Pool(range(6))
        hp = Pool([6, 7])
        bp = Pool(range(NB))
        wq_state = {}
        po_bank = {}

        def prep(c):
            def g():
                if c % 2 == 0:
                    wq_state["w"] = wblock(9 + c // 2)
                wsq, wsqc = wq_state["w"]
                for half in range(2):
                    pb = pp.get()
                    while pb is None:
                        yield
                        pb = pp.get()
                    for kc in range(KC):
                        MM(psum[pb][:, :], wsq[:, kc, (c % 2) * 128:(c % 2 + 1) * 128], nT.ap(H(kc, half)), kc == 0, kc == KC - 1,
                           [wsqc, nT.c(H(kc, half))], [pcell[pb]])
                    yield
                    rope(pb, ropeS, half, 64, sqT.ap((c % 2) * 2 + half), sqT.c((c % 2) * 2 + half), t1, t2)
                    pp.put(pb)
                    yield
            return g

        def block(c, half, nb):
            def g():
                n = half * 4 + nb
                sq_ = sqT.ap((c % 2) * 2 + half)
                sqc = sqT.c((c % 2) * 2 + half)
                it = bp.get()
                while it is None:
                    yield
                    it = bp.get()
                if (c, half) not in po_bank:
                    po = hp.get()
                    while po is None:
                        yield
                        po = hp.get()
                    po_bank[(c, half)] = po
                po = po_bank[(c, half)]
                ps_ = pp.get()
                while ps_ is None:
                    yield
                    ps_ = pp.get()
                ps3 = psum[ps_][:, :].rearrange("p (i k) -> p i k", i=2)
                for i, skx in enumerate((skA, skB)):
                    MM(ps3[:, i, :], sq_[:, nb * 128:(nb + 1) * 128], skx.flat[:, n * 128:n * 128 + 256], True, True,
                       [sqc, skx.c(n), skx.c(n + 1)], [pcell[ps_]])
                yield
                msk = (mask0.ap(0) if n == 0 else maskg)
                ss3 = Ss.ap(it).rearrange("p (i k) -> p i k", i=2)
                sv_ = st.ap(it)
                V("vector", "scalar_tensor_tensor", [pcell[ps_], mask0.c(0), mixc.c(0)], [Ss.c(it)], out=ss3, in0=ps3, scalar=0.125,
                  in1=msk.unsqueeze(1).to_broadcast([128, 2, 256]), op0=ALU.mult, op1=ALU.add)
                pp.put(ps_)
                V("vector", "tensor_reduce", [Ss.c(it)], [st.c(it)], out=sv_[:, 0:2], in_=ss3, axis=AX.X, op=ALU.max)
                V("vector", "tensor_tensor", [st.c(it), mixc.c(0)], [st.c(it)], out=sv_[:, 2:4], in0=sv_[:, 0:2], in1=sinks[:, 2 * c:2 * c + 2], op=ALU.max)
                V("vector", "tensor_scalar", [st.c(it)], [st.c(it)], out=sv_[:, 4:6], in0=sv_[:, 2:4], scalar1=-1.0, scalar2=None, op0=ALU.mult)
                V("vector", "tensor_tensor", [st.c(it), mixc.c(0)], [st.c(it)], out=sv_[:, 8:10], in0=sinks[:, 2 * c:2 * c + 2], in1=sv_[:, 2:4], op=ALU.subtract)
                yield
                pb3 = Pb.ap(it).rearrange("p (i k) -> p i k", i=2)
                for i in range(2):
                    V("scalar", "activation", [Ss.c(it), st.c(it)], [Pb.c(it), st.c(it)], out=pb3[:, i, :], in_=ss3[:, i, :], func=AF.Exp,
                      bias=sv_[:, 4 + i:5 + i], scale=1.0, accum_out=sv_[:, 6 + i:7 + i])
                V("scalar", "activation", [st.c(it)], [st.c(it)], out=sv_[:, 8:10], in_=sv_[:, 8:10], func=AF.Exp)
                yield
                V("vector", "tensor_tensor", [st.c(it)], [st.c(it)], out=sv_[:, 10:12], in0=sv_[:, 6:8], in1=sv_[:, 8:10], op=ALU.add)
                V("vector", "reciprocal", [st.c(it)], [st.c(it)], out=sv_[:, 12:14], in_=sv_[:, 10:12])
                pn3 = Pn.ap(it).rearrange("p (i k) -> p i k", i=2)
                for i in range(2):
                    V("vector", "tensor_scalar", [Pb.c(it), st.c(it)], [Pn.c(it)], out=pn3[:, i, :], in0=pb3[:, i, :], scalar1=sv_[:, 12 + i:13 + i],
                      scalar2=None, op0=ALU.mult)
                yield
                pt_ = pp.get()
                while pt_ is None:
                    yield
                    pt_ = pp.get()
                ptv = psum[pt_][:, :].bitcast(BF16)[:, 0:512]
                for i in range(2):
                    for kt in range(2):
                        S.add("tensor", "transpose", dict(out=ptv[:, (i * 2 + kt) * 128:(i * 2 + kt + 1) * 128],
                                                           in_=pn3[:, i, kt * 128:(kt + 1) * 128], identity=identb.ap(0)),
                              reads=[Pn.c(it), identb.c(0)], writes=[pcell[pt_]])
                yield
                V("scalar", "activation", [pcell[pt_]], [PT.c(it)], out=PT.ap(it), in_=ptv, func=AF.Copy)
                pp.put(pt_)
                yield
                for i in range(2):
                    for kt in range(2):
                        MM(psum[po][64 * i:64 * i + 64, nb * 128:(nb + 1) * 128], svt.ap(n + kt)[:, 64 * i:64 * i + 64],
                           PT.ap(it)[:, (i * 2 + kt) * 128:(i * 2 + kt + 1) * 128], kt == 0, kt == 1,
                           [svt.c(n + kt), PT.c(it)], [pcell[po]])
                bp.put(it)
            return g

        def finish(c, half):
            def g():
                po = po_bank[(c, half)]
                V("scalar", "activation", [pcell[po]], [aT.c((8 + c) * 2 + half)], out=aT.ap((8 + c) * 2 + half), in_=psum[po][:, :], func=AF.Copy)
                hp.put(po)
                yield
            return g

        tasks = []
        prep_id, fin_ids = {}, {}
        for c in range(8):
            deps = [prep_id[c - 1]] if c >= 1 else []
            if c >= 2:
                deps += fin_ids[c - 2]
            prep_id[c] = len(tasks)
            tasks.append((prep(c), deps))
            fin_ids[c] = []
            for half in range(2):
                bl = []
                for nb in range(4):
                    bl.append(len(tasks))
                    tasks.append((block(c, half, nb), [prep_id[c]]))
                fin_ids[c].append(len(tasks))
                tasks.append((finish(c, half), bl))
        run_tasks(tasks, 4)

        receive(list(range(8)))
        W3 = Arena(S, arena_t, ARENA_BYTES)
        W3.top = p2_top
        Ra = Arena(S, arena_t, ARENA_BYTES)
        Ra.top = ropeS.off
        Rb = Arena(S, arena_t, ARENA_BYTES)
        Rb.top = skA.off
        hb = []
        for k_ in range(2):
            A1 = W3 if k_ == 0 else Ra
            A2 = W3 if k_ == 0 else Rb
            d_ = dict(qT=A1.alloc(f"qT{k_}", 2, TH, BF16), qxi=A1.alloc(f"qxi{k_}", 2, TH, BF16), q32=A1.alloc(f"q32{k_}", 1, TH, F32),
                      rstd=A1.alloc(f"rstdg{k_}", 1, TH, F32), sgt=A2.alloc(f"sgt{k_}", 2, TH, F32), sTm=A2.alloc(f"sTm{k_}", 2, TH, BF16),
                      SinS=W3.alloc(f"SinS{k_}", 8, 128, BF16))
            hb.append(d_)
        assert Ra.top <= ropeS.off + ropeS.nbytes and Rb.top <= svt.off + svt.nbytes
        r2 = t1
        mean = t2
        pp = Pool(range(6))
        hbp = Pool(range(2))

        def getbank():
            b_ = pp.get()
            while b_ is None:
                yield None
                b_ = pp.get()
            yield b_

        def head(h):
            def g():
                k_ = hbp.get()
                while k_ is None:
                    yield
                    k_ = hbp.get()
                B_ = hb[k_]
                qT, qxi, q32, rstd, sgt, sTm, SinS = B_["qT"], B_["qxi"], B_["q32"], B_["rstd"], B_["sgt"], B_["sTm"], B_["SinS"]
                r32 = q32
                wq_, wqc = wblock(13 + h)
                for n in range(8):
                    V("scalar", "mul", [Sin32.c(h)], [SinS.c(n)], out=SinS.ap(n), in_=Sin32.ap(h), mul=float(GAMMA[h] ** (128 * n)))
                for half in range(2):
                    pb = None
                    while pb is None:
                        pb = pp.get()
                        if pb is None:
                            yield
                    for kc in range(KC):
                        MM(psum[pb][:, :], wq_[:, kc, 0:128], nT.ap(H(kc, half)), kc == 0, kc == KC - 1, [wqc, nT.c(H(kc, half))], [pcell[pb]])
                    yield
                    rope(pb, ropeR, half, 128, q32.ap(0), q32.c(0), t1, t2)
                    pp.put(pb)
                    V("scalar", "activation", [q32.c(0)], [qT.c(half)], out=qT.ap(half), in_=q32.ap(0), func=AF.Copy)
                    V("vector", "tensor_tensor", [q32.c(0), mixc.c(0)], [qxi.c(half)], out=qxi.ap(half).rearrange("p (n c) -> p n c", n=4),
                      in0=q32.ap(0).rearrange("p (n c) -> p n c", n=4), in1=xi[:, h, :].unsqueeze(1).to_broadcast([128, 4, 128]), op=ALU.mult)
                    yield
                    pg = None
                    while pg is None:
                        pg = pp.get()
                        if pg is None:
                            yield
                    for kc in range(KC):
                        MM(psum[pg][:, :], wq_[:, kc, 128:256], nT.ap(H(kc, half)), kc == 0, kc == KC - 1, [wqc, nT.c(H(kc, half))], [pcell[pg]])
                    yield
                    V("scalar", "activation", [pcell[pg]], [sgt.c(half)], out=sgt.ap(half), in_=psum[pg][:, :], func=AF.Silu)
                    pp.put(pg)
                    yield
                for half in range(2):
                    pss = None
                    while pss is None:
                        pss = pp.get()
                        if pss is None:
                            yield
                    for nb in range(4):
                        MM(psum[pss][:, nb * 128:(nb + 1) * 128], kT.ap(h * 2 + half)[:, nb * 128:(nb + 1) * 128], qT.ap(half)[:, nb * 128:(nb + 1) * 128],
                           True, True, [kT.c(h * 2 + half), qT.c(half)], [pcell[pss]])
                    yield
                    V("vector", "tensor_tensor", [pcell[pss], mixc.c(0)], [sTm.c(half)], out=sTm.ap(half).rearrange("p (n c) -> p n c", n=4),
                      in0=psum[pss][:, :].rearrange("p (n c) -> p n c", n=4), in1=dmat[:, h, :].unsqueeze(1).to_broadcast([128, 4, 128]), op=ALU.mult)
                    pp.put(pss)
                    yield
                    pr = None
                    while pr is None:
                        pr = pp.get()
                        if pr is None:
                            yield
                    for nb in range(4):
                        n = half * 4 + nb
                        oap = psum[pr][:, nb * 128:(nb + 1) * 128]
                        qx = qxi.ap(half)[:, nb * 128:(nb + 1) * 128]
                        MM(oap, vtok.ap(n * 8 + h), sTm.ap(half)[:, nb * 128:(nb + 1) * 128], True, False,
                           [vtok.c(n * 8 + h), sTm.c(half)], [pcell[pr]])
                        if n > 0:
                            MM(oap, Sloc.ap(h * 8 + n), qx, False, False, [Sloc.c(h * 8 + n), qxi.c(half)], [pcell[pr]])
                        MM(oap, SinS.ap(n), qx, False, True, [SinS.c(n), qxi.c(half)], [pcell[pr]])
                    yield
                    p1 = None
                    while p1 is None:
                        p1 = pp.get()
                        if p1 is None:
                            yield
                    p2 = None
                    while p2 is None:
                        p2 = pp.get()
                        if p2 is None:
                            yield
                    V("scalar", "activation", [pcell[pr]], [r32.c(0)], out=r32.ap(0), in_=psum[pr][:, :], func=AF.Copy)
                    V("scalar", "activation", [pcell[pr]], [r2.c(0)], out=r2.ap(0), in_=psum[pr][:, :], func=AF.Square)
                    pp.put(pr)
                    MM(psum[p1][:, :], ones_f.ap(0), r32.ap(0), True, True, [ones_f.c(0), r32.c(0)], [pcell[p1]])
                    MM(psum[p2][:, :], ones_f.ap(0), r2.ap(0), True, True, [ones_f.c(0), r2.c(0)], [pcell[p2]])
                    V("vector", "tensor_scalar", [pcell[p1]], [mean.c(0)], out=mean.ap(0), in0=psum[p1][:, :], scalar1=1.0 / 128, scalar2=None, op0=ALU.mult)
                    V("vector", "tensor_tensor", [mean.c(0)], [rstd.c(0)], out=rstd.ap(0), in0=mean.ap(0), in1=mean.ap(0), op=ALU.mult)
                    V("vector", "scalar_tensor_tensor", [pcell[p2], rstd.c(0)], [rstd.c(0)], out=rstd.ap(0), in0=psum[p2][:, :], scalar=1.0 / 128,
                      in1=rstd.ap(0), op0=ALU.mult, op1=ALU.subtract)
                    pp.put(p1)
                    pp.put(p2)
                    V("scalar", "activation", [rstd.c(0), epsc.c(0)], [rstd.c(0)], out=rstd.ap(0), in_=rstd.ap(0), func=AF.Sqrt, bias=epsc.ap(0)[:, 0:1], scale=1.0)
                    V("vector", "reciprocal", [rstd.c(0)], [rstd.c(0)], out=rstd.ap(0), in_=rstd.ap(0))
                    V("vector", "tensor_tensor", [r32.c(0), mean.c(0)], [r32.c(0)], out=r32.ap(0), in0=r32.ap(0), in1=mean.ap(0), op=ALU.subtract)
                    V("vector", "tensor_tensor", [r32.c(0), rstd.c(0)], [r32.c(0)], out=r32.ap(0), in0=r32.ap(0), in1=rstd.ap(0), op=ALU.mult)
                    V("vector", "scalar_tensor_tensor", [r32.c(0), mixc.c(0), sgt.c(half)], [aT.c(h * 2 + half)], out=aT.ap(h * 2 + half), in0=r32.ap(0),
                      scalar=gng[:, h:h + 1], in1=sgt.ap(half), op0=ALU.mult, op1=ALU.mult)
                    yield
                hbp.put(k_)
            return g

        run_tasks([(head(h), []) for h in range(8)], 2)

        if DEBUG_A:
            for kc in range(KC):
                for half in range(2):
                    V("vector", "tensor_copy", [aT.c(H(kc, half))], [hT.c(H(kc, half))], out=hT.ap(H(kc, half)), in_=aT.ap(H(kc, half)))
        else:
            proj_residual(w_out, aT, reload=hspill)

    def xattn():
        X = Arena(S, arena_t, ARENA_BYTES)
        X.top = phase_base
        qo = X.alloc("qo", 32, TH, BF16)
        kmT = X.alloc("kmT", 16, 256, BF16)
        vm = X.alloc("vm", 2 * 8, 256, BF16)
        mnT = X.alloc("mnT", 16, 256, BF16)
        xtop = X.top
        m32 = X.alloc("m32", 16, 256, F32)
        msq = X.alloc("msq", 2, 256, BF16)
        mrs = X.alloc("mrs", 1, 256, F32)
        if "mixer" not in stages:
            S.add("gpsimd", "dma_start", dict(out=identb.ap(0), in_=ident_d2), writes=[identb.c(0)], dma=True)
        for kc in range(KC):
            LD(m32.ap(kc), memT[kc * 128:(kc + 1) * 128, :], [m32.c(kc)])
        pb = next_ps()
        for kc in range(KC):
            s_ = kc % 2
            V("vector", "tensor_tensor", [m32.c(kc)], [msq.c(s_)], out=msq.ap(s_), in0=m32.ap(kc), in1=m32.ap(kc), op=ALU.mult)
            MM(psum[pb][:, 0:256], ones_b.ap(0), msq.ap(s_), kc == 0, kc == KC - 1, [ones_b.c(0), msq.c(s_)], [pcell[pb]])
        V("vector", "tensor_scalar", [pcell[pb]], [mrs.c(0)], out=mrs.ap(0), in0=psum[pb][:, 0:256], scalar1=1.0 / D, scalar2=EPS, op0=ALU.mult, op1=ALU.add)
        V("scalar", "activation", [mrs.c(0)], [mrs.c(0)], out=mrs.ap(0), in_=mrs.ap(0), func=AF.Sqrt)
        V("vector", "reciprocal", [mrs.c(0)], [mrs.c(0)], out=mrs.ap(0), in_=mrs.ap(0))
        for kc in range(KC):
            V("vector", "scalar_tensor_tensor", [m32.c(kc), gn.c(0), mrs.c(0)], [mnT.c(kc)], out=mnT.ap(kc), in0=m32.ap(kc),
              scalar=gn.ap(0)[:, 3 * KC + kc:3 * KC + kc + 1], in1=mrs.ap(0), op0=ALU.mult, op1=ALU.mult)
        for blk in range(8):
            wt, ci = next_wcell()
            wv = wt.v[:, ci, :].rearrange("p (k c) -> p k c", k=KC)
            wdma(wv, wkv_d[:, blk * 256:(blk + 1) * 256], KC, [wt.c(ci)])
            for j in range(2):
                pb = next_ps()
                for kc in range(KC):
                    MM(psum[pb][:, 0:256], wv[:, kc, j * 128:(j + 1) * 128], mnT.ap(kc), kc == 0, kc == KC - 1, [wt.c(ci), mnT.c(kc)], [pcell[pb]])
                V("scalar", "activation", [pcell[pb]], [kmT.c(blk * 2 + j)], out=kmT.ap(blk * 2 + j), in_=psum[pb][:, 0:256], func=AF.Copy)
        for blk in range(8):
            wt, ci = next_wcell()
            wv = wt.v[:, ci, :].rearrange("p (k c) -> p k c", k=KC)
            wdma(wv, wkv_d[:, D + blk * 256:D + (blk + 1) * 256], KC, [wt.c(ci)])
            for mt in range(2):
                pb = next_ps()
                for kc in range(KC):
                    MM(psum[pb][:, 0:256], mnT.ap(kc)[:, mt * 128:(mt + 1) * 128], wv[:, kc, :], kc == 0, kc == KC - 1, [wt.c(ci), mnT.c(kc)], [pcell[pb]])
                V("scalar", "activation", [pcell[pb]], [vm.c(mt * 8 + blk)], out=vm.ap(mt * 8 + blk), in_=psum[pb][:, 0:256], func=AF.Copy)
        rmsnorm(2, phase_base)
        for blk in range(8):
            wt, ci = next_wcell()
            wv = wt.v[:, ci, :].rearrange("p (k c) -> p k c", k=KC)
            wdma(wv, wq_d[:, blk * 256:(blk + 1) * 256], KC, [wt.c(ci)])
            for j in range(2):
                for half in range(2):
                    pb = next_ps()
                    for kc in range(KC):
                        MM(psum[pb][:, :], wv[:, kc, j * 128:(j + 1) * 128], nT.ap(H(kc, half)), kc == 0, kc == KC - 1, [wt.c(ci), nT.c(H(kc, half))], [pcell[pb]])
                    qc = H(blk * 2 + j, half)
                    V("scalar", "activation", [pcell[pb]], [qo.c(qc)], out=qo.ap(qc), in_=psum[pb][:, :], func=AF.Copy)
        X2 = Arena(S, arena_t, ARENA_BYTES)
        X2.top = xtop
        NX = 3
        Px = X2.alloc("Px", NX, 256, BF16)
        Pnx = X2.alloc("Pnx", NX, 256, BF16)
        stx = X2.alloc("stx", NX, 8, F32)
        PTx = X2.alloc("PTx", 2, 1024, BF16)
        SC = float(512 ** -0.5)
        pp = Pool(range(6))
        xbp = Pool(range(NX))
        ptp = Pool(range(2))
        pt_slot = {}

        def xtile(hd, half, tq):
            def g():
                it = xbp.get()
                while it is None:
                    yield
                    it = xbp.get()
                if (hd, half) not in pt_slot:
                    sl = ptp.get()
                    while sl is None:
                        yield
                        sl = ptp.get()
                    pt_slot[(hd, half)] = sl
                sl = pt_slot[(hd, half)]
                ptx3 = PTx.ap(sl).rearrange("p (m q) -> p m q", m=2)
                ps_ = pp.get()
                while ps_ is None:
                    yield
                    ps_ = pp.get()
                for cc in range(4):
                    MM(psum[ps_][:, 0:256], qo.ap(H(hd * 4 + cc, half))[:, tq * 128:(tq + 1) * 128], kmT.ap(hd * 4 + cc), cc == 0, cc == 3,
                       [qo.c(H(hd * 4 + cc, half)), kmT.c(hd * 4 + cc)], [pcell[ps_]])
                yield
                sv_ = stx.ap(it)
                V("vector", "tensor_reduce", [pcell[ps_]], [stx.c(it)], out=sv_[:, 0:1], in_=psum[ps_][:, 0:256], axis=AX.X, op=ALU.max)
                V("vector", "tensor_scalar", [stx.c(it)], [stx.c(it)], out=sv_[:, 1:2], in0=sv_[:, 0:1], scalar1=-SC, scalar2=None, op0=ALU.mult)
                yield
                V("scalar", "activation", [pcell[ps_], stx.c(it)], [Px.c(it), stx.c(it)], out=Px.ap(it), in_=psum[ps_][:, 0:256], func=AF.Exp,
                  bias=sv_[:, 1:2], scale=SC, accum_out=sv_[:, 2:3])
                pp.put(ps_)
                yield
                V("vector", "reciprocal", [stx.c(it)], [stx.c(it)], out=sv_[:, 3:4], in_=sv_[:, 2:3])
                V("vector", "tensor_scalar", [Px.c(it), stx.c(it)], [Pnx.c(it)], out=Pnx.ap(it), in0=Px.ap(it), scalar1=sv_[:, 3:4], scalar2=None, op0=ALU.mult)
                yield
                pt_ = pp.get()
                while pt_ is None:
                    yield
                    pt_ = pp.get()
                ptv = psum[pt_][:, :].bitcast(BF16)[:, 0:256]
                for mt in range(2):
                    S.add("tensor", "transpose", dict(out=ptv[:, mt * 128:(mt + 1) * 128], in_=Pnx.ap(it)[:, mt * 128:(mt + 1) * 128], identity=identb.ap(0)),
                          reads=[Pnx.c(it), identb.c(0)], writes=[pcell[pt_]])
                yield
                V("scalar", "activation", [pcell[pt_]], [PTx.c(sl)], out=ptx3[:, :, tq * 128:(tq + 1) * 128],
                  in_=ptv.rearrange("p (m q) -> p m q", m=2), func=AF.Copy)
                pp.put(pt_)
                xbp.put(it)
            return g

        def xfin(hd, half):
            def g():
                sl = pt_slot[(hd, half)]
                ptx3 = PTx.ap(sl).rearrange("p (m q) -> p m q", m=2)
                for cc in range(4):
                    dchunk = hd * 4 + cc
                    po = pp.get()
                    while po is None:
                        yield
                        po = pp.get()
                    for mt in range(2):
                        MM(psum[po][:, :], vm.ap(mt * 8 + dchunk // 2)[:, (dchunk % 2) * 128:(dchunk % 2 + 1) * 128], ptx3[:, mt, :], mt == 0, mt == 1,
                           [vm.c(mt * 8 + dchunk // 2), PTx.c(sl)], [pcell[po]])
                    yield
                    V("scalar", "activation", [pcell[po]], [qo.c(H(dchunk, half))], out=qo.ap(H(dchunk, half)), in_=psum[po][:, :], func=AF.Copy)
                    pp.put(po)
                ptp.put(sl)
            return g

        xt = []
        for hd in range(4):
            for half in range(2):
                ids = []
                for tq in range(4):
                    ids.append(len(xt))
                    xt.append((xtile(hd, half, tq), []))
                xt.append((xfin(hd, half), ids))
        run_tasks(xt, 4)
        proj_residual(wo_d, qo)

    hsp_cell = [S.cell(f"hsp{i}") for i in range(32)]
    xs_cell = S.cell("xsend")
    xr_cell = S.cell("xrecv")
    xs1_cell = S.cell("xsend1")
    xr1_cell = S.cell("xrecv1")
    if "ffn1" in stages:
        rmsnorm(0, phase_base)
        ffn(wg1, wu1, wd1, phase_base)
    if "mixer" in stages:
        mixer()
    if "xattn" in stages:
        xattn()
    if "ffn2" in stages:
        rmsnorm(4, phase_base)
        ffn(wg2, wu2, wd2, phase_base)
    if "final" in stages:
        final = rmsnorm(5, phase_base, to_out=True)
    else:
        final = []
        for kc in range(KC):
            for half in range(2):
                final.append(LD(outT[kc * 128:(kc + 1) * 128, half * TH:(half + 1) * TH], hT.ap(H(kc, half)), [], reads=[hT.c(H(kc, half))]))
    S.add("sync", "nop", dict(), after=final)
    S.emit(nc, es)
    es.close()
    return nc


_CACHE = {}


def _gain_cols(v):
    return np.ascontiguousarray(np.asarray(v, np.float32).reshape(KC, 128).T)


def _rope_table(d, pos):
    f32 = np.float32
    inv = (f32(10000.0) ** (-np.arange(0, d, 2, dtype=f32) / f32(d))).astype(f32)
    ang = (pos.astype(f32)[None, :] * inv[:, None]).astype(f32)
    cos = np.cos(ang.astype(np.float64)).astype(f32)
    sin = np.sin(ang.astype(np.float64)).astype(f32)
    p = np.arange(128)
    pp = p % d
    fi = pp % (d // 2)
    sign = np.where(pp < d // 2, -1.0, 1.0).astype(f32)
    return np.ascontiguousarray(np.concatenate([cos[fi], sin[fi] * sign[:, None]], axis=1), f32)


def _mix_consts(ret_gn_gain, swa_sinks):
    g = np.array(GAMMA, np.float64)
    j = np.arange(128)[:, None]
    c = np.arange(128)[None, :]
    m = np.zeros((128, MIXC_W), np.float32)
    sc = 128.0 ** -0.5
    for h in range(8):
        dm = np.where(c >= j, sc * g[h] ** np.maximum(c - j, 0), 0.0)
        m[:, MC_DMAT + h * 128:MC_DMAT + (h + 1) * 128] = dm
        m[:, MC_XI + h * 128:MC_XI + (h + 1) * 128] = (g[h] ** (np.arange(128) + 1))[None, :]
        m[:, MC_ZETA + h] = sc * g[h] ** (127 - np.arange(128))
    m[:, MC_GNG:MC_GNG + 8] = np.asarray(ret_gn_gain, np.float32).reshape(8, 128).T
    sk = np.asarray(swa_sinks, np.float32).reshape(16)
    order = [cc + 8 * i for cc in range(8) for i in range(2)]
    m[:, MC_SINK:MC_SINK + 16] = sk[order][None, :]
    i = np.arange(128)[:, None]
    kk = np.arange(256)[None, :]
    m[:, MC_MASK:MC_MASK + 256] = np.where((kk > i) & (kk <= i + 128), 0.0, NEG)
    return m


def _w_in_perm():
    cols = []
    for h in range(8):
        cols += list(range(1024 + h * 128, 1024 + (h + 1) * 128))
    for h in range(8):
        cols += list(range(2048 + h * 128, 2048 + (h + 1) * 128))
    cols += list(range(5120, 5376))
    for c in range(8):
        cols += list(range(4096 + c * 64, 4096 + (c + 1) * 64)) + list(range(4096 + (c + 8) * 64, 4096 + (c + 9) * 64))
    for h in range(8):
        cols += list(range(h * 128, (h + 1) * 128)) + list(range(3072 + h * 128, 3072 + (h + 1) * 128))
    return np.array(cols)


def _w_out_perm():
    rows = list(range(1024))
    for c in range(8):
        rows += list(range(1024 + c * 64, 1024 + (c + 1) * 64)) + list(range(1024 + (c + 8) * 64, 1024 + (c + 9) * 64))
    return np.array(rows)


def kernel(x, mem, ffn1_norm, ffn1_w_gate, ffn1_w_up, ffn1_w_down, mix_norm, w_in, ret_gn_gain,
           swa_sinks, w_out, xa_norm, mem_norm, xa_wq, xa_wkv, xa_wo, ffn2_norm, ffn2_w_gate,
           ffn2_w_up, ffn2_w_down, final_norm):
    st = tuple(STAGES)
    x = np.asarray(x, np.float32)
    mem = np.asarray(mem, np.float32)
    key = ("nc", st)
    if key not in _CACHE:
        _CACHE[key] = build_program(st)
    nc = _CACHE[key]
    f = lambda a: np.ascontiguousarray(np.asarray(a, np.float32)[0])
    gains = np.concatenate([_gain_cols(np.asarray(g).reshape(-1)) for g in
                            (ffn1_norm, mix_norm, xa_norm, mem_norm, ffn2_norm, final_norm)], axis=1)
    shared = {"gains": np.ascontiguousarray(gains, np.float32)}
    if "ffn1" in st:
        shared.update(ffn1_w_gate=f(ffn1_w_gate), ffn1_w_up=f(ffn1_w_up), ffn1_w_down=f(ffn1_w_down))
    if "ffn2" in st:
        shared.update(ffn2_w_gate=f(ffn2_w_gate), ffn2_w_up=f(ffn2_w_up), ffn2_w_down=f(ffn2_w_down))
    if "mixer" in st:
        shared["w_in_p"] = np.ascontiguousarray(f(w_in)[:, _w_in_perm()])
        shared["w_out_p"] = np.ascontiguousarray(f(w_out)[_w_out_perm(), :])
        shared["mixc"] = _mix_consts(np.asarray(ret_gn_gain).reshape(-1), np.asarray(swa_sinks).reshape(-1))
    if "mixer" in st or "xattn" in st:
        shared["ident"] = np.eye(128, dtype=np.float32)
    if "xattn" in st:
        shared.update(xa_wq=f(xa_wq), xa_wkv=f(xa_wkv), xa_wo=f(xa_wo))
    in_maps = []
    g64 = np.array(GAMMA, np.float64)
    for c in range(NCORES):
        b, q = c // 4, c % 4
        m = dict(shared)
        m["xT"] = np.ascontiguousarray(x[b, q * T:(q + 1) * T, :].T)
        if "mixer" in st:
            pos = np.arange(q * T, (q + 1) * T)
            m["ropeR"] = _rope_table(128, pos)
            m["ropeS"] = _rope_table(64, pos)
            cf = np.zeros((128, 40), np.float32)
            for r in range(4):
                if r < q:
                    cf[:, r * 8:(r + 1) * 8] = (g64 ** (1024 * (q - 1 - r)))[None, :]
                if r == q - 1:
                    cf[:, 32 + r] = 1.0
            m["coef"] = cf
            mk = shared["mixc"][:, MC_MASK:MC_MASK + 256].copy()
            if q == 0:
                mk[:, 0:128] = NEG
            m["mask0"] = np.ascontiguousarray(mk)
        if "xattn" in st:
            m["memT"] = np.ascontiguousarray(mem[b].T)
        in_maps.append(m)
    res = run_bass_kernel_spmd(nc, in_maps, core_ids=list(range(NCORES)))
    out = np.empty((2, 4096, D), np.float32)
    for c in range(NCORES):
        b, q = c // 4, c % 4
        out[b, q * T:(q + 1) * T, :] = res.results[c]["outT"].T
    return out
```

```python
import numpy as np
from contextlib import ExitStack
import concourse.bass as bass
import concourse.mybir as mybir
from concourse.bass_utils import run_bass_kernel_spmd

F32 = mybir.dt.float32
BF16 = mybir.dt.bfloat16
ALU = mybir.AluOpType
AF = mybir.ActivationFunctionType
AX = mybir.AxisListType

D = 2048
KC = 16
T = 1024
TH = 512
DFF = 5632
FC = 44
EPS = 1e-6
NCORES = 8

STAGES = ("ffn1", "mixer", "xattn", "ffn2", "final")
GAMMA = [1.0 - 2.0 ** (-5 - h) for h in range(8)]
MC_DMAT, MC_XI, MC_ZETA, MC_GNG, MC_SINK, MC_MASK = 0, 1024, 2048, 2056, 2064, 2080
MIXC_W = 2336
NEG = -30000.0
DEBUG_A = False


class Cell:
    __slots__ = ("name", "space", "off", "size", "last_w", "readers", "ov")

    def __init__(self, name, space, off=0, size=0):
        self.name, self.space, self.off, self.size = name, space, off, size
        self.last_w = []
        self.readers = {}
        self.ov = []


class Op:
    __slots__ = ("eng", "fn", "deps", "pos", "signal", "tick", "is_dma", "sem", "val", "waits", "gidx", "group", "own")

    def __init__(self, eng, fn, is_dma):
        self.eng, self.fn, self.is_dma = eng, fn, is_dma
        self.deps = []
        self.signal = False
        self.tick = None
        self.sem = None
        self.val = None
        self.waits = []


class Sched:
    ENGS = ["sync", "gpsimd", "scalar", "vector", "tensor"]
    NDQ = 8

    def __init__(self):
        self.ops = []
        self.sb_cells = []
        self.count = {e: 0 for e in self.ENGS}

    def sb_cell(self, name, off, size):
        c = Cell(name, "sb", off, size)
        for o in self.sb_cells:
            if o.off < off + size and off < o.off + o.size:
                o.ov.append(c)
                c.ov.append(o)
        self.sb_cells.append(c)
        return c

    def cell(self, name):
        return Cell(name, "x")

    def add(self, eng, meth, kw, reads=(), writes=(), dma=False, after=(), group=None, own_sem=False):
        op = Op(eng, (meth, kw), dma)
        op.group = group
        op.own = own_sem
        op.pos = self.count[eng]
        self.count[eng] += 1
        op.gidx = len(self.ops)
        deps = {}

        def dep(o):
            if o is not None and o is not op and not (group is not None and o.group == group):
                deps[id(o)] = o

        for o in after:
            dep(o)
        for c in reads:
            for y in [c] + c.ov:
                for w in y.last_w:
                    dep(w)
        for c in writes:
            for y in [c] + c.ov:
                for w in y.last_w:
                    dep(w)
                for r in y.readers.values():
                    dep(r)
        for c in reads:
            for y in [c] + c.ov:
                key = ("d", op.gidx) if dma else eng
                y.readers[key] = op
        for c in writes:
            for y in [c] + c.ov:
                if group is not None and y.last_w and y.last_w[0].group == group:
                    y.last_w.append(op)
                else:
                    y.last_w = [op]
                y.readers = {}
        best = {}
        for o in deps.values():
            if o.is_dma or o.own:
                best[("d", id(o))] = o
            else:
                if eng == "tensor" and o.eng == "tensor":
                    continue
                k = o.eng
                if k not in best or best[k].pos < o.pos:
                    best[k] = o
        op.deps = list(best.values())
        for o in op.deps:
            o.signal = True
        self.ops.append(op)
        return op

    def emit(self, nc, es):
        sems = {e: es.enter_context(nc.semaphore("c_" + e)) for e in self.ENGS}
        dq = {e: [es.enter_context(nc.semaphore(f"dq_{e}_{i}")) for i in range(self.NDQ)]
              for e in ("sync", "gpsimd", "scalar")}
        streams = {e: [] for e in self.ENGS}
        ticks = {e: 0 for e in self.ENGS}
        ndma = {e: 0 for e in self.ENGS}
        for op in self.ops:
            streams[op.eng].append(op)
            if op.is_dma:
                i = ndma[op.eng]
                ndma[op.eng] += 1
                op.sem = dq[op.eng][i % self.NDQ]
                op.val = 16 * (i // self.NDQ + 1)
                op.signal = True
            elif op.own:
                op.sem = es.enter_context(nc.semaphore(f"own_{op.gidx}"))
                op.val = 1
                op.signal = True
            elif op.signal:
                ticks[op.eng] += 1
                op.sem = sems[op.eng]
                op.val = ticks[op.eng]
        for e in self.ENGS:
            seen = {}
            for op in streams[e]:
                w = []
                if op.is_dma and op.val > 16:
                    w.append((op.sem, op.val - 16))
                for d in op.deps:
                    w.append((d.sem, d.val))
                for (s, v) in w:
                    if seen.get(id(s), 0) >= v:
                        continue
                    seen[id(s)] = v
                    op.waits.append((s, v))
        block = es.enter_context(nc.Block())

        def run(e, stream):
            for op in stream:
                for (s, v) in op.waits:
                    e.wait_ge(s, v)
                ins = getattr(e, op.fn[0])(**op.fn[1])
                if op.signal:
                    ins.then_inc(op.sem, 16 if op.is_dma else 1)

        @block.sync
        def _(e):
            run(e, streams["sync"])

        @block.gpsimd
        def _(e):
            run(e, streams["gpsimd"])

        @block.scalar
        def _(e):
            run(e, streams["scalar"])

        @block.vector
        def _(e):
            run(e, streams["vector"])

        @block.tensor
        def _(e):
            run(e, streams["tensor"])


class SbT:
    def __init__(self, S, arena, name, off, ncell, clen, dt):
        self.esz = 4 if dt == F32 else 2
        self.ncell, self.clen, self.dt = ncell, clen, dt
        self.off = off
        self.nbytes = ncell * clen * self.esz
        assert off % 4 == 0 and self.nbytes % 4 == 0
        w0, w1 = off // 4, (off + self.nbytes) // 4
        v = arena[:, w0:w1]
        if dt != F32:
            v = v.bitcast(dt)
        self.flat = v
        self.v = v.rearrange("p (a b) -> p a b", b=clen)
        self.cells = [S.sb_cell(f"{name}{i}", off + i * clen * self.esz, clen * self.esz) for i in range(ncell)]

    def ap(self, i):
        return self.v[:, i, :]

    def c(self, i):
        return self.cells[i]


class Arena:
    def __init__(self, S, arena, total):
        self.S, self.arena, self.total = S, arena, total
        self.top = 0

    def alloc(self, name, ncell, clen, dt, at=None):
        esz = 4 if dt == F32 else 2
        nb = ncell * clen * esz
        nb4 = (nb + 31) // 32 * 32
        if at is None:
            at = self.top
            self.top += nb4
        assert at + nb <= self.total, (name, at, nb, self.total)
        return SbT(self.S, self.arena, name, at, ncell, clen, dt)


ARENA_BYTES = 207 * 1024


class Pool:
    def __init__(self, items):
        self.free = list(items)

    def get(self):
        return self.free.pop(0) if self.free else None

    def put(self, x):
        self.free.append(x)


def run_tasks(tasks, width):
    done, started, active = set(), set(), []
    while len(done) < len(tasks):
        for i in range(len(tasks)):
            if len(active) >= width:
                break
            if i in started:
                continue
            if all(d in done for d in tasks[i][1]):
                started.add(i)
                active.append((i, tasks[i][0]()))
        assert active, "task deadlock"
        for (i, g) in list(active):
            try:
                next(g)
            except StopIteration:
                active.remove((i, g))
                done.add(i)


def build_program(stages=("ffn1", "mixer", "xattn", "ffn2", "final")):
    stages = set(stages)
    nc = bass.Bass("TRN2", target_bir_lowering=False)
    S = Sched()
    es = ExitStack()

    def din(name, shape, dt=F32):
        return nc.dram_tensor(name, shape, dt, kind="ExternalInput").ap()

    xT = din("xT", [D, T])
    gains = din("gains", [128, 6 * KC])
    if "ffn1" in stages:
        wg1 = din("ffn1_w_gate", [D, DFF]); wu1 = din("ffn1_w_up", [D, DFF]); wd1 = din("ffn1_w_down", [DFF, D])
    if "ffn2" in stages:
        wg2 = din("ffn2_w_gate", [D, DFF]); wu2 = din("ffn2_w_up", [D, DFF]); wd2 = din("ffn2_w_down", [DFF, D])
    if "mixer" in stages:
        w_in = din("w_in_p", [D, 42 * 128])
        w_out = din("w_out_p", [D, D])
        ropeR_d = din("ropeR", [128, 2 * T])
        ropeS_d = din("ropeS", [128, 2 * T])
        mixc_d = din("mixc", [128, MIXC_W])
        coef_d = din("coef", [128, 40])
        mask0_d = din("mask0", [128, 256])
        ident_d = din("ident", [128, 128])
        hspill = nc.dram_tensor("hspill", [D, T], F32).ap()
        send_t = nc.dram_tensor("xsend", [8 * 128, 128], F32)
        recv_t = nc.dram_tensor("xrecv", [4 * 8 * 128, 128], F32)
        send1_t = nc.dram_tensor("xsend1", [2 * 128, 128], F32)
        recv1_t = nc.dram_tensor("xrecv1", [4 * 2 * 128, 128], F32)
    if "xattn" in stages:
        memT = din("memT", [D, 256])
        wq_d = din("xa_wq", [D, D]); wkv_d = din("xa_wkv", [D, 2 * D]); wo_d = din("xa_wo", [D, D])
        ident_d2 = ident_d if "mixer" in stages else din("ident", [128, 128])
    outT = nc.dram_tensor("outT", [D, T], F32, kind="ExternalOutput").ap()

    arena_t = es.enter_context(nc.sbuf_tensor("arena", [128, ARENA_BYTES // 4], F32))
    A = Arena(S, arena_t, ARENA_BYTES)
    psum_all = es.enter_context(nc.psum_tensor("psall", [128, 8 * 512], F32))
    psum = [psum_all[:, i * 512:(i + 1) * 512] for i in range(8)]
    pcell = [S.cell(f"ps{i}") for i in range(8)]

    hT = A.alloc("hT", KC * 2, TH, F32)
    nT = A.alloc("nT", KC * 2, TH, BF16)
    wslot = [A.alloc(f"ws{i}", 2, 4096, BF16) for i in range(2)]
    gn = A.alloc("gn", 1, 6 * KC, F32)
    ones_b = A.alloc("ones_b", 1, 128, BF16)
    ones_f = A.alloc("ones_f", 1, 128, F32)
    identb = A.alloc("identb", 1, 128, BF16)
    epsc = A.alloc("epsc", 1, 8, F32)
    phase_base = A.top

    psi = [0]

    def next_ps():
        i = psi[0] % 6
        psi[0] += 1
        return i

    def next_ps_pair():
        if psi[0] % 2:
            psi[0] += 1
        i = psi[0] % 6
        psi[0] += 2
        return i

    hpsi = [0]

    def hold_ps():
        i = 6 + hpsi[0] % 2
        hpsi[0] += 1
        return i

    wsi = [0]

    def next_wcell():
        i = wsi[0] % 4
        wsi[0] += 1
        return wslot[i // 2], i % 2

    def next_ws():
        if wsi[0] % 2:
            wsi[0] += 1
        i = (wsi[0] // 2) % 2
        wsi[0] += 2
        return wslot[i]

    def H(kc, half):
        return kc * 2 + half

    gid = [0]

    def wdma(dst3, src2, nk, cells):
        gid[0] += 1
        for k0 in range(0, nk, 4):
            k1 = min(nk, k0 + 4)
            S.add("gpsimd", "dma_start", dict(out=dst3[:, k0:k1, :],
                                               in_=src2[k0 * 128:k1 * 128, :].rearrange("(k p) c -> p k c", p=128)),
                  writes=cells, dma=True, group=gid[0])

    def V(eng, meth, reads, writes, **kw):
        return S.add(eng, meth, kw, reads=reads, writes=writes)

    def MM(out, lhsT, rhs, start, stop, reads, writes):
        return S.add("tensor", "matmul", dict(out=out, lhsT=lhsT, rhs=rhs, start=start, stop=stop), reads=reads, writes=writes)

    def LD(out, in_, writes, reads=(), eng="sync"):
        return S.add(eng, "dma_start", dict(out=out, in_=in_), reads=reads, writes=writes, dma=True)

    V("vector", "memset", [], [ones_b.c(0)], ap=ones_b.ap(0), constant=1.0)
    V("vector", "memset", [], [ones_f.c(0)], ap=ones_f.ap(0), constant=1.0)
    V("vector", "memset", [], [epsc.c(0)], ap=epsc.ap(0), constant=EPS)
    LD(gn.ap(0), gains, [gn.c(0)])
    for half in range(2):
        for kc in range(KC):
            LD(hT.ap(H(kc, half)), xT[kc * 128:(kc + 1) * 128, half * TH:(half + 1) * TH], [hT.c(H(kc, half))])

    def rmsnorm(gi, tmp_base, to_out=False):
        At = Arena(S, arena_t, ARENA_BYTES)
        At.top = tmp_base
        sq = At.alloc("sq", 4, TH, BF16)
        rstd = At.alloc("rstd", 2, TH, F32)
        ot = At.alloc("ot", 4, TH, F32) if to_out else None
        fin = []
        for half in range(2):
            pb = next_ps()
            for kc in range(KC):
                s = kc % 4
                hc = H(kc, half)
                if kc % 2 == 0:
                    V("scalar", "activation", [hT.c(hc)], [sq.c(s)], out=sq.ap(s), in_=hT.ap(hc), func=AF.Square)
                else:
                    V("vector", "tensor_tensor", [hT.c(hc)], [sq.c(s)], out=sq.ap(s), in0=hT.ap(hc), in1=hT.ap(hc), op=ALU.mult)
                MM(psum[pb][:, :], ones_b.ap(0), sq.ap(s), kc == 0, kc == KC - 1, [sq.c(s), ones_b.c(0)], [pcell[pb]])
            V("vector", "tensor_scalar", [pcell[pb]], [rstd.c(half)], out=rstd.ap(half), in0=psum[pb][:, :],
              scalar1=1.0 / D, scalar2=EPS, op0=ALU.mult, op1=ALU.add)
            V("scalar", "activation", [rstd.c(half)], [rstd.c(half)], out=rstd.ap(half), in_=rstd.ap(half), func=AF.Sqrt)
            V("vector", "reciprocal", [rstd.c(half)], [rstd.c(half)], out=rstd.ap(half), in_=rstd.ap(half))
            for kc in range(KC):
                hc = H(kc, half)
                gsc = gn.ap(0)[:, gi * KC + kc:gi * KC + kc + 1]
                if not to_out:
                    V("vector", "scalar_tensor_tensor", [hT.c(hc), gn.c(0), rstd.c(half)], [nT.c(hc)],
                      out=nT.ap(hc), in0=hT.ap(hc), scalar=gsc, in1=rstd.ap(half), op0=ALU.mult, op1=ALU.mult)
                else:
                    o = kc % 4
                    V("vector", "scalar_tensor_tensor", [hT.c(hc), gn.c(0), rstd.c(half)], [ot.c(o)],
                      out=ot.ap(o), in0=hT.ap(hc), scalar=gsc, in1=rstd.ap(half), op0=ALU.mult, op1=ALU.mult)
                    fin.append(LD(outT[kc * 128:(kc + 1) * 128, half * TH:(half + 1) * TH], ot.ap(o), [], reads=[ot.c(o)]))
        return fin

    def ffn(wg, wu, wd, tmp_base):
        At = Arena(S, arena_t, ARENA_BYTES)
        At.top = tmp_base
        act = At.alloc("act", 22 * 2, TH, BF16)
        sg = At.alloc("sg", 2, TH, F32)
        sgi = 0
        for ffh in range(2):
            for p in range(11):
                f0 = (ffh * 22 + 2 * p) * 128
                wt = next_ws()
                wv = wt.flat.rearrange("p (g k c) -> p g k c", g=2, k=KC)
                wdma(wv[:, 0], wg[:, f0:f0 + 256], KC, [wt.c(0)])
                wdma(wv[:, 1], wu[:, f0:f0 + 256], KC, [wt.c(1)])
                for j in range(2):
                    fl = 2 * p + j
                    for half in range(2):
                        pg, pu = next_ps(), next_ps()
                        for kc in range(KC):
                            MM(psum[pg][:, :], wv[:, 0, kc, j * 128:(j + 1) * 128], nT.ap(H(kc, half)), kc == 0, kc == KC - 1,
                               [wt.c(0), nT.c(H(kc, half))], [pcell[pg]])
                        for kc in range(KC):
                            MM(psum[pu][:, :], wv[:, 1, kc, j * 128:(j + 1) * 128], nT.ap(H(kc, half)), kc == 0, kc == KC - 1,
                               [wt.c(1), nT.c(H(kc, half))], [pcell[pu]])
                        si = sgi % 2
                        sgi += 1
                        V("scalar", "activation", [pcell[pg]], [sg.c(si)], out=sg.ap(si), in_=psum[pg][:, :], func=AF.Silu)
                        V("vector", "tensor_tensor", [pcell[pu], sg.c(si)], [act.c(fl * 2 + half)],
                          out=act.ap(fl * 2 + half), in0=psum[pu][:, :], in1=sg.ap(si), op=ALU.mult)
            for dblk in range(8):
                wt = next_ws()
                wv = wt.flat[:, 0:22 * 256].rearrange("p (k c) -> p k c", k=22)
                r0 = ffh * 22 * 128
                wdma(wv, wd[r0:r0 + 22 * 128, dblk * 256:(dblk + 1) * 256], 22, [wt.c(0), wt.c(1)])
                for j in range(2):
                    dc = dblk * 2 + j
                    for half in range(2):
                        pb = next_ps()
                        for fl in range(22):
                            MM(psum[pb][:, :], wv[:, fl, j * 128:(j + 1) * 128], act.ap(fl * 2 + half), fl == 0, fl == 21,
                               [wt.c(0), wt.c(1), act.c(fl * 2 + half)], [pcell[pb]])
                        V("vector", "scalar_tensor_tensor", [pcell[pb], hT.c(H(dc, half))], [hT.c(H(dc, half))],
                          out=hT.ap(H(dc, half)), in0=psum[pb][:, :], scalar=0.5, in1=hT.ap(H(dc, half)), op0=ALU.mult, op1=ALU.add)

    def proj_residual(w_d, src, reload=None):
        for dblk in range(8):
            wt, ci = next_wcell()
            wv = wt.v[:, ci, :].rearrange("p (k c) -> p k c", k=KC)
            wdma(wv, w_d[:, dblk * 256:(dblk + 1) * 256], KC, [wt.c(ci)])
            for j in range(2):
                dc = dblk * 2 + j
                for half in range(2):
                    pb = next_ps()
                    for ac in range(KC):
                        MM(psum[pb][:, :], wv[:, ac, j * 128:(j + 1) * 128], src.ap(H(ac, half)), ac == 0, ac == KC - 1,
                           [wt.c(ci), src.c(H(ac, half))], [pcell[pb]])
                    hc = H(dc, half)
                    if reload is not None:
                        LD(hT.ap(hc), reload[dc * 128:(dc + 1) * 128, half * TH:(half + 1) * TH], [hT.c(hc)])
                    V("vector", "tensor_tensor", [pcell[pb], hT.c(hc)], [hT.c(hc)],
                      out=hT.ap(hc), in0=psum[pb][:, :], in1=hT.ap(hc), op=ALU.add)

    def rope(pb, tab, half, dh, out_ap, out_cell, t1, t2):
        V("vector", "tensor_tensor", [pcell[pb], tab.c(half)], [t1.c(0)], out=t1.ap(0), in0=psum[pb][:, :], in1=tab.ap(half), op=ALU.mult)
        hd = dh // 2
        for base in range(0, 128, dh):
            for (dst, src) in ((base, base + hd), (base + hd, base)):
                V("vector", "tensor_tensor", [pcell[pb], tab.c(2 + half)], [t2.c(0)],
                  out=t2.ap(0)[dst:dst + hd, :], in0=psum[pb][src:src + hd, :], in1=tab.ap(2 + half)[dst:dst + hd, :], op=ALU.mult)
        if isinstance(out_ap, tuple):
            for (lo, oap) in ((0, out_ap[0]), (64, out_ap[1])):
                V("vector", "tensor_tensor", [t1.c(0), t2.c(0)], out_cell, out=oap[lo:lo + 64, :], in0=t1.ap(0)[lo:lo + 64, :], in1=t2.ap(0)[lo:lo + 64, :], op=ALU.add)
        else:
            V("vector", "tensor_tensor", [t1.c(0), t2.c(0)], out_cell if isinstance(out_cell, list) else [out_cell], out=out_ap, in0=t1.ap(0), in1=t2.ap(0), op=ALU.add)

    def mixer():
        rmsnorm(1, phase_base)
        for kc in range(KC):
            for half in range(2):
                S.add("sync", "dma_start", dict(out=hspill[kc * 128:(kc + 1) * 128, half * TH:(half + 1) * TH], in_=hT.ap(H(kc, half))),
                      reads=[hT.c(H(kc, half))], writes=[hsp_cell[H(kc, half)]], dma=True)
        R = Arena(S, arena_t, ARENA_BYTES)
        R.top = hT.off
        kT = R.alloc("kT", 16, TH, BF16)
        vtok = R.alloc("vtok", 64, 128, BF16)
        Sloc = R.alloc("Sloc", 64, 128, BF16)
        ropeR = R.alloc("ropeR", 4, TH, F32)
        ropeS = R.alloc("ropeS", 4, TH, F32)
        assert R.top <= hT.off + hT.nbytes
        P = Arena(S, arena_t, ARENA_BYTES)
        P.top = phase_base
        aT = P.alloc("aT", 32, TH, BF16)
        mixc = P.alloc("mixc", 1, MIXC_W, F32)
        coef = P.alloc("coef", 1, 40, F32)
        mask0 = P.alloc("mask0", 1, 256, F32)
        skA = P.alloc("skA", 9, 128, BF16)
        skB = P.alloc("skB", 9, 128, BF16)
        svt = P.alloc("svt", 9, 128, BF16)
        t1 = P.alloc("t1", 1, TH, F32)
        t2 = P.alloc("t2", 1, TH, F32)
        common_top = P.top
        mc = mixc.ap(0)
        dmat = mc[:, MC_DMAT:MC_DMAT + 1024].rearrange("p (h c) -> p h c", h=8)
        xi = mc[:, MC_XI:MC_XI + 1024].rearrange("p (h c) -> p h c", h=8)
        zeta = mc[:, MC_ZETA:MC_ZETA + 8]
        gng = mc[:, MC_GNG:MC_GNG + 8]
        sinks = mc[:, MC_SINK:MC_SINK + 16]
        maskg = mc[:, MC_MASK:MC_MASK + 256]
        V("vector", "memset", [], [skA.c(i) for i in range(9)], ap=skA.flat, constant=0.0)
        V("vector", "memset", [], [skB.c(i) for i in range(9)], ap=skB.flat, constant=0.0)
        LD(mixc.ap(0), mixc_d, [mixc.c(0)])
        LD(coef.ap(0), coef_d, [coef.c(0)])
        LD(mask0.ap(0), mask0_d, [mask0.c(0)])
        nsk = P.alloc("nsk", 1, 16, F32)
        V("vector", "tensor_scalar", [mixc.c(0)], [nsk.c(0)], out=nsk.ap(0), in0=sinks, scalar1=-1.0, scalar2=None, op0=ALU.mult)
        for i in range(4):
            LD(ropeR.ap(i), ropeR_d[:, i * TH:(i + 1) * TH], [ropeR.c(i)])
            LD(ropeS.ap(i), ropeS_d[:, i * TH:(i + 1) * TH], [ropeS.c(i)])
        S.add("gpsimd", "dma_start", dict(out=identb.ap(0), in_=ident_d), writes=[identb.c(0)], dma=True)

        def wblock(b):
            wt, ci = next_wcell()
            wv = wt.v[:, ci, :].rearrange("p (k c) -> p k c", k=KC)
            wdma(wv, w_in[:, b * 256:(b + 1) * 256], KC, [wt.c(ci)])
            return wv, wt.c(ci)

        def proj_fm(wv, wc, j, half):
            pb = next_ps()
            for kc in range(KC):
                MM(psum[pb][:, :], wv[:, kc, j * 128:(j + 1) * 128], nT.ap(H(kc, half)), kc == 0, kc == KC - 1,
                   [wc, nT.c(H(kc, half))], [pcell[pb]])
            return pb

        P1 = Arena(S, arena_t, ARENA_BYTES)
        P1.top = common_top
        kz = P1.alloc("kz", 2, 1024, BF16)
        Rst = P1.alloc("Rst", 2, 128, F32)
        Lst = P1.alloc("Lst", 10, 128, F32)
        wsk, wskc = wblock(8)
        for half in range(2):
            pb = proj_fm(wsk, wskc, 0, half)
            rope(pb, ropeS, half, 64, (skA.flat[:, 128 + half * TH:128 + (half + 1) * TH], skB.flat[:, 128 + half * TH:128 + (half + 1) * TH]),
                 [skA.c(1 + half * 4 + nn) for nn in range(4)] + [skB.c(1 + half * 4 + nn) for nn in range(4)], t1, t2)
        for n in range(8):
            pb = next_ps()
            half, o = n // 4, (n % 4) * 128
            for kc in range(KC):
                MM(psum[pb][:, 0:128], nT.ap(H(kc, half))[:, o:o + 128], wsk[:, kc, 128:256], kc == 0, kc == KC - 1,
                   [wskc, nT.c(H(kc, half))], [pcell[pb]])
            V("scalar", "activation", [pcell[pb]], [svt.c(1 + n)], out=svt.ap(1 + n), in_=psum[pb][:, 0:128], func=AF.Copy)
        V("vector", "tensor_copy", [skA.c(8)], [Lst.c(8)], out=Lst.ap(8)[0:64, :], in_=skA.ap(8)[0:64, :])
        V("vector", "tensor_copy", [skB.c(8)], [Lst.c(8)], out=Lst.ap(8)[64:128, :], in_=skB.ap(8)[64:128, :])
        V("vector", "tensor_copy", [svt.c(8)], [Lst.c(9)], out=Lst.ap(9), in_=svt.ap(8))
        S.add("gpsimd", "dma_start", dict(out=send1_t.ap().rearrange("(a p) e -> p a e", p=128), in_=Lst.v[:, 8:10, :]),
              reads=[Lst.c(8), Lst.c(9)], writes=[xs1_cell], dma=True)
        S.add("gpsimd", "collective_compute", dict(kind="AllGather", op=ALU.bypass, replica_groups=[[0, 1, 2, 3], [4, 5, 6, 7]],
                                                    ins=[send1_t.ap().opt()], outs=[recv1_t.ap().opt()]),
              reads=[xs1_cell], writes=[xr1_cell], own_sem=True)
        for hp in range(4):
            wk, wkc = wblock(hp)
            wvv, wvc = wblock(4 + hp)
            for j in range(2):
                h = 2 * hp + j
                for half in range(2):
                    pb = proj_fm(wk, wkc, j, half)
                    rope(pb, ropeR, half, 128, kT.ap(h * 2 + half), kT.c(h * 2 + half), t1, t2)
            for n in range(8):
                pb = next_ps()
                half, o = n // 4, (n % 4) * 128
                for kc in range(KC):
                    MM(psum[pb][:, 0:256], nT.ap(H(kc, half))[:, o:o + 128], wvv[:, kc, :], kc == 0, kc == KC - 1,
                       [wvc, nT.c(H(kc, half))], [pcell[pb]])
                for j in range(2):
                    h = 2 * hp + j
                    V("scalar", "activation", [pcell[pb]], [vtok.c(n * 8 + h)], out=vtok.ap(n * 8 + h), in_=psum[pb][:, j * 128:(j + 1) * 128], func=AF.Copy)
            for j in range(2):
                h = 2 * hp + j
                gC = float(GAMMA[h] ** 128)
                pb = next_ps()
                pbv = psum[pb][:, :].bitcast(BF16)
                for n in range(8):
                    half, o = n // 4, (n % 4) * 128
                    S.add("tensor", "transpose", dict(out=pbv[:, n * 128:(n + 1) * 128], in_=kT.ap(h * 2 + half)[:, o:o + 128], identity=identb.ap(0)),
                          reads=[kT.c(h * 2 + half), identb.c(0)], writes=[pcell[pb]])
                kzi = h % 2
                V("vector", "tensor_scalar", [pcell[pb], mixc.c(0)], [kz.c(kzi)], out=kz.ap(kzi), in0=pbv, scalar1=zeta[:, h:h + 1], scalar2=None, op0=ALU.mult)
                pbs = [next_ps(), next_ps()]
                for n in range(8):
                    pb2 = pbs[n // 4]
                    o = (n % 4) * 128
                    MM(psum[pb2][:, o:o + 128], kz.ap(kzi)[:, n * 128:(n + 1) * 128], vtok.ap(n * 8 + h), True, True,
                       [kz.c(kzi), vtok.c(n * 8 + h)], [pcell[pb2]])
                ri = 0
                for n in range(8):
                    pb2 = pbs[n // 4]
                    o = (n % 4) * 128
                    dst_ap, dst_c = (Rst.ap(1 - ri), Rst.c(1 - ri)) if n < 7 else (Lst.ap(h), Lst.c(h))
                    if n == 0:
                        V("vector", "tensor_copy", [pcell[pb2]], [dst_c], out=dst_ap, in_=psum[pb2][:, o:o + 128])
                    else:
                        V("vector", "scalar_tensor_tensor", [pcell[pb2], Rst.c(ri)], [dst_c], out=dst_ap, in0=Rst.ap(ri), scalar=gC,
                          in1=psum[pb2][:, o:o + 128], op0=ALU.mult, op1=ALU.add)
                    ri = 1 - ri
                    if n < 7:
                        V("scalar", "activation", [Rst.c(ri)], [Sloc.c(h * 8 + n + 1)], out=Sloc.ap(h * 8 + n + 1), in_=Rst.ap(ri), func=AF.Copy)
        S.add("gpsimd", "dma_start", dict(out=send_t.ap().rearrange("(a p) e -> p a e", p=128), in_=Lst.v[:, 0:8, :]),
              reads=[Lst.c(i) for i in range(8)], writes=[xs_cell], dma=True)
        S.add("gpsimd", "collective_compute", dict(kind="AllGather", op=ALU.bypass, replica_groups=[[0, 1, 2, 3], [4, 5, 6, 7]],
                                                    ins=[send_t.ap().opt()], outs=[recv_t.ap().opt()]),
              reads=[xs_cell], writes=[xr_cell], own_sem=True)
        recv4 = recv_t.ap().rearrange("(r a p) e -> a p r e", r=4, a=8)
        recv1 = recv1_t.ap().rearrange("(r a p) e -> a p r e", r=4, a=2)
        P2 = Arena(S, arena_t, ARENA_BYTES)
        P2.top = common_top
        Sin32 = P2.alloc("Sin32", 8, 128, F32)
        p2_top = P2.top
        rcv = P2.alloc("rcv", 2, 512, F32)
        acc = P2.alloc("acc", 2, 128, F32)
        cf = coef.ap(0)

        def receive(pieces):
            for piece in pieces:
                ri = piece % 2
                src = recv4[piece] if piece < 8 else recv1[piece - 8]
                LD(rcv.ap(ri).rearrange("p (r e) -> p r e", r=4), src, [rcv.c(ri)], reads=[xr_cell if piece < 8 else xr1_cell])
                rv3 = rcv.ap(ri).rearrange("p (r e) -> p r e", r=4)
                for r in range(4):
                    csc = cf[:, r * 8 + piece:r * 8 + piece + 1] if piece < 8 else cf[:, 32 + r:33 + r]
                    if r == 0:
                        V("vector", "tensor_scalar", [rcv.c(ri), coef.c(0)], [acc.c(ri)], out=acc.ap(ri), in0=rv3[:, 0, :], scalar1=csc, scalar2=None, op0=ALU.mult)
                    else:
                        V("vector", "scalar_tensor_tensor", [rcv.c(ri), coef.c(0), acc.c(ri)], [acc.c(ri)], out=acc.ap(ri), in0=rv3[:, r, :], scalar=csc,
                          in1=acc.ap(ri), op0=ALU.mult, op1=ALU.add)
                if piece < 8:
                    V("vector", "tensor_copy", [acc.c(ri)], [Sin32.c(piece)], out=Sin32.ap(piece), in_=acc.ap(ri))
                elif piece == 8:
                    V("vector", "tensor_copy", [acc.c(ri)], [skA.c(0)], out=skA.ap(0)[0:64, :], in_=acc.ap(ri)[0:64, :])
                    V("vector", "tensor_copy", [acc.c(ri)], [skB.c(0)], out=skB.ap(0)[64:128, :], in_=acc.ap(ri)[64:128, :])
                else:
                    V("vector", "tensor_copy", [acc.c(ri)], [svt.c(0)], out=svt.ap(0), in_=acc.ap(ri))

        receive([8, 9])

        W2 = Arena(S, arena_t, ARENA_BYTES)
        W2.top = p2_top
        NB = 3
        sqT = W2.alloc("sqT", 4, TH, BF16)
        Ss = W2.alloc("Ss", NB, 512, F32)
        Pb = W2.alloc("Pb", NB, 512, BF16)
        Pn = W2.alloc("Pn", NB, 512, BF16)
        PT = W2.alloc("PT", NB, 512, BF16)
        st = W2.alloc("st", NB, 16, F32)
        pp = Pool(range(6))
        hp = Pool([6, 7])
        bp = Pool(range(NB))
        wq_state = {}
        po_bank = {}

        def prep(c):
            def g():
                if c % 2 == 0:
                    wq_state["w"] = wblock(9 + c // 2)
                wsq, wsqc = wq_state["w"]
                for half in range(2):
                    pb = pp.get()
                    while pb is None:
                        yield
                        pb = pp.get()
                    for kc in range(KC):
                        MM(psum[pb][:, :], wsq[:, kc, (c % 2) * 128:(c % 2 + 1) * 128], nT.ap(H(kc, half)), kc == 0, kc == KC - 1,
                           [wsqc, nT.c(H(kc, half))], [pcell[pb]])
                    yield
                    rope(pb, ropeS, half, 64, sqT.ap((c % 2) * 2 + half), sqT.c((c % 2) * 2 + half), t1, t2)
                    pp.put(pb)
                    yield
            return g

        def block(c, half, nb):
            def g():
                n = half * 4 + nb
                sq_ = sqT.ap((c % 2) * 2 + half)
                sqc = sqT.c((c % 2) * 2 + half)
                it = bp.get()
                while it is None:
                    yield
                    it = bp.get()
                if (c, half) not in po_bank:
                    po = hp.get()
                    while po is None:
                        yield
                        po = hp.get()
                    po_bank[(c, half)] = po
                po = po_bank[(c, half)]
                ps_ = pp.get()
                while ps_ is None:
                    yield
                    ps_ = pp.get()
                ps3 = psum[ps_][:, :].rearrange("p (i k) -> p i k", i=2)
                for i, skx in enumerate((skA, skB)):
                    MM(ps3[:, i, :], sq_[:, nb * 128:(nb + 1) * 128], skx.flat[:, n * 128:n * 128 + 256], True, True,
                       [sqc, skx.c(n), skx.c(n + 1)], [pcell[ps_]])
                yield
                msk = (mask0.ap(0) if n == 0 else maskg)
                ss3 = Ss.ap(it).rearrange("p (i k) -> p i k", i=2)
                sv_ = st.ap(it)
                V("vector", "scalar_tensor_tensor", [pcell[ps_], mask0.c(0), mixc.c(0)], [Ss.c(it)], out=ss3, in0=ps3, scalar=0.125,
                  in1=msk.unsqueeze(1).to_broadcast([128, 2, 256]), op0=ALU.mult, op1=ALU.add)
                pp.put(ps_)
                V("vector", "tensor_reduce", [Ss.c(it)], [st.c(it)], out=sv_[:, 0:2], in_=ss3, axis=AX.X, op=ALU.max, negate=True)
                V("vector", "tensor_tensor", [st.c(it), nsk.c(0)], [st.c(it)], out=sv_[:, 4:6], in0=sv_[:, 0:2], in1=nsk.ap(0)[:, 2 * c:2 * c + 2], op=ALU.min)
                V("vector", "tensor_tensor", [st.c(it), mixc.c(0)], [st.c(it)], out=sv_[:, 8:10], in0=sinks[:, 2 * c:2 * c + 2], in1=sv_[:, 4:6], op=ALU.add)
                yield
                pb3 = Pb.ap(it).rearrange("p (i k) -> p i k", i=2)
                for i in range(2):
                    V("scalar", "activation", [Ss.c(it), st.c(it)], [Pb.c(it), st.c(it)], out=pb3[:, i, :], in_=ss3[:, i, :], func=AF.Exp,
                      bias=sv_[:, 4 + i:5 + i], scale=1.0, accum_out=sv_[:, 6 + i:7 + i])
                V("scalar", "activation", [st.c(it)], [st.c(it)], out=sv_[:, 8:10], in_=sv_[:, 8:10], func=AF.Exp)
                yield
                V("vector", "tensor_tensor", [st.c(it)], [st.c(it)], out=sv_[:, 10:12], in0=sv_[:, 6:8], in1=sv_[:, 8:10], op=ALU.add)
                V("vector", "reciprocal", [st.c(it)], [st.c(it)], out=sv_[:, 12:14], in_=sv_[:, 10:12])
                pn3 = Pn.ap(it).rearrange("p (i k) -> p i k", i=2)
                for i in range(2):
                    V("vector", "tensor_scalar", [Pb.c(it), st.c(it)], [Pn.c(it)], out=pn3[:, i, :], in0=pb3[:, i, :], scalar1=sv_[:, 12 + i:13 + i],
                      scalar2=None, op0=ALU.mult)
                yield
                pt_ = pp.get()
                while pt_ is None:
                    yield
                    pt_ = pp.get()
                ptv = psum[pt_][:, :].bitcast(BF16)[:, 0:512]
                for i in range(2):
                    for kt in range(2):
                        S.add("tensor", "transpose", dict(out=ptv[:, (i * 2 + kt) * 128:(i * 2 + kt + 1) * 128],
                                                           in_=pn3[:, i, kt * 128:(kt + 1) * 128], identity=identb.ap(0)),
                              reads=[Pn.c(it), identb.c(0)], writes=[pcell[pt_]])
                yield
                V("scalar", "activation", [pcell[pt_]], [PT.c(it)], out=PT.ap(it), in_=ptv, func=AF.Copy)
                pp.put(pt_)
                yield
                for i in range(2):
                    for kt in range(2):
                        MM(psum[po][64 * i:64 * i + 64, nb * 128:(nb + 1) * 128], svt.ap(n + kt)[:, 64 * i:64 * i + 64],
                           PT.ap(it)[:, (i * 2 + kt) * 128:(i * 2 + kt + 1) * 128], kt == 0, kt == 1,
                           [svt.c(n + kt), PT.c(it)], [pcell[po]])
                bp.put(it)
            return g

        def finish(c, half):
            def g():
                po = po_bank[(c, half)]
                V("scalar", "activation", [pcell[po]], [aT.c((8 + c) * 2 + half)], out=aT.ap((8 + c) * 2 + half), in_=psum[po][:, :], func=AF.Copy)
                hp.put(po)
                yield
            return g

        tasks = []
        prep_id, fin_ids = {}, {}
        for c in range(8):
            deps = [prep_id[c - 1]] if c >= 1 else []
            if c >= 2:
                deps += fin_ids[c - 2]
            prep_id[c] = len(tasks)
            tasks.append((prep(c), deps))
            fin_ids[c] = []
            for half in range(2):
                bl = []
                for nb in range(4):
                    bl.append(len(tasks))
                    tasks.append((block(c, half, nb), [prep_id[c]]))
                fin_ids[c].append(len(tasks))
                tasks.append((finish(c, half), bl))
        run_tasks(tasks, 4)

        receive(list(range(8)))
        W3 = Arena(S, arena_t, ARENA_BYTES)
        W3.top = p2_top
        Ra = Arena(S, arena_t, ARENA_BYTES)
        Ra.top = ropeS.off
        Rb = Arena(S, arena_t, ARENA_BYTES)
        Rb.top = skA.off
        hb = []
        for k_ in range(2):
            A1 = W3 if k_ == 0 else Ra
            A2 = W3 if k_ == 0 else Rb
            d_ = dict(qT=A1.alloc(f"qT{k_}", 2, TH, BF16), qxi=A1.alloc(f"qxi{k_}", 2, TH, BF16), q32=A1.alloc(f"q32{k_}", 1, TH, F32),
                      rstd=A1.alloc(f"rstdg{k_}", 1, TH, F32), sgt=A2.alloc(f"sgt{k_}", 2, TH, F32), sTm=A2.alloc(f"sTm{k_}", 2, TH, BF16),
                      SinS=W3.alloc(f"SinS{k_}", 8, 128, BF16))
            hb.append(d_)
        assert Ra.top <= ropeS.off + ropeS.nbytes and Rb.top <= svt.off + svt.nbytes
        r2 = t1
        mean = t2
        pp = Pool(range(6))
        hbp = Pool(range(2))

        def getbank():
            b_ = pp.get()
            while b_ is None:
                yield None
                b_ = pp.get()
            yield b_

        def head(h):
            def g():
                k_ = hbp.get()
                while k_ is None:
                    yield
                    k_ = hbp.get()
                B_ = hb[k_]
                qT, qxi, q32, rstd, sgt, sTm, SinS = B_["qT"], B_["qxi"], B_["q32"], B_["rstd"], B_["sgt"], B_["sTm"], B_["SinS"]
                r32 = q32
                wq_, wqc = wblock(13 + h)
                for n in range(8):
                    V("scalar", "mul", [Sin32.c(h)], [SinS.c(n)], out=SinS.ap(n), in_=Sin32.ap(h), mul=float(GAMMA[h] ** (128 * n)))
                for half in range(2):
                    pb = None
                    while pb is None:
                        pb = pp.get()
                        if pb is None:
                            yield
                    for kc in range(KC):
                        MM(psum[pb][:, :], wq_[:, kc, 0:128], nT.ap(H(kc, half)), kc == 0, kc == KC - 1, [wqc, nT.c(H(kc, half))], [pcell[pb]])
                    yield
                    rope(pb, ropeR, half, 128, q32.ap(0), q32.c(0), t1, t2)
                    pp.put(pb)
                    V("scalar", "activation", [q32.c(0)], [qT.c(half)], out=qT.ap(half), in_=q32.ap(0), func=AF.Copy)
                    V("vector", "tensor_tensor", [q32.c(0), mixc.c(0)], [qxi.c(half)], out=qxi.ap(half).rearrange("p (n c) -> p n c", n=4),
                      in0=q32.ap(0).rearrange("p (n c) -> p n c", n=4), in1=xi[:, h, :].unsqueeze(1).to_broadcast([128, 4, 128]), op=ALU.mult)
                    yield
                    pg = None
                    while pg is None:
                        pg = pp.get()
                        if pg is None:
                            yield
                    for kc in range(KC):
                        MM(psum[pg][:, :], wq_[:, kc, 128:256], nT.ap(H(kc, half)), kc == 0, kc == KC - 1, [wqc, nT.c(H(kc, half))], [pcell[pg]])
                    yield
                    V("scalar", "activation", [pcell[pg]], [sgt.c(half)], out=sgt.ap(half), in_=psum[pg][:, :], func=AF.Silu)
                    pp.put(pg)
                    yield
                for half in range(2):
                    pss = None
                    while pss is None:
                        pss = pp.get()
                        if pss is None:
                            yield
                    for nb in range(4):
                        MM(psum[pss][:, nb * 128:(nb + 1) * 128], kT.ap(h * 2 + half)[:, nb * 128:(nb + 1) * 128], qT.ap(half)[:, nb * 128:(nb + 1) * 128],
                           True, True, [kT.c(h * 2 + half), qT.c(half)], [pcell[pss]])
                    yield
                    V("vector", "tensor_tensor", [pcell[pss], mixc.c(0)], [sTm.c(half)], out=sTm.ap(half).rearrange("p (n c) -> p n c", n=4),
                      in0=psum[pss][:, :].rearrange("p (n c) -> p n c", n=4), in1=dmat[:, h, :].unsqueeze(1).to_broadcast([128, 4, 128]), op=ALU.mult)
                    pp.put(pss)
                    yield
                    pr = None
                    while pr is None:
                        pr = pp.get()
                        if pr is None:
                            yield
                    for nb in range(4):
                        n = half * 4 + nb
                        oap = psum[pr][:, nb * 128:(nb + 1) * 128]
                        qx = qxi.ap(half)[:, nb * 128:(nb + 1) * 128]
                        MM(oap, vtok.ap(n * 8 + h), sTm.ap(half)[:, nb * 128:(nb + 1) * 128], True, False,
                           [vtok.c(n * 8 + h), sTm.c(half)], [pcell[pr]])
                        if n > 0:
                            MM(oap, Sloc.ap(h * 8 + n), qx, False, False, [Sloc.c(h * 8 + n), qxi.c(half)], [pcell[pr]])
                        MM(oap, SinS.ap(n), qx, False, True, [SinS.c(n), qxi.c(half)], [pcell[pr]])
                    yield
                    p1 = None
                    while p1 is None:
                        p1 = pp.get()
                        if p1 is None:
                            yield
                    p2 = None
                    while p2 is None:
                        p2 = pp.get()
                        if p2 is None:
                            yield
                    V("scalar", "activation", [pcell[pr]], [r32.c(0)], out=r32.ap(0), in_=psum[pr][:, :], func=AF.Copy)
                    V("scalar", "activation", [pcell[pr]], [r2.c(0)], out=r2.ap(0), in_=psum[pr][:, :], func=AF.Square)
                    pp.put(pr)
                    MM(psum[p1][:, :], ones_f.ap(0), r32.ap(0), True, True, [ones_f.c(0), r32.c(0)], [pcell[p1]])
                    MM(psum[p2][:, :], ones_f.ap(0), r2.ap(0), True, True, [ones_f.c(0), r2.c(0)], [pcell[p2]])
                    V("vector", "tensor_scalar", [pcell[p1]], [mean.c(0)], out=mean.ap(0), in0=psum[p1][:, :], scalar1=1.0 / 128, scalar2=None, op0=ALU.mult)
                    V("vector", "tensor_tensor", [mean.c(0)], [rstd.c(0)], out=rstd.ap(0), in0=mean.ap(0), in1=mean.ap(0), op=ALU.mult)
                    V("vector", "scalar_tensor_tensor", [pcell[p2], rstd.c(0)], [rstd.c(0)], out=rstd.ap(0), in0=psum[p2][:, :], scalar=1.0 / 128,
                      in1=rstd.ap(0), op0=ALU.mult, op1=ALU.subtract)
                    pp.put(p1)
                    pp.put(p2)
                    V("scalar", "activation", [rstd.c(0), epsc.c(0)], [rstd.c(0)], out=rstd.ap(0), in_=rstd.ap(0), func=AF.Sqrt, bias=epsc.ap(0)[:, 0:1], scale=1.0)
                    V("vector", "reciprocal", [rstd.c(0)], [rstd.c(0)], out=rstd.ap(0), in_=rstd.ap(0))
                    V("vector", "tensor_tensor", [r32.c(0), mean.c(0)], [r32.c(0)], out=r32.ap(0), in0=r32.ap(0), in1=mean.ap(0), op=ALU.subtract)
                    V("vector", "tensor_tensor", [r32.c(0), rstd.c(0)], [r32.c(0)], out=r32.ap(0), in0=r32.ap(0), in1=rstd.ap(0), op=ALU.mult)
                    V("vector", "scalar_tensor_tensor", [r32.c(0), mixc.c(0), sgt.c(half)], [aT.c(h * 2 + half)], out=aT.ap(h * 2 + half), in0=r32.ap(0),
                      scalar=gng[:, h:h + 1], in1=sgt.ap(half), op0=ALU.mult, op1=ALU.mult)
                    yield
                hbp.put(k_)
            return g

        run_tasks([(head(h), []) for h in range(8)], 2)

        if DEBUG_A:
            for kc in range(KC):
                for half in range(2):
                    V("vector", "tensor_copy", [aT.c(H(kc, half))], [hT.c(H(kc, half))], out=hT.ap(H(kc, half)), in_=aT.ap(H(kc, half)))
        else:
            proj_residual(w_out, aT, reload=hspill)

    def xattn():
        X = Arena(S, arena_t, ARENA_BYTES)
        X.top = phase_base
        qo = X.alloc("qo", 32, TH, BF16)
        kmT = X.alloc("kmT", 16, 256, BF16)
        vm = X.alloc("vm", 2 * 8, 256, BF16)
        mnT = X.alloc("mnT", 16, 256, BF16)
        xtop = X.top
        m32 = X.alloc("m32", 16, 256, F32)
        msq = X.alloc("msq", 2, 256, BF16)
        mrs = X.alloc("mrs", 1, 256, F32)
        if "mixer" not in stages:
            S.add("gpsimd", "dma_start", dict(out=identb.ap(0), in_=ident_d2), writes=[identb.c(0)], dma=True)
        for kc in range(KC):
            LD(m32.ap(kc), memT[kc * 128:(kc + 1) * 128, :], [m32.c(kc)])
        pb = next_ps()
        for kc in range(KC):
            s_ = kc % 2
            V("vector", "tensor_tensor", [m32.c(kc)], [msq.c(s_)], out=msq.ap(s_), in0=m32.ap(kc), in1=m32.ap(kc), op=ALU.mult)
            MM(psum[pb][:, 0:256], ones_b.ap(0), msq.ap(s_), kc == 0, kc == KC - 1, [ones_b.c(0), msq.c(s_)], [pcell[pb]])
        V("vector", "tensor_scalar", [pcell[pb]], [mrs.c(0)], out=mrs.ap(0), in0=psum[pb][:, 0:256], scalar1=1.0 / D, scalar2=EPS, op0=ALU.mult, op1=ALU.add)
        V("scalar", "activation", [mrs.c(0)], [mrs.c(0)], out=mrs.ap(0), in_=mrs.ap(0), func=AF.Sqrt)
        V("vector", "reciprocal", [mrs.c(0)], [mrs.c(0)], out=mrs.ap(0), in_=mrs.ap(0))
        for kc in range(KC):
            V("vector", "scalar_tensor_tensor", [m32.c(kc), gn.c(0), mrs.c(0)], [mnT.c(kc)], out=mnT.ap(kc), in0=m32.ap(kc),
              scalar=gn.ap(0)[:, 3 * KC + kc:3 * KC + kc + 1], in1=mrs.ap(0), op0=ALU.mult, op1=ALU.mult)
        for blk in range(8):
            wt, ci = next_wcell()
            wv = wt.v[:, ci, :].rearrange("p (k c) -> p k c", k=KC)
            wdma(wv, wkv_d[:, blk * 256:(blk + 1) * 256], KC, [wt.c(ci)])
            for j in range(2):
                pb = next_ps()
                for kc in range(KC):
                    MM(psum[pb][:, 0:256], wv[:, kc, j * 128:(j + 1) * 128], mnT.ap(kc), kc == 0, kc == KC - 1, [wt.c(ci), mnT.c(kc)], [pcell[pb]])
                V("scalar", "activation", [pcell[pb]], [kmT.c(blk * 2 + j)], out=kmT.ap(blk * 2 + j), in_=psum[pb][:, 0:256], func=AF.Copy)
        for blk in range(8):
            wt, ci = next_wcell()
            wv = wt.v[:, ci, :].rearrange("p (k c) -> p k c", k=KC)
            wdma(wv, wkv_d[:, D + blk * 256:D + (blk + 1) * 256], KC, [wt.c(ci)])
            for mt in range(2):
                pb = next_ps()
                for kc in range(KC):
                    MM(psum[pb][:, 0:256], mnT.ap(kc)[:, mt * 128:(mt + 1) * 128], wv[:, kc, :], kc == 0, kc == KC - 1, [wt.c(ci), mnT.c(kc)], [pcell[pb]])
                V("scalar", "activation", [pcell[pb]], [vm.c(mt * 8 + blk)], out=vm.ap(mt * 8 + blk), in_=psum[pb][:, 0:256], func=AF.Copy)
        rmsnorm(2, phase_base)
        for blk in range(8):
            wt, ci = next_wcell()
            wv = wt.v[:, ci, :].rearrange("p (k c) -> p k c", k=KC)
            wdma(wv, wq_d[:, blk * 256:(blk + 1) * 256], KC, [wt.c(ci)])
            for j in range(2):
                for half in range(2):
                    pb = next_ps()
                    for kc in range(KC):
                        MM(psum[pb][:, :], wv[:, kc, j * 128:(j + 1) * 128], nT.ap(H(kc, half)), kc == 0, kc == KC - 1, [wt.c(ci), nT.c(H(kc, half))], [pcell[pb]])
                    qc = H(blk * 2 + j, half)
                    V("scalar", "activation", [pcell[pb]], [qo.c(qc)], out=qo.ap(qc), in_=psum[pb][:, :], func=AF.Copy)
        X2 = Arena(S, arena_t, ARENA_BYTES)
        X2.top = xtop
        NX = 3
        Px = X2.alloc("Px", NX, 256, BF16)
        Pnx = X2.alloc("Pnx", NX, 256, BF16)
        stx = X2.alloc("stx", NX, 8, F32)
        PTx = X2.alloc("PTx", 2, 1024, BF16)
        SC = float(512 ** -0.5)
        pp = Pool(range(6))
        xbp = Pool(range(NX))
        ptp = Pool(range(2))
        pt_slot = {}

        def xtile(hd, half, tq):
            def g():
                it = xbp.get()
                while it is None:
                    yield
                    it = xbp.get()
                if (hd, half) not in pt_slot:
                    sl = ptp.get()
                    while sl is None:
                        yield
                        sl = ptp.get()
                    pt_slot[(hd, half)] = sl
                sl = pt_slot[(hd, half)]
                ptx3 = PTx.ap(sl).rearrange("p (m q) -> p m q", m=2)
                ps_ = pp.get()
                while ps_ is None:
                    yield
                    ps_ = pp.get()
                for cc in range(4):
                    MM(psum[ps_][:, 0:256], qo.ap(H(hd * 4 + cc, half))[:, tq * 128:(tq + 1) * 128], kmT.ap(hd * 4 + cc), cc == 0, cc == 3,
                       [qo.c(H(hd * 4 + cc, half)), kmT.c(hd * 4 + cc)], [pcell[ps_]])
                yield
                sv_ = stx.ap(it)
                V("vector", "tensor_reduce", [pcell[ps_]], [stx.c(it)], out=sv_[:, 0:1], in_=psum[ps_][:, 0:256], axis=AX.X, op=ALU.max)
                V("vector", "tensor_scalar", [stx.c(it)], [stx.c(it)], out=sv_[:, 1:2], in0=sv_[:, 0:1], scalar1=-SC, scalar2=None, op0=ALU.mult)
                yield
                V("scalar", "activation", [pcell[ps_], stx.c(it)], [Px.c(it), stx.c(it)], out=Px.ap(it), in_=psum[ps_][:, 0:256], func=AF.Exp,
                  bias=sv_[:, 1:2], scale=SC, accum_out=sv_[:, 2:3])
                pp.put(ps_)
                yield
                V("vector", "reciprocal", [stx.c(it)], [stx.c(it)], out=sv_[:, 3:4], in_=sv_[:, 2:3])
                V("vector", "tensor_scalar", [Px.c(it), stx.c(it)], [Pnx.c(it)], out=Pnx.ap(it), in0=Px.ap(it), scalar1=sv_[:, 3:4], scalar2=None, op0=ALU.mult)
                yield
                pt_ = pp.get()
                while pt_ is None:
                    yield
                    pt_ = pp.get()
                ptv = psum[pt_][:, :].bitcast(BF16)[:, 0:256]
                for mt in range(2):
                    S.add("tensor", "transpose", dict(out=ptv[:, mt * 128:(mt + 1) * 128], in_=Pnx.ap(it)[:, mt * 128:(mt + 1) * 128], identity=identb.ap(0)),
                          reads=[Pnx.c(it), identb.c(0)], writes=[pcell[pt_]])
                yield
                V("scalar", "activation", [pcell[pt_]], [PTx.c(sl)], out=ptx3[:, :, tq * 128:(tq + 1) * 128],
                  in_=ptv.rearrange("p (m q) -> p m q", m=2), func=AF.Copy)
                pp.put(pt_)
                xbp.put(it)
            return g

        def xfin(hd, half):
            def g():
                sl = pt_slot[(hd, half)]
                ptx3 = PTx.ap(sl).rearrange("p (m q) -> p m q", m=2)
                for cc in range(4):
                    dchunk = hd * 4 + cc
                    po = pp.get()
                    while po is None:
                        yield
                        po = pp.get()
                    for mt in range(2):
                        MM(psum[po][:, :], vm.ap(mt * 8 + dchunk // 2)[:, (dchunk % 2) * 128:(dchunk % 2 + 1) * 128], ptx3[:, mt, :], mt == 0, mt == 1,
                           [vm.c(mt * 8 + dchunk // 2), PTx.c(sl)], [pcell[po]])
                    yield
                    V("scalar", "activation", [pcell[po]], [qo.c(H(dchunk, half))], out=qo.ap(H(dchunk, half)), in_=psum[po][:, :], func=AF.Copy)
                    pp.put(po)
                ptp.put(sl)
            return g

        xt = []
        for hd in range(4):
            for half in range(2):
                ids = []
                for tq in range(4):
                    ids.append(len(xt))
                    xt.append((xtile(hd, half, tq), []))
                xt.append((xfin(hd, half), ids))
        run_tasks(xt, 4)
        proj_residual(wo_d, qo)

    hsp_cell = [S.cell(f"hsp{i}") for i in range(32)]
    xs_cell = S.cell("xsend")
    xr_cell = S.cell("xrecv")
    xs1_cell = S.cell("xsend1")
    xr1_cell = S.cell("xrecv1")
    if "ffn1" in stages:
        rmsnorm(0, phase_base)
        ffn(wg1, wu1, wd1, phase_base)
    if "mixer" in stages:
        mixer()
    if "xattn" in stages:
        xattn()
    if "ffn2" in stages:
        rmsnorm(4, phase_base)
        ffn(wg2, wu2, wd2, phase_base)
    if "final" in stages:
        final = rmsnorm(5, phase_base, to_out=True)
    else:
        final = []
        for kc in range(KC):
            for half in range(2):
                final.append(LD(outT[kc * 128:(kc + 1) * 128, half * TH:(half + 1) * TH], hT.ap(H(kc, half)), [], reads=[hT.c(H(kc, half))]))
    S.add("sync", "nop", dict(), after=final)
    S.emit(nc, es)
    es.close()
    return nc


_CACHE = {}


def _gain_cols(v):
    return np.ascontiguousarray(np.asarray(v, np.float32).reshape(KC, 128).T)


def _rope_table(d, pos):
    f32 = np.float32
    inv = (f32(10000.0) ** (-np.arange(0, d, 2, dtype=f32) / f32(d))).astype(f32)
    ang = (pos.astype(f32)[None, :] * inv[:, None]).astype(f32)
    cos = np.cos(ang.astype(np.float64)).astype(f32)
    sin = np.sin(ang.astype(np.float64)).astype(f32)
    p = np.arange(128)
    pp = p % d
    fi = pp % (d // 2)
    sign = np.where(pp < d // 2, -1.0, 1.0).astype(f32)
    return np.ascontiguousarray(np.concatenate([cos[fi], sin[fi] * sign[:, None]], axis=1), f32)


def _mix_consts(ret_gn_gain, swa_sinks):
    g = np.array(GAMMA, np.float64)
    j = np.arange(128)[:, None]
    c = np.arange(128)[None, :]
    m = np.zeros((128, MIXC_W), np.float32)
    sc = 128.0 ** -0.5
    for h in range(8):
        dm = np.where(c >= j, sc * g[h] ** np.maximum(c - j, 0), 0.0)
        m[:, MC_DMAT + h * 128:MC_DMAT + (h + 1) * 128] = dm
        m[:, MC_XI + h * 128:MC_XI + (h + 1) * 128] = (g[h] ** (np.arange(128) + 1))[None, :]
        m[:, MC_ZETA + h] = sc * g[h] ** (127 - np.arange(128))
    m[:, MC_GNG:MC_GNG + 8] = np.asarray(ret_gn_gain, np.float32).reshape(8, 128).T
    sk = np.asarray(swa_sinks, np.float32).reshape(16)
    order = [cc + 8 * i for cc in range(8) for i in range(2)]
    m[:, MC_SINK:MC_SINK + 16] = sk[order][None, :]
    i = np.arange(128)[:, None]
    kk = np.arange(256)[None, :]
    m[:, MC_MASK:MC_MASK + 256] = np.where((kk > i) & (kk <= i + 128), 0.0, NEG)
    return m


def _w_in_perm():
    cols = []
    for h in range(8):
        cols += list(range(1024 + h * 128, 1024 + (h + 1) * 128))
    for h in range(8):
        cols += list(range(2048 + h * 128, 2048 + (h + 1) * 128))
    cols += list(range(5120, 5376))
    for c in range(8):
        cols += list(range(4096 + c * 64, 4096 + (c + 1) * 64)) + list(range(4096 + (c + 8) * 64, 4096 + (c + 9) * 64))
    for h in range(8):
        cols += list(range(h * 128, (h + 1) * 128)) + list(range(3072 + h * 128, 3072 + (h + 1) * 128))
    return np.array(cols)


def _w_out_perm():
    rows = list(range(1024))
    for c in range(8):
        rows += list(range(1024 + c * 64, 1024 + (c + 1) * 64)) + list(range(1024 + (c + 8) * 64, 1024 + (c + 9) * 64))
    return np.array(rows)


def kernel(x, mem, ffn1_norm, ffn1_w_gate, ffn1_w_up, ffn1_w_down, mix_norm, w_in, ret_gn_gain,
           swa_sinks, w_out, xa_norm, mem_norm, xa_wq, xa_wkv, xa_wo, ffn2_norm, ffn2_w_gate,
           ffn2_w_up, ffn2_w_down, final_norm):
    st = tuple(STAGES)
    x = np.asarray(x, np.float32)
    mem = np.asarray(mem, np.float32)
    key = ("nc", st)
    if key not in _CACHE:
        _CACHE[key] = build_program(st)
    nc = _CACHE[key]
    f = lambda a: np.ascontiguousarray(np.asarray(a, np.float32)[0])
    gains = np.concatenate([_gain_cols(np.asarray(g).reshape(-1)) for g in
                            (ffn1_norm, mix_norm, xa_norm, mem_norm, ffn2_norm, final_norm)], axis=1)
    shared = {"gains": np.ascontiguousarray(gains, np.float32)}
    if "ffn1" in st:
        shared.update(ffn1_w_gate=f(ffn1_w_gate), ffn1_w_up=f(ffn1_w_up), ffn1_w_down=f(ffn1_w_down))
    if "ffn2" in st:
        shared.update(ffn2_w_gate=f(ffn2_w_gate), ffn2_w_up=f(ffn2_w_up), ffn2_w_down=f(ffn2_w_down))
    if "mixer" in st:
        shared["w_in_p"] = np.ascontiguousarray(f(w_in)[:, _w_in_perm()])
        shared["w_out_p"] = np.ascontiguousarray(f(w_out)[_w_out_perm(), :])
        shared["mixc"] = _mix_consts(np.asarray(ret_gn_gain).reshape(-1), np.asarray(swa_sinks).reshape(-1))
    if "mixer" in st or "xattn" in st:
        shared["ident"] = np.eye(128, dtype=np.float32)
    if "xattn" in st:
        shared.update(xa_wq=f(xa_wq), xa_wkv=f(xa_wkv), xa_wo=f(xa_wo))
    in_maps = []
    g64 = np.array(GAMMA, np.float64)
    for c in range(NCORES):
        b, q = c // 4, c % 4
        m = dict(shared)
        m["xT"] = np.ascontiguousarray(x[b, q * T:(q + 1) * T, :].T)
        if "mixer" in st:
            pos = np.arange(q * T, (q + 1) * T)
            m["ropeR"] = _rope_table(128, pos)
            m["ropeS"] = _rope_table(64, pos)
            cf = np.zeros((128, 40), np.float32)
            for r in range(4):
                if r < q:
                    cf[:, r * 8:(r + 1) * 8] = (g64 ** (1024 * (q - 1 - r)))[None, :]
                if r == q - 1:
                    cf[:, 32 + r] = 1.0
            m["coef"] = cf
            mk = shared["mixc"][:, MC_MASK:MC_MASK + 256].copy()
            if q == 0:
                mk[:, 0:128] = NEG
            m["mask0"] = np.ascontiguousarray(mk)
        if "xattn" in st:
            m["memT"] = np.ascontiguousarray(mem[b].T)
        in_maps.append(m)
    res = run_bass_kernel_spmd(nc, in_maps, core_ids=list(range(NCORES)))
    out = np.empty((2, 4096, D), np.float32)
    for c in range(NCORES):
        b, q = c // 4, c % 4
        out[b, q * T:(q + 1) * T, :] = res.results[c]["outT"].T
    return out
```

```python
import numpy as np
from contextlib import ExitStack
import concourse.bass as bass
import concourse.mybir as mybir
from concourse.bass_utils import run_bass_kernel_spmd

F32 = mybir.dt.float32
BF16 = mybir.dt.bfloat16
ALU = mybir.AluOpType
AF = mybir.ActivationFunctionType
AX = mybir.AxisListType

D = 2048
KC = 16
T = 1024
TH = 512
DFF = 5632
FC = 44
EPS = 1e-6
NCORES = 8

STAGES = ("ffn1", "mixer", "xattn", "ffn2", "final")
GAMMA = [1.0 - 2.0 ** (-5 - h) for h in range(8)]
MC_DMAT, MC_XI, MC_ZETA, MC_GNG, MC_SINK, MC_MASK = 0, 1024, 2048, 2056, 2064, 2080
MIXC_W = 2336
NEG = -30000.0
DEBUG_A = False


class Cell:
    __slots__ = ("name", "space", "off", "size", "last_w", "readers", "ov")

    def __init__(self, name, space, off=0, size=0):
        self.name, self.space, self.off, self.size = name, space, off, size
        self.last_w = []
        self.readers = {}
        self.ov = []


class Op:
    __slots__ = ("eng", "fn", "deps", "pos", "signal", "tick", "is_dma", "sem", "val", "waits", "gidx", "group", "own")

    def __init__(self, eng, fn, is_dma):
        self.eng, self.fn, self.is_dma = eng, fn, is_dma
        self.deps = []
        self.signal = False
        self.tick = None
        self.sem = None
        self.val = None
        self.waits = []


class Sched:
    ENGS = ["sync", "gpsimd", "scalar", "vector", "tensor"]
    NDQ = 8

    def __init__(self):
        self.ops = []
        self.sb_cells = []
        self.count = {e: 0 for e in self.ENGS}

    def sb_cell(self, name, off, size):
        c = Cell(name, "sb", off, size)
        for o in self.sb_cells:
            if o.off < off + size and off < o.off + o.size:
                o.ov.append(c)
                c.ov.append(o)
        self.sb_cells.append(c)
        return c

    def cell(self, name):
        return Cell(name, "x")

    def add(self, eng, meth, kw, reads=(), writes=(), dma=False, after=(), group=None, own_sem=False):
        op = Op(eng, (meth, kw), dma)
        op.group = group
        op.own = own_sem
        op.pos = self.count[eng]
        self.count[eng] += 1
        op.gidx = len(self.ops)
        deps = {}

        def dep(o):
            if o is not None and o is not op and not (group is not None and o.group == group):
                deps[id(o)] = o

        for o in after:
            dep(o)
        for c in reads:
            for y in [c] + c.ov:
                for w in y.last_w:
                    dep(w)
        for c in writes:
            for y in [c] + c.ov:
                for w in y.last_w:
                    dep(w)
                for r in y.readers.values():
                    dep(r)
        for c in reads:
            for y in [c] + c.ov:
                key = ("d", op.gidx) if dma else eng
                y.readers[key] = op
        for c in writes:
            for y in [c] + c.ov:
                if group is not None and y.last_w and y.last_w[0].group == group:
                    y.last_w.append(op)
                else:
                    y.last_w = [op]
                y.readers = {}
        best = {}
        for o in deps.values():
            if o.is_dma or o.own:
                best[("d", id(o))] = o
            else:
                if eng == "tensor" and o.eng == "tensor":
                    continue
                k = o.eng
                if k not in best or best[k].pos < o.pos:
                    best[k] = o
        op.deps = list(best.values())
        for o in op.deps:
            o.signal = True
        self.ops.append(op)
        return op

    def emit(self, nc, es):
        sems = {e: es.enter_context(nc.semaphore("c_" + e)) for e in self.ENGS}
        dq = {e: [es.enter_context(nc.semaphore(f"dq_{e}_{i}")) for i in range(self.NDQ)]
              for e in ("sync", "gpsimd", "scalar")}
        streams = {e: [] for e in self.ENGS}
        ticks = {e: 0 for e in self.ENGS}
        ndma = {e: 0 for e in self.ENGS}
        for op in self.ops:
            streams[op.eng].append(op)
            if op.is_dma:
                i = ndma[op.eng]
                ndma[op.eng] += 1
                op.sem = dq[op.eng][i % self.NDQ]
                op.val = 16 * (i // self.NDQ + 1)
                op.signal = True
            elif op.own:
                op.sem = es.enter_context(nc.semaphore(f"own_{op.gidx}"))
                op.val = 1
                op.signal = True
            elif op.signal:
                ticks[op.eng] += 1
                op.sem = sems[op.eng]
                op.val = ticks[op.eng]
        for e in self.ENGS:
            seen = {}
            for op in streams[e]:
                w = []
                if op.is_dma and op.val > 16:
                    w.append((op.sem, op.val - 16))
                for d in op.deps:
                    w.append((d.sem, d.val))
                for (s, v) in w:
                    if seen.get(id(s), 0) >= v:
                        continue
                    seen[id(s)] = v
                    op.waits.append((s, v))
        block = es.enter_context(nc.Block())

        def run(e, stream):
            for op in stream:
                for (s, v) in op.waits:
                    e.wait_ge(s, v)
                ins = getattr(e, op.fn[0])(**op.fn[1])
                if op.signal:
                    ins.then_inc(op.sem, 16 if op.is_dma else 1)

        @block.sync
        def _(e):
            run(e, streams["sync"])

        @block.gpsimd
        def _(e):
            run(e, streams["gpsimd"])

        @block.scalar
        def _(e):
            run(e, streams["scalar"])

        @block.vector
        def _(e):
            run(e, streams["vector"])

        @block.tensor
        def _(e):
            run(e, streams["tensor"])


class SbT:
    def __init__(self, S, arena, name, off, ncell, clen, dt):
        self.esz = 4 if dt == F32 else 2
        self.ncell, self.clen, self.dt = ncell, clen, dt
        self.off = off
        self.nbytes = ncell * clen * self.esz
        assert off % 4 == 0 and self.nbytes % 4 == 0
        w0, w1 = off // 4, (off + self.nbytes) // 4
        v = arena[:, w0:w1]
        if dt != F32:
            v = v.bitcast(dt)
        self.flat = v
        self.v = v.rearrange("p (a b) -> p a b", b=clen)
        self.cells = [S.sb_cell(f"{name}{i}", off + i * clen * self.esz, clen * self.esz) for i in range(ncell)]

    def ap(self, i):
        return self.v[:, i, :]

    def c(self, i):
        return self.cells[i]


class Arena:
    def __init__(self, S, arena, total):
        self.S, self.arena, self.total = S, arena, total
        self.top = 0

    def alloc(self, name, ncell, clen, dt, at=None):
        esz = 4 if dt == F32 else 2
        nb = ncell * clen * esz
        nb4 = (nb + 31) // 32 * 32
        if at is None:
            at = self.top
            self.top += nb4
        assert at + nb <= self.total, (name, at, nb, self.total)
        return SbT(self.S, self.arena, name, at, ncell, clen, dt)


ARENA_BYTES = 207 * 1024


class Pool:
    def __init__(self, items):
        self.free = list(items)

    def get(self):
        return self.free.pop(0) if self.free else None

    def put(self, x):
        self.free.append(x)


def run_tasks(tasks, width):
    done, started, active = set(), set(), []
    while len(done) < len(tasks):
        for i in range(len(tasks)):
            if len(active) >= width:
                break
            if i in started:
                continue
            if all(d in done for d in tasks[i][1]):
                started.add(i)
                active.append((i, tasks[i][0]()))
        assert active, "task deadlock"
        for (i, g) in list(active):
            try:
                next(g)
            except StopIteration:
                active.remove((i, g))
                done.add(i)


def build_program(stages=("ffn1", "mixer", "xattn", "ffn2", "final")):
    stages = set(stages)
    nc = bass.Bass("TRN2", target_bir_lowering=False)
    S = Sched()
    es = ExitStack()

    def din(name, shape, dt=F32):
        return nc.dram_tensor(name, shape, dt, kind="ExternalInput").ap()

    xT = din("xT", [D, T])
    gains = din("gains", [128, 6 * KC])
    if "ffn1" in stages:
        wg1 = din("ffn1_w_gate", [D, DFF]); wu1 = din("ffn1_w_up", [D, DFF]); wd1 = din("ffn1_w_down", [DFF, D])
    if "ffn2" in stages:
        wg2 = din("ffn2_w_gate", [D, DFF]); wu2 = din("ffn2_w_up", [D, DFF]); wd2 = din("ffn2_w_down", [DFF, D])
    if "mixer" in stages:
        w_in = din("w_in_p", [D, 42 * 128])
        w_out = din("w_out_p", [D, D])
        ropeR_d = din("ropeR", [128, 2 * T])
        ropeS_d = din("ropeS", [128, 2 * T])
        mixc_d = din("mixc", [128, MIXC_W])
        coef_d = din("coef", [128, 40])
        mask0_d = din("mask0", [128, 256])
        ident_d = din("ident", [128, 128])
        hspill = nc.dram_tensor("hspill", [D, T], F32).ap()
        send_t = nc.dram_tensor("xsend", [8 * 128, 128], F32)
        recv_t = nc.dram_tensor("xrecv", [4 * 8 * 128, 128], F32)
        send1_t = nc.dram_tensor("xsend1", [2 * 128, 128], F32)
        recv1_t = nc.dram_tensor("xrecv1", [4 * 2 * 128, 128], F32)
    if "xattn" in stages:
        memT = din("memT", [D, 256])
        wq_d = din("xa_wq", [D, D]); wkv_d = din("xa_wkv", [D, 2 * D]); wo_d = din("xa_wo", [D, D])
        ident_d2 = ident_d if "mixer" in stages else din("ident", [128, 128])
    outT = nc.dram_tensor("outT", [D, T], F32, kind="ExternalOutput").ap()

    arena_t = es.enter_context(nc.sbuf_tensor("arena", [128, ARENA_BYTES // 4], F32))
    A = Arena(S, arena_t, ARENA_BYTES)
    psum_all = es.enter_context(nc.psum_tensor("psall", [128, 8 * 512], F32))
    psum = [psum_all[:, i * 512:(i + 1) * 512] for i in range(8)]
    pcell = [S.cell(f"ps{i}") for i in range(8)]

    hT = A.alloc("hT", KC * 2, TH, F32)
    nT = A.alloc("nT", KC * 2, TH, BF16)
    wslot = [A.alloc(f"ws{i}", 2, 4096, BF16) for i in range(2)]
    gn = A.alloc("gn", 1, 6 * KC, F32)
    ones_b = A.alloc("ones_b", 1, 128, BF16)
    ones_f = A.alloc("ones_f", 1, 128, F32)
    identb = A.alloc("identb", 1, 128, BF16)
    epsc = A.alloc("epsc", 1, 8, F32)
    phase_base = A.top

    psi = [0]

    def next_ps():
        i = psi[0] % 6
        psi[0] += 1
        return i

    def next_ps_pair():
        if psi[0] % 2:
            psi[0] += 1
        i = psi[0] % 6
        psi[0] += 2
        return i

    hpsi = [0]

    def hold_ps():
        i = 6 + hpsi[0] % 2
        hpsi[0] += 1
        return i

    wsi = [0]

    def next_wcell():
        i = wsi[0] % 4
        wsi[0] += 1
        return wslot[i // 2], i % 2

    def next_ws():
        if wsi[0] % 2:
            wsi[0] += 1
        i = (wsi[0] // 2) % 2
        wsi[0] += 2
        return wslot[i]

    def H(kc, half):
        return kc * 2 + half

    gid = [0]

    def wdma(dst3, src2, nk, cells):
        gid[0] += 1
        for k0 in range(0, nk, 4):
            k1 = min(nk, k0 + 4)
            S.add("gpsimd", "dma_start", dict(out=dst3[:, k0:k1, :],
                                               in_=src2[k0 * 128:k1 * 128, :].rearrange("(k p) c -> p k c", p=128)),
                  writes=cells, dma=True, group=gid[0])

    def V(eng, meth, reads, writes, **kw):
        return S.add(eng, meth, kw, reads=reads, writes=writes)

    def MM(out, lhsT, rhs, start, stop, reads, writes):
        return S.add("tensor", "matmul", dict(out=out, lhsT=lhsT, rhs=rhs, start=start, stop=stop), reads=reads, writes=writes)

    def LD(out, in_, writes, reads=(), eng="sync"):
        return S.add(eng, "dma_start", dict(out=out, in_=in_), reads=reads, writes=writes, dma=True)

    V("vector", "memset", [], [ones_b.c(0)], ap=ones_b.ap(0), constant=1.0)
    V("vector", "memset", [], [ones_f.c(0)], ap=ones_f.ap(0), constant=1.0)
    V("vector", "memset", [], [epsc.c(0)], ap=epsc.ap(0), constant=EPS)
    LD(gn.ap(0), gains, [gn.c(0)])
    for half in range(2):
        for kc in range(KC):
            LD(hT.ap(H(kc, half)), xT[kc * 128:(kc + 1) * 128, half * TH:(half + 1) * TH], [hT.c(H(kc, half))])

    def rmsnorm(gi, tmp_base, to_out=False):
        At = Arena(S, arena_t, ARENA_BYTES)
        At.top = tmp_base
        sq = At.alloc("sq", 4, TH, BF16)
        rstd = At.alloc("rstd", 2, TH, F32)
        ot = At.alloc("ot", 4, TH, F32) if to_out else None
        fin = []
        for half in range(2):
            pb = next_ps()
            for kc in range(KC):
                s = kc % 4
                hc = H(kc, half)
                if kc % 2 == 0:
                    V("scalar", "activation", [hT.c(hc)], [sq.c(s)], out=sq.ap(s), in_=hT.ap(hc), func=AF.Square)
                else:
                    V("vector", "tensor_tensor", [hT.c(hc)], [sq.c(s)], out=sq.ap(s), in0=hT.ap(hc), in1=hT.ap(hc), op=ALU.mult)
                MM(psum[pb][:, :], ones_b.ap(0), sq.ap(s), kc == 0, kc == KC - 1, [sq.c(s), ones_b.c(0)], [pcell[pb]])
            V("vector", "tensor_scalar", [pcell[pb]], [rstd.c(half)], out=rstd.ap(half), in0=psum[pb][:, :],
              scalar1=1.0 / D, scalar2=EPS, op0=ALU.mult, op1=ALU.add)
            V("scalar", "activation", [rstd.c(half)], [rstd.c(half)], out=rstd.ap(half), in_=rstd.ap(half), func=AF.Sqrt)
            V("vector", "reciprocal", [rstd.c(half)], [rstd.c(half)], out=rstd.ap(half), in_=rstd.ap(half))
            for kc in range(KC):
                hc = H(kc, half)
                gsc = gn.ap(0)[:, gi * KC + kc:gi * KC + kc + 1]
                if not to_out:
                    V("vector", "scalar_tensor_tensor", [hT.c(hc), gn.c(0), rstd.c(half)], [nT.c(hc)],
                      out=nT.ap(hc), in0=hT.ap(hc), scalar=gsc, in1=rstd.ap(half), op0=ALU.mult, op1=ALU.mult)
                else:
                    o = kc % 4
                    V("vector", "scalar_tensor_tensor", [hT.c(hc), gn.c(0), rstd.c(half)], [ot.c(o)],
                      out=ot.ap(o), in0=hT.ap(hc), scalar=gsc, in1=rstd.ap(half), op0=ALU.mult, op1=ALU.mult)
                    fin.append(LD(outT[kc * 128:(kc + 1) * 128, half * TH:(half + 1) * TH], ot.ap(o), [], reads=[ot.c(o)]))
        return fin

    def ffn(wg, wu, wd, tmp_base):
        At = Arena(S, arena_t, ARENA_BYTES)
        At.top = tmp_base
        act = At.alloc("act", 22 * 2, TH, BF16)
        sg = At.alloc("sg", 2, TH, F32)
        sgi = 0
        for ffh in range(2):
            for p in range(11):
                f0 = (ffh * 22 + 2 * p) * 128
                wt = next_ws()
                wv = wt.flat.rearrange("p (g k c) -> p g k c", g=2, k=KC)
                wdma(wv[:, 0], wg[:, f0:f0 + 256], KC, [wt.c(0)])
                wdma(wv[:, 1], wu[:, f0:f0 + 256], KC, [wt.c(1)])
                for j in range(2):
                    fl = 2 * p + j
                    for half in range(2):
                        pg, pu = next_ps(), next_ps()
                        for kc in range(KC):
                            MM(psum[pg][:, :], wv[:, 0, kc, j * 128:(j + 1) * 128], nT.ap(H(kc, half)), kc == 0, kc == KC - 1,
                               [wt.c(0), nT.c(H(kc, half))], [pcell[pg]])
                        for kc in range(KC):
                            MM(psum[pu][:, :], wv[:, 1, kc, j * 128:(j + 1) * 128], nT.ap(H(kc, half)), kc == 0, kc == KC - 1,
                               [wt.c(1), nT.c(H(kc, half))], [pcell[pu]])
                        si = sgi % 2
                        sgi += 1
                        V("scalar", "activation", [pcell[pg]], [sg.c(si)], out=sg.ap(si), in_=psum[pg][:, :], func=AF.Silu)
                        V("vector", "tensor_tensor", [pcell[pu], sg.c(si)], [act.c(fl * 2 + half)],
                          out=act.ap(fl * 2 + half), in0=psum[pu][:, :], in1=sg.ap(si), op=ALU.mult)
            for dblk in range(8):
                wt = next_ws()
                wv = wt.flat[:, 0:22 * 256].rearrange("p (k c) -> p k c", k=22)
                r0 = ffh * 22 * 128
                wdma(wv, wd[r0:r0 + 22 * 128, dblk * 256:(dblk + 1) * 256], 22, [wt.c(0), wt.c(1)])
                for j in range(2):
                    dc = dblk * 2 + j
                    for half in range(2):
                        pb = next_ps()
                        for fl in range(22):
                            MM(psum[pb][:, :], wv[:, fl, j * 128:(j + 1) * 128], act.ap(fl * 2 + half), fl == 0, fl == 21,
                               [wt.c(0), wt.c(1), act.c(fl * 2 + half)], [pcell[pb]])
                        V("vector", "scalar_tensor_tensor", [pcell[pb], hT.c(H(dc, half))], [hT.c(H(dc, half))],
                          out=hT.ap(H(dc, half)), in0=psum[pb][:, :], scalar=0.5, in1=hT.ap(H(dc, half)), op0=ALU.mult, op1=ALU.add)

    def proj_residual(w_d, src, reload=None):
        for dblk in range(8):
            wt, ci = next_wcell()
            wv = wt.v[:, ci, :].rearrange("p (k c) -> p k c", k=KC)
            wdma(wv, w_d[:, dblk * 256:(dblk + 1) * 256], KC, [wt.c(ci)])
            for j in range(2):
                dc = dblk * 2 + j
                for half in range(2):
                    pb = next_ps()
                    for ac in range(KC):
                        MM(psum[pb][:, :], wv[:, ac, j * 128:(j + 1) * 128], src.ap(H(ac, half)), ac == 0, ac == KC - 1,
                           [wt.c(ci), src.c(H(ac, half))], [pcell[pb]])
                    hc = H(dc, half)
                    if reload is not None:
                        LD(hT.ap(hc), reload[dc * 128:(dc + 1) * 128, half * TH:(half + 1) * TH], [hT.c(hc)])
                    V("vector", "tensor_tensor", [pcell[pb], hT.c(hc)], [hT.c(hc)],
                      out=hT.ap(hc), in0=psum[pb][:, :], in1=hT.ap(hc), op=ALU.add)

    def rope(pb, tab, half, dh, out_ap, out_cell, t1, t2):
        V("vector", "tensor_tensor", [pcell[pb], tab.c(half)], [t1.c(0)], out=t1.ap(0), in0=psum[pb][:, :], in1=tab.ap(half), op=ALU.mult)
        hd = dh // 2
        for base in range(0, 128, dh):
            for (dst, src) in ((base, base + hd), (base + hd, base)):
                V("vector", "tensor_tensor", [pcell[pb], tab.c(2 + half)], [t2.c(0)],
                  out=t2.ap(0)[dst:dst + hd, :], in0=psum[pb][src:src + hd, :], in1=tab.ap(2 + half)[dst:dst + hd, :], op=ALU.mult)
        if isinstance(out_ap, tuple):
            for (lo, oap) in ((0, out_ap[0]), (64, out_ap[1])):
                V("vector", "tensor_tensor", [t1.c(0), t2.c(0)], out_cell, out=oap[lo:lo + 64, :], in0=t1.ap(0)[lo:lo + 64, :], in1=t2.ap(0)[lo:lo + 64, :], op=ALU.add)
        else:
            V("vector", "tensor_tensor", [t1.c(0), t2.c(0)], out_cell if isinstance(out_cell, list) else [out_cell], out=out_ap, in0=t1.ap(0), in1=t2.ap(0), op=ALU.add)

    def mixer():
        rmsnorm(1, phase_base)
        for kc in range(KC):
            for half in range(2):
                S.add("sync", "dma_start", dict(out=hspill[kc * 128:(kc + 1) * 128, half * TH:(half + 1) * TH], in_=hT.ap(H(kc, half))),
                      reads=[hT.c(H(kc, half))], writes=[hsp_cell[H(kc, half)]], dma=True)
        R = Arena(S, arena_t, ARENA_BYTES)
        R.top = hT.off
        kT = R.alloc("kT", 16, TH, BF16)
        vtok = R.alloc("vtok", 64, 128, BF16)
        Sloc = R.alloc("Sloc", 64, 128, BF16)
        ropeR = R.alloc("ropeR", 4, TH, F32)
        ropeS = R.alloc("ropeS", 4, TH, F32)
        assert R.top <= hT.off + hT.nbytes
        P = Arena(S, arena_t, ARENA_BYTES)
        P.top = phase_base
        aT = P.alloc("aT", 32, TH, BF16)
        mixc = P.alloc("mixc", 1, MIXC_W, F32)
        coef = P.alloc("coef", 1, 40, F32)
        mask0 = P.alloc("mask0", 1, 256, F32)
        skA = P.alloc("skA", 9, 128, BF16)
        skB = P.alloc("skB", 9, 128, BF16)
        svt = P.alloc("svt", 9, 128, BF16)
        t1 = P.alloc("t1", 1, TH, F32)
        t2 = P.alloc("t2", 1, TH, F32)
        common_top = P.top
        mc = mixc.ap(0)
        dmat = mc[:, MC_DMAT:MC_DMAT + 1024].rearrange("p (h c) -> p h c", h=8)
        xi = mc[:, MC_XI:MC_XI + 1024].rearrange("p (h c) -> p h c", h=8)
        zeta = mc[:, MC_ZETA:MC_ZETA + 8]
        gng = mc[:, MC_GNG:MC_GNG + 8]
        sinks = mc[:, MC_SINK:MC_SINK + 16]
        maskg = mc[:, MC_MASK:MC_MASK + 256]
        V("vector", "memset", [], [skA.c(i) for i in range(9)], ap=skA.flat, constant=0.0)
        V("vector", "memset", [], [skB.c(i) for i in range(9)], ap=skB.flat, constant=0.0)
        LD(mixc.ap(0), mixc_d, [mixc.c(0)])
        LD(coef.ap(0), coef_d, [coef.c(0)])
        LD(mask0.ap(0), mask0_d, [mask0.c(0)])
        for i in range(4):
            LD(ropeR.ap(i), ropeR_d[:, i * TH:(i + 1) * TH], [ropeR.c(i)])
            LD(ropeS.ap(i), ropeS_d[:, i * TH:(i + 1) * TH], [ropeS.c(i)])
        S.add("gpsimd", "dma_start", dict(out=identb.ap(0), in_=ident_d), writes=[identb.c(0)], dma=True)

        def wblock(b):
            wt, ci = next_wcell()
            wv = wt.v[:, ci, :].rearrange("p (k c) -> p k c", k=KC)
            wdma(wv, w_in[:, b * 256:(b + 1) * 256], KC, [wt.c(ci)])
            return wv, wt.c(ci)

        def proj_fm(wv, wc, j, half):
            pb = next_ps()
            for kc in range(KC):
                MM(psum[pb][:, :], wv[:, kc, j * 128:(j + 1) * 128], nT.ap(H(kc, half)), kc == 0, kc == KC - 1,
                   [wc, nT.c(H(kc, half))], [pcell[pb]])
            return pb

        P1 = Arena(S, arena_t, ARENA_BYTES)
        P1.top = common_top
        kz = P1.alloc("kz", 2, 1024, BF16)
        Rst = P1.alloc("Rst", 2, 128, F32)
        Lst = P1.alloc("Lst", 10, 128, F32)
        wsk, wskc = wblock(8)
        for half in range(2):
            pb = proj_fm(wsk, wskc, 0, half)
            rope(pb, ropeS, half, 64, (skA.flat[:, 128 + half * TH:128 + (half + 1) * TH], skB.flat[:, 128 + half * TH:128 + (half + 1) * TH]),
                 [skA.c(1 + half * 4 + nn) for nn in range(4)] + [skB.c(1 + half * 4 + nn) for nn in range(4)], t1, t2)
        for n in range(8):
            pb = next_ps()
            half, o = n // 4, (n % 4) * 128
            for kc in range(KC):
                MM(psum[pb][:, 0:128], nT.ap(H(kc, half))[:, o:o + 128], wsk[:, kc, 128:256], kc == 0, kc == KC - 1,
                   [wskc, nT.c(H(kc, half))], [pcell[pb]])
            V("scalar", "activation", [pcell[pb]], [svt.c(1 + n)], out=svt.ap(1 + n), in_=psum[pb][:, 0:128], func=AF.Copy)
        V("vector", "tensor_copy", [skA.c(8)], [Lst.c(8)], out=Lst.ap(8)[0:64, :], in_=skA.ap(8)[0:64, :])
        V("vector", "tensor_copy", [skB.c(8)], [Lst.c(8)], out=Lst.ap(8)[64:128, :], in_=skB.ap(8)[64:128, :])
        V("vector", "tensor_copy", [svt.c(8)], [Lst.c(9)], out=Lst.ap(9), in_=svt.ap(8))
        S.add("gpsimd", "dma_start", dict(out=send1_t.ap().rearrange("(a p) e -> p a e", p=128), in_=Lst.v[:, 8:10, :]),
              reads=[Lst.c(8), Lst.c(9)], writes=[xs1_cell], dma=True)
        S.add("gpsimd", "collective_compute", dict(kind="AllGather", op=ALU.bypass, replica_groups=[[0, 1, 2, 3], [4, 5, 6, 7]],
                                                    ins=[send1_t.ap().opt()], outs=[recv1_t.ap().opt()]),
              reads=[xs1_cell], writes=[xr1_cell], own_sem=True)
        for hp in range(4):
            wk, wkc = wblock(hp)
            wvv, wvc = wblock(4 + hp)
            for j in range(2):
                h = 2 * hp + j
                for half in range(2):
                    pb = proj_fm(wk, wkc, j, half)
                    rope(pb, ropeR, half, 128, kT.ap(h * 2 + half), kT.c(h * 2 + half), t1, t2)
            for n in range(8):
                pb = next_ps()
                half, o = n // 4, (n % 4) * 128
                for kc in range(KC):
                    MM(psum[pb][:, 0:256], nT.ap(H(kc, half))[:, o:o + 128], wvv[:, kc, :], kc == 0, kc == KC - 1,
                       [wvc, nT.c(H(kc, half))], [pcell[pb]])
                for j in range(2):
                    h = 2 * hp + j
                    V("scalar", "activation", [pcell[pb]], [vtok.c(n * 8 + h)], out=vtok.ap(n * 8 + h), in_=psum[pb][:, j * 128:(j + 1) * 128], func=AF.Copy)
            for j in range(2):
                h = 2 * hp + j
                gC = float(GAMMA[h] ** 128)
                pb = next_ps()
                pbv = psum[pb][:, :].bitcast(BF16)
                for n in range(8):
                    half, o = n // 4, (n % 4) * 128
                    S.add("tensor", "transpose", dict(out=pbv[:, n * 128:(n + 1) * 128], in_=kT.ap(h * 2 + half)[:, o:o + 128], identity=identb.ap(0)),
                          reads=[kT.c(h * 2 + half), identb.c(0)], writes=[pcell[pb]])
                kzi = h % 2
                V("vector", "tensor_scalar", [pcell[pb], mixc.c(0)], [kz.c(kzi)], out=kz.ap(kzi), in0=pbv, scalar1=zeta[:, h:h + 1], scalar2=None, op0=ALU.mult)
                pbs = [next_ps(), next_ps()]
                for n in range(8):
                    pb2 = pbs[n // 4]
                    o = (n % 4) * 128
                    MM(psum[pb2][:, o:o + 128], kz.ap(kzi)[:, n * 128:(n + 1) * 128], vtok.ap(n * 8 + h), True, True,
                       [kz.c(kzi), vtok.c(n * 8 + h)], [pcell[pb2]])
                ri = 0
                for n in range(8):
                    pb2 = pbs[n // 4]
                    o = (n % 4) * 128
                    dst_ap, dst_c = (Rst.ap(1 - ri), Rst.c(1 - ri)) if n < 7 else (Lst.ap(h), Lst.c(h))
                    if n == 0:
                        V("vector", "tensor_copy", [pcell[pb2]], [dst_c], out=dst_ap, in_=psum[pb2][:, o:o + 128])
                    else:
                        V("vector", "scalar_tensor_tensor", [pcell[pb2], Rst.c(ri)], [dst_c], out=dst_ap, in0=Rst.ap(ri), scalar=gC,
                          in1=psum[pb2][:, o:o + 128], op0=ALU.mult, op1=ALU.add)
                    ri = 1 - ri
                    if n < 7:
                        V("scalar", "activation", [Rst.c(ri)], [Sloc.c(h * 8 + n + 1)], out=Sloc.ap(h * 8 + n + 1), in_=Rst.ap(ri), func=AF.Copy)
        S.add("gpsimd", "dma_start", dict(out=send_t.ap().rearrange("(a p) e -> p a e", p=128), in_=Lst.v[:, 0:8, :]),
              reads=[Lst.c(i) for i in range(8)], writes=[xs_cell], dma=True)
        S.add("gpsimd", "collective_compute", dict(kind="AllGather", op=ALU.bypass, replica_groups=[[0, 1, 2, 3], [4, 5, 6, 7]],
                                                    ins=[send_t.ap().opt()], outs=[recv_t.ap().opt()]),
              reads=[xs_cell], writes=[xr_cell], own_sem=True)
        recv4 = recv_t.ap().rearrange("(r a p) e -> a p r e", r=4, a=8)
        recv1 = recv1_t.ap().rearrange("(r a p) e -> a p r e", r=4, a=2)
        P2 = Arena(S, arena_t, ARENA_BYTES)
        P2.top = common_top
        Sin32 = P2.alloc("Sin32", 8, 128, F32)
        p2_top = P2.top
        rcv = P2.alloc("rcv", 2, 512, F32)
        acc = P2.alloc("acc", 2, 128, F32)
        cf = coef.ap(0)

        def receive(pieces):
            for piece in pieces:
                ri = piece % 2
                src = recv4[piece] if piece < 8 else recv1[piece - 8]
                LD(rcv.ap(ri).rearrange("p (r e) -> p r e", r=4), src, [rcv.c(ri)], reads=[xr_cell if piece < 8 else xr1_cell])
                rv3 = rcv.ap(ri).rearrange("p (r e) -> p r e", r=4)
                for r in range(4):
                    csc = cf[:, r * 8 + piece:r * 8 + piece + 1] if piece < 8 else cf[:, 32 + r:33 + r]
                    if r == 0:
                        V("vector", "tensor_scalar", [rcv.c(ri), coef.c(0)], [acc.c(ri)], out=acc.ap(ri), in0=rv3[:, 0, :], scalar1=csc, scalar2=None, op0=ALU.mult)
                    else:
                        V("vector", "scalar_tensor_tensor", [rcv.c(ri), coef.c(0), acc.c(ri)], [acc.c(ri)], out=acc.ap(ri), in0=rv3[:, r, :], scalar=csc,
                          in1=acc.ap(ri), op0=ALU.mult, op1=ALU.add)
                if piece < 8:
                    V("vector", "tensor_copy", [acc.c(ri)], [Sin32.c(piece)], out=Sin32.ap(piece), in_=acc.ap(ri))
                elif piece == 8:
                    V("vector", "tensor_copy", [acc.c(ri)], [skA.c(0)], out=skA.ap(0)[0:64, :], in_=acc.ap(ri)[0:64, :])
                    V("vector", "tensor_copy", [acc.c(ri)], [skB.c(0)], out=skB.ap(0)[64:128, :], in_=acc.ap(ri)[64:128, :])
                else:
                    V("vector", "tensor_copy", [acc.c(ri)], [svt.c(0)], out=svt.ap(0), in_=acc.ap(ri))

        receive([8, 9])

        W2 = Arena(S, arena_t, ARENA_BYTES)
        W2.top = p2_top
        NB = 4
        sqT = W2.alloc("sqT", 4, TH, BF16)
        Ss = W2.alloc("Ss", NB, 512, F32)
        Pb = W2.alloc("Pb", NB, 512, BF16)
        Pn = Pb
        PT = W2.alloc("PT", NB, 512, BF16)
        st = W2.alloc("st", NB, 16, F32)
        pp = Pool(range(6))
        hp = Pool([6, 7])
        bp = Pool(range(NB))
        wq_state = {}
        po_bank = {}

        def prep(c):
            def g():
                if c % 2 == 0:
                    wq_state["w"] = wblock(9 + c // 2)
                wsq, wsqc = wq_state["w"]
                for half in range(2):
                    pb = pp.get()
                    while pb is None:
                        yield
                        pb = pp.get()
                    for kc in range(KC):
                        MM(psum[pb][:, :], wsq[:, kc, (c % 2) * 128:(c % 2 + 1) * 128], nT.ap(H(kc, half)), kc == 0, kc == KC - 1,
                           [wsqc, nT.c(H(kc, half))], [pcell[pb]])
                    yield
                    rope(pb, ropeS, half, 64, sqT.ap((c % 2) * 2 + half), sqT.c((c % 2) * 2 + half), t1, t2)
                    pp.put(pb)
                    yield
            return g

        def block(c, half, nb):
            def g():
                n = half * 4 + nb
                sq_ = sqT.ap((c % 2) * 2 + half)
                sqc = sqT.c((c % 2) * 2 + half)
                it = bp.get()
                while it is None:
                    yield
                    it = bp.get()
                if (c, half) not in po_bank:
                    po = hp.get()
                    while po is None:
                        yield
                        po = hp.get()
                    po_bank[(c, half)] = po
                po = po_bank[(c, half)]
                ps_ = pp.get()
                while ps_ is None:
                    yield
                    ps_ = pp.get()
                ps3 = psum[ps_][:, :].rearrange("p (i k) -> p i k", i=2)
                for i, skx in enumerate((skA, skB)):
                    MM(ps3[:, i, :], sq_[:, nb * 128:(nb + 1) * 128], skx.flat[:, n * 128:n * 128 + 256], True, True,
                       [sqc, skx.c(n), skx.c(n + 1)], [pcell[ps_]])
                yield
                msk = (mask0.ap(0) if n == 0 else maskg)
                ss3 = Ss.ap(it).rearrange("p (i k) -> p i k", i=2)
                sv_ = st.ap(it)
                V("vector", "scalar_tensor_tensor", [pcell[ps_], mask0.c(0), mixc.c(0)], [Ss.c(it)], out=ss3, in0=ps3, scalar=0.125,
                  in1=msk.unsqueeze(1).to_broadcast([128, 2, 256]), op0=ALU.mult, op1=ALU.add)
                pp.put(ps_)
                V("vector", "tensor_reduce", [Ss.c(it)], [st.c(it)], out=sv_[:, 0:2], in_=ss3, axis=AX.X, op=ALU.max)
                V("vector", "tensor_tensor", [st.c(it), mixc.c(0)], [st.c(it)], out=sv_[:, 2:4], in0=sv_[:, 0:2], in1=sinks[:, 2 * c:2 * c + 2], op=ALU.max)
                V("vector", "tensor_scalar", [st.c(it)], [st.c(it)], out=sv_[:, 4:6], in0=sv_[:, 2:4], scalar1=-1.0, scalar2=None, op0=ALU.mult)
                V("vector", "tensor_tensor", [st.c(it), mixc.c(0)], [st.c(it)], out=sv_[:, 8:10], in0=sinks[:, 2 * c:2 * c + 2], in1=sv_[:, 2:4], op=ALU.subtract)
                yield
                pb3 = Pb.ap(it).rearrange("p (i k) -> p i k", i=2)
                for i in range(2):
                    V("scalar", "activation", [Ss.c(it), st.c(it)], [Pb.c(it), st.c(it)], out=pb3[:, i, :], in_=ss3[:, i, :], func=AF.Exp,
                      bias=sv_[:, 4 + i:5 + i], scale=1.0, accum_out=sv_[:, 6 + i:7 + i])
                V("scalar", "activation", [st.c(it)], [st.c(it)], out=sv_[:, 8:10], in_=sv_[:, 8:10], func=AF.Exp)
                yield
                V("vector", "tensor_tensor", [st.c(it)], [st.c(it)], out=sv_[:, 10:12], in0=sv_[:, 6:8], in1=sv_[:, 8:10], op=ALU.add)
                V("vector", "reciprocal", [st.c(it)], [st.c(it)], out=sv_[:, 12:14], in_=sv_[:, 10:12])
                pn3 = Pn.ap(it).rearrange("p (i k) -> p i k", i=2)
                for i in range(2):
                    V("vector", "tensor_scalar", [Pb.c(it), st.c(it)], [Pn.c(it)], out=pn3[:, i, :], in0=pb3[:, i, :], scalar1=sv_[:, 12 + i:13 + i],
                      scalar2=None, op0=ALU.mult)
                yield
                pt_ = pp.get()
                while pt_ is None:
                    yield
                    pt_ = pp.get()
                ptv = psum[pt_][:, :].bitcast(BF16)[:, 0:512]
                for i in range(2):
                    for kt in range(2):
                        S.add("tensor", "transpose", dict(out=ptv[:, (i * 2 + kt) * 128:(i * 2 + kt + 1) * 128],
                                                           in_=pn3[:, i, kt * 128:(kt + 1) * 128], identity=identb.ap(0)),
                              reads=[Pn.c(it), identb.c(0)], writes=[pcell[pt_]])
                yield
                V("scalar", "activation", [pcell[pt_]], [PT.c(it)], out=PT.ap(it), in_=ptv, func=AF.Copy)
                pp.put(pt_)
                yield
                for i in range(2):
                    for kt in range(2):
                        MM(psum[po][64 * i:64 * i + 64, nb * 128:(nb + 1) * 128], svt.ap(n + kt)[:, 64 * i:64 * i + 64],
                           PT.ap(it)[:, (i * 2 + kt) * 128:(i * 2 + kt + 1) * 128], kt == 0, kt == 1,
                           [svt.c(n + kt), PT.c(it)], [pcell[po]])
                bp.put(it)
            return g

        def finish(c, half):
            def g():
                po = po_bank[(c, half)]
                V("scalar", "activation", [pcell[po]], [aT.c((8 + c) * 2 + half)], out=aT.ap((8 + c) * 2 + half), in_=psum[po][:, :], func=AF.Copy)
                hp.put(po)
                yield
            return g

        tasks = []
        prep_id, fin_ids = {}, {}
        def add_prep(c):
            deps = [prep_id[c - 1]] if c >= 1 else []
            if c >= 2:
                deps += fin_ids[c - 2]
            prep_id[c] = len(tasks)
            tasks.append((prep(c), deps))

        add_prep(0)
        for c in range(8):
            fin_ids[c] = []
            for half in range(2):
                bl = []
                for nb in range(4):
                    bl.append(len(tasks))
                    tasks.append((block(c, half, nb), [prep_id[c]]))
                fin_ids[c].append(len(tasks))
                tasks.append((finish(c, half), bl))
                if half == 0 and c + 1 < 8:
                    add_prep(c + 1)
        run_tasks(tasks, 6)

        receive(list(range(8)))
        W3 = Arena(S, arena_t, ARENA_BYTES)
        W3.top = p2_top
        Ra = Arena(S, arena_t, ARENA_BYTES)
        Ra.top = ropeS.off
        Rb = Arena(S, arena_t, ARENA_BYTES)
        Rb.top = skA.off
        hb = []
        for k_ in range(2):
            A1 = W3 if k_ == 0 else Ra
            A2 = W3 if k_ == 0 else Rb
            d_ = dict(qT=A1.alloc(f"qT{k_}", 2, TH, BF16), qxi=A1.alloc(f"qxi{k_}", 2, TH, BF16), q32=A1.alloc(f"q32{k_}", 1, TH, F32),
                      rstd=A1.alloc(f"rstdg{k_}", 1, TH, F32), sgt=A2.alloc(f"sgt{k_}", 2, TH, F32), sTm=A2.alloc(f"sTm{k_}", 2, TH, BF16),
                      SinS=W3.alloc(f"SinS{k_}", 8, 128, BF16))
            hb.append(d_)
        assert Ra.top <= ropeS.off + ropeS.nbytes and Rb.top <= svt.off + svt.nbytes
        r2 = t1
        mean = t2
        pp = Pool(range(6))
        hbp = Pool(range(2))

        def getbank():
            b_ = pp.get()
            while b_ is None:
                yield None
                b_ = pp.get()
            yield b_

        def head(h):
            def g():
                k_ = hbp.get()
                while k_ is None:
                    yield
                    k_ = hbp.get()
                B_ = hb[k_]
                qT, qxi, q32, rstd, sgt, sTm, SinS = B_["qT"], B_["qxi"], B_["q32"], B_["rstd"], B_["sgt"], B_["sTm"], B_["SinS"]
                r32 = q32
                wq_, wqc = wblock(13 + h)
                for n in range(8):
                    V("scalar", "mul", [Sin32.c(h)], [SinS.c(n)], out=SinS.ap(n), in_=Sin32.ap(h), mul=float(GAMMA[h] ** (128 * n)))
                for half in range(2):
                    pb = None
                    while pb is None:
                        pb = pp.get()
                        if pb is None:
                            yield
                    for kc in range(KC):
                        MM(psum[pb][:, :], wq_[:, kc, 0:128], nT.ap(H(kc, half)), kc == 0, kc == KC - 1, [wqc, nT.c(H(kc, half))], [pcell[pb]])
                    yield
                    rope(pb, ropeR, half, 128, q32.ap(0), q32.c(0), t1, t2)
                    pp.put(pb)
                    V("scalar", "activation", [q32.c(0)], [qT.c(half)], out=qT.ap(half), in_=q32.ap(0), func=AF.Copy)
                    V("vector", "tensor_tensor", [q32.c(0), mixc.c(0)], [qxi.c(half)], out=qxi.ap(half).rearrange("p (n c) -> p n c", n=4),
                      in0=q32.ap(0).rearrange("p (n c) -> p n c", n=4), in1=xi[:, h, :].unsqueeze(1).to_broadcast([128, 4, 128]), op=ALU.mult)
                    yield
                    pg = None
                    while pg is None:
                        pg = pp.get()
                        if pg is None:
                            yield
                    for kc in range(KC):
                        MM(psum[pg][:, :], wq_[:, kc, 128:256], nT.ap(H(kc, half)), kc == 0, kc == KC - 1, [wqc, nT.c(H(kc, half))], [pcell[pg]])
                    yield
                    V("scalar", "activation", [pcell[pg]], [sgt.c(half)], out=sgt.ap(half), in_=psum[pg][:, :], func=AF.Silu)
                    pp.put(pg)
                    yield
                for half in range(2):
                    pss = None
                    while pss is None:
                        pss = pp.get()
                        if pss is None:
                            yield
                    for nb in range(4):
                        MM(psum[pss][:, nb * 128:(nb + 1) * 128], kT.ap(h * 2 + half)[:, nb * 128:(nb + 1) * 128], qT.ap(half)[:, nb * 128:(nb + 1) * 128],
                           True, True, [kT.c(h * 2 + half), qT.c(half)], [pcell[pss]])
                    yield
                    V("vector", "tensor_tensor", [pcell[pss], mixc.c(0)], [sTm.c(half)], out=sTm.ap(half).rearrange("p (n c) -> p n c", n=4),
                      in0=psum[pss][:, :].rearrange("p (n c) -> p n c", n=4), in1=dmat[:, h, :].unsqueeze(1).to_broadcast([128, 4, 128]), op=ALU.mult)
                    pp.put(pss)
                    yield
                    pr = None
                    while pr is None:
                        pr = pp.get()
                        if pr is None:
                            yield
                    for nb in range(4):
                        n = half * 4 + nb
                        oap = psum[pr][:, nb * 128:(nb + 1) * 128]
                        qx = qxi.ap(half)[:, nb * 128:(nb + 1) * 128]
                        MM(oap, vtok.ap(n * 8 + h), sTm.ap(half)[:, nb * 128:(nb + 1) * 128], True, False,
                           [vtok.c(n * 8 + h), sTm.c(half)], [pcell[pr]])
                        if n > 0:
                            MM(oap, Sloc.ap(h * 8 + n), qx, False, False, [Sloc.c(h * 8 + n), qxi.c(half)], [pcell[pr]])
                        MM(oap, SinS.ap(n), qx, False, True, [SinS.c(n), qxi.c(half)], [pcell[pr]])
                    yield
                    p1 = None
                    while p1 is None:
                        p1 = pp.get()
                        if p1 is None:
                            yield
                    p2 = None
                    while p2 is None:
                        p2 = pp.get()
                        if p2 is None:
                            yield
                    V("scalar", "activation", [pcell[pr]], [r32.c(0)], out=r32.ap(0), in_=psum[pr][:, :], func=AF.Copy)
                    V("scalar", "activation", [pcell[pr]], [r2.c(0)], out=r2.ap(0), in_=psum[pr][:, :], func=AF.Square)
                    pp.put(pr)
                    MM(psum[p1][:, :], ones_f.ap(0), r32.ap(0), True, True, [ones_f.c(0), r32.c(0)], [pcell[p1]])
                    MM(psum[p2][:, :], ones_f.ap(0), r2.ap(0), True, True, [ones_f.c(0), r2.c(0)], [pcell[p2]])
                    V("vector", "tensor_scalar", [pcell[p1]], [mean.c(0)], out=mean.ap(0), in0=psum[p1][:, :], scalar1=1.0 / 128, scalar2=None, op0=ALU.mult)
                    V("vector", "tensor_tensor", [mean.c(0)], [rstd.c(0)], out=rstd.ap(0), in0=mean.ap(0), in1=mean.ap(0), op=ALU.mult)
                    V("vector", "scalar_tensor_tensor", [pcell[p2], rstd.c(0)], [rstd.c(0)], out=rstd.ap(0), in0=psum[p2][:, :], scalar=1.0 / 128,
                      in1=rstd.ap(0), op0=ALU.mult, op1=ALU.subtract)
                    pp.put(p1)
                    pp.put(p2)
                    V("scalar", "activation", [rstd.c(0), epsc.c(0)], [rstd.c(0)], out=rstd.ap(0), in_=rstd.ap(0), func=AF.Sqrt, bias=epsc.ap(0)[:, 0:1], scale=1.0)
                    V("vector", "reciprocal", [rstd.c(0)], [rstd.c(0)], out=rstd.ap(0), in_=rstd.ap(0))
                    V("vector", "tensor_tensor", [r32.c(0), mean.c(0)], [r32.c(0)], out=r32.ap(0), in0=r32.ap(0), in1=mean.ap(0), op=ALU.subtract)
                    V("vector", "tensor_tensor", [r32.c(0), rstd.c(0)], [r32.c(0)], out=r32.ap(0), in0=r32.ap(0), in1=rstd.ap(0), op=ALU.mult)
                    V("vector", "scalar_tensor_tensor", [r32.c(0), mixc.c(0), sgt.c(half)], [aT.c(h * 2 + half)], out=aT.ap(h * 2 + half), in0=r32.ap(0),
                      scalar=gng[:, h:h + 1], in1=sgt.ap(half), op0=ALU.mult, op1=ALU.mult)
                    yield
                hbp.put(k_)
            return g

        run_tasks([(head(h), []) for h in range(8)], 2)

        if DEBUG_A:
            for kc in range(KC):
                for half in range(2):
                    V("vector", "tensor_copy", [aT.c(H(kc, half))], [hT.c(H(kc, half))], out=hT.ap(H(kc, half)), in_=aT.ap(H(kc, half)))
        else:
            proj_residual(w_out, aT, reload=hspill)

    def xattn():
        X = Arena(S, arena_t, ARENA_BYTES)
        X.top = phase_base
        qo = X.alloc("qo", 32, TH, BF16)
        kmT = X.alloc("kmT", 16, 256, BF16)
        vm = X.alloc("vm", 2 * 8, 256, BF16)
        mnT = X.alloc("mnT", 16, 256, BF16)
        xtop = X.top
        m32 = X.alloc("m32", 16, 256, F32)
        msq = X.alloc("msq", 2, 256, BF16)
        mrs = X.alloc("mrs", 1, 256, F32)
        if "mixer" not in stages:
            S.add("gpsimd", "dma_start", dict(out=identb.ap(0), in_=ident_d2), writes=[identb.c(0)], dma=True)
        for kc in range(KC):
            LD(m32.ap(kc), memT[kc * 128:(kc + 1) * 128, :], [m32.c(kc)])
        pb = next_ps()
        for kc in range(KC):
            s_ = kc % 2
            V("vector", "tensor_tensor", [m32.c(kc)], [msq.c(s_)], out=msq.ap(s_), in0=m32.ap(kc), in1=m32.ap(kc), op=ALU.mult)
            MM(psum[pb][:, 0:256], ones_b.ap(0), msq.ap(s_), kc == 0, kc == KC - 1, [ones_b.c(0), msq.c(s_)], [pcell[pb]])
        V("vector", "tensor_scalar", [pcell[pb]], [mrs.c(0)], out=mrs.ap(0), in0=psum[pb][:, 0:256], scalar1=1.0 / D, scalar2=EPS, op0=ALU.mult, op1=ALU.add)
        V("scalar", "activation", [mrs.c(0)], [mrs.c(0)], out=mrs.ap(0), in_=mrs.ap(0), func=AF.Sqrt)
        V("vector", "reciprocal", [mrs.c(0)], [mrs.c(0)], out=mrs.ap(0), in_=mrs.ap(0))
        for kc in range(KC):
            V("vector", "scalar_tensor_tensor", [m32.c(kc), gn.c(0), mrs.c(0)], [mnT.c(kc)], out=mnT.ap(kc), in0=m32.ap(kc),
              scalar=gn.ap(0)[:, 3 * KC + kc:3 * KC + kc + 1], in1=mrs.ap(0), op0=ALU.mult, op1=ALU.mult)
        for blk in range(8):
            wt, ci = next_wcell()
            wv = wt.v[:, ci, :].rearrange("p (k c) -> p k c", k=KC)
            wdma(wv, wkv_d[:, blk * 256:(blk + 1) * 256], KC, [wt.c(ci)])
            for j in range(2):
                pb = next_ps()
                for kc in range(KC):
                    MM(psum[pb][:, 0:256], wv[:, kc, j * 128:(j + 1) * 128], mnT.ap(kc), kc == 0, kc == KC - 1, [wt.c(ci), mnT.c(kc)], [pcell[pb]])
                V("scalar", "activation", [pcell[pb]], [kmT.c(blk * 2 + j)], out=kmT.ap(blk * 2 + j), in_=psum[pb][:, 0:256], func=AF.Copy)
        for blk in range(8):
            wt, ci = next_wcell()
            wv = wt.v[:, ci, :].rearrange("p (k c) -> p k c", k=KC)
            wdma(wv, wkv_d[:, D + blk * 256:D + (blk + 1) * 256], KC, [wt.c(ci)])
            for mt in range(2):
                pb = next_ps()
                for kc in range(KC):
                    MM(psum[pb][:, 0:256], mnT.ap(kc)[:, mt * 128:(mt + 1) * 128], wv[:, kc, :], kc == 0, kc == KC - 1, [wt.c(ci), mnT.c(kc)], [pcell[pb]])
                V("scalar", "activation", [pcell[pb]], [vm.c(mt * 8 + blk)], out=vm.ap(mt * 8 + blk), in_=psum[pb][:, 0:256], func=AF.Copy)
        rmsnorm(2, phase_base)
        for blk in range(8):
            wt, ci = next_wcell()
            wv = wt.v[:, ci, :].rearrange("p (k c) -> p k c", k=KC)
            wdma(wv, wq_d[:, blk * 256:(blk + 1) * 256], KC, [wt.c(ci)])
            for j in range(2):
                for half in range(2):
                    pb = next_ps()
                    for kc in range(KC):
                        MM(psum[pb][:, :], wv[:, kc, j * 128:(j + 1) * 128], nT.ap(H(kc, half)), kc == 0, kc == KC - 1, [wt.c(ci), nT.c(H(kc, half))], [pcell[pb]])
                    qc = H(blk * 2 + j, half)
                    V("scalar", "activation", [pcell[pb]], [qo.c(qc)], out=qo.ap(qc), in_=psum[pb][:, :], func=AF.Copy)
        X2 = Arena(S, arena_t, ARENA_BYTES)
        X2.top = xtop
        NX = 3
        Px = X2.alloc("Px", NX, 256, BF16)
        Pnx = X2.alloc("Pnx", NX, 256, BF16)
        stx = X2.alloc("stx", NX, 8, F32)
        PTx = X2.alloc("PTx", 2, 1024, BF16)
        SC = float(512 ** -0.5)
        pp = Pool(range(6))
        xbp = Pool(range(NX))
        ptp = Pool(range(2))
        pt_slot = {}

        def xtile(hd, half, tq):
            def g():
                it = xbp.get()
                while it is None:
                    yield
                    it = xbp.get()
                if (hd, half) not in pt_slot:
                    sl = ptp.get()
                    while sl is None:
                        yield
                        sl = ptp.get()
                    pt_slot[(hd, half)] = sl
                sl = pt_slot[(hd, half)]
                ptx3 = PTx.ap(sl).rearrange("p (m q) -> p m q", m=2)
                ps_ = pp.get()
                while ps_ is None:
                    yield
                    ps_ = pp.get()
                for cc in range(4):
                    MM(psum[ps_][:, 0:256], qo.ap(H(hd * 4 + cc, half))[:, tq * 128:(tq + 1) * 128], kmT.ap(hd * 4 + cc), cc == 0, cc == 3,
                       [qo.c(H(hd * 4 + cc, half)), kmT.c(hd * 4 + cc)], [pcell[ps_]])
                yield
                sv_ = stx.ap(it)
                V("vector", "tensor_reduce", [pcell[ps_]], [stx.c(it)], out=sv_[:, 0:1], in_=psum[ps_][:, 0:256], axis=AX.X, op=ALU.max)
                V("vector", "tensor_scalar", [stx.c(it)], [stx.c(it)], out=sv_[:, 1:2], in0=sv_[:, 0:1], scalar1=-SC, scalar2=None, op0=ALU.mult)
                yield
                V("scalar", "activation", [pcell[ps_], stx.c(it)], [Px.c(it), stx.c(it)], out=Px.ap(it), in_=psum[ps_][:, 0:256], func=AF.Exp,
                  bias=sv_[:, 1:2], scale=SC, accum_out=sv_[:, 2:3])
                pp.put(ps_)
                yield
                V("vector", "reciprocal", [stx.c(it)], [stx.c(it)], out=sv_[:, 3:4], in_=sv_[:, 2:3])
                V("vector", "tensor_scalar", [Px.c(it), stx.c(it)], [Pnx.c(it)], out=Pnx.ap(it), in0=Px.ap(it), scalar1=sv_[:, 3:4], scalar2=None, op0=ALU.mult)
                yield
                pt_ = pp.get()
                while pt_ is None:
                    yield
                    pt_ = pp.get()
                ptv = psum[pt_][:, :].bitcast(BF16)[:, 0:256]
                for mt in range(2):
                    S.add("tensor", "transpose", dict(out=ptv[:, mt * 128:(mt + 1) * 128], in_=Pnx.ap(it)[:, mt * 128:(mt + 1) * 128], identity=identb.ap(0)),
                          reads=[Pnx.c(it), identb.c(0)], writes=[pcell[pt_]])
                yield
                V("scalar", "activation", [pcell[pt_]], [PTx.c(sl)], out=ptx3[:, :, tq * 128:(tq + 1) * 128],
                  in_=ptv.rearrange("p (m q) -> p m q", m=2), func=AF.Copy)
                pp.put(pt_)
                xbp.put(it)
            return g

        def xfin(hd, half):
            def g():
                sl = pt_slot[(hd, half)]
                ptx3 = PTx.ap(sl).rearrange("p (m q) -> p m q", m=2)
                for cc in range(4):
                    dchunk = hd * 4 + cc
                    po = pp.get()
                    while po is None:
                        yield
                        po = pp.get()
                    for mt in range(2):
                        MM(psum[po][:, :], vm.ap(mt * 8 + dchunk // 2)[:, (dchunk % 2) * 128:(dchunk % 2 + 1) * 128], ptx3[:, mt, :], mt == 0, mt == 1,
                           [vm.c(mt * 8 + dchunk // 2), PTx.c(sl)], [pcell[po]])
                    yield
                    V("scalar", "activation", [pcell[po]], [qo.c(H(dchunk, half))], out=qo.ap(H(dchunk, half)), in_=psum[po][:, :], func=AF.Copy)
                    pp.put(po)
                ptp.put(sl)
            return g

        xt = []
        for hd in range(4):
            for half in range(2):
                ids = []
                for tq in range(4):
                    ids.append(len(xt))
                    xt.append((xtile(hd, half, tq), []))
                xt.append((xfin(hd, half), ids))
        run_tasks(xt, 4)
        proj_residual(wo_d, qo)

    hsp_cell = [S.cell(f"hsp{i}") for i in range(32)]
    xs_cell = S.cell("xsend")
    xr_cell = S.cell("xrecv")
    xs1_cell = S.cell("xsend1")
    xr1_cell = S.cell("xrecv1")
    if "ffn1" in stages:
        rmsnorm(0, phase_base)
        ffn(wg1, wu1, wd1, phase_base)
    if "mixer" in stages:
        mixer()
    if "xattn" in stages:
        xattn()
    if "ffn2" in stages:
        rmsnorm(4, phase_base)
        ffn(wg2, wu2, wd2, phase_base)
    if "final" in stages:
        final = rmsnorm(5, phase_base, to_out=True)
    else:
        final = []
        for kc in range(KC):
            for half in range(2):
                final.append(LD(outT[kc * 128:(kc + 1) * 128, half * TH:(half + 1) * TH], hT.ap(H(kc, half)), [], reads=[hT.c(H(kc, half))]))
    S.add("sync", "nop", dict(), after=final)
    S.emit(nc, es)
    es.close()
    return nc


_CACHE = {}


def _gain_cols(v):
    return np.ascontiguousarray(np.asarray(v, np.float32).reshape(KC, 128).T)


def _rope_table(d, pos):
    f32 = np.float32
    inv = (f32(10000.0) ** (-np.arange(0, d, 2, dtype=f32) / f32(d))).astype(f32)
    ang = (pos.astype(f32)[None, :] * inv[:, None]).astype(f32)
    cos = np.cos(ang.astype(np.float64)).astype(f32)
    sin = np.sin(ang.astype(np.float64)).astype(f32)
    p = np.arange(128)
    pp = p % d
    fi = pp % (d // 2)
    sign = np.where(pp < d // 2, -1.0, 1.0).astype(f32)
    return np.ascontiguousarray(np.concatenate([cos[fi], sin[fi] * sign[:, None]], axis=1), f32)


def _mix_consts(ret_gn_gain, swa_sinks):
    g = np.array(GAMMA, np.float64)
    j = np.arange(128)[:, None]
    c = np.arange(128)[None, :]
    m = np.zeros((128, MIXC_W), np.float32)
    sc = 128.0 ** -0.5
    for h in range(8):
        dm = np.where(c >= j, sc * g[h] ** np.maximum(c - j, 0), 0.0)
        m[:, MC_DMAT + h * 128:MC_DMAT + (h + 1) * 128] = dm
        m[:, MC_XI + h * 128:MC_XI + (h + 1) * 128] = (g[h] ** (np.arange(128) + 1))[None, :]
        m[:, MC_ZETA + h] = sc * g[h] ** (127 - np.arange(128))
    m[:, MC_GNG:MC_GNG + 8] = np.asarray(ret_gn_gain, np.float32).reshape(8, 128).T
    sk = np.asarray(swa_sinks, np.float32).reshape(16)
    order = [cc + 8 * i for cc in range(8) for i in range(2)]
    m[:, MC_SINK:MC_SINK + 16] = sk[order][None, :]
    i = np.arange(128)[:, None]
    kk = np.arange(256)[None, :]
    m[:, MC_MASK:MC_MASK + 256] = np.where((kk > i) & (kk <= i + 128), 0.0, NEG)
    return m


def _w_in_perm():
    cols = []
    for h in range(8):
        cols += list(range(1024 + h * 128, 1024 + (h + 1) * 128))
    for h in range(8):
        cols += list(range(2048 + h * 128, 2048 + (h + 1) * 128))
    cols += list(range(5120, 5376))
    for c in range(8):
        cols += list(range(4096 + c * 64, 4096 + (c + 1) * 64)) + list(range(4096 + (c + 8) * 64, 4096 + (c + 9) * 64))
    for h in range(8):
        cols += list(range(h * 128, (h + 1) * 128)) + list(range(3072 + h * 128, 3072 + (h + 1) * 128))
    return np.array(cols)


def _w_out_perm():
    rows = list(range(1024))
    for c in range(8):
        rows += list(range(1024 + c * 64, 1024 + (c + 1) * 64)) + list(range(1024 + (c + 8) * 64, 1024 + (c + 9) * 64))
    return np.array(rows)


def kernel(x, mem, ffn1_norm, ffn1_w_gate, ffn1_w_up, ffn1_w_down, mix_norm, w_in, ret_gn_gain,
           swa_sinks, w_out, xa_norm, mem_norm, xa_wq, xa_wkv, xa_wo, ffn2_norm, ffn2_w_gate,
           ffn2_w_up, ffn2_w_down, final_norm):
    st = tuple(STAGES)
    x = np.asarray(x, np.float32)
    mem = np.asarray(mem, np.float32)
    key = ("nc", st)
    if key not in _CACHE:
        _CACHE[key] = build_program(st)
    nc = _CACHE[key]
    f = lambda a: np.ascontiguousarray(np.asarray(a, np.float32)[0])
    gains = np.concatenate([_gain_cols(np.asarray(g).reshape(-1)) for g in
                            (ffn1_norm, mix_norm, xa_norm, mem_norm, ffn2_norm, final_norm)], axis=1)
    shared = {"gains": np.ascontiguousarray(gains, np.float32)}
    if "ffn1" in st:
        shared.update(ffn1_w_gate=f(ffn1_w_gate), ffn1_w_up=f(ffn1_w_up), ffn1_w_down=f(ffn1_w_down))
    if "ffn2" in st:
        shared.update(ffn2_w_gate=f(ffn2_w_gate), ffn2_w_up=f(ffn2_w_up), ffn2_w_down=f(ffn2_w_down))
    if "mixer" in st:
        shared["w_in_p"] = np.ascontiguousarray(f(w_in)[:, _w_in_perm()])
        shared["w_out_p"] = np.ascontiguousarray(f(w_out)[_w_out_perm(), :])
        shared["mixc"] = _mix_consts(np.asarray(ret_gn_gain).reshape(-1), np.asarray(swa_sinks).reshape(-1))
    if "mixer" in st or "xattn" in st:
        shared["ident"] = np.eye(128, dtype=np.float32)
    if "xattn" in st:
        shared.update(xa_wq=f(xa_wq), xa_wkv=f(xa_wkv), xa_wo=f(xa_wo))
    in_maps = []
    g64 = np.array(GAMMA, np.float64)
    for c in range(NCORES):
        b, q = c // 4, c % 4
        m = dict(shared)
        m["xT"] = np.ascontiguousarray(x[b, q * T:(q + 1) * T, :].T)
        if "mixer" in st:
            pos = np.arange(q * T, (q + 1) * T)
            m["ropeR"] = _rope_table(128, pos)
            m["ropeS"] = _rope_table(64, pos)
            cf = np.zeros((128, 40), np.float32)
            for r in range(4):
                if r < q:
                    cf[:, r * 8:(r + 1) * 8] = (g64 ** (1024 * (q - 1 - r)))[None, :]
                if r == q - 1:
                    cf[:, 32 + r] = 1.0
            m["coef"] = cf
            mk = shared["mixc"][:, MC_MASK:MC_MASK + 256].copy()
            if q == 0:
                mk[:, 0:128] = NEG
            m["mask0"] = np.ascontiguousarray(mk)
        if "xattn" in st:
            m["memT"] = np.ascontiguousarray(mem[b].T)
        in_maps.append(m)
    res = run_bass_kernel_spmd(nc, in_maps, core_ids=list(range(NCORES)))
    out = np.empty((2, 4096, D), np.float32)
    for c in range(NCORES):
        b, q = c // 4, c % 4
        out[b, q * T:(q + 1) * T, :] = res.results[c]["outT"].T
    return out
```
